# Optimizing a Trainium2 kernel written in Bass

```python
import math
import jax, jax.numpy as jnp
from jax import lax
import numpy as np

D_MODEL = 1024
BATCH = 16
SEQ = 2048
DEPTH = 2
DEC_BATCH = 32
DEC_SEQ = 2048
PAST_LEN = 128

GRID_W = 64
NA_HEADS = 8
NA_HEAD_DIM = 64
NA_W = NA_HEADS * NA_HEAD_DIM
NA_WIN_R = 8
NA_WIN_C = 16
NA_QBLK_C = 16
NA_KBAND_C = 32
ML_HEADS = 4
ML_HEAD_DIM = 128
ML_W = ML_HEADS * ML_HEAD_DIM
ML_CHUNK = 128
EVEN_MIX_W = NA_W + ML_W
EVEN_SPLITS = tuple(int(s) for s in np.cumsum([NA_W] * 3 + [ML_W] * 4))
EVEN_IN = 3 * NA_W + 4 * ML_W + 4 * ML_HEADS
DA_HEADS = 16
DA_HEAD_DIM = 64
DA_W = DA_HEADS * DA_HEAD_DIM
DA_BRANCHES = ((128, 1), (512, 4), (2048, 16))
DA_QBLK = 64
ROPE_THETA = 10000.0
N_EXPERTS = 16
EC_CAPACITY_FACTOR = 2
D_FF_EXPERT = 1408
LN_EPS = 1e-5
DN_ALPHA = (2 * DEPTH) ** 0.25
DN_BETA = (8 * DEPTH) ** -0.25
N_EVEN = (DEPTH + 1) // 2
N_ODD = DEPTH // 2

kernel_name = "hybrid_natten_mlstm_dilated_ec_encoder"


def layer_norm(x, g, b):
    xf = x.astype(jnp.float32)
    mu = xf.mean(-1, keepdims=True)
    var = jnp.square(xf - mu).mean(-1, keepdims=True)
    return ((xf - mu) * lax.rsqrt(var + LN_EPS) * g + b).astype(x.dtype)


def to_heads(a, h):
    B, T, _ = a.shape
    return a.reshape(B, T, h, -1).transpose(0, 2, 1, 3)


def from_heads(a):
    B, H, T, d = a.shape
    return a.transpose(0, 2, 1, 3).reshape(B, T, H * d)


def rope(x):
    T, d = x.shape[2], x.shape[3]
    inv = ROPE_THETA ** (-jnp.arange(0, d, 2, dtype=jnp.float32) / d)
    ang = jnp.arange(T, dtype=jnp.float32)[:, None] * inv[None, :]
    cos, sin = jnp.cos(ang), jnp.sin(ang)
    xf = x.astype(jnp.float32)
    x1, x2 = xf[..., : d // 2], xf[..., d // 2:]
    return jnp.concatenate([x1 * cos - x2 * sin, x1 * sin + x2 * cos], -1).astype(x.dtype)


def neighbourhood_attention(q, k, v, rpb):
    B, H, T, dh = q.shape
    rows = T // GRID_W
    wr = min(NA_WIN_R, rows)
    n_cb = GRID_W // NA_QBLK_C
    qg = (q * dh ** -0.5).reshape(B, H, rows, n_cb, NA_QBLK_C, dh)
    kg = k.reshape(B, H, rows, GRID_W, dh)
    vg = v.reshape(B, H, rows, GRID_W, dh)
    band_start = np.clip(np.arange(n_cb) * NA_QBLK_C - NA_WIN_C // 2, 0, GRID_W - NA_KBAND_C)
    key_col = band_start[:, None] + np.arange(NA_KBAND_C)[None, :]
    q_col = np.arange(GRID_W).reshape(n_cb, NA_QBLK_C)
    win_c0 = np.clip(q_col - NA_WIN_C // 2, 0, GRID_W - NA_WIN_C)
    col_valid = (key_col[:, None, :] >= win_c0[:, :, None]) & (key_col[:, None, :] < win_c0[:, :, None] + NA_WIN_C)
    dc = np.clip(key_col[:, None, :] - q_col[:, :, None] + NA_WIN_C - 1, 0, 2 * NA_WIN_C - 2)
    rpb_c = rpb[:, :, dc].astype(jnp.float32)
    kb = kg[:, :, :, key_col]
    vb = vg[:, :, :, key_col]
    mask = col_valid[None, None, :, :, None, :]

    def one_row(r):
        rs = jnp.clip(r - wr // 2, 0, rows - wr)
        k_r = lax.dynamic_slice_in_dim(kb, rs, wr, axis=2)
        v_r = lax.dynamic_slice_in_dim(vb, rs, wr, axis=2)
        q_r = lax.dynamic_index_in_dim(qg, r, axis=2, keepdims=False)
        s = jnp.einsum('bhcqd,bhwckd->bhcqwk', q_r, k_r, preferred_element_type=jnp.float32)
        dr = rs + jnp.arange(wr) - r + NA_WIN_R - 1
        bias = jnp.take(rpb_c, dr, axis=1).transpose(0, 2, 3, 1, 4)
        s = jnp.where(mask, s + bias[None], -jnp.inf)
        p = jax.nn.softmax(s.reshape(B, H, n_cb, NA_QBLK_C, wr * NA_KBAND_C), axis=-1).reshape(s.shape)
        return jnp.einsum('bhcqwk,bhwckd->bhcqd', p.astype(v.dtype), v_r)

    out = lax.map(one_row, jnp.arange(rows))
    return out.transpose(1, 2, 0, 3, 4, 5).reshape(B, H, T, dh)


def mlstm_direction(q, k, v, log_i, log_f):
    B, H, T, d = q.shape
    L = ML_CHUNK
    nc = T // L

    def chunks(a):
        return jnp.moveaxis(a.reshape(B, H, nc, L, *a.shape[3:]), 2, 0)

    lower = np.tril(np.ones((L, L), dtype=bool))

    def step(carry, inp):
        C, n, m = carry
        qt, kt, vt, it, ft = inp
        b = jnp.cumsum(ft, axis=-1)
        log_d = jnp.where(lower, b[..., :, None] - b[..., None, :] + it[..., None, :], -jnp.inf)
        log_inter = b + m[..., None]
        m_t = jnp.maximum(log_inter, log_d.max(-1))
        w_inter = jnp.exp(log_inter - m_t)
        s = jnp.einsum('bhtd,bhsd->bhts', qt, kt) * jnp.exp(log_d - m_t[..., None])
        num = w_inter[..., None] * jnp.einsum('bhtd,bhde->bhte', qt, C) + jnp.einsum('bhts,bhse->bhte', s, vt)
        den = w_inter * jnp.einsum('bhtd,bhd->bht', qt, n) + s.sum(-1)
        h = num / jnp.maximum(jnp.abs(den), jnp.exp(-m_t))[..., None]
        b_last = b[..., -1]
        log_w = b_last[..., None] - b + it
        m_new = jnp.maximum(b_last + m, log_w.max(-1))
        w_s = jnp.exp(log_w - m_new[..., None])
        decay = jnp.exp(b_last + m - m_new)
        kw = kt * w_s[..., None]
        C_new = decay[..., None, None] * C + jnp.einsum('bhsd,bhse->bhde', kw, vt)
        n_new = decay[..., None] * n + kw.sum(2)
        return (C_new, n_new, m_new), h

    init = (jnp.zeros((B, H, d, d), jnp.float32), jnp.zeros((B, H, d), jnp.float32), jnp.zeros((B, H), jnp.float32))
    _, h = lax.scan(step, init, (chunks(q), chunks(k), chunks(v), chunks(log_i), chunks(log_f)))
    return jnp.moveaxis(h, 0, 2).reshape(B, H, T, d)


def mlstm_bidirectional(q, k, v, i_fwd, i_bwd, f_fwd, f_bwd):
    h_f = mlstm_direction(q, k, v, i_fwd, jax.nn.log_sigmoid(f_fwd))
    fl = lambda a: jnp.flip(a, axis=2)
    h_b = fl(mlstm_direction(fl(q), fl(k), fl(v), fl(i_bwd), fl(jax.nn.log_sigmoid(f_bwd))))
    return h_f + h_b


def even_mixer(x, w_in, gate_bias, rpb, norm_g, w_out):
    B, T, _ = x.shape
    proj = x @ w_in
    qa, ka, va, qb, kb, vb, ob, gates = jnp.split(proj, EVEN_SPLITS, axis=-1)
    ya = from_heads(neighbourhood_attention(to_heads(qa, NA_HEADS), to_heads(ka, NA_HEADS), to_heads(va, NA_HEADS), rpb))
    g = (gates + gate_bias).astype(jnp.float32).reshape(B, T, 4, ML_HEADS).transpose(2, 0, 3, 1)
    qm = to_heads(qb, ML_HEADS).astype(jnp.float32)
    km = to_heads(kb, ML_HEADS).astype(jnp.float32) * ML_HEAD_DIM ** -0.5
    vm = to_heads(vb, ML_HEADS).astype(jnp.float32)
    h = mlstm_bidirectional(qm, km, vm, g[0], g[1], g[2], g[3])
    mu = h.mean(-1, keepdims=True)
    var = jnp.square(h - mu).mean(-1, keepdims=True)
    h = from_heads((h - mu) * lax.rsqrt(var + LN_EPS)) * norm_g
    yb = (jax.nn.sigmoid(ob.astype(jnp.float32)) * h).astype(x.dtype)
    return jnp.concatenate([ya, yb], axis=-1) @ w_out


def dilated_branch(q, k, v, window, dil):
    B, H, T, dh = q.shape
    half = window // (2 * dil)
    n = T // dil
    Q = DA_QBLK
    nb = -(-n // Q)
    nh = -(-half // Q)
    KB = (2 * nh + 1) * Q
    pad_q = nb * Q - n
    sub = lambda a: a.reshape(B, H, n, dil, dh).transpose(0, 1, 3, 2, 4)
    qp = jnp.pad(sub(q), ((0, 0),) * 3 + ((0, pad_q), (0, 0)))
    kp = jnp.pad(sub(k), ((0, 0),) * 3 + ((nh * Q, pad_q + nh * Q), (0, 0)))
    vp = jnp.pad(sub(v), ((0, 0),) * 3 + ((nh * Q, pad_q + nh * Q), (0, 0)))

    def block(j):
        q_j = lax.dynamic_slice_in_dim(qp, j * Q, Q, axis=3)
        k_j = lax.dynamic_slice_in_dim(kp, j * Q, KB, axis=3)
        v_j = lax.dynamic_slice_in_dim(vp, j * Q, KB, axis=3)
        qi = j * Q + jnp.arange(Q)
        ki = j * Q - nh * Q + jnp.arange(KB)
        valid = (jnp.abs(ki[None, :] - qi[:, None]) <= half) & (ki[None, :] >= 0) & (ki[None, :] < n)
        s = jnp.einsum('bhgqd,bhgkd->bhgqk', q_j, k_j, preferred_element_type=jnp.float32)
        s = jnp.where(valid, s, -jnp.inf)
        lse = jax.nn.logsumexp(s, axis=-1)
        o = jnp.einsum('bhgqk,bhgkd->bhgqd', jnp.exp(s - lse[..., None]).astype(v.dtype), v_j)
        return o, lse

    o, lse = lax.map(block, jnp.arange(nb))
    o = jnp.moveaxis(o, 0, 3).reshape(B, H, dil, nb * Q, dh)[:, :, :, :n]
    lse = jnp.moveaxis(lse, 0, 3).reshape(B, H, dil, nb * Q)[:, :, :, :n]
    return o.transpose(0, 1, 3, 2, 4).reshape(B, H, T, dh), lse.transpose(0, 1, 3, 2).reshape(B, H, T)


def odd_mixer(x, w_in, w_out):
    q, k, v = jnp.split(x @ w_in, 3, axis=-1)
    q = rope(to_heads(q, DA_HEADS)) * DA_HEAD_DIM ** -0.5
    k = rope(to_heads(k, DA_HEADS))
    v = to_heads(v, DA_HEADS)
    res = [dilated_branch(q, k, v, w, d) for (w, d) in DA_BRANCHES]
    wts = jax.nn.softmax(jnp.stack([r[1] for r in res]), axis=0)
    o = jnp.einsum('gbht,gbhtd->bhtd', wts, jnp.stack([r[0] for r in res]).astype(jnp.float32))
    return from_heads(o.astype(x.dtype)) @ w_out


def expert_choice_ffn(x, w_router, w_gate, w_up, w_down):
    B, T, D = x.shape
    N = B * T
    cap = EC_CAPACITY_FACTOR * N // N_EXPERTS
    xt = x.reshape(N, D)
    aff = jax.nn.softmax((xt @ w_router).astype(jnp.float32), axis=-1)
    g, idx = lax.top_k(aff.T, cap)
    xe = xt[idx]
    h = jax.nn.silu(jnp.einsum('ecd,edf->ecf', xe, w_gate)) * jnp.einsum('ecd,edf->ecf', xe, w_up)
    ye = jnp.einsum('ecf,efd->ecd', h, w_down) * g[..., None].astype(x.dtype)
    y = jnp.zeros_like(xt).at[idx.reshape(-1)].add(ye.reshape(-1, D))
    return y.reshape(B, T, D)


def setup_inputs(seed: int = 0) -> dict:
    key = jax.random.key(seed)
    ks = jax.random.split(key, 20)
    nrm = lambda k, shape, s: jax.random.normal(k, shape, jnp.float32) * s
    f_bias = jnp.tile(jnp.linspace(3.0, 6.0, ML_HEADS, dtype=jnp.float32), 2)
    gate_base = jnp.concatenate([jnp.zeros((2 * ML_HEADS,), jnp.float32), f_bias])
    return {
        "x_prompt": nrm(ks[0], (BATCH, SEQ, D_MODEL), 1.0),
        "x_sample": nrm(ks[1], (DEC_BATCH, DEC_SEQ, D_MODEL), 1.0),
        "even_w_in": nrm(ks[2], (N_EVEN, D_MODEL, EVEN_IN), D_MODEL ** -0.5),
        "ml_gate_bias": gate_base[None, :] + nrm(ks[3], (N_EVEN, 4 * ML_HEADS), 0.1),
        "na_rpb": nrm(ks[4], (N_EVEN, NA_HEADS, 2 * NA_WIN_R - 1, 2 * NA_WIN_C - 1), 0.3),
        "ml_norm_g": 1.0 + nrm(ks[5], (N_EVEN, ML_W), 0.05),
        "even_w_out": nrm(ks[6], (N_EVEN, EVEN_MIX_W, D_MODEL), EVEN_MIX_W ** -0.5 * DN_BETA),
        "da_w_in": nrm(ks[7], (N_ODD, D_MODEL, 3 * DA_W), D_MODEL ** -0.5),
        "da_w_out": nrm(ks[8], (N_ODD, DA_W, D_MODEL), DA_W ** -0.5 * DN_BETA),
        "ln_mix_g": 1.0 + nrm(ks[9], (DEPTH, D_MODEL), 0.05),
        "ln_mix_b": nrm(ks[10], (DEPTH, D_MODEL), 0.02),
        "ec_router": nrm(ks[11], (DEPTH, D_MODEL, N_EXPERTS), D_MODEL ** -0.5),
        "ec_w_gate": nrm(ks[12], (DEPTH, N_EXPERTS, D_MODEL, D_FF_EXPERT), D_MODEL ** -0.5),
        "ec_w_up": nrm(ks[13], (DEPTH, N_EXPERTS, D_MODEL, D_FF_EXPERT), D_MODEL ** -0.5),
        "ec_w_down": nrm(ks[14], (DEPTH, N_EXPERTS, D_FF_EXPERT, D_MODEL), D_FF_EXPERT ** -0.5 * DN_BETA),
        "ln_ffn_g": 1.0 + nrm(ks[15], (DEPTH, D_MODEL), 0.05),
        "ln_ffn_b": nrm(ks[16], (DEPTH, D_MODEL), 0.02),
    }


def reference(x_prompt, x_sample, even_w_in, ml_gate_bias, na_rpb, ml_norm_g, even_w_out, da_w_in, da_w_out,
              ln_mix_g, ln_mix_b, ec_router, ec_w_gate, ec_w_up, ec_w_down, ln_ffn_g, ln_ffn_b):
    def trunk(x):
        for layer in range(DEPTH):
            if layer % 2 == 0:
                e = layer // 2
                mix = even_mixer(x, even_w_in[e], ml_gate_bias[e], na_rpb[e], ml_norm_g[e], even_w_out[e])
            else:
                o = layer // 2
                mix = odd_mixer(x, da_w_in[o], da_w_out[o])
            x = layer_norm(DN_ALPHA * x + mix, ln_mix_g[layer], ln_mix_b[layer])
            ffn = expert_choice_ffn(x, ec_router[layer], ec_w_gate[layer], ec_w_up[layer], ec_w_down[layer])
            x = layer_norm(DN_ALPHA * x + ffn, ln_ffn_g[layer], ln_ffn_b[layer])
        return x

    y_prompt = trunk(x_prompt)
    y_sample = trunk(x_sample)
    return (y_prompt, y_sample)
```

```python
from concourse.bass_utils import run_bass_kernel_spmd
import numpy as np
import concourse.bass as bass
import concourse.mybir as mybir

F32 = mybir.dt.float32
BF16 = mybir.dt.bfloat16
I32 = mybir.dt.int32
U32 = mybir.dt.uint32
ALU = mybir.AluOpType
AF = mybir.ActivationFunctionType
AX = mybir.AxisListType


class Buf:
    __slots__ = ("name", "writers", "readers")

    def __init__(self, name=""):
        self.name = name
        self.writers = {}
        self.readers = []


class EngS:
    def __init__(self, name, self_wait):
        self.name = name
        self.ops = []
        self.self_wait = self_wait
        self.known = {}
        self.n = 0


class Ctx:
    ENG = ("pe", "act", "dve", "pool", "sp")

    def __init__(self, nc, dma_pool=8):
        self.nc = nc
        self.E = {
            "pe": EngS("pe", False),
            "act": EngS("act", True),
            "dve": EngS("dve", True),
            "pool": EngS("pool", True),
            "sp": EngS("sp", False),
        }
        self.sems = {}
        self.stack = []
        cm = nc.semaphore("s_cc")
        self.sems["cc"] = cm.__enter__()
        self.stack.append(cm)
        self.cc_n = 0
        self.cc_scr = None
        for e in self.ENG:
            cm = nc.semaphore("s_" + e)
            self.sems[e] = cm.__enter__()
            self.stack.append(cm)
        self.dma_pool = {}
        self.dma_tot = {}
        self.dma_i = {}
        for q in ("sp", "pool", "act"):
            lst = []
            for i in range(dma_pool):
                cm = nc.semaphore("d_%s%d" % (q, i))
                lst.append(cm.__enter__())
                self.stack.append(cm)
            self.dma_pool[q] = lst
            self.dma_i[q] = 0
        self.ctxs = []

    def sb(self, name, shape, dt):
        cm = self.nc.sbuf_tensor(name, list(shape), dt)
        t = cm.__enter__()
        self.ctxs.append(cm)
        return t

    def ps(self, name, shape, dt):
        cm = self.nc.psum_tensor(name, list(shape), dt)
        t = cm.__enter__()
        self.ctxs.append(cm)
        return t

    def _need(self, es, ev, waits):
        key, val = ev
        if key == es.name and not es.self_wait:
            return
        if es.known.get(key, -1) >= val:
            return
        es.known[key] = val
        waits.append(ev)

    def _deps(self, es, reads, writes):
        waits = []
        for b in reads:
            for ev in b.writers.values():
                self._need(es, ev, waits)
        for b in writes:
            for ev in b.writers.values():
                self._need(es, ev, waits)
            for ev in b.readers:
                self._need(es, ev, waits)
        return waits

    def _mark(self, ev, reads, writes):
        for b in reads:
            b.readers.append(ev)
            if len(b.readers) > 64:
                b.readers = b.readers[-48:]
        for b in writes:
            b.writers = {ev[0]: ev}
            b.readers = []

    def op(self, eng, fn, reads=(), writes=()):
        es = self.E[eng]
        waits = self._deps(es, reads, writes)
        es.n += 1
        ev = (eng, es.n)
        es.ops.append((fn, waits, ("eng", None)))
        self._mark(ev, reads, writes)
        return ev

    def dma(self, q, fn, reads=(), writes=()):
        es = self.E[q]
        pool = self.dma_pool[q]
        i = self.dma_i[q]
        self.dma_i[q] = i + 1
        sem = pool[i % len(pool)]
        skey = ("dma", q, i % len(pool))
        prev = self.dma_tot.get(skey, 0)
        waits = self._deps(es, reads, writes)
        if prev > 0:
            self._need(es, (skey, prev), waits)
        tot = prev + 16
        self.dma_tot[skey] = tot
        es.n += 1
        es.ops.append((fn, waits, ("dma", sem)))
        ev = (skey, tot)
        self._mark(ev, reads, writes)
        return ev

    def cc(self, fn, reads=(), writes=()):
        es = self.E["pool"]
        if self.cc_scr is None:
            self.cc_scr = self.sb("cc_scr", [128, 8], F32)
        waits = self._deps(es, reads, writes)
        self.cc_n += 1
        es.n += 1
        es.ops.append((fn, waits, ("cc", self.sems["cc"])))
        es.n += 1
        scr = self.cc_scr
        es.ops.append((lambda e: e.memset(scr[:], 0.0), [("cc", self.cc_n)], ("eng", None)))
        ev = ("pool", es.n)
        es.known["pool"] = es.n
        self._mark(ev, reads, writes)
        return ev

    def _sem_of(self, key):
        if isinstance(key, tuple):
            return self.dma_pool[key[1]][key[2]]
        return self.sems[key]

    def finish(self):
        nc = self.nc
        es = self.E["sp"]
        waits = []
        for skey, tot in self.dma_tot.items():
            self._need(es, (skey, tot), waits)
        for e in ("pe", "act", "dve", "pool"):
            if self.E[e].n > 0:
                self._need(es, (e, self.E[e].n), waits)
        es.ops.append((None, waits, None))
        engmap = {"pe": "tensor", "act": "scalar", "dve": "vector", "pool": "gpsimd", "sp": "sync"}
        with nc.Block() as block:
            for e in self.ENG:
                st = self.E[e]

                def body(engine, st=st, e=e):
                    own = self.sems[e]
                    for fn, waits, inc in st.ops:
                        for key, val in waits:
                            engine.wait_ge(self._sem_of(key), val)
                        if fn is None:
                            continue
                        ins = fn(engine)
                        if inc[0] == "eng":
                            ins.then_inc(own, 1)
                        elif inc[0] == "cc":
                            ins.then_inc(inc[1])
                        else:
                            ins.then_inc(inc[1], 16)

                getattr(block, engmap[e])(body)
        for cm in reversed(self.ctxs):
            cm.__exit__(None, None, None)
        for cm in reversed(self.stack):
            cm.__exit__(None, None, None)


D = 1024
ALPHA = 4 ** 0.25
EPS = 1e-5


def emit_consts(c):
    nc = c.nc
    K = {}
    K["ident_f"] = c.sb("ident_f", [128, 128], F32)
    K["ident_b"] = c.sb("ident_b", [128, 128], BF16)
    K["ones_f"] = c.sb("ones_f", [128, 128], F32)
    K["b"] = Buf("consts")
    idf, idb, of = K["ident_f"], K["ident_b"], K["ones_f"]
    c.op("pool", lambda e: e.memset(of[:], 1.0), writes=[K["b"]])
    c.op("pool", lambda e: e.affine_select(out=idf[:], in_=of[:], pattern=[[-1, 128]], compare_op=ALU.is_equal,
                                           fill=0.0, base=0, channel_multiplier=1), reads=[K["b"]], writes=[K["b"]])
    c.op("pool", lambda e: e.tensor_copy(out=idb[:], in_=idf[:]), reads=[K["b"]], writes=[K["b"]])
    return K


def emit_ln(c, z, bz, out, bout, gam, bet, bgb, tmp):
    st, mv, rs, bt = tmp["st"], tmp["mv"], tmp["rs"], tmp["b"]
    c.op("dve", lambda e: e.bn_stats(out=st[:, 0, :], in_=z[:, 0:512]), reads=[bz], writes=[bt])
    c.op("dve", lambda e: e.bn_stats(out=st[:, 1, :], in_=z[:, 512:1024]), reads=[bz], writes=[bt])
    c.op("dve", lambda e: e.bn_aggr(out=mv[:], in_=st[:].rearrange("p a b -> p (a b)")), reads=[bt], writes=[bt])
    c.op("dve", lambda e: e.tensor_scalar(out=rs[:], in0=mv[:, 1:2], scalar1=EPS, scalar2=None, op0=ALU.add), reads=[bt], writes=[bt])
    c.op("act", lambda e: e.activation(out=rs[:], in_=rs[:], func=AF.Sqrt), reads=[bt], writes=[bt])
    c.op("dve", lambda e: e.reciprocal(out=rs[:], in_=rs[:]), reads=[bt], writes=[bt])
    c.op("dve", lambda e: e.tensor_scalar(out=out[:], in0=z[:], scalar1=mv[:, 0:1], scalar2=rs[:, 0:1],
                                          op0=ALU.subtract, op1=ALU.mult), reads=[bz, bt], writes=[bout])
    c.op("pool", lambda e: e.tensor_tensor(out=out[:], in0=out[:], in1=gam[:], op=ALU.mult), reads=[bout, bgb], writes=[bout])
    c.op("pool", lambda e: e.tensor_tensor(out=out[:], in0=out[:], in1=bet[:], op=ALU.add), reads=[bout, bgb], writes=[bout])


def emit_thresholds(c, K, affT_dram, n_p, n_s, cap_p, cap_s, E, iters=26):
    A = c.sb("bis_A", [128, E, n_p + n_s], F32)
    bA = Buf("bisA")
    c.dma("sp", lambda e: e.dma_start(out=A[:], in_=affT_dram), writes=[bA])
    lo = c.sb("bis_lo", [128, 2 * E], F32)
    hi = c.sb("bis_hi", [128, 2 * E], F32)
    mid = c.sb("bis_mid", [128, 2 * E], F32)
    cnt = c.sb("bis_cnt", [128, 2 * E], F32)
    capv = c.sb("bis_cap", [128, 2 * E], F32)
    ge = c.sb("bis_ge", [128, 2 * E], U32)
    lt = c.sb("bis_lt", [128, 2 * E], U32)
    junk = c.sb("bis_junk", [128, max(n_p, n_s)], BF16)
    tot = c.ps("bis_tot", [128, 2 * E], F32)
    b = Buf("bis")
    bcnt = Buf("bcnt")
    btot = Buf("btot")
    bj = Buf("junk")
    c.op("dve", lambda e: e.memset(lo[:], 0.0), writes=[b])
    c.op("dve", lambda e: e.memset(hi[:], 1.0), writes=[b])
    c.op("dve", lambda e: e.memset(mid[:], 0.5), writes=[b])
    c.op("dve", lambda e: e.memset(capv[:, 0:E], float(cap_p)), writes=[b])
    c.op("dve", lambda e: e.memset(capv[:, E:2 * E], float(cap_s)), writes=[b])
    ones = K["ones_f"]
    for it in range(iters):
        for v in range(2 * E):
            g, ex = divmod(v, E)
            src = A[:, ex, 0:n_p] if g == 0 else A[:, ex, n_p:n_p + n_s]
            n = n_p if g == 0 else n_s
            c.op("dve", lambda e, src=src, v=v, n=n: e.tensor_scalar(
                out=junk[:, 0:n], in0=src, scalar1=mid[:, v:v + 1], scalar2=None,
                op0=ALU.is_ge, op1=ALU.add, accum_out=cnt[:, v:v + 1]), reads=[bA, b], writes=[bj, bcnt])
        c.op("pe", lambda e: e.matmul(tot[:], ones[:], cnt[:], start=True, stop=True), reads=[bcnt, K["b"]], writes=[btot])
        c.op("dve", lambda e: e.tensor_tensor(out=ge[:], in0=tot[:], in1=capv[:], op=ALU.is_ge), reads=[btot, b], writes=[b])
        c.op("dve", lambda e: e.tensor_tensor(out=lt[:], in0=tot[:], in1=capv[:], op=ALU.is_lt), reads=[btot, b], writes=[b])
        c.op("dve", lambda e: e.copy_predicated(out=lo[:], mask=ge[:], data=mid[:]), reads=[b], writes=[b])
        c.op("dve", lambda e: e.copy_predicated(out=hi[:], mask=lt[:], data=mid[:]), reads=[b], writes=[b])
        c.op("dve", lambda e: e.tensor_tensor(out=mid[:], in0=lo[:], in1=hi[:], op=ALU.add), reads=[b], writes=[b])
        c.op("dve", lambda e: e.tensor_scalar(out=mid[:], in0=mid[:], scalar1=0.5, scalar2=None, op0=ALU.mult), reads=[b], writes=[b])
    return lo, b


def emit_ffn(c, K, x_dram, aff_own_dram, affT_dram, wg, wu, wd, lng, lnb, y_dram,
             tiles_p, tiles_s, n_p, n_s, cap_p, cap_s, E=16, DFF=1408, TB=8, iters=26, dbg=None, stage=99):
    NT = tiles_p + tiles_s
    FC = DFF // 128
    theta, bth = emit_thresholds(c, K, affT_dram, n_p, n_s, cap_p, cap_s, E, iters)
    if stage == 1:
        c.dma("sp", lambda e: e.dma_start(out=dbg[:, 0:2 * E], in_=theta[:]), reads=[bth])
        return
    aff = c.sb("aff_own", [128, NT, E], F32)
    gp = c.sb("gprime", [128, NT, E], F32)
    bg = Buf("gp")
    c.dma("sp", lambda e: e.dma_start(out=aff[:], in_=aff_own_dram.rearrange("(j p) e -> p j e", p=128)), writes=[bg])
    for g, (t0, t1) in enumerate(((0, tiles_p), (tiles_p, NT))):
        if t1 == t0:
            continue
        th = theta[:, g * E:(g + 1) * E].unsqueeze(1).to_broadcast([128, t1 - t0, E])
        c.op("dve", lambda e, t0=t0, t1=t1, th=th: e.tensor_tensor(out=gp[:, t0:t1, :], in0=aff[:, t0:t1, :], in1=th, op=ALU.is_ge),
             reads=[bg, bth], writes=[bg])
        c.op("dve", lambda e, t0=t0, t1=t1: e.tensor_tensor(out=gp[:, t0:t1, :], in0=gp[:, t0:t1, :], in1=aff[:, t0:t1, :], op=ALU.mult),
             reads=[bg], writes=[bg])
    if stage == 2:
        c.dma("sp", lambda e: e.dma_start(out=dbg[:, 0:NT * E], in_=gp[:].rearrange("p a b -> p (a b)")), reads=[bg])
        return
    gam = c.sb("ffn_gam", [128, D], F32)
    bet = c.sb("ffn_bet", [128, D], F32)
    bgb = Buf("gb")
    c.dma("sp", lambda e: e.dma_start(out=gam[:], in_=lng.partition_broadcast(128)), writes=[bgb])
    c.dma("sp", lambda e: e.dma_start(out=bet[:], in_=lnb.partition_broadcast(128)), writes=[bgb])
    if stage == 30:
        return
    lntmp = {"st": c.sb("ln_st", [128, 2, 6], F32), "mv": c.sb("ln_mv", [128, 2], F32), "rs": c.sb("ln_rs", [128, 1], F32), "b": Buf("lnt")}

    TOK = TB * 128
    xT = c.sb("ffn_xT", [128, 8, TOK], BF16)
    hT = c.sb("ffn_hT", [128, FC, TOK], BF16)
    yacc = c.sb("ffn_yacc", [128, TB, D], F32)
    wgs = c.sb("ffn_wg", [128, 8, DFF], BF16)
    wus = c.sb("ffn_wu", [128, 8, DFF], BF16)
    wds = c.sb("ffn_wd", [128, FC, D], BF16)
    bwg, bwu, bwd = Buf("wg"), Buf("wu"), Buf("wd")
    bxT = Buf("xT")
    bhT = [Buf("hT%d" % i) for i in range(FC)]
    byacc = [Buf("yacc%d" % i) for i in range(TB)]
    xin = [c.sb("ffn_xin%d" % i, [128, D], F32) for i in range(2)]
    bxin = [Buf("xin0"), Buf("xin1")]
    sg = [c.sb("ffn_sg%d" % i, [128, 512], BF16) for i in range(2)]
    bsg = [Buf("sg0"), Buf("sg1")]
    zt = [c.sb("ffn_z%d" % i, [128, D], F32) for i in range(2)]
    bz = [Buf("z0"), Buf("z1")]
    ot, bo = zt, bz
    ps_t = c.ps("ps_t", [128, 512], F32); bps_t = Buf("ps_t")
    ps_g = [c.ps("ps_g%d" % i, [128, 512], F32) for i in range(2)]; bps_g = [Buf("psg0"), Buf("psg1")]
    ps_u = [c.ps("ps_u%d" % i, [128, 512], F32) for i in range(2)]; bps_u = [Buf("psu0"), Buf("psu1")]
    ps_d = [c.ps("ps_d%d" % i, [128, 512], F32) for i in range(2)]; bps_d = [Buf("psd0"), Buf("psd1")]
    idf = K["ident_f"]
    cnt = [0, 0, 0, 0]
    wst = [c.sb('ffn_wst%d' % i, [128, max(DFF, D)], F32) for i in range(2)]
    bwst = [Buf('wst%d' % i) for i in range(2)]
    nblk = (NT + TB - 1) // TB
    for blk in range(nblk):
        tl0 = blk * TB
        ntl = min(TB, NT - tl0)
        ntok = ntl * 128
        for t in range(ntl):
            i = cnt[0] % 2; cnt[0] += 1
            r0 = (tl0 + t) * 128
            c.dma("sp", lambda e, i=i, r0=r0: e.dma_start(out=xin[i][:], in_=x_dram[r0:r0 + 128, :]), writes=[bxin[i]])
            if stage == 31:
                continue
            for kh in range(2):
                for k in range(4):
                    kk = kh * 4 + k
                    c.op("pe", lambda e, i=i, k=k, kk=kk: e.transpose(ps_t[:, k * 128:(k + 1) * 128], xin[i][:, kk * 128:(kk + 1) * 128], idf[:]),
                         reads=[bxin[i], K["b"]], writes=[bps_t])
                if stage == 32:
                    continue
                c.op("dve", lambda e, t=t, kh=kh: e.tensor_copy(out=xT[:, kh * 4:(kh + 1) * 4, t * 128:(t + 1) * 128],
                                                      in_=ps_t[:].rearrange("p (k t) -> p k t", k=4)),
                     reads=[bps_t], writes=[bxT])
        if stage in (3, 31, 32):
            return
        for ex in range(E):
            if stage == 4 and ex == 1:
                return
            for (dst, src, nchunk, bw) in ((wgs, wg, 8, bwg), (wus, wu, 8, bwu), (wds, wd, FC, bwd)):
                wdt = src.shape[2]
                for k in range(nchunk):
                    i = cnt[3] % 2; cnt[3] += 1
                    c.dma("sp", lambda e, i=i, k=k, src=src, ex=ex, wdt=wdt: e.dma_start(out=wst[i][:, 0:wdt], in_=src[ex, k * 128:(k + 1) * 128, :]),
                          writes=[bwst[i]])
                    c.op("pool", lambda e, i=i, k=k, dst=dst, wdt=wdt: e.tensor_copy(out=dst[:, k, :], in_=wst[i][:, 0:wdt]),
                         reads=[bwst[i]], writes=[bw])
            if stage == 40:
                return
            for fc in range(FC):
                for n0 in range(0, ntok, 512):
                    nn = min(512, ntok - n0)
                    i = cnt[1] % 2; cnt[1] += 1
                    for k in range(8):
                        c.op("pe", lambda e, i=i, k=k, fc=fc, n0=n0, nn=nn: e.matmul(
                            ps_g[i][:, 0:nn], wgs[:, k, fc * 128:(fc + 1) * 128], xT[:, k, n0:n0 + nn], start=(k == 0), stop=(k == 7)),
                            reads=[bwg, bxT], writes=[bps_g[i]])
                    for k in range(8):
                        c.op("pe", lambda e, i=i, k=k, fc=fc, n0=n0, nn=nn: e.matmul(
                            ps_u[i][:, 0:nn], wus[:, k, fc * 128:(fc + 1) * 128], xT[:, k, n0:n0 + nn], start=(k == 0), stop=(k == 7)),
                            reads=[bwu, bxT], writes=[bps_u[i]])
                    if stage == 41:
                        continue
                    c.op("act", lambda e, i=i, nn=nn: e.activation(out=sg[i][:, 0:nn], in_=ps_g[i][:, 0:nn], func=AF.Silu),
                         reads=[bps_g[i]], writes=[bsg[i]])
                    c.op("dve", lambda e, i=i, fc=fc, n0=n0, nn=nn: e.tensor_tensor(
                        out=hT[:, fc, n0:n0 + nn], in0=ps_u[i][:, 0:nn], in1=sg[i][:, 0:nn], op=ALU.mult),
                        reads=[bps_u[i], bsg[i]], writes=[bhT[fc]])
            if stage in (5, 41):
                return
            for t in range(ntl):
                for h in range(2):
                    i = cnt[2] % 2; cnt[2] += 1
                    for fc in range(FC):
                        c.op("pe", lambda e, i=i, fc=fc, t=t, h=h: e.matmul(
                            ps_d[i][:], hT[:, fc, t * 128:(t + 1) * 128], wds[:, fc, h * 512:(h + 1) * 512], start=(fc == 0), stop=(fc == FC - 1)),
                            reads=[bwd, bhT[fc]], writes=[bps_d[i]])
                    gcol = gp[:, tl0 + t, ex:ex + 1]
                    if ex == 0:
                        c.op("dve", lambda e, i=i, t=t, h=h, gcol=gcol: e.tensor_scalar(
                            out=yacc[:, t, h * 512:(h + 1) * 512], in0=ps_d[i][:], scalar1=gcol, scalar2=None, op0=ALU.mult),
                            reads=[bps_d[i], bg], writes=[byacc[t]])
                    else:
                        c.op("dve", lambda e, i=i, t=t, h=h, gcol=gcol: e.scalar_tensor_tensor(
                            out=yacc[:, t, h * 512:(h + 1) * 512], in0=ps_d[i][:], scalar=gcol, in1=yacc[:, t, h * 512:(h + 1) * 512],
                            op0=ALU.mult, op1=ALU.add), reads=[bps_d[i], bg, byacc[t]], writes=[byacc[t]])
        if stage == 6:
            return
        for t in range(ntl):
            i = cnt[0] % 2; cnt[0] += 1
            r0 = (tl0 + t) * 128
            c.dma("sp", lambda e, i=i, r0=r0: e.dma_start(out=xin[i][:], in_=x_dram[r0:r0 + 128, :]), writes=[bxin[i]])
            c.op("dve", lambda e, i=i, t=t: e.scalar_tensor_tensor(out=zt[i][:], in0=xin[i][:], scalar=float(ALPHA), in1=yacc[:, t, :],
                                                                 op0=ALU.mult, op1=ALU.add), reads=[bxin[i], byacc[t]], writes=[bz[i]])
            emit_ln(c, zt[i], bz[i], ot[i], bo[i], gam, bet, bgb, lntmp)
            c.dma("sp", lambda e, i=i, r0=r0: e.dma_start(out=y_dram[r0:r0 + 128, :], in_=ot[i][:]), reads=[bo[i]])


NAW, MLW = 512, 512
EVEN_IN = 3600
SEQ = 2048
NTILE = 16


def na_table_host(rpb):
    NEG = -30000.0
    qc = np.arange(64)
    kc = np.arange(64)
    c0 = np.clip(qc - 8, 0, 48)
    colvalid = (kc[:, None] >= c0[None, :]) & (kc[:, None] < c0[None, :] + 16)
    dc = np.clip(kc[:, None] - qc[None, :] + 15, 0, 30)
    T = np.full((8, 3, 128, 16, 64), NEG, np.float32)
    for i in range(16):
        for half in range(2):
            dr = i - 1 + half
            if dr < 0 or dr > 14:
                continue
            vals = np.where(colvalid[None], rpb[:, dr][:, dc], NEG).astype(np.float32)
            sl = slice(half * 64, half * 64 + 64)
            T[:, 0, sl, i, :] = vals
            if half == 1:
                T[:, 1, sl, i, :] = vals
            else:
                T[:, 2, sl, i, :] = vals
    return T


class PSB:
    def __init__(self, c, n=8):
        self.t = [c.ps("psb%d" % i, [128, 512], F32) for i in range(n)]
        self.b = [Buf("psb%d" % i) for i in range(n)]


def load_cast_weight(c, dst, bdst, src, ncol, st, bst, cnt):
    for k in range(8):
        for c0 in range(0, ncol, 1024):
            cw = min(1024, ncol - c0)
            i = cnt[0] % len(st); cnt[0] += 1
            c.dma("sp", lambda e, i=i, k=k, c0=c0, cw=cw: e.dma_start(out=st[i][:, 0:cw], in_=src[k * 128:(k + 1) * 128, c0:c0 + cw]), writes=[bst[i]])
            c.op("pool", lambda e, i=i, k=k, c0=c0, cw=cw: e.tensor_copy(out=dst[:, k, c0:c0 + cw], in_=st[i][:, 0:cw]), reads=[bst[i]], writes=[bdst])


def emit_xT(c, K, P, x_dram, r0, ntiles, xT, bxT, xin, bxin, cnt):
    idf = K["ident_f"]
    for t in range(ntiles):
        i = cnt[0] % 2; cnt[0] += 1
        c.dma("sp", lambda e, i=i, t=t: e.dma_start(out=xin[i][:], in_=x_dram[r0 + t * 128:r0 + (t + 1) * 128, :]), writes=[bxin[i]])
        for kh in range(2):
            for k in range(4):
                kk = kh * 4 + k
                c.op("pe", lambda e, i=i, k=k, kk=kk: e.transpose(P.t[0][:, k * 128:(k + 1) * 128], xin[i][:, kk * 128:(kk + 1) * 128], idf[:]),
                     reads=[bxin[i], K["b"]], writes=[P.b[0]])
            c.op("dve", lambda e, t=t, kh=kh: e.tensor_copy(out=xT[:, kh * 4:(kh + 1) * 4, t * 128:(t + 1) * 128],
                                                         in_=P.t[0][:].rearrange("p (k t) -> p k t", k=4)), reads=[P.b[0]], writes=[bxT])


def proj_fm(c, P, W, bW, xT, bxT, col0, dst, bdst, scale, ntok, pc):
    for n0 in range(0, ntok, 512):
        j = 1 + (pc[0] % 2); pc[0] += 1
        for k in range(8):
            c.op("pe", lambda e, j=j, k=k, n0=n0: e.matmul(P.t[j][:], W[:, k, col0:col0 + 128], xT[:, k, n0:n0 + 512], start=(k == 0), stop=(k == 7)),
                 reads=[bW, bxT], writes=[P.b[j]])
        c.op("dve", lambda e, j=j, n0=n0: e.tensor_scalar(out=dst[:, n0:n0 + 512], in0=P.t[j][:], scalar1=float(scale), scalar2=None, op0=ALU.mult),
             reads=[P.b[j]], writes=[bdst])


def proj_tm(c, P, W, bW, xT, bxT, col0, ncol, dstfn, bdst, scale, ntile, pc):
    g = max(1, 512 // ncol)
    g = min(g, 4)
    for t0 in range(0, ntile, g):
        j = 1 + (pc[0] % 2); pc[0] += 1
        gg = min(g, ntile - t0)
        for ti in range(gg):
            t = t0 + ti
            for k in range(8):
                c.op("pe", lambda e, j=j, k=k, t=t, ti=ti: e.matmul(P.t[j][:, ti * ncol:(ti + 1) * ncol], xT[:, k, t * 128:(t + 1) * 128], W[:, k, col0:col0 + ncol],
                                                                     start=(k == 0), stop=(k == 7)), reads=[bW, bxT], writes=[P.b[j]])
        c.op("dve", lambda e, j=j, t0=t0, gg=gg: e.tensor_scalar(out=dstfn(t0, gg), in0=P.t[j][:, 0:gg * ncol].rearrange("p (g n) -> p g n", g=gg),
                                                                scalar1=float(scale), scalar2=None, op0=ALU.mult), reads=[P.b[j]], writes=[bdst])


def emit_mix_a(c, K, P, x_dram, w_in, w_out, gate_bias, norm_g, lng, lnb, w_r, na_tab, x1_dram, aff_dram, nseq, stage=99, dbg=None):
    Wout = c.sb("a_wout", [128, 8, D], BF16); bWout = Buf("wout")
    stg = [c.sb("a_stg%d" % i, [128, 8, 128], F32) for i in range(2)]; bstg = [Buf("stg0"), Buf("stg1")]
    st = [t[:].rearrange("p k n -> p (k n)") for t in stg]; bst = bstg
    scnt = [0]
    load_cast_weight(c, Wout, bWout, w_out, D, st, bst, scnt)
    WS = {n: c.sb("a_ws_" + n, [128, 8, 128], BF16) for n in ("q", "k", "v", "o")}
    bWS = {n: Buf("ws_" + n) for n in WS}
    WG = c.sb("a_wg16", [128, 8, 16], BF16); bWG = Buf("wg16")

    def load_w(name, col0, ncol=128):
        dst, bd = (WS[name], bWS[name]) if name != "g" else (WG, bWG)
        i = scnt[0] % 2; scnt[0] += 1
        c.dma("sp", lambda e, i=i: e.dma_start(out=stg[i][:, :, 0:ncol], in_=w_in[:, col0:col0 + ncol].rearrange("(k p) n -> p k n", p=128)), writes=[bstg[i]])
        c.op("pool", lambda e, i=i: e.tensor_copy(out=dst[:, :, 0:ncol], in_=stg[i][:, :, 0:ncol]), reads=[bstg[i]], writes=[bd])
    Wr = c.sb("a_wr", [128, 8, 16], F32); bWr = Buf("wr")
    c.dma("sp", lambda e: e.dma_start(out=Wr[:], in_=w_r.rearrange("(k p) e -> p k e", p=128)), writes=[bWr])
    gb = c.sb("a_gb", [128, 16], F32)
    ng = c.sb("a_ng", [128, 512], F32)
    gam = c.sb("a_gam", [128, D], F32)
    bet = c.sb("a_bet", [128, D], F32)
    bgb = Buf("gb")
    c.dma("sp", lambda e: e.dma_start(out=gb[:], in_=gate_bias.partition_broadcast(128)), writes=[bgb])
    c.dma("sp", lambda e: e.dma_start(out=ng[:], in_=norm_g.partition_broadcast(128)), writes=[bgb])
    c.dma("sp", lambda e: e.dma_start(out=gam[:], in_=lng.partition_broadcast(128)), writes=[bgb])
    c.dma("sp", lambda e: e.dma_start(out=bet[:], in_=lnb.partition_broadcast(128)), writes=[bgb])
    ones_f = K["ones_f"]
    tri = c.sb("a_tri", [128, 128], F32)
    trir = c.sb("a_trir", [128, 128], F32)
    ones_b = c.sb("a_onesb", [128, 128], BF16)
    c.op("pool", lambda e: e.affine_select(out=tri[:], in_=ones_f[:], pattern=[[1, 128]], compare_op=ALU.is_ge, fill=0.0, base=0, channel_multiplier=-1),
         reads=[K["b"]], writes=[K["b"]])
    c.op("pool", lambda e: e.affine_select(out=trir[:], in_=ones_f[:], pattern=[[-1, 128]], compare_op=ALU.is_ge, fill=0.0, base=0, channel_multiplier=1),
         reads=[K["b"]], writes=[K["b"]])
    c.op("pool", lambda e: e.tensor_copy(out=ones_b[:], in_=ones_f[:]), reads=[K["b"]], writes=[K["b"]])
    lntmp = {"st": c.sb("ln_st", [128, 2, 6], F32), "mv": c.sb("ln_mv", [128, 2], F32), "rs": c.sb("ln_rs", [128, 1], F32), "b": Buf("lnt")}

    xT = c.sb("a_xT", [128, 8, SEQ], BF16); bxT = Buf("xT")
    yT = c.sb("a_yT", [128, 8, SEQ], BF16); byT = [Buf("yT%d" % i) for i in range(8)]
    xin = [c.sb("a_xin%d" % i, [128, D], F32) for i in range(2)]; bxin = [Buf("xin0"), Buf("xin1")]
    QT = c.sb("a_QT", [128, SEQ], BF16); bQT = Buf("QT")
    KT = c.sb("a_KT", [128, SEQ], BF16); bKT = Buf("KT")
    Vb = c.sb("a_V", [128, NTILE * 132], BF16); bV = Buf("V")
    Ktm = c.sb("a_Ktm", [128, NTILE, 128], BF16); bKtm = Buf("Ktm")
    sgob = c.sb("a_sgob", [128, NTILE, 128], F32); bsgob = Buf("sgob")
    hacc = c.sb("a_hacc", [128, NTILE, 128], F32); bhacc = Buf("hacc")
    EB = c.sb("a_EB", [128, 3, 16, 64], F32); bEB = Buf("EB")
    pexp = [c.sb("a_pexp%d" % i, [128, 5 * 64], F32) for i in range(2)]; bpexp = [Buf("pexp0"), Buf("pexp1")]
    PT = [c.sb("a_PT%d" % i, [128, 5, 64], BF16) for i in range(2)]; bPT = [Buf("PT0"), Buf("PT1")]
    rec = c.sb("a_rec", [128, 512], F32); brec = Buf("rec")
    G = c.sb("a_G", [128, NTILE, 16], F32); bG = Buf("G")
    nlf = c.sb("a_nlf", [128, NTILE, 8], F32)
    uu = c.sb("a_u", [128, NTILE, 8], F32)
    vv = c.sb("a_v", [128, NTILE, 8], F32)
    eL = c.sb("a_eL", [128, NTILE, 8], F32)
    bgate = Buf("gate")
    CN = c.sb("a_CN", [128, 129], F32); bCN = Buf("CN")
    CNb = c.sb("a_CNb", [128, 129], BF16); bCNb = Buf("CNb")
    ctmp = c.sb("a_ctmp", [128, 129], F32); bctmp = Buf("ctmp")
    St = [c.sb("a_St%d" % i, [128, 128], BF16) for i in range(2)]; bSt = [Buf("St0"), Buf("St1")]
    Kt = [c.sb("a_Kt%d" % i, [128, 128], BF16) for i in range(2)]; bKt = [Buf("Kt0"), Buf("Kt1")]
    sm = c.sb("a_sm", [128, 8], F32); bsm = Buf("sm")
    lnw = c.sb("a_lnw", [128, NTILE, 128], F32); blnw = Buf("lnw")
    lns = c.sb("a_lns", [128, NTILE, 4], F32); blns = Buf("lns")
    x1T = c.sb("a_x1T", [128, 8, 128], F32); bx1T = Buf("x1T")
    rt = c.sb("a_rt", [128, 16], F32); brt = Buf("rt")
    rtm = c.sb("a_rtm", [128, 4], F32)
    affo = [c.sb("a_aff%d" % i, [128, 16], F32) for i in range(2)]; baffo = [Buf("aff0"), Buf("aff1")]
    xc = [0]; pc = [0]; sc = [0]; hc = [0]; zc = [0]

    for s in range(nseq):
        r0 = s * SEQ
        emit_xT(c, K, P, x_dram, r0, NTILE, xT, bxT, xin, bxin, xc)
        for hp in range(4):
            load_w("q", hp * 128); load_w("k", 512 + hp * 128); load_w("v", 1024 + hp * 128)
            proj_fm(c, P, WS["q"], bWS["q"], xT, bxT, 0, QT, bQT, 0.125, SEQ, pc)
            proj_fm(c, P, WS["k"], bWS["k"], xT, bxT, 0, KT, bKT, 1.0, SEQ, pc)
            Vv = Vb[:, 0:NTILE * 128].rearrange("p (t n) -> p t n", t=NTILE)
            proj_tm(c, P, WS["v"], bWS["v"], xT, bxT, 0, 128, lambda t0, gg: Vv[:, t0:t0 + gg, :], bV, 1.0, NTILE, pc)
            for hh in range(2):
                pb = hh * 64
                c.dma("sp", lambda e, hp=hp, hh=hh: e.dma_start(out=EB[:].rearrange("p v i q -> p v (i q)"),
                                                         in_=na_tab[2 * hp + hh].rearrange("v p i q -> p v (i q)")), writes=[bEB])
                c.op("act", lambda e: e.activation(out=EB[:].rearrange("p v i q -> p (v i q)"), in_=EB[:].rearrange("p v i q -> p (v i q)"), func=AF.Exp),
                     reads=[bEB], writes=[bEB])
                for r in range(32):
                    rs = min(max(r - 4, 0), 24)
                    a0, a1 = rs // 2, (rs + 7) // 2
                    nt = a1 - a0 + 1
                    si = sc[0] % 2; sc[0] += 1
                    psS = P.t[3 + si]; bS = P.b[3 + si]
                    for j in range(nt):
                        a = a0 + j
                        c.op("pe", lambda e, j=j, a=a, psS=psS, pb=pb, r=r: e.matmul(psS[:, j * 64:(j + 1) * 64], KT[pb:pb + 64, a * 128:(a + 1) * 128],
                                                                                    QT[pb:pb + 64, r * 64:(r + 1) * 64], start=True, stop=True),
                             reads=[bKT, bQT], writes=[bS])
                    c.op("act", lambda e, si=si, psS=psS, nt=nt: e.activation(out=pexp[si][:, 0:nt * 64], in_=psS[:, 0:nt * 64], func=AF.Exp),
                         reads=[bS], writes=[bpexp[si]])
                    for j in range(nt):
                        a = a0 + j
                        var = 1 if 2 * a < rs else (2 if 2 * a + 1 >= rs + 8 else 0)
                        ii = 2 * a - r + 7 + 1
                        c.op("dve", lambda e, si=si, j=j, var=var, ii=ii, hh=hh: e.tensor_tensor(out=PT[si][:, j, :], in0=pexp[si][:, j * 64:(j + 1) * 64],
                                                                                          in1=EB[:, var, ii, :], op=ALU.mult),
                             reads=[bpexp[si], bEB], writes=[bPT[si]])
                    rr = r % 8
                    for j in range(nt):
                        a = a0 + j
                        c.op("pe", lambda e, si=si, j=j, a=a, rr=rr, nt=nt: e.matmul(P.t[5][:, rr * 64:(rr + 1) * 64], Vv[:, a, :], PT[si][:, j, :],
                                                                                  start=(j == 0), stop=(j == nt - 1)), reads=[bV, bPT[si]], writes=[P.b[5]])
                    for j in range(nt):
                        c.op("pe", lambda e, si=si, j=j, rr=rr, nt=nt: e.matmul(P.t[6][:, rr * 64:(rr + 1) * 64], ones_b[:], PT[si][:, j, :],
                                                                             start=(j == 0), stop=(j == nt - 1)), reads=[K["b"], bPT[si]], writes=[P.b[6]])
                    if rr == 7:
                        q0 = (r - 7) * 64
                        c.op("dve", lambda e: e.reciprocal(out=rec[:], in_=P.t[6][:]), reads=[P.b[6]], writes=[brec])
                        c.op("dve", lambda e, pb=pb, hp=hp, q0=q0: e.tensor_tensor(out=yT[pb:pb + 64, hp, q0:q0 + 512], in0=P.t[5][pb:pb + 64, :],
                                                                                in1=rec[pb:pb + 64, :], op=ALU.mult),
                             reads=[P.b[5], brec], writes=[byT[hp]])
        if stage == 1:
            continue
        load_w("g", 3584, 16)
        Gps = P.t[1][:, 0:NTILE * 16].rearrange("p (t n) -> p t n", t=NTILE)
        for t in range(NTILE):
            for k in range(8):
                c.op("pe", lambda e, t=t, k=k: e.matmul(Gps[:, t, :], xT[:, k, t * 128:(t + 1) * 128], WG[:, k, :], start=(k == 0), stop=(k == 7)),
                     reads=[bxT, bWG], writes=[P.b[1]])
        c.op("dve", lambda e: e.tensor_tensor(out=G[:], in0=Gps, in1=gb[:].unsqueeze(1).to_broadcast([128, NTILE, 16]), op=ALU.add),
             reads=[P.b[1], bgb], writes=[bG])
        c.op("act", lambda e: e.activation(out=nlf[:], in_=G[:, :, 8:16], func=AF.Exp, scale=-1.0), reads=[bG], writes=[bgate])
        c.op("act", lambda e: e.activation(out=nlf[:], in_=nlf[:], func=AF.Ln, bias=1.0, scale=1.0), reads=[bgate], writes=[bgate])
        cum = P.t[2][:, 0:NTILE * 8].rearrange("p (t n) -> p t n", t=NTILE)
        tot = P.t[2][:, 256:256 + NTILE * 8].rearrange("p (t n) -> p t n", t=NTILE)
        for t in range(NTILE):
            c.op("pe", lambda e, t=t: e.matmul(cum[:, t, 0:4], tri[:], nlf[:, t, 0:4], start=True, stop=True), reads=[bgate, K["b"]], writes=[P.b[2]])
            c.op("pe", lambda e, t=t: e.matmul(cum[:, t, 4:8], trir[:], nlf[:, t, 4:8], start=True, stop=True), reads=[bgate, K["b"]], writes=[P.b[2]])
            c.op("pe", lambda e, t=t: e.matmul(tot[:, t, :], ones_f[:], nlf[:, t, :], start=True, stop=True), reads=[bgate, K["b"]], writes=[P.b[2]])
        c.op("act", lambda e: e.activation(out=uu[:], in_=cum, func=AF.Exp, scale=-1.0), reads=[P.b[2]], writes=[bgate])
        c.op("act", lambda e: e.activation(out=eL[:], in_=tot, func=AF.Exp, scale=-1.0), reads=[P.b[2]], writes=[bgate])
        c.op("dve", lambda e: e.tensor_tensor(out=vv[:], in0=cum, in1=G[:, :, 0:8], op=ALU.add), reads=[P.b[2], bG], writes=[bgate])
        c.op("act", lambda e: e.activation(out=vv[:], in_=vv[:], func=AF.Exp), reads=[bgate], writes=[bgate])
        for h in range(4):
            load_w("q", 1536 + h * 128); load_w("k", 2048 + h * 128); load_w("v", 2560 + h * 128); load_w("o", 3072 + h * 128)
            proj_fm(c, P, WS["q"], bWS["q"], xT, bxT, 0, QT, bQT, 1.0, SEQ, pc)
            proj_fm(c, P, WS["k"], bWS["k"], xT, bxT, 0, KT, bKT, 128 ** -0.5, SEQ, pc)
            proj_tm(c, P, WS["k"], bWS["k"], xT, bxT, 0, 128, lambda t0, gg: Ktm[:, t0:t0 + gg, :], bKtm, 128 ** -0.5, NTILE, pc)
            Va = Vb[:, 0:NTILE * 129].rearrange("p (t n) -> p t n", t=NTILE)
            proj_tm(c, P, WS["v"], bWS["v"], xT, bxT, 0, 128, lambda t0, gg: Va[:, t0:t0 + gg, 0:128], bV, 1.0, NTILE, pc)
            c.op("pool", lambda e: e.memset(Va[:, :, 128:129], 1.0), reads=[], writes=[bV])
            proj_tm(c, P, WS["o"], bWS["o"], xT, bxT, 0, 128, lambda t0, gg: sgob[:, t0:t0 + gg, :], bsgob, 1.0, NTILE, pc)
            c.op("act", lambda e: e.activation(out=sgob[:].rearrange("p t n -> p (t n)"), in_=sgob[:].rearrange("p t n -> p (t n)"), func=AF.Sigmoid),
                 reads=[bsgob], writes=[bsgob])
            for dr in range(2):
                gi = dr * 4 + h
                mask = tri if dr == 0 else trir
                c.op("pool", lambda e: e.memset(CN[:], 0.0), writes=[bCN])
                c.op("pool", lambda e: e.memset(CNb[:], 0.0), writes=[bCNb])
                order = range(NTILE) if dr == 0 else range(NTILE - 1, -1, -1)
                for ci in order:
                    si = sc[0] % 2; sc[0] += 1
                    hi_ = hc[0] % 2; hc[0] += 1
                    psS = P.t[3 + si]; bS = P.b[3 + si]
                    psH = P.t[5] if hi_ == 0 else P.t[7]; bH = P.b[5] if hi_ == 0 else P.b[7]
                    cs = slice(ci * 128, (ci + 1) * 128)
                    c.op("pe", lambda e, psS=psS, cs=cs: e.matmul(psS[:, 0:128], KT[:, cs], QT[:, cs], start=True, stop=True), reads=[bKT, bQT], writes=[bS])
                    c.op("dve", lambda e, psS=psS, si=si, ci=ci, gi=gi, mask=mask: e.scalar_tensor_tensor(
                        out=St[si][:], in0=psS[:, 0:128], scalar=vv[:, ci, gi:gi + 1], in1=mask[:], op0=ALU.mult, op1=ALU.mult),
                        reads=[bS, bgate, K["b"]], writes=[bSt[si]])
                    c.op("pe", lambda e, psH=psH, si=si, ci=ci: e.matmul(psH[:, 0:129], St[si][:], Va[:, ci, :], start=True, stop=False),
                         reads=[bSt[si], bV], writes=[bH])
                    c.op("pe", lambda e, psH=psH, cs=cs: e.matmul(psH[:, 0:129], QT[:, cs], CNb[:], start=False, stop=True),
                         reads=[bQT, bCNb], writes=[bH])
                    c.op("dve", lambda e, psH=psH, ci=ci, gi=gi: e.tensor_tensor(out=sm[:, 0:1], in0=psH[:, 128:129], in1=uu[:, ci, gi:gi + 1], op=ALU.mult),
                         reads=[bH, bgate], writes=[bsm])
                    c.op("dve", lambda e: e.tensor_scalar(out=sm[:, 4:5], in0=sm[:, 0:1], scalar1=-1.0, scalar2=None, op0=ALU.mult), reads=[bsm], writes=[bsm])
                    c.op("dve", lambda e: e.tensor_tensor(out=sm[:, 5:6], in0=sm[:, 0:1], in1=sm[:, 4:5], op=ALU.max), reads=[bsm], writes=[bsm])
                    c.op("dve", lambda e: e.tensor_scalar(out=sm[:, 1:2], in0=sm[:, 5:6], scalar1=1.0, scalar2=None, op0=ALU.max), reads=[bsm], writes=[bsm])
                    c.op("dve", lambda e: e.reciprocal(out=sm[:, 2:3], in_=sm[:, 1:2]), reads=[bsm], writes=[bsm])
                    c.op("dve", lambda e, ci=ci, gi=gi: e.tensor_tensor(out=sm[:, 3:4], in0=sm[:, 2:3], in1=uu[:, ci, gi:gi + 1], op=ALU.mult),
                         reads=[bsm, bgate], writes=[bsm])
                    if dr == 0:
                        c.op("dve", lambda e, psH=psH, ci=ci: e.tensor_scalar(out=hacc[:, ci, :], in0=psH[:, 0:128], scalar1=sm[:, 3:4], scalar2=None, op0=ALU.mult),
                             reads=[bH, bsm], writes=[bhacc])
                    else:
                        c.op("dve", lambda e, psH=psH, ci=ci: e.scalar_tensor_tensor(out=hacc[:, ci, :], in0=psH[:, 0:128], scalar=sm[:, 3:4], in1=hacc[:, ci, :],
                                                                                   op0=ALU.mult, op1=ALU.add), reads=[bH, bsm, bhacc], writes=[bhacc])
                    c.op("pool", lambda e, si=si, ci=ci, gi=gi: e.tensor_scalar(out=Kt[si][:], in0=Ktm[:, ci, :], scalar1=vv[:, ci, gi:gi + 1], scalar2=None, op0=ALU.mult),
                         reads=[bKtm, bgate], writes=[bKt[si]])
                    c.op("pe", lambda e, si=si, ci=ci: e.matmul(P.t[6][:, 0:129], Kt[si][:], Va[:, ci, :], start=True, stop=True),
                         reads=[bKt[si], bV], writes=[P.b[6]])
                    c.op("dve", lambda e: e.tensor_tensor(out=ctmp[:], in0=P.t[6][:, 0:129], in1=CN[:], op=ALU.add), reads=[P.b[6], bCN], writes=[bctmp])
                    c.op("dve", lambda e, ci=ci, gi=gi: e.tensor_scalar(out=CN[:], in0=ctmp[:], scalar1=eL[:, ci, gi:gi + 1], scalar2=None, op0=ALU.mult),
                         reads=[bctmp, bgate], writes=[bCN])
                    c.op("pool", lambda e, ci=ci, gi=gi: e.tensor_scalar(out=CNb[:], in0=ctmp[:], scalar1=eL[:, ci, gi:gi + 1], scalar2=None, op0=ALU.mult),
                         reads=[bctmp, bgate], writes=[bCNb])
            c.op("dve", lambda e: e.tensor_reduce(out=lns[:, :, 0], in_=hacc[:], axis=AX.X, op=ALU.add), reads=[bhacc], writes=[blns])
            c.op("dve", lambda e: e.tensor_scalar(out=lns[:, :, 0], in0=lns[:, :, 0], scalar1=1.0 / 128, scalar2=None, op0=ALU.mult), reads=[blns], writes=[blns])
            c.op("dve", lambda e: e.tensor_tensor(out=lnw[:], in0=hacc[:], in1=lns[:, :, 0:1].to_broadcast([128, NTILE, 128]), op=ALU.subtract),
                 reads=[bhacc, blns], writes=[blnw])
            c.op("pool", lambda e: e.tensor_tensor(out=hacc[:], in0=lnw[:], in1=lnw[:], op=ALU.mult), reads=[blnw, bhacc], writes=[bhacc])
            c.op("dve", lambda e: e.tensor_reduce(out=lns[:, :, 1], in_=hacc[:], axis=AX.X, op=ALU.add), reads=[bhacc], writes=[blns])
            c.op("dve", lambda e: e.tensor_scalar(out=lns[:, :, 1], in0=lns[:, :, 1], scalar1=1.0 / 128, scalar2=EPS, op0=ALU.mult, op1=ALU.add), reads=[blns], writes=[blns])
            c.op("act", lambda e: e.activation(out=lns[:, :, 2], in_=lns[:, :, 1], func=AF.Sqrt), reads=[blns], writes=[blns])
            c.op("dve", lambda e: e.reciprocal(out=lns[:, :, 3], in_=lns[:, :, 2]), reads=[blns], writes=[blns])
            c.op("dve", lambda e: e.tensor_tensor(out=lnw[:], in0=lnw[:], in1=lns[:, :, 3:4].to_broadcast([128, NTILE, 128]), op=ALU.mult), reads=[blnw, blns], writes=[blnw])
            c.op("pool", lambda e, h=h: e.tensor_tensor(out=lnw[:], in0=lnw[:], in1=ng[:, h * 128:(h + 1) * 128].unsqueeze(1).to_broadcast([128, NTILE, 128]), op=ALU.mult),
                 reads=[blnw, bgb], writes=[blnw])
            c.op("dve", lambda e: e.tensor_tensor(out=lnw[:], in0=lnw[:], in1=sgob[:], op=ALU.mult), reads=[blnw, bsgob], writes=[blnw])
            for tq in range(4):
                for k in range(4):
                    t = tq * 4 + k
                    c.op("pe", lambda e, t=t, k=k: e.transpose(P.t[0][:, k * 128:(k + 1) * 128], lnw[:, t, :], K["ident_f"][:]), reads=[blnw, K["b"]], writes=[P.b[0]])
                c.op("dve", lambda e, tq=tq, h=h: e.tensor_copy(out=yT[:, 4 + h, tq * 512:(tq + 1) * 512], in_=P.t[0][:]), reads=[P.b[0]], writes=[byT[4 + h]])
        if stage == 2:
            continue
        for t in range(NTILE):
            i = xc[0] % 2; xc[0] += 1
            zi = i
            rr0 = r0 + t * 128
            c.dma("sp", lambda e, i=i, rr0=rr0: e.dma_start(out=xin[i][:], in_=x_dram[rr0:rr0 + 128, :]), writes=[bxin[i]])
            for hf in range(2):
                for k in range(8):
                    c.op("pe", lambda e, hf=hf, k=k, t=t: e.matmul(P.t[1 + hf][:], yT[:, k, t * 128:(t + 1) * 128], Wout[:, k, hf * 512:(hf + 1) * 512],
                                                                start=(k == 0), stop=(k == 7)), reads=[byT[k], bWout], writes=[P.b[1 + hf]])
                c.op("dve", lambda e, hf=hf, i=i, zi=zi: e.scalar_tensor_tensor(out=xin[zi][:, hf * 512:(hf + 1) * 512], in0=xin[i][:, hf * 512:(hf + 1) * 512], scalar=float(ALPHA),
                                                                       in1=P.t[1 + hf][:], op0=ALU.mult, op1=ALU.add), reads=[bxin[i], P.b[1 + hf]], writes=[bxin[zi]])
            emit_ln(c, xin[zi], bxin[zi], xin[zi], bxin[zi], gam, bet, bgb, lntmp)
            c.dma("sp", lambda e, zi=zi, rr0=rr0: e.dma_start(out=x1_dram[rr0:rr0 + 128, :], in_=xin[zi][:]), reads=[bxin[zi]])
            emit_router(c, K, P, xin[zi], bxin[zi], Wr, bWr, x1T, bx1T, rt, brt, rtm, affo[zi], baffo[zi], aff_dram, rr0)


def emit_router(c, K, P, z, bz, Wr, bWr, x1T, bx1T, rt, brt, rtm, affo, baffo, aff_dram, rr0):
    idf = K["ident_f"]
    for kh in range(2):
        for k in range(4):
            kk = kh * 4 + k
            c.op("pe", lambda e, k=k, kk=kk: e.transpose(P.t[0][:, k * 128:(k + 1) * 128], z[:, kk * 128:(kk + 1) * 128], idf[:]), reads=[bz, K["b"]], writes=[P.b[0]])
        c.op("dve", lambda e, kh=kh: e.tensor_copy(out=x1T[:, kh * 4:(kh + 1) * 4, :], in_=P.t[0][:].rearrange("p (k t) -> p k t", k=4)), reads=[P.b[0]], writes=[bx1T])
    for k in range(8):
        c.op("pe", lambda e, k=k: e.matmul(P.t[3][:, 0:16], x1T[:, k, :], Wr[:, k, :], start=(k == 0), stop=(k == 7)), reads=[bx1T, bWr], writes=[P.b[3]])
    c.op("dve", lambda e: e.tensor_reduce(out=rtm[:, 0:1], in_=P.t[3][:, 0:16], axis=AX.X, op=ALU.max), reads=[P.b[3]], writes=[brt])
    c.op("dve", lambda e: e.tensor_scalar(out=rtm[:, 1:2], in0=rtm[:, 0:1], scalar1=-1.0, scalar2=None, op0=ALU.mult), reads=[brt], writes=[brt])
    c.op("act", lambda e: e.activation(out=rt[:], in_=P.t[3][:, 0:16], func=AF.Exp, bias=rtm[:, 1:2], scale=1.0, accum_out=rtm[:, 2:3]), reads=[P.b[3], brt], writes=[brt])
    c.op("dve", lambda e: e.reciprocal(out=rtm[:, 3:4], in_=rtm[:, 2:3]), reads=[brt], writes=[brt])
    c.op("dve", lambda e: e.tensor_scalar(out=affo[:], in0=rt[:], scalar1=rtm[:, 3:4], scalar2=None, op0=ALU.mult), reads=[brt], writes=[baffo])
    c.dma("sp", lambda e: e.dma_start(out=aff_dram[rr0:rr0 + 128, :], in_=affo[:]), reads=[baffo])


def rope_tables_host():
    d = 64
    inv = (10000.0 ** (-np.arange(0, d, 2, dtype=np.float32) / d)).astype(np.float32)
    ang = np.arange(SEQ, dtype=np.float32)[:, None] * inv[None, :]
    cos, sin = np.cos(ang).astype(np.float32), np.sin(ang).astype(np.float32)
    COS = np.concatenate([cos, cos], 1).T
    SINS = np.concatenate([-sin, sin], 1).T
    COS = np.ascontiguousarray(np.concatenate([COS, COS], 0), dtype=np.float32)
    SINS = np.ascontiguousarray(np.concatenate([SINS, SINS], 0), dtype=np.float32)
    p = np.arange(128)[:, None]
    x = np.arange(3968)[None, :]
    dl = p - x + 1920
    m = (np.abs(dl) <= 64).astype(np.float32) + ((dl % 4 == 0) & (np.abs(dl) <= 256)) + ((dl % 16 == 0) & (np.abs(dl) <= 1024))
    return COS, SINS, np.ascontiguousarray(m.astype(np.float32))


def emit_mix_b(c, K, P, x_dram, w_in, w_out, lng, lnb, w_r, cos_d, sins_d, tab_d, x1_dram, aff_dram, nseq):
    Wout = c.sb("b_wout", [128, 8, D], BF16); bWout = Buf("wout")
    stg = [c.sb("b_stg%d" % i, [128, 8, 128], F32) for i in range(2)]; bstg = [Buf("stg0"), Buf("stg1")]
    st = [t[:].rearrange("p k n -> p (k n)") for t in stg]
    scnt = [0]
    load_cast_weight(c, Wout, bWout, w_out, D, st, bstg, scnt)
    WS = {n: c.sb("b_ws_" + n, [128, 8, 128], BF16) for n in ("q", "qs", "k", "ks", "v")}
    bWS = {n: Buf("ws_" + n) for n in WS}

    def load_w(name, col0, swap=False):
        dst, bd = WS[name], bWS[name]
        i = scnt[0] % 2; scnt[0] += 1
        src = w_in[:, col0:col0 + 128]
        if not swap:
            c.dma("sp", lambda e, i=i: e.dma_start(out=stg[i][:], in_=src.rearrange("(k p) n -> p k n", p=128)), writes=[bstg[i]])
        else:
            s5 = src.rearrange("(k p) (h two d) -> p k h two d", p=128, h=2, two=2)
            d5 = stg[i][:].rearrange("p k (h two d) -> p k h two d", h=2, two=2)
            for a in range(2):
                for hd in range(2):
                    c.dma("sp", lambda e, i=i, a=a, hd=hd: e.dma_start(out=d5[:, :, hd, 1 - a, :], in_=s5[:, :, hd, a, :]), writes=[bstg[i]])
        c.op("pool", lambda e, i=i: e.tensor_copy(out=dst[:], in_=stg[i][:]), reads=[bstg[i]], writes=[bd])

    Wr = c.sb("b_wr", [128, 8, 16], F32); bWr = Buf("wr")
    c.dma("sp", lambda e: e.dma_start(out=Wr[:], in_=w_r.rearrange("(k p) e -> p k e", p=128)), writes=[bWr])
    gam = c.sb("b_gam", [128, D], F32)
    bet = c.sb("b_bet", [128, D], F32)
    bgb = Buf("gb")
    c.dma("sp", lambda e: e.dma_start(out=gam[:], in_=lng.partition_broadcast(128)), writes=[bgb])
    c.dma("sp", lambda e: e.dma_start(out=bet[:], in_=lnb.partition_broadcast(128)), writes=[bgb])
    COS = c.sb("b_cos", [128, SEQ], F32); SINS = c.sb("b_sins", [128, SEQ], F32)
    TABf = c.sb("b_tabf", [128, 3968], F32); TAB = c.sb("b_tab", [128, 3968], BF16)
    btab = Buf("tab")
    c.dma("sp", lambda e: e.dma_start(out=COS[:], in_=cos_d), writes=[btab])
    c.dma("sp", lambda e: e.dma_start(out=SINS[:], in_=sins_d), writes=[btab])
    c.dma("sp", lambda e: e.dma_start(out=TABf[:], in_=tab_d), writes=[btab])
    c.op("pool", lambda e: e.tensor_copy(out=TAB[:], in_=TABf[:]), reads=[btab], writes=[btab])
    ones_f = K["ones_f"]
    ones_b = c.sb("b_onesb", [128, 128], BF16)
    c.op("pool", lambda e: e.tensor_copy(out=ones_b[:], in_=ones_f[:]), reads=[K["b"]], writes=[K["b"]])
    lntmp = {"st": c.sb("ln_st", [128, 2, 6], F32), "mv": c.sb("ln_mv", [128, 2], F32), "rs": c.sb("ln_rs", [128, 1], F32), "b": Buf("lnt")}

    xT = c.sb("b_xT", [128, 8, SEQ], BF16); bxT = Buf("xT")
    yT = c.sb("b_yT", [128, 8, SEQ], BF16); byT = [Buf("yT%d" % i) for i in range(8)]
    xin = [c.sb("b_xin%d" % i, [128, D], F32) for i in range(2)]; bxin = [Buf("xin0"), Buf("xin1")]
    QT = c.sb("b_QT", [128, SEQ], BF16); bQT = Buf("QT")
    KT = c.sb("b_KT", [128, SEQ], BF16); bKT = Buf("KT")
    Vv = c.sb("b_V", [128, NTILE, 128], BF16); bV = Buf("V")
    t1 = c.sb("b_t1", [128, 512], F32); bt1 = Buf("t1")
    t2 = c.sb("b_t2", [128, 512], F32); bt2 = Buf("t2")
    pexp = [c.sb("b_pexp%d" % i, [128, 512], BF16) for i in range(2)]; bpexp = [Buf("pexp0"), Buf("pexp1")]
    PT = [c.sb("b_PT%d" % i, [128, 512], BF16) for i in range(2)]; bPT = [Buf("PT0"), Buf("PT1")]
    rec = c.sb("b_rec", [128, 512], F32); brec = Buf("rec")
    x1T = c.sb("b_x1T", [128, 8, 128], F32); bx1T = Buf("x1T")
    rt = c.sb("b_rt", [128, 16], F32); brt = Buf("rt")
    rtm = c.sb("b_rtm", [128, 4], F32)
    affo = [c.sb("b_aff%d" % i, [128, 16], F32) for i in range(2)]; baffo = [Buf("aff0"), Buf("aff1")]
    xc = [0]; pc = [0]; sc = [0]

    def proj_rope(wa, wb, dst, bdst):
        for n0 in range(0, SEQ, 512):
            for k in range(8):
                c.op("pe", lambda e, k=k, n0=n0: e.matmul(P.t[1][:], WS[wa][:, k, :], xT[:, k, n0:n0 + 512], start=(k == 0), stop=(k == 7)),
                     reads=[bWS[wa], bxT], writes=[P.b[1]])
            for k in range(8):
                c.op("pe", lambda e, k=k, n0=n0: e.matmul(P.t[2][:], WS[wb][:, k, :], xT[:, k, n0:n0 + 512], start=(k == 0), stop=(k == 7)),
                     reads=[bWS[wb], bxT], writes=[P.b[2]])
            c.op("dve", lambda e, n0=n0: e.tensor_tensor(out=t1[:], in0=P.t[1][:], in1=COS[:, n0:n0 + 512], op=ALU.mult), reads=[P.b[1], btab], writes=[bt1])
            c.op("dve", lambda e, n0=n0: e.tensor_tensor(out=t2[:], in0=P.t[2][:], in1=SINS[:, n0:n0 + 512], op=ALU.mult), reads=[P.b[2], btab], writes=[bt2])
            c.op("pool", lambda e, n0=n0: e.tensor_tensor(out=dst[:, n0:n0 + 512], in0=t1[:], in1=t2[:], op=ALU.add), reads=[bt1, bt2], writes=[bdst])

    for s in range(nseq):
        r0 = s * SEQ
        emit_xT(c, K, P, x_dram, r0, NTILE, xT, bxT, xin, bxin, xc)
        for hp in range(8):
            load_w("q", hp * 128); load_w("qs", hp * 128, swap=True)
            load_w("k", 1024 + hp * 128); load_w("ks", 1024 + hp * 128, swap=True)
            load_w("v", 2048 + hp * 128)
            proj_rope("q", "qs", QT, bQT)
            proj_rope("k", "ks", KT, bKT)
            proj_tm(c, P, WS["v"], bWS["v"], xT, bxT, 0, 128, lambda t0, gg: Vv[:, t0:t0 + gg, :], bV, 1.0, NTILE, pc)
            for hh in range(2):
                pb = hh * 64
                for qb in range(4):
                    kts = []
                    for kt in range(NTILE):
                        lo = 128 * kt - 512 * qb - 511
                        hi = 128 * kt + 127 - 512 * qb
                        if lo > 1024 or hi < -1024:
                            continue
                        kts.append(kt)
                    for j, kt in enumerate(kts):
                        si = sc[0] % 2; sc[0] += 1
                        psS = P.t[3 + si]; bS = P.b[3 + si]
                        c.op("pe", lambda e, psS=psS, kt=kt, qb=qb, pb=pb: e.matmul(psS[:], KT[pb:pb + 64, kt * 128:(kt + 1) * 128], QT[pb:pb + 64, qb * 512:(qb + 1) * 512],
                                                                                 start=True, stop=True), reads=[bKT, bQT], writes=[bS])
                        c.op("act", lambda e, psS=psS, si=si: e.activation(out=pexp[si][:], in_=psS[:], func=AF.Exp, scale=0.125), reads=[bS], writes=[bpexp[si]])
                        x0 = 512 * qb - 128 * kt + 1920
                        c.op("dve", lambda e, si=si, x0=x0: e.tensor_tensor(out=PT[si][:], in0=pexp[si][:], in1=TAB[:, x0:x0 + 512], op=ALU.mult),
                             reads=[bpexp[si], btab], writes=[bPT[si]])
                        c.op("pe", lambda e, si=si, kt=kt, j=j, n=len(kts): e.matmul(P.t[5][:], Vv[:, kt, :], PT[si][:], start=(j == 0), stop=(j == n - 1)),
                             reads=[bV, bPT[si]], writes=[P.b[5]])
                        c.op("pe", lambda e, si=si, j=j, n=len(kts): e.matmul(P.t[6][:], ones_b[:], PT[si][:], start=(j == 0), stop=(j == n - 1)),
                             reads=[K["b"], bPT[si]], writes=[P.b[6]])
                    c.op("dve", lambda e: e.reciprocal(out=rec[:], in_=P.t[6][:]), reads=[P.b[6]], writes=[brec])
                    c.op("dve", lambda e, pb=pb, hp=hp, qb=qb: e.tensor_tensor(out=yT[pb:pb + 64, hp, qb * 512:(qb + 1) * 512], in0=P.t[5][pb:pb + 64, :],
                                                                            in1=rec[pb:pb + 64, :], op=ALU.mult), reads=[P.b[5], brec], writes=[byT[hp]])
        for t in range(NTILE):
            i = xc[0] % 2; xc[0] += 1
            rr0 = r0 + t * 128
            c.dma("sp", lambda e, i=i, rr0=rr0: e.dma_start(out=xin[i][:], in_=x_dram[rr0:rr0 + 128, :]), writes=[bxin[i]])
            for hf in range(2):
                for k in range(8):
                    c.op("pe", lambda e, hf=hf, k=k, t=t: e.matmul(P.t[1 + hf][:], yT[:, k, t * 128:(t + 1) * 128], Wout[:, k, hf * 512:(hf + 1) * 512],
                                                                start=(k == 0), stop=(k == 7)), reads=[byT[k], bWout], writes=[P.b[1 + hf]])
                c.op("dve", lambda e, hf=hf, i=i: e.scalar_tensor_tensor(out=xin[i][:, hf * 512:(hf + 1) * 512], in0=xin[i][:, hf * 512:(hf + 1) * 512], scalar=float(ALPHA),
                                                                 in1=P.t[1 + hf][:], op0=ALU.mult, op1=ALU.add), reads=[bxin[i], P.b[1 + hf]], writes=[bxin[i]])
            emit_ln(c, xin[i], bxin[i], xin[i], bxin[i], gam, bet, bgb, lntmp)
            c.dma("sp", lambda e, i=i, rr0=rr0: e.dma_start(out=x1_dram[rr0:rr0 + 128, :], in_=xin[i][:]), reads=[bxin[i]])
            emit_router(c, K, P, xin[i], bxin[i], Wr, bWr, x1T, bx1T, rt, brt, rtm, affo[i], baffo[i], aff_dram, rr0)


NCORES = 8
TILES_P, TILES_S = 32, 64
E_ = 16
DFF_ = 1408


def build_ffn_program():
    NT = TILES_P + TILES_S
    n_p, n_s = NCORES * TILES_P, NCORES * TILES_S
    cap_p, cap_s = 2 * n_p * 128 // E_, 2 * n_s * 128 // E_
    nc = bass.Bass("TRN2", target_bir_lowering=False)
    di = lambda n, s: nc.dram_tensor(n, s, F32, kind="ExternalInput").ap()
    x = di("x", [NT * 128, D]); aff = di("aff", [NT * 128, E_]); affT = di("affT", [128, E_, n_p + n_s])
    wg = di("wg", [E_, D, DFF_]); wu = di("wu", [E_, D, DFF_]); wd = di("wd", [E_, DFF_, D])
    lng = di("lng", [D]); lnb = di("lnb", [D])
    y = nc.dram_tensor("y", [NT * 128, D], F32, kind="ExternalOutput").ap()
    c = Ctx(nc)
    K = emit_consts(c)
    emit_ffn(c, K, x, aff, affT, wg, wu, wd, lng, lnb, y, TILES_P, TILES_S, n_p, n_s, cap_p, cap_s, E=E_, DFF=DFF_, TB=4)
    c.finish()
    return nc


def build_a_program(nseq=6):
    nc = bass.Bass("TRN2", target_bir_lowering=False)
    di = lambda n, s: nc.dram_tensor(n, s, F32, kind="ExternalInput").ap()
    x = di("x", [nseq * 2048, D]); w_in = di("w_in", [D, 3600]); w_out = di("w_out", [D, D]); gbias = di("gbias", [16]); ng = di("ng", [512])
    lng = di("lng", [D]); lnb = di("lnb", [D]); wr = di("wr", [D, 16]); tab = di("tab", [8, 3, 128, 16, 64])
    x1 = nc.dram_tensor("x1", [nseq * 2048, D], F32, kind="ExternalOutput").ap()
    aff = nc.dram_tensor("aff", [nseq * 2048, 16], F32, kind="ExternalOutput").ap()
    c = Ctx(nc); K = emit_consts(c); P = PSB(c)
    emit_mix_a(c, K, P, x, w_in, w_out, gbias, ng, lng, lnb, wr, tab, x1, aff, nseq)
    c.finish()
    return nc


def build_b_program(nseq=6):
    nc = bass.Bass("TRN2", target_bir_lowering=False)
    di = lambda n, s: nc.dram_tensor(n, s, F32, kind="ExternalInput").ap()
    x = di("x", [nseq * 2048, D]); w_in = di("w_in", [D, 3072]); w_out = di("w_out", [D, D])
    lng = di("lng", [D]); lnb = di("lnb", [D]); wr = di("wr", [D, 16]); cos = di("cos", [128, 2048]); sins = di("sins", [128, 2048]); tab = di("tab", [128, 3968])
    x1 = nc.dram_tensor("x1", [nseq * 2048, D], F32, kind="ExternalOutput").ap()
    aff = nc.dram_tensor("aff", [nseq * 2048, 16], F32, kind="ExternalOutput").ap()
    c = Ctx(nc); K = emit_consts(c); P = PSB(c)
    emit_mix_b(c, K, P, x, w_in, w_out, lng, lnb, wr, cos, sins, tab, x1, aff, nseq)
    c.finish()
    return nc


def _aff_layout(affs):
    def toT(a):
        return a.reshape(-1, 128, E_).transpose(1, 2, 0)
    ap = np.concatenate([a[:TILES_P * 128] for a in affs], 0)
    as_ = np.concatenate([a[TILES_P * 128:] for a in affs], 0)
    return np.ascontiguousarray(np.concatenate([toT(ap), toT(as_)], axis=2), dtype=np.float32)


def kernel(x_prompt, x_sample, even_w_in, ml_gate_bias, na_rpb, ml_norm_g, even_w_out, da_w_in, da_w_out,
           ln_mix_g, ln_mix_b, ec_router, ec_w_gate, ec_w_up, ec_w_down, ln_ffn_g, ln_ffn_b):
    f32 = lambda a: np.ascontiguousarray(np.asarray(a), dtype=np.float32)
    x_prompt, x_sample = f32(x_prompt), f32(x_sample)
    cores = list(range(NCORES))
    xs = [np.concatenate([x_prompt[2 * c:2 * c + 2].reshape(-1, D), x_sample[4 * c:4 * c + 4].reshape(-1, D)], 0) for c in cores]
    tabA = na_table_host(f32(na_rpb)[0])
    feedA = {"w_in": f32(even_w_in)[0], "w_out": f32(even_w_out)[0], "gbias": f32(ml_gate_bias)[0], "ng": f32(ml_norm_g)[0],
             "lng": f32(ln_mix_g)[0], "lnb": f32(ln_mix_b)[0], "wr": f32(ec_router)[0], "tab": tabA}
    ncA = build_a_program()
    res = run_bass_kernel_spmd(ncA, [dict(feedA, x=xs[c]) for c in cores], core_ids=cores)
    x1 = [res.results[c]["x1"] for c in cores]; aff = [res.results[c]["aff"] for c in cores]
    del ncA
    ncF = build_ffn_program()
    def run_ffn(xl, affl, layer):
        affT = _aff_layout(affl)
        feed = {"affT": affT, "wg": f32(ec_w_gate)[layer], "wu": f32(ec_w_up)[layer], "wd": f32(ec_w_down)[layer],
                "lng": f32(ln_ffn_g)[layer], "lnb": f32(ln_ffn_b)[layer]}
        r = run_bass_kernel_spmd(ncF, [dict(feed, x=xl[c], aff=affl[c]) for c in cores], core_ids=cores)
        return [r.results[c]["y"] for c in cores]
    x2 = run_ffn(x1, aff, 0)
    COS, SINS, TAB = rope_tables_host()
    feedB = {"w_in": f32(da_w_in)[0], "w_out": f32(da_w_out)[0], "lng": f32(ln_mix_g)[1], "lnb": f32(ln_mix_b)[1],
             "wr": f32(ec_router)[1], "cos": COS, "sins": SINS, "tab": TAB}
    ncB = build_b_program()
    res = run_bass_kernel_spmd(ncB, [dict(feedB, x=x2[c]) for c in cores], core_ids=cores)
    x3 = [res.results[c]["x1"] for c in cores]; aff2 = [res.results[c]["aff"] for c in cores]
    del ncB
    y = run_ffn(x3, aff2, 1)
    y_prompt = np.stack([y[c][:TILES_P * 128].reshape(2, 2048, D) for c in cores], 0).reshape(16, 2048, D)
    y_sample = np.stack([y[c][TILES_P * 128:].reshape(4, 2048, D) for c in cores], 0).reshape(32, 2048, D)
    return (np.ascontiguousarray(y_prompt, dtype=np.float32), np.ascontiguousarray(y_sample, dtype=np.float32))
```

```python
from concourse.bass_utils import run_bass_kernel_spmd
import numpy as np
import concourse.bass as bass
import concourse.mybir as mybir

F32 = mybir.dt.float32
BF16 = mybir.dt.bfloat16
I32 = mybir.dt.int32
U32 = mybir.dt.uint32
ALU = mybir.AluOpType
AF = mybir.ActivationFunctionType
AX = mybir.AxisListType


class Buf:
    __slots__ = ("name", "writers", "readers")

    def __init__(self, name=""):
        self.name = name
        self.writers = {}
        self.readers = []


class EngS:
    def __init__(self, name, self_wait):
        self.name = name
        self.ops = []
        self.self_wait = self_wait
        self.known = {}
        self.n = 0


class Ctx:
    ENG = ("pe", "act", "dve", "pool", "sp")

    def __init__(self, nc, dma_pool=8):
        self.nc = nc
        self.E = {
            "pe": EngS("pe", False),
            "act": EngS("act", True),
            "dve": EngS("dve", True),
            "pool": EngS("pool", True),
            "sp": EngS("sp", False),
        }
        self.sems = {}
        self.stack = []
        cm = nc.semaphore("s_cc")
        self.sems["cc"] = cm.__enter__()
        self.stack.append(cm)
        self.cc_n = 0
        self.cc_scr = None
        for e in self.ENG:
            cm = nc.semaphore("s_" + e)
            self.sems[e] = cm.__enter__()
            self.stack.append(cm)
        self.dma_pool = {}
        self.dma_tot = {}
        self.dma_i = {}
        for q in ("sp", "pool", "act"):
            lst = []
            for i in range(dma_pool):
                cm = nc.semaphore("d_%s%d" % (q, i))
                lst.append(cm.__enter__())
                self.stack.append(cm)
            self.dma_pool[q] = lst
            self.dma_i[q] = 0
        self.ctxs = []

    def sb(self, name, shape, dt):
        cm = self.nc.sbuf_tensor(name, list(shape), dt)
        t = cm.__enter__()
        self.ctxs.append(cm)
        return t

    def ps(self, name, shape, dt):
        cm = self.nc.psum_tensor(name, list(shape), dt)
        t = cm.__enter__()
        self.ctxs.append(cm)
        return t

    def _need(self, es, ev, waits):
        key, val = ev
        if key == es.name and not es.self_wait:
            return
        if es.known.get(key, -1) >= val:
            return
        es.known[key] = val
        waits.append(ev)

    def _deps(self, es, reads, writes):
        waits = []
        for b in reads:
            for ev in b.writers.values():
                self._need(es, ev, waits)
        for b in writes:
            for ev in b.writers.values():
                self._need(es, ev, waits)
            for ev in b.readers:
                self._need(es, ev, waits)
        return waits

    def _mark(self, ev, reads, writes):
        for b in reads:
            b.readers.append(ev)
            if len(b.readers) > 64:
                b.readers = b.readers[-48:]
        for b in writes:
            b.writers = {ev[0]: ev}
            b.readers = []

    def op(self, eng, fn, reads=(), writes=()):
        es = self.E[eng]
        waits = self._deps(es, reads, writes)
        es.n += 1
        ev = (eng, es.n)
        es.ops.append((fn, waits, ("eng", None)))
        self._mark(ev, reads, writes)
        return ev

    def dma(self, q, fn, reads=(), writes=()):
        es = self.E[q]
        pool = self.dma_pool[q]
        i = self.dma_i[q]
        self.dma_i[q] = i + 1
        sem = pool[i % len(pool)]
        skey = ("dma", q, i % len(pool))
        prev = self.dma_tot.get(skey, 0)
        waits = self._deps(es, reads, writes)
        if prev > 0:
            self._need(es, (skey, prev), waits)
        tot = prev + 16
        self.dma_tot[skey] = tot
        es.n += 1
        es.ops.append((fn, waits, ("dma", sem)))
        ev = (skey, tot)
        self._mark(ev, reads, writes)
        return ev

    def cc(self, fn, reads=(), writes=()):
        es = self.E["pool"]
        if self.cc_scr is None:
            self.cc_scr = self.sb("cc_scr", [128, 8], F32)
        waits = self._deps(es, reads, writes)
        self.cc_n += 1
        es.n += 1
        es.ops.append((fn, waits, ("cc", self.sems["cc"])))
        es.n += 1
        scr = self.cc_scr
        es.ops.append((lambda e: e.memset(scr[:], 0.0), [("cc", self.cc_n)], ("eng", None)))
        ev = ("pool", es.n)
        es.known["pool"] = es.n
        self._mark(ev, reads, writes)
        return ev

    def _sem_of(self, key):
        if isinstance(key, tuple):
            return self.dma_pool[key[1]][key[2]]
        return self.sems[key]

    def finish(self):
        nc = self.nc
        es = self.E["sp"]
        waits = []
        for skey, tot in self.dma_tot.items():
            self._need(es, (skey, tot), waits)
        for e in ("pe", "act", "dve", "pool"):
            if self.E[e].n > 0:
                self._need(es, (e, self.E[e].n), waits)
        es.ops.append((None, waits, None))
        engmap = {"pe": "tensor", "act": "scalar", "dve": "vector", "pool": "gpsimd", "sp": "sync"}
        with nc.Block() as block:
            for e in self.ENG:
                st = self.E[e]

                def body(engine, st=st, e=e):
                    own = self.sems[e]
                    for fn, waits, inc in st.ops:
                        for key, val in waits:
                            engine.wait_ge(self._sem_of(key), val)
                        if fn is None:
                            continue
                        ins = fn(engine)
                        if inc[0] == "eng":
                            ins.then_inc(own, 1)
                        elif inc[0] == "cc":
                            ins.then_inc(inc[1])
                        else:
                            ins.then_inc(inc[1], 16)

                getattr(block, engmap[e])(body)
        for cm in reversed(self.ctxs):
            cm.__exit__(None, None, None)
        for cm in reversed(self.stack):
            cm.__exit__(None, None, None)


D = 1024
ALPHA = 4 ** 0.25
EPS = 1e-5


def emit_consts(c):
    nc = c.nc
    K = {}
    K["ident_f"] = c.sb("ident_f", [128, 128], F32)
    K["ident_b"] = c.sb("ident_b", [128, 128], BF16)
    K["ones_f"] = c.sb("ones_f", [128, 128], F32)
    K["b"] = Buf("consts")
    idf, idb, of = K["ident_f"], K["ident_b"], K["ones_f"]
    c.op("pool", lambda e: e.memset(of[:], 1.0), writes=[K["b"]])
    c.op("pool", lambda e: e.affine_select(out=idf[:], in_=of[:], pattern=[[-1, 128]], compare_op=ALU.is_equal,
                                           fill=0.0, base=0, channel_multiplier=1), reads=[K["b"]], writes=[K["b"]])
    c.op("pool", lambda e: e.tensor_copy(out=idb[:], in_=idf[:]), reads=[K["b"]], writes=[K["b"]])
    return K


def emit_ln(c, z, bz, out, bout, gam, bet, bgb, tmp):
    st, mv, rs, bt = tmp["st"], tmp["mv"], tmp["rs"], tmp["b"]
    c.op("dve", lambda e: e.bn_stats(out=st[:, 0, :], in_=z[:, 0:512]), reads=[bz], writes=[bt])
    c.op("dve", lambda e: e.bn_stats(out=st[:, 1, :], in_=z[:, 512:1024]), reads=[bz], writes=[bt])
    c.op("dve", lambda e: e.bn_aggr(out=mv[:], in_=st[:].rearrange("p a b -> p (a b)")), reads=[bt], writes=[bt])
    c.op("dve", lambda e: e.tensor_scalar(out=rs[:], in0=mv[:, 1:2], scalar1=EPS, scalar2=None, op0=ALU.add), reads=[bt], writes=[bt])
    c.op("act", lambda e: e.activation(out=rs[:], in_=rs[:], func=AF.Sqrt), reads=[bt], writes=[bt])
    c.op("dve", lambda e: e.reciprocal(out=rs[:], in_=rs[:]), reads=[bt], writes=[bt])
    c.op("dve", lambda e: e.tensor_scalar(out=out[:], in0=z[:], scalar1=mv[:, 0:1], scalar2=rs[:, 0:1],
                                          op0=ALU.subtract, op1=ALU.mult), reads=[bz, bt], writes=[bout])
    c.op("pool", lambda e: e.tensor_tensor(out=out[:], in0=out[:], in1=gam[:], op=ALU.mult), reads=[bout, bgb], writes=[bout])
    c.op("pool", lambda e: e.tensor_tensor(out=out[:], in0=out[:], in1=bet[:], op=ALU.add), reads=[bout, bgb], writes=[bout])


def emit_thresholds(c, K, affT_dram, n_p, n_s, cap_p, cap_s, E, iters=26, ret_A=False):
    A = c.sb("bis_A", [128, E, n_p + n_s], F32)
    bA = Buf("bisA")
    c.dma("sp", lambda e: e.dma_start(out=A[:], in_=affT_dram), writes=[bA])
    lo = c.sb("bis_lo", [128, 2 * E], F32)
    hi = c.sb("bis_hi", [128, 2 * E], F32)
    mid = c.sb("bis_mid", [128, 2 * E], F32)
    cnt = c.sb("bis_cnt", [128, 2 * E], F32)
    capv = c.sb("bis_cap", [128, 2 * E], F32)
    ge = c.sb("bis_ge", [128, 2 * E], U32)
    lt = c.sb("bis_lt", [128, 2 * E], U32)
    junk = c.sb("bis_junk", [128, max(n_p, n_s)], BF16)
    tot = c.ps("bis_tot", [128, 2 * E], F32)
    b = Buf("bis")
    bcnt = Buf("bcnt")
    btot = Buf("btot")
    bj = Buf("junk")
    c.op("dve", lambda e: e.memset(lo[:], 0.0), writes=[b])
    c.op("dve", lambda e: e.memset(hi[:], 1.0), writes=[b])
    c.op("dve", lambda e: e.memset(mid[:], 0.5), writes=[b])
    c.op("dve", lambda e: e.memset(capv[:, 0:E], float(cap_p)), writes=[b])
    c.op("dve", lambda e: e.memset(capv[:, E:2 * E], float(cap_s)), writes=[b])
    ones = K["ones_f"]
    for it in range(iters):
        for v in range(2 * E):
            g, ex = divmod(v, E)
            src = A[:, ex, 0:n_p] if g == 0 else A[:, ex, n_p:n_p + n_s]
            n = n_p if g == 0 else n_s
            c.op("dve", lambda e, src=src, v=v, n=n: e.tensor_scalar(
                out=junk[:, 0:n], in0=src, scalar1=mid[:, v:v + 1], scalar2=None,
                op0=ALU.is_ge, op1=ALU.add, accum_out=cnt[:, v:v + 1]), reads=[bA, b], writes=[bj, bcnt])
        c.op("pe", lambda e: e.matmul(tot[:], ones[:], cnt[:], start=True, stop=True), reads=[bcnt, K["b"]], writes=[btot])
        c.op("dve", lambda e: e.tensor_tensor(out=ge[:], in0=tot[:], in1=capv[:], op=ALU.is_ge), reads=[btot, b], writes=[b])
        c.op("dve", lambda e: e.tensor_tensor(out=lt[:], in0=tot[:], in1=capv[:], op=ALU.is_lt), reads=[btot, b], writes=[b])
        c.op("dve", lambda e: e.copy_predicated(out=lo[:], mask=ge[:], data=mid[:]), reads=[b], writes=[b])
        c.op("dve", lambda e: e.copy_predicated(out=hi[:], mask=lt[:], data=mid[:]), reads=[b], writes=[b])
        c.op("dve", lambda e: e.tensor_tensor(out=mid[:], in0=lo[:], in1=hi[:], op=ALU.add), reads=[b], writes=[b])
        c.op("dve", lambda e: e.tensor_scalar(out=mid[:], in0=mid[:], scalar1=0.5, scalar2=None, op0=ALU.mult), reads=[b], writes=[b])
    if ret_A:
        return lo, b, A, bA
    return lo, b


def emit_ffn(c, K, x_dram, aff_own_dram, affT_dram, wg, wu, wd, lng, lnb, y_dram,
             tiles_p, tiles_s, n_p, n_s, cap_p, cap_s, E=16, DFF=1408, TB=8, iters=26, dbg=None, stage=99):
    NT = tiles_p + tiles_s
    FC = DFF // 128
    theta, bth = emit_thresholds(c, K, affT_dram, n_p, n_s, cap_p, cap_s, E, iters)
    if stage == 1:
        c.dma("sp", lambda e: e.dma_start(out=dbg[:, 0:2 * E], in_=theta[:]), reads=[bth])
        return
    aff = c.sb("aff_own", [128, NT, E], F32)
    gp = c.sb("gprime", [128, NT, E], F32)
    bg = Buf("gp")
    c.dma("sp", lambda e: e.dma_start(out=aff[:], in_=aff_own_dram.rearrange("(j p) e -> p j e", p=128)), writes=[bg])
    for g, (t0, t1) in enumerate(((0, tiles_p), (tiles_p, NT))):
        if t1 == t0:
            continue
        th = theta[:, g * E:(g + 1) * E].unsqueeze(1).to_broadcast([128, t1 - t0, E])
        c.op("dve", lambda e, t0=t0, t1=t1, th=th: e.tensor_tensor(out=gp[:, t0:t1, :], in0=aff[:, t0:t1, :], in1=th, op=ALU.is_ge),
             reads=[bg, bth], writes=[bg])
        c.op("dve", lambda e, t0=t0, t1=t1: e.tensor_tensor(out=gp[:, t0:t1, :], in0=gp[:, t0:t1, :], in1=aff[:, t0:t1, :], op=ALU.mult),
             reads=[bg], writes=[bg])
    if stage == 2:
        c.dma("sp", lambda e: e.dma_start(out=dbg[:, 0:NT * E], in_=gp[:].rearrange("p a b -> p (a b)")), reads=[bg])
        return
    gam = c.sb("ffn_gam", [128, D], F32)
    bet = c.sb("ffn_bet", [128, D], F32)
    bgb = Buf("gb")
    c.dma("sp", lambda e: e.dma_start(out=gam[:], in_=lng.partition_broadcast(128)), writes=[bgb])
    c.dma("sp", lambda e: e.dma_start(out=bet[:], in_=lnb.partition_broadcast(128)), writes=[bgb])
    if stage == 30:
        return
    lntmp = {"st": c.sb("ln_st", [128, 2, 6], F32), "mv": c.sb("ln_mv", [128, 2], F32), "rs": c.sb("ln_rs", [128, 1], F32), "b": Buf("lnt")}

    TOK = TB * 128
    xT = c.sb("ffn_xT", [128, 8, TOK], BF16)
    hT = c.sb("ffn_hT", [128, FC, TOK], BF16)
    yacc = c.sb("ffn_yacc", [128, TB, D], F32)
    wgs = c.sb("ffn_wg", [128, 8, DFF], BF16)
    wus = c.sb("ffn_wu", [128, 8, DFF], BF16)
    wds = c.sb("ffn_wd", [128, FC, D], BF16)
    bwg, bwu, bwd = Buf("wg"), Buf("wu"), Buf("wd")
    bxT = Buf("xT")
    bhT = [Buf("hT%d" % i) for i in range(FC)]
    byacc = [Buf("yacc%d" % i) for i in range(TB)]
    xin = [c.sb("ffn_xin%d" % i, [128, D], F32) for i in range(2)]
    bxin = [Buf("xin0"), Buf("xin1")]
    sg = [c.sb("ffn_sg%d" % i, [128, 512], BF16) for i in range(2)]
    bsg = [Buf("sg0"), Buf("sg1")]
    zt = [c.sb("ffn_z%d" % i, [128, D], F32) for i in range(2)]
    bz = [Buf("z0"), Buf("z1")]
    ot, bo = zt, bz
    ps_t = c.ps("ps_t", [128, 512], F32); bps_t = Buf("ps_t")
    ps_g = [c.ps("ps_g%d" % i, [128, 512], F32) for i in range(2)]; bps_g = [Buf("psg0"), Buf("psg1")]
    ps_u = [c.ps("ps_u%d" % i, [128, 512], F32) for i in range(2)]; bps_u = [Buf("psu0"), Buf("psu1")]
    ps_d = [c.ps("ps_d%d" % i, [128, 512], F32) for i in range(2)]; bps_d = [Buf("psd0"), Buf("psd1")]
    idf = K["ident_f"]
    cnt = [0, 0, 0, 0]
    wst = [c.sb('ffn_wst%d' % i, [128, max(DFF, D)], F32) for i in range(2)]
    bwst = [Buf('wst%d' % i) for i in range(2)]
    nblk = (NT + TB - 1) // TB
    for blk in range(nblk):
        tl0 = blk * TB
        ntl = min(TB, NT - tl0)
        ntok = ntl * 128
        for t in range(ntl):
            i = cnt[0] % 2; cnt[0] += 1
            r0 = (tl0 + t) * 128
            c.dma("sp", lambda e, i=i, r0=r0: e.dma_start(out=xin[i][:], in_=x_dram[r0:r0 + 128, :]), writes=[bxin[i]])
            if stage == 31:
                continue
            for kh in range(2):
                for k in range(4):
                    kk = kh * 4 + k
                    c.op("pe", lambda e, i=i, k=k, kk=kk: e.transpose(ps_t[:, k * 128:(k + 1) * 128], xin[i][:, kk * 128:(kk + 1) * 128], idf[:]),
                         reads=[bxin[i], K["b"]], writes=[bps_t])
                if stage == 32:
                    continue
                c.op("dve", lambda e, t=t, kh=kh: e.tensor_copy(out=xT[:, kh * 4:(kh + 1) * 4, t * 128:(t + 1) * 128],
                                                      in_=ps_t[:].rearrange("p (k t) -> p k t", k=4)),
                     reads=[bps_t], writes=[bxT])
        if stage in (3, 31, 32):
            return
        for ex in range(E):
            if stage == 4 and ex == 1:
                return
            for (dst, src, nchunk, bw) in ((wgs, wg, 8, bwg), (wus, wu, 8, bwu), (wds, wd, FC, bwd)):
                wdt = src.shape[2]
                for k in range(nchunk):
                    i = cnt[3] % 2; cnt[3] += 1
                    c.dma("sp", lambda e, i=i, k=k, src=src, ex=ex, wdt=wdt: e.dma_start(out=wst[i][:, 0:wdt], in_=src[ex, k * 128:(k + 1) * 128, :]),
                          writes=[bwst[i]])
                    c.op("pool", lambda e, i=i, k=k, dst=dst, wdt=wdt: e.tensor_copy(out=dst[:, k, :], in_=wst[i][:, 0:wdt]),
                         reads=[bwst[i]], writes=[bw])
            if stage == 40:
                return
            for fc in range(FC):
                for n0 in range(0, ntok, 512):
                    nn = min(512, ntok - n0)
                    i = cnt[1] % 2; cnt[1] += 1
                    for k in range(8):
                        c.op("pe", lambda e, i=i, k=k, fc=fc, n0=n0, nn=nn: e.matmul(
                            ps_g[i][:, 0:nn], wgs[:, k, fc * 128:(fc + 1) * 128], xT[:, k, n0:n0 + nn], start=(k == 0), stop=(k == 7)),
                            reads=[bwg, bxT], writes=[bps_g[i]])
                    for k in range(8):
                        c.op("pe", lambda e, i=i, k=k, fc=fc, n0=n0, nn=nn: e.matmul(
                            ps_u[i][:, 0:nn], wus[:, k, fc * 128:(fc + 1) * 128], xT[:, k, n0:n0 + nn], start=(k == 0), stop=(k == 7)),
                            reads=[bwu, bxT], writes=[bps_u[i]])
                    if stage == 41:
                        continue
                    c.op("act", lambda e, i=i, nn=nn: e.activation(out=sg[i][:, 0:nn], in_=ps_g[i][:, 0:nn], func=AF.Silu),
                         reads=[bps_g[i]], writes=[bsg[i]])
                    c.op("dve", lambda e, i=i, fc=fc, n0=n0, nn=nn: e.tensor_tensor(
                        out=hT[:, fc, n0:n0 + nn], in0=ps_u[i][:, 0:nn], in1=sg[i][:, 0:nn], op=ALU.mult),
                        reads=[bps_u[i], bsg[i]], writes=[bhT[fc]])
            if stage in (5, 41):
                return
            for t in range(ntl):
                for h in range(2):
                    i = cnt[2] % 2; cnt[2] += 1
                    for fc in range(FC):
                        c.op("pe", lambda e, i=i, fc=fc, t=t, h=h: e.matmul(
                            ps_d[i][:], hT[:, fc, t * 128:(t + 1) * 128], wds[:, fc, h * 512:(h + 1) * 512], start=(fc == 0), stop=(fc == FC - 1)),
                            reads=[bwd, bhT[fc]], writes=[bps_d[i]])
                    gcol = gp[:, tl0 + t, ex:ex + 1]
                    if ex == 0:
                        c.op("dve", lambda e, i=i, t=t, h=h, gcol=gcol: e.tensor_scalar(
                            out=yacc[:, t, h * 512:(h + 1) * 512], in0=ps_d[i][:], scalar1=gcol, scalar2=None, op0=ALU.mult),
                            reads=[bps_d[i], bg], writes=[byacc[t]])
                    else:
                        c.op("dve", lambda e, i=i, t=t, h=h, gcol=gcol: e.scalar_tensor_tensor(
                            out=yacc[:, t, h * 512:(h + 1) * 512], in0=ps_d[i][:], scalar=gcol, in1=yacc[:, t, h * 512:(h + 1) * 512],
                            op0=ALU.mult, op1=ALU.add), reads=[bps_d[i], bg, byacc[t]], writes=[byacc[t]])
        if stage == 6:
            return
        for t in range(ntl):
            i = cnt[0] % 2; cnt[0] += 1
            r0 = (tl0 + t) * 128
            c.dma("sp", lambda e, i=i, r0=r0: e.dma_start(out=xin[i][:], in_=x_dram[r0:r0 + 128, :]), writes=[bxin[i]])
            c.op("dve", lambda e, i=i, t=t: e.scalar_tensor_tensor(out=zt[i][:], in0=xin[i][:], scalar=float(ALPHA), in1=yacc[:, t, :],
                                                                 op0=ALU.mult, op1=ALU.add), reads=[bxin[i], byacc[t]], writes=[bz[i]])
            emit_ln(c, zt[i], bz[i], ot[i], bo[i], gam, bet, bgb, lntmp)
            c.dma("sp", lambda e, i=i, r0=r0: e.dma_start(out=y_dram[r0:r0 + 128, :], in_=ot[i][:]), reads=[bo[i]])


def emit_ffn2(c, K, x_dram, aff_own_dram, affT_dram, wg, wu, wd, lng, lnb, y_dram,
              tiles_p, tiles_s, n_p, n_s, cap_p, cap_s, E=16, DFF=1408, TB=8, CAP=256, iters=26):
    NT = tiles_p + tiles_s
    FC = DFF // 128
    NST = CAP // 128
    theta, bth, A, bA = emit_thresholds(c, K, affT_dram, n_p, n_s, cap_p, cap_s, E, iters, ret_A=True)
    aff = c.sb("aff_own", [128, NT, E], F32)
    gp = c.sb("gprime", [128, NT, E], F32)
    self_ = c.sb("self", [128, NT, E], F32)
    selb = c.sb("selb", [128, NT, E], BF16)
    bg = Buf("gp")
    c.dma("sp", lambda e: e.dma_start(out=aff[:], in_=aff_own_dram.rearrange("(j p) e -> p j e", p=128)), writes=[bg])
    for g, (t0, t1) in enumerate(((0, tiles_p), (tiles_p, NT))):
        if t1 == t0:
            continue
        th = theta[:, g * E:(g + 1) * E].unsqueeze(1).to_broadcast([128, t1 - t0, E])
        c.op("dve", lambda e, t0=t0, t1=t1, th=th: e.tensor_tensor(out=self_[:, t0:t1, :], in0=aff[:, t0:t1, :], in1=th, op=ALU.is_ge),
             reads=[bg, bth], writes=[bg])
        c.op("dve", lambda e, t0=t0, t1=t1: e.tensor_tensor(out=gp[:, t0:t1, :], in0=self_[:, t0:t1, :], in1=aff[:, t0:t1, :], op=ALU.mult),
             reads=[bg], writes=[bg])
    c.op("dve", lambda e: e.tensor_copy(out=selb[:], in_=self_[:]), reads=[bg], writes=[bg])
    gam = c.sb("ffn_gam", [128, D], F32)
    bet = c.sb("ffn_bet", [128, D], F32)
    bgb = Buf("gb")
    c.dma("sp", lambda e: e.dma_start(out=gam[:], in_=lng.partition_broadcast(128)), writes=[bgb])
    c.dma("sp", lambda e: e.dma_start(out=bet[:], in_=lnb.partition_broadcast(128)), writes=[bgb])
    lntmp = {"st": c.sb("ln_st", [128, 2, 6], F32), "mv": c.sb("ln_mv", [128, 2], F32), "rs": c.sb("ln_rs", [128, 1], F32), "b": Buf("lnt")}
    stri = c.sb("f_stri", [128, 128], BF16)
    ones_b = c.sb("f_onesb", [128, 128], BF16)
    iot = c.sb("f_iota", [128, CAP], F32)
    trif = c.sb("f_trif", [128, 128], F32)
    c.op("pool", lambda e: e.affine_select(out=trif[:], in_=K["ones_f"][:], pattern=[[1, 128]], compare_op=ALU.is_gt, fill=0.0, base=0, channel_multiplier=-1),
         reads=[K["b"]], writes=[K["b"]])
    c.op("pool", lambda e: e.tensor_copy(out=stri[:], in_=trif[:]), reads=[K["b"]], writes=[K["b"]])
    c.op("pool", lambda e: e.tensor_copy(out=ones_b[:], in_=K["ones_f"][:]), reads=[K["b"]], writes=[K["b"]])
    c.op("pool", lambda e: e.iota(iot[:], pattern=[[1, CAP]], base=0, channel_multiplier=0, allow_small_or_imprecise_dtypes=True), writes=[K["b"]])

    asz = E * (n_p + n_s)
    need = 8 * DFF
    if asz >= need:
        Ab = A[:].rearrange("p e n -> p (e n)")
        wgs = Ab[:, 0:4 * DFF].bitcast(BF16).rearrange("p (k f) -> p k f", k=8)
        wus = Ab[:, 4 * DFF:8 * DFF].bitcast(BF16).rearrange("p (k f) -> p k f", k=8)
        alias = [bA]
    else:
        wgs = c.sb("ffn_wg", [128, 8, DFF], BF16)[:]
        wus = c.sb("ffn_wu", [128, 8, DFF], BF16)[:]
        alias = []
    wds = c.sb("ffn_wd", [128, FC, D], BF16)
    bwg, bwu, bwd = Buf("wg"), Buf("wu"), Buf("wd")
    xtok = c.sb("f_xtok", [128, TB, D], BF16); bxtok = Buf("xtok")
    xeT = c.sb("f_xeT", [128, 8, CAP], BF16); bxeT = Buf("xeT")
    hT = c.sb("f_hT", [128, FC, CAP], BF16); bhT = [Buf("hT%d" % i) for i in range(FC)]
    osb = c.sb("f_os", [128, NST, D], BF16); bos = Buf("os")
    OH = c.sb("f_OH", [128, TB, CAP], BF16); bOH = Buf("OH")
    OHT = c.sb("f_OHT", [128, NST, TB * 128], BF16); bOHT = Buf("OHT")
    yacc = c.sb("ffn_yacc", [128, TB, D], F32); byacc = [Buf("yacc%d" % i) for i in range(TB)]
    csel = c.sb("f_csel", [128, TB, E], BF16); bcsel = Buf("csel")
    rank = c.sb("f_rank", [128, TB, E], F32); brank = Buf("rank")
    xin = [c.sb("ffn_xin%d" % i, [128, D], F32) for i in range(2)]; bxin = [Buf("xin0"), Buf("xin1")]
    sg = [c.sb("ffn_sg%d" % i, [128, CAP], BF16) for i in range(2)]; bsg = [Buf("sg0"), Buf("sg1")]
    wst = [c.sb("ffn_wst%d" % i, [128, max(DFF, D)], F32) for i in range(2)]; bwst = [Buf("wst0"), Buf("wst1")]
    ps_rank = c.ps("ps_rank", [128, 512], F32); bps_rank = Buf("psrank")
    ps_tp = c.ps("ps_tp", [128, 1024], BF16); bps_tp = Buf("pstp")
    ps_ga = c.ps("ps_ga", [128, 512], F32); bps_ga = Buf("psga")
    ps_g = c.ps("ps_g", [128, 512], F32); bps_g = Buf("psg")
    ps_u = c.ps("ps_u", [128, 512], F32); bps_u = Buf("psu")
    ps_d = c.ps("ps_d", [128, 512], F32); bps_d = Buf("psd")
    ps_s0 = c.ps("ps_s0", [128, 512], F32); ps_s = [ps_s0, ps_s0]; _b = Buf("pss0"); bps_s = [_b, _b]
    idb = K["ident_b"]
    cnt = [0, 0, 0, 0]
    nblk = (NT + TB - 1) // TB
    first_w = [True]
    for blk in range(nblk):
        tl0 = blk * TB
        ntl = min(TB, NT - tl0)
        for t in range(ntl):
            i = cnt[0] % 2; cnt[0] += 1
            r0 = (tl0 + t) * 128
            c.dma("sp", lambda e, i=i, r0=r0: e.dma_start(out=xin[i][:], in_=x_dram[r0:r0 + 128, :]), writes=[bxin[i]])
            c.op("dve", lambda e, i=i, t=t: e.tensor_copy(out=xtok[:, t, :], in_=xin[i][:]), reads=[bxin[i]], writes=[bxtok])
        c.op("dve", lambda e: e.memset(csel[:, 0, :], 0.0), writes=[bcsel])
        for t in range(1, ntl):
            c.op("dve", lambda e, t=t, tl0=tl0: e.tensor_tensor(out=csel[:, t, :], in0=csel[:, t - 1, :], in1=selb[:, tl0 + t - 1, :], op=ALU.add),
                 reads=[bg, bcsel], writes=[bcsel])
        c.op("pe", lambda e, ntl=ntl, tl0=tl0: e.matmul(ps_rank[:, 0:ntl * E], stri[:], selb[:, tl0:tl0 + ntl, :].rearrange("p t e -> p (t e)"), start=True, stop=False),
             reads=[bg, K["b"]], writes=[bps_rank])
        c.op("pe", lambda e, ntl=ntl: e.matmul(ps_rank[:, 0:ntl * E], ones_b[:], csel[:, 0:ntl, :].rearrange("p t e -> p (t e)"), start=False, stop=True),
             reads=[bcsel, K["b"]], writes=[bps_rank])
        c.op("dve", lambda e, ntl=ntl: e.tensor_copy(out=rank[:, 0:ntl, :].rearrange("p t e -> p (t e)"), in_=ps_rank[:, 0:ntl * E]), reads=[bps_rank], writes=[brank])
        for ex in range(E):
            for (dst, src, nchunk, bw, al) in ((wgs, wg, 8, bwg, alias), (wus, wu, 8, bwu, alias), (wds[:], wd, FC, bwd, [])):
                wdt = src.shape[2]
                for k in range(nchunk):
                    i = cnt[3] % 2; cnt[3] += 1
                    c.dma("sp", lambda e, i=i, k=k, src=src, ex=ex, wdt=wdt: e.dma_start(out=wst[i][:, 0:wdt], in_=src[ex, k * 128:(k + 1) * 128, :]),
                          writes=[bwst[i]])
                    c.op("pool", lambda e, i=i, k=k, dst=dst, wdt=wdt: e.tensor_copy(out=dst[:, k, :], in_=wst[i][:, 0:wdt]),
                         reads=[bwst[i]], writes=[bw] + (al if first_w[0] else []))
            first_w[0] = False
            for t in range(ntl):
                c.op("dve", lambda e, t=t, ex=ex, tl0=tl0: e.tensor_scalar(out=OH[:, t, :], in0=iot[:], scalar1=rank[:, t, ex:ex + 1], scalar2=self_[:, tl0 + t, ex:ex + 1],
                                                               op0=ALU.is_equal, op1=ALU.mult), reads=[brank, bg, K["b"]], writes=[bOH])
            kper = 512 // CAP
            for k0 in range(0, 8, kper):
                for kk in range(kper):
                    k = k0 + kk
                    for t in range(ntl):
                        c.op("pe", lambda e, k=k, kk=kk, t=t, ntl=ntl: e.matmul(ps_ga[:, kk * CAP:(kk + 1) * CAP], xtok[:, t, k * 128:(k + 1) * 128], OH[:, t, :],
                                                                               start=(t == 0), stop=(t == ntl - 1)), reads=[bxtok, bOH], writes=[bps_ga])
                c.op("dve", lambda e, k0=k0: e.tensor_copy(out=xeT[:, k0:k0 + kper, :], in_=ps_ga[:, 0:kper * CAP].rearrange("p (k n) -> p k n", k=kper)),
                     reads=[bps_ga], writes=[bxeT])
            for st_ in range(NST):
                for t in range(ntl):
                    c.op("pe", lambda e, st_=st_, t=t: e.transpose(ps_tp[:, t * 128:(t + 1) * 128], OH[:, t, st_ * 128:(st_ + 1) * 128], idb[:]),
                         reads=[bOH, K["b"]], writes=[bps_tp])
                c.op("dve", lambda e, st_=st_, ntl=ntl: e.tensor_copy(out=OHT[:, st_, 0:ntl * 128], in_=ps_tp[:, 0:ntl * 128]), reads=[bps_tp], writes=[bOHT])
            for fc in range(FC):
                i = cnt[1] % 2; cnt[1] += 1
                for k in range(8):
                    c.op("pe", lambda e, k=k, fc=fc: e.matmul(ps_g[:, 0:CAP], wgs[:, k, fc * 128:(fc + 1) * 128], xeT[:, k, :], start=(k == 0), stop=(k == 7)),
                         reads=[bwg, bxeT], writes=[bps_g])
                for k in range(8):
                    c.op("pe", lambda e, k=k, fc=fc: e.matmul(ps_u[:, 0:CAP], wus[:, k, fc * 128:(fc + 1) * 128], xeT[:, k, :], start=(k == 0), stop=(k == 7)),
                         reads=[bwu, bxeT], writes=[bps_u])
                c.op("act", lambda e, i=i: e.activation(out=sg[i][:], in_=ps_g[:, 0:CAP], func=AF.Silu), reads=[bps_g], writes=[bsg[i]])
                c.op("dve", lambda e, i=i, fc=fc: e.tensor_tensor(out=hT[:, fc, :], in0=ps_u[:, 0:CAP], in1=sg[i][:], op=ALU.mult),
                     reads=[bps_u, bsg[i]], writes=[bhT[fc]])
            for st_ in range(NST):
                for h in range(2):
                    for fc in range(FC):
                        c.op("pe", lambda e, st_=st_, h=h, fc=fc: e.matmul(ps_d[:], hT[:, fc, st_ * 128:(st_ + 1) * 128], wds[:, fc, h * 512:(h + 1) * 512],
                                                                        start=(fc == 0), stop=(fc == FC - 1)), reads=[bwd, bhT[fc]], writes=[bps_d])
                    c.op("dve", lambda e, st_=st_, h=h: e.tensor_copy(out=osb[:, st_, h * 512:(h + 1) * 512], in_=ps_d[:]), reads=[bps_d], writes=[bos])
            for t in range(ntl):
                for h in range(2):
                    i = cnt[2] % 2; cnt[2] += 1
                    for st_ in range(NST):
                        c.op("pe", lambda e, i=i, st_=st_, t=t, h=h: e.matmul(ps_s[i][:], OHT[:, st_, t * 128:(t + 1) * 128], osb[:, st_, h * 512:(h + 1) * 512],
                                                                           start=(st_ == 0), stop=(st_ == NST - 1)), reads=[bOHT, bos], writes=[bps_s[i]])
                    gcol = gp[:, tl0 + t, ex:ex + 1]
                    if ex == 0:
                        c.op("dve", lambda e, i=i, t=t, h=h, gcol=gcol: e.tensor_scalar(
                            out=yacc[:, t, h * 512:(h + 1) * 512], in0=ps_s[i][:], scalar1=gcol, scalar2=None, op0=ALU.mult),
                            reads=[bps_s[i], bg], writes=[byacc[t]])
                    else:
                        c.op("dve", lambda e, i=i, t=t, h=h, gcol=gcol: e.scalar_tensor_tensor(
                            out=yacc[:, t, h * 512:(h + 1) * 512], in0=ps_s[i][:], scalar=gcol, in1=yacc[:, t, h * 512:(h + 1) * 512],
                            op0=ALU.mult, op1=ALU.add), reads=[bps_s[i], bg, byacc[t]], writes=[byacc[t]])
        for t in range(ntl):
            i = cnt[0] % 2; cnt[0] += 1
            r0 = (tl0 + t) * 128
            c.dma("sp", lambda e, i=i, r0=r0: e.dma_start(out=xin[i][:], in_=x_dram[r0:r0 + 128, :]), writes=[bxin[i]])
            c.op("dve", lambda e, i=i, t=t: e.scalar_tensor_tensor(out=xin[i][:], in0=xin[i][:], scalar=float(ALPHA), in1=yacc[:, t, :],
                                                                 op0=ALU.mult, op1=ALU.add), reads=[bxin[i], byacc[t]], writes=[bxin[i]])
            emit_ln(c, xin[i], bxin[i], xin[i], bxin[i], gam, bet, bgb, lntmp)
            c.dma("sp", lambda e, i=i, r0=r0: e.dma_start(out=y_dram[r0:r0 + 128, :], in_=xin[i][:]), reads=[bxin[i]])


NAW, MLW = 512, 512
EVEN_IN = 3600
SEQ = 2048
NTILE = 16


def na_table_host(rpb):
    NEG = -30000.0
    qc = np.arange(64)
    kc = np.arange(64)
    c0 = np.clip(qc - 8, 0, 48)
    colvalid = (kc[:, None] >= c0[None, :]) & (kc[:, None] < c0[None, :] + 16)
    dc = np.clip(kc[:, None] - qc[None, :] + 15, 0, 30)
    T = np.full((8, 3, 128, 16, 64), NEG, np.float32)
    for i in range(16):
        for half in range(2):
            dr = i - 1 + half
            if dr < 0 or dr > 14:
                continue
            vals = np.where(colvalid[None], rpb[:, dr][:, dc], NEG).astype(np.float32)
            sl = slice(half * 64, half * 64 + 64)
            T[:, 0, sl, i, :] = vals
            if half == 1:
                T[:, 1, sl, i, :] = vals
            else:
                T[:, 2, sl, i, :] = vals
    return T


class PSB:
    def __init__(self, c, n=8):
        self.t = [c.ps("psb%d" % i, [128, 512], F32) for i in range(n)]
        self.b = [Buf("psb%d" % i) for i in range(n)]


def load_cast_weight(c, dst, bdst, src, ncol, st, bst, cnt):
    for k in range(8):
        for c0 in range(0, ncol, 1024):
            cw = min(1024, ncol - c0)
            i = cnt[0] % len(st); cnt[0] += 1
            c.dma("sp", lambda e, i=i, k=k, c0=c0, cw=cw: e.dma_start(out=st[i][:, 0:cw], in_=src[k * 128:(k + 1) * 128, c0:c0 + cw]), writes=[bst[i]])
            c.op("pool", lambda e, i=i, k=k, c0=c0, cw=cw: e.tensor_copy(out=dst[:, k, c0:c0 + cw], in_=st[i][:, 0:cw]), reads=[bst[i]], writes=[bdst])


def emit_xT(c, K, P, x_dram, r0, ntiles, xT, bxT, xin, bxin, cnt):
    idf = K["ident_f"]
    for t in range(ntiles):
        i = cnt[0] % 2; cnt[0] += 1
        c.dma("sp", lambda e, i=i, t=t: e.dma_start(out=xin[i][:], in_=x_dram[r0 + t * 128:r0 + (t + 1) * 128, :]), writes=[bxin[i]])
        for kh in range(2):
            for k in range(4):
                kk = kh * 4 + k
                c.op("pe", lambda e, i=i, k=k, kk=kk: e.transpose(P.t[0][:, k * 128:(k + 1) * 128], xin[i][:, kk * 128:(kk + 1) * 128], idf[:]),
                     reads=[bxin[i], K["b"]], writes=[P.b[0]])
            c.op("dve", lambda e, t=t, kh=kh: e.tensor_copy(out=xT[:, kh * 4:(kh + 1) * 4, t * 128:(t + 1) * 128],
                                                         in_=P.t[0][:].rearrange("p (k t) -> p k t", k=4)), reads=[P.b[0]], writes=[bxT])


def proj_fm(c, P, W, bW, xT, bxT, col0, dst, bdst, scale, ntok, pc):
    for n0 in range(0, ntok, 512):
        j = 1 + (pc[0] % 2); pc[0] += 1
        for k in range(8):
            c.op("pe", lambda e, j=j, k=k, n0=n0: e.matmul(P.t[j][:], W[:, k, col0:col0 + 128], xT[:, k, n0:n0 + 512], start=(k == 0), stop=(k == 7)),
                 reads=[bW, bxT], writes=[P.b[j]])
        c.op("dve", lambda e, j=j, n0=n0: e.tensor_scalar(out=dst[:, n0:n0 + 512], in0=P.t[j][:], scalar1=float(scale), scalar2=None, op0=ALU.mult),
             reads=[P.b[j]], writes=[bdst])


def proj_tm(c, P, W, bW, xT, bxT, col0, ncol, dstfn, bdst, scale, ntile, pc):
    g = max(1, 512 // ncol)
    g = min(g, 4)
    for t0 in range(0, ntile, g):
        j = 1 + (pc[0] % 2); pc[0] += 1
        gg = min(g, ntile - t0)
        for ti in range(gg):
            t = t0 + ti
            for k in range(8):
                c.op("pe", lambda e, j=j, k=k, t=t, ti=ti: e.matmul(P.t[j][:, ti * ncol:(ti + 1) * ncol], xT[:, k, t * 128:(t + 1) * 128], W[:, k, col0:col0 + ncol],
                                                                     start=(k == 0), stop=(k == 7)), reads=[bW, bxT], writes=[P.b[j]])
        c.op("dve", lambda e, j=j, t0=t0, gg=gg: e.tensor_scalar(out=dstfn(t0, gg), in0=P.t[j][:, 0:gg * ncol].rearrange("p (g n) -> p g n", g=gg),
                                                                scalar1=float(scale), scalar2=None, op0=ALU.mult), reads=[P.b[j]], writes=[bdst])


def emit_mix_a(c, K, P, x_dram, w_in, w_out, gate_bias, norm_g, lng, lnb, w_r, na_tab, x1_dram, aff_dram, nseq, stage=99, dbg=None):
    Wout = c.sb("a_wout", [128, 8, D], BF16); bWout = Buf("wout")
    stg = [c.sb("a_stg%d" % i, [128, 8, 128], F32) for i in range(2)]; bstg = [Buf("stg0"), Buf("stg1")]
    st = [t[:].rearrange("p k n -> p (k n)") for t in stg]; bst = bstg
    scnt = [0]
    load_cast_weight(c, Wout, bWout, w_out, D, st, bst, scnt)
    WS = {n: c.sb("a_ws_" + n, [128, 8, 128], BF16) for n in ("q", "k", "v", "o")}
    bWS = {n: Buf("ws_" + n) for n in WS}
    WG = c.sb("a_wg16", [128, 8, 16], BF16); bWG = Buf("wg16")

    def load_w(name, col0, ncol=128):
        dst, bd = (WS[name], bWS[name]) if name != "g" else (WG, bWG)
        i = scnt[0] % 2; scnt[0] += 1
        c.dma("sp", lambda e, i=i: e.dma_start(out=stg[i][:, :, 0:ncol], in_=w_in[:, col0:col0 + ncol].rearrange("(k p) n -> p k n", p=128)), writes=[bstg[i]])
        c.op("pool", lambda e, i=i: e.tensor_copy(out=dst[:, :, 0:ncol], in_=stg[i][:, :, 0:ncol]), reads=[bstg[i]], writes=[bd])
    Wr = c.sb("a_wr", [128, 8, 16], F32); bWr = Buf("wr")
    c.dma("sp", lambda e: e.dma_start(out=Wr[:], in_=w_r.rearrange("(k p) e -> p k e", p=128)), writes=[bWr])
    gb = c.sb("a_gb", [128, 16], F32)
    ng = c.sb("a_ng", [128, 512], F32)
    gam = c.sb("a_gam", [128, D], F32)
    bet = c.sb("a_bet", [128, D], F32)
    bgb = Buf("gb")
    c.dma("sp", lambda e: e.dma_start(out=gb[:], in_=gate_bias.partition_broadcast(128)), writes=[bgb])
    c.dma("sp", lambda e: e.dma_start(out=ng[:], in_=norm_g.partition_broadcast(128)), writes=[bgb])
    c.dma("sp", lambda e: e.dma_start(out=gam[:], in_=lng.partition_broadcast(128)), writes=[bgb])
    c.dma("sp", lambda e: e.dma_start(out=bet[:], in_=lnb.partition_broadcast(128)), writes=[bgb])
    ones_f = K["ones_f"]
    tri = c.sb("a_tri", [128, 128], F32)
    trir = c.sb("a_trir", [128, 128], F32)
    ones_b = c.sb("a_onesb", [128, 128], BF16)
    c.op("pool", lambda e: e.affine_select(out=tri[:], in_=ones_f[:], pattern=[[1, 128]], compare_op=ALU.is_ge, fill=0.0, base=0, channel_multiplier=-1),
         reads=[K["b"]], writes=[K["b"]])
    c.op("pool", lambda e: e.affine_select(out=trir[:], in_=ones_f[:], pattern=[[-1, 128]], compare_op=ALU.is_ge, fill=0.0, base=0, channel_multiplier=1),
         reads=[K["b"]], writes=[K["b"]])
    c.op("pool", lambda e: e.tensor_copy(out=ones_b[:], in_=ones_f[:]), reads=[K["b"]], writes=[K["b"]])
    lntmp = {"st": c.sb("ln_st", [128, 2, 6], F32), "mv": c.sb("ln_mv", [128, 2], F32), "rs": c.sb("ln_rs", [128, 1], F32), "b": Buf("lnt")}

    xT = c.sb("a_xT", [128, 8, SEQ], BF16); bxT = Buf("xT")
    yT = c.sb("a_yT", [128, 8, SEQ], BF16); byT = [Buf("yT%d" % i) for i in range(8)]
    xin = [c.sb("a_xin%d" % i, [128, D], F32) for i in range(2)]; bxin = [Buf("xin0"), Buf("xin1")]
    QT = c.sb("a_QT", [128, SEQ], BF16); bQT = Buf("QT")
    KT = c.sb("a_KT", [128, SEQ], BF16); bKT = Buf("KT")
    Vb = c.sb("a_V", [128, NTILE * 132], BF16); bV = Buf("V")
    Ktm = c.sb("a_Ktm", [128, NTILE, 128], BF16); bKtm = Buf("Ktm")
    sgob = c.sb("a_sgob", [128, NTILE, 128], F32); bsgob = Buf("sgob")
    hacc = c.sb("a_hacc", [128, NTILE, 128], F32); bhacc = Buf("hacc")
    EB = c.sb("a_EB", [128, 3, 16, 64], F32); bEB = Buf("EB")
    pexp = [c.sb("a_pexp%d" % i, [128, 5 * 64], F32) for i in range(2)]; bpexp = [Buf("pexp0"), Buf("pexp1")]
    PT = [c.sb("a_PT%d" % i, [128, 5, 64], BF16) for i in range(2)]; bPT = [Buf("PT0"), Buf("PT1")]
    rec = c.sb("a_rec", [128, 512], F32); brec = Buf("rec")
    G = c.sb("a_G", [128, NTILE, 16], F32); bG = Buf("G")
    nlf = c.sb("a_nlf", [128, NTILE, 8], F32)
    uu = c.sb("a_u", [128, NTILE, 8], F32)
    vv = c.sb("a_v", [128, NTILE, 8], F32)
    eL = c.sb("a_eL", [128, NTILE, 8], F32)
    bgate = Buf("gate")
    CN = c.sb("a_CN", [128, 129], F32); bCN = Buf("CN")
    CNb = c.sb("a_CNb", [128, 129], BF16); bCNb = Buf("CNb")
    ctmp = c.sb("a_ctmp", [128, 129], F32); bctmp = Buf("ctmp")
    St = [c.sb("a_St%d" % i, [128, 128], BF16) for i in range(2)]; bSt = [Buf("St0"), Buf("St1")]
    Kt = [c.sb("a_Kt%d" % i, [128, 128], BF16) for i in range(2)]; bKt = [Buf("Kt0"), Buf("Kt1")]
    sm = c.sb("a_sm", [128, 8], F32); bsm = Buf("sm")
    lnw = c.sb("a_lnw", [128, NTILE, 128], F32); blnw = Buf("lnw")
    lns = c.sb("a_lns", [128, NTILE, 4], F32); blns = Buf("lns")
    x1T = c.sb("a_x1T", [128, 8, 128], F32); bx1T = Buf("x1T")
    rt = c.sb("a_rt", [128, 16], F32); brt = Buf("rt")
    rtm = c.sb("a_rtm", [128, 4], F32)
    affo = [c.sb("a_aff%d" % i, [128, 16], F32) for i in range(2)]; baffo = [Buf("aff0"), Buf("aff1")]
    xc = [0]; pc = [0]; sc = [0]; hc = [0]; zc = [0]

    for s in range(nseq):
        r0 = s * SEQ
        emit_xT(c, K, P, x_dram, r0, NTILE, xT, bxT, xin, bxin, xc)
        for hp in range(4):
            load_w("q", hp * 128); load_w("k", 512 + hp * 128); load_w("v", 1024 + hp * 128)
            proj_fm(c, P, WS["q"], bWS["q"], xT, bxT, 0, QT, bQT, 0.125, SEQ, pc)
            proj_fm(c, P, WS["k"], bWS["k"], xT, bxT, 0, KT, bKT, 1.0, SEQ, pc)
            Vv = Vb[:, 0:NTILE * 128].rearrange("p (t n) -> p t n", t=NTILE)
            proj_tm(c, P, WS["v"], bWS["v"], xT, bxT, 0, 128, lambda t0, gg: Vv[:, t0:t0 + gg, :], bV, 1.0, NTILE, pc)
            for hh in range(2):
                pb = hh * 64
                c.dma("sp", lambda e, hp=hp, hh=hh: e.dma_start(out=EB[:].rearrange("p v i q -> p v (i q)"),
                                                         in_=na_tab[2 * hp + hh].rearrange("v p i q -> p v (i q)")), writes=[bEB])
                c.op("act", lambda e: e.activation(out=EB[:].rearrange("p v i q -> p (v i q)"), in_=EB[:].rearrange("p v i q -> p (v i q)"), func=AF.Exp),
                     reads=[bEB], writes=[bEB])
                for r in range(32):
                    rs = min(max(r - 4, 0), 24)
                    a0, a1 = rs // 2, (rs + 7) // 2
                    nt = a1 - a0 + 1
                    si = sc[0] % 2; sc[0] += 1
                    psS = P.t[3 + si]; bS = P.b[3 + si]
                    for j in range(nt):
                        a = a0 + j
                        c.op("pe", lambda e, j=j, a=a, psS=psS, pb=pb, r=r: e.matmul(psS[:, j * 64:(j + 1) * 64], KT[pb:pb + 64, a * 128:(a + 1) * 128],
                                                                                    QT[pb:pb + 64, r * 64:(r + 1) * 64], start=True, stop=True),
                             reads=[bKT, bQT], writes=[bS])
                    c.op("act", lambda e, si=si, psS=psS, nt=nt: e.activation(out=pexp[si][:, 0:nt * 64], in_=psS[:, 0:nt * 64], func=AF.Exp),
                         reads=[bS], writes=[bpexp[si]])
                    for j in range(nt):
                        a = a0 + j
                        var = 1 if 2 * a < rs else (2 if 2 * a + 1 >= rs + 8 else 0)
                        ii = 2 * a - r + 7 + 1
                        c.op("dve", lambda e, si=si, j=j, var=var, ii=ii, hh=hh: e.tensor_tensor(out=PT[si][:, j, :], in0=pexp[si][:, j * 64:(j + 1) * 64],
                                                                                          in1=EB[:, var, ii, :], op=ALU.mult),
                             reads=[bpexp[si], bEB], writes=[bPT[si]])
                    rr = r % 8
                    for j in range(nt):
                        a = a0 + j
                        c.op("pe", lambda e, si=si, j=j, a=a, rr=rr, nt=nt: e.matmul(P.t[5][:, rr * 64:(rr + 1) * 64], Vv[:, a, :], PT[si][:, j, :],
                                                                                  start=(j == 0), stop=(j == nt - 1)), reads=[bV, bPT[si]], writes=[P.b[5]])
                    for j in range(nt):
                        c.op("pe", lambda e, si=si, j=j, rr=rr, nt=nt: e.matmul(P.t[6][:, rr * 64:(rr + 1) * 64], ones_b[:], PT[si][:, j, :],
                                                                             start=(j == 0), stop=(j == nt - 1)), reads=[K["b"], bPT[si]], writes=[P.b[6]])
                    if rr == 7:
                        q0 = (r - 7) * 64
                        c.op("dve", lambda e: e.reciprocal(out=rec[:], in_=P.t[6][:]), reads=[P.b[6]], writes=[brec])
                        c.op("dve", lambda e, pb=pb, hp=hp, q0=q0: e.tensor_tensor(out=yT[pb:pb + 64, hp, q0:q0 + 512], in0=P.t[5][pb:pb + 64, :],
                                                                                in1=rec[pb:pb + 64, :], op=ALU.mult),
                             reads=[P.b[5], brec], writes=[byT[hp]])
        if stage == 1:
            continue
        load_w("g", 3584, 16)
        Gps = P.t[1][:, 0:NTILE * 16].rearrange("p (t n) -> p t n", t=NTILE)
        for t in range(NTILE):
            for k in range(8):
                c.op("pe", lambda e, t=t, k=k: e.matmul(Gps[:, t, :], xT[:, k, t * 128:(t + 1) * 128], WG[:, k, :], start=(k == 0), stop=(k == 7)),
                     reads=[bxT, bWG], writes=[P.b[1]])
        c.op("dve", lambda e: e.tensor_tensor(out=G[:], in0=Gps, in1=gb[:].unsqueeze(1).to_broadcast([128, NTILE, 16]), op=ALU.add),
             reads=[P.b[1], bgb], writes=[bG])
        c.op("act", lambda e: e.activation(out=nlf[:], in_=G[:, :, 8:16], func=AF.Exp, scale=-1.0), reads=[bG], writes=[bgate])
        c.op("act", lambda e: e.activation(out=nlf[:], in_=nlf[:], func=AF.Ln, bias=1.0, scale=1.0), reads=[bgate], writes=[bgate])
        cum = P.t[2][:, 0:NTILE * 8].rearrange("p (t n) -> p t n", t=NTILE)
        tot = P.t[2][:, 256:256 + NTILE * 8].rearrange("p (t n) -> p t n", t=NTILE)
        for t in range(NTILE):
            c.op("pe", lambda e, t=t: e.matmul(cum[:, t, 0:4], tri[:], nlf[:, t, 0:4], start=True, stop=True), reads=[bgate, K["b"]], writes=[P.b[2]])
            c.op("pe", lambda e, t=t: e.matmul(cum[:, t, 4:8], trir[:], nlf[:, t, 4:8], start=True, stop=True), reads=[bgate, K["b"]], writes=[P.b[2]])
            c.op("pe", lambda e, t=t: e.matmul(tot[:, t, :], ones_f[:], nlf[:, t, :], start=True, stop=True), reads=[bgate, K["b"]], writes=[P.b[2]])
        c.op("act", lambda e: e.activation(out=uu[:], in_=cum, func=AF.Exp, scale=-1.0), reads=[P.b[2]], writes=[bgate])
        c.op("act", lambda e: e.activation(out=eL[:], in_=tot, func=AF.Exp, scale=-1.0), reads=[P.b[2]], writes=[bgate])
        c.op("dve", lambda e: e.tensor_tensor(out=vv[:], in0=cum, in1=G[:, :, 0:8], op=ALU.add), reads=[P.b[2], bG], writes=[bgate])
        c.op("act", lambda e: e.activation(out=vv[:], in_=vv[:], func=AF.Exp), reads=[bgate], writes=[bgate])
        for h in range(4):
            load_w("q", 1536 + h * 128); load_w("k", 2048 + h * 128); load_w("v", 2560 + h * 128); load_w("o", 3072 + h * 128)
            proj_fm(c, P, WS["q"], bWS["q"], xT, bxT, 0, QT, bQT, 1.0, SEQ, pc)
            proj_fm(c, P, WS["k"], bWS["k"], xT, bxT, 0, KT, bKT, 128 ** -0.5, SEQ, pc)
            proj_tm(c, P, WS["k"], bWS["k"], xT, bxT, 0, 128, lambda t0, gg: Ktm[:, t0:t0 + gg, :], bKtm, 128 ** -0.5, NTILE, pc)
            Va = Vb[:, 0:NTILE * 129].rearrange("p (t n) -> p t n", t=NTILE)
            proj_tm(c, P, WS["v"], bWS["v"], xT, bxT, 0, 128, lambda t0, gg: Va[:, t0:t0 + gg, 0:128], bV, 1.0, NTILE, pc)
            c.op("pool", lambda e: e.memset(Va[:, :, 128:129], 1.0), reads=[], writes=[bV])
            proj_tm(c, P, WS["o"], bWS["o"], xT, bxT, 0, 128, lambda t0, gg: sgob[:, t0:t0 + gg, :], bsgob, 1.0, NTILE, pc)
            c.op("act", lambda e: e.activation(out=sgob[:].rearrange("p t n -> p (t n)"), in_=sgob[:].rearrange("p t n -> p (t n)"), func=AF.Sigmoid),
                 reads=[bsgob], writes=[bsgob])
            for dr in range(2):
                gi = dr * 4 + h
                mask = tri if dr == 0 else trir
                c.op("pool", lambda e: e.memset(CN[:], 0.0), writes=[bCN])
                c.op("pool", lambda e: e.memset(CNb[:], 0.0), writes=[bCNb])
                order = range(NTILE) if dr == 0 else range(NTILE - 1, -1, -1)
                for ci in order:
                    si = sc[0] % 2; sc[0] += 1
                    hi_ = hc[0] % 2; hc[0] += 1
                    psS = P.t[3 + si]; bS = P.b[3 + si]
                    psH = P.t[5] if hi_ == 0 else P.t[7]; bH = P.b[5] if hi_ == 0 else P.b[7]
                    cs = slice(ci * 128, (ci + 1) * 128)
                    c.op("pe", lambda e, psS=psS, cs=cs: e.matmul(psS[:, 0:128], KT[:, cs], QT[:, cs], start=True, stop=True), reads=[bKT, bQT], writes=[bS])
                    c.op("dve", lambda e, psS=psS, si=si, ci=ci, gi=gi, mask=mask: e.scalar_tensor_tensor(
                        out=St[si][:], in0=psS[:, 0:128], scalar=vv[:, ci, gi:gi + 1], in1=mask[:], op0=ALU.mult, op1=ALU.mult),
                        reads=[bS, bgate, K["b"]], writes=[bSt[si]])
                    c.op("pe", lambda e, psH=psH, si=si, ci=ci: e.matmul(psH[:, 0:129], St[si][:], Va[:, ci, :], start=True, stop=False),
                         reads=[bSt[si], bV], writes=[bH])
                    c.op("pe", lambda e, psH=psH, cs=cs: e.matmul(psH[:, 0:129], QT[:, cs], CNb[:], start=False, stop=True),
                         reads=[bQT, bCNb], writes=[bH])
                    c.op("dve", lambda e, psH=psH, ci=ci, gi=gi: e.tensor_tensor(out=sm[:, 0:1], in0=psH[:, 128:129], in1=uu[:, ci, gi:gi + 1], op=ALU.mult),
                         reads=[bH, bgate], writes=[bsm])
                    c.op("dve", lambda e: e.tensor_scalar(out=sm[:, 4:5], in0=sm[:, 0:1], scalar1=-1.0, scalar2=None, op0=ALU.mult), reads=[bsm], writes=[bsm])
                    c.op("dve", lambda e: e.tensor_tensor(out=sm[:, 5:6], in0=sm[:, 0:1], in1=sm[:, 4:5], op=ALU.max), reads=[bsm], writes=[bsm])
                    c.op("dve", lambda e: e.tensor_scalar(out=sm[:, 1:2], in0=sm[:, 5:6], scalar1=1.0, scalar2=None, op0=ALU.max), reads=[bsm], writes=[bsm])
                    c.op("dve", lambda e: e.reciprocal(out=sm[:, 2:3], in_=sm[:, 1:2]), reads=[bsm], writes=[bsm])
                    c.op("dve", lambda e, ci=ci, gi=gi: e.tensor_tensor(out=sm[:, 3:4], in0=sm[:, 2:3], in1=uu[:, ci, gi:gi + 1], op=ALU.mult),
                         reads=[bsm, bgate], writes=[bsm])
                    if dr == 0:
                        c.op("dve", lambda e, psH=psH, ci=ci: e.tensor_scalar(out=hacc[:, ci, :], in0=psH[:, 0:128], scalar1=sm[:, 3:4], scalar2=None, op0=ALU.mult),
                             reads=[bH, bsm], writes=[bhacc])
                    else:
                        c.op("dve", lambda e, psH=psH, ci=ci: e.scalar_tensor_tensor(out=hacc[:, ci, :], in0=psH[:, 0:128], scalar=sm[:, 3:4], in1=hacc[:, ci, :],
                                                                                   op0=ALU.mult, op1=ALU.add), reads=[bH, bsm, bhacc], writes=[bhacc])
                    c.op("pool", lambda e, si=si, ci=ci, gi=gi: e.tensor_scalar(out=Kt[si][:], in0=Ktm[:, ci, :], scalar1=vv[:, ci, gi:gi + 1], scalar2=None, op0=ALU.mult),
                         reads=[bKtm, bgate], writes=[bKt[si]])
                    c.op("pe", lambda e, si=si, ci=ci: e.matmul(P.t[6][:, 0:129], Kt[si][:], Va[:, ci, :], start=True, stop=True),
                         reads=[bKt[si], bV], writes=[P.b[6]])
                    c.op("dve", lambda e: e.tensor_tensor(out=ctmp[:], in0=P.t[6][:, 0:129], in1=CN[:], op=ALU.add), reads=[P.b[6], bCN], writes=[bctmp])
                    c.op("dve", lambda e, ci=ci, gi=gi: e.tensor_scalar(out=CN[:], in0=ctmp[:], scalar1=eL[:, ci, gi:gi + 1], scalar2=None, op0=ALU.mult),
                         reads=[bctmp, bgate], writes=[bCN])
                    c.op("pool", lambda e, ci=ci, gi=gi: e.tensor_scalar(out=CNb[:], in0=ctmp[:], scalar1=eL[:, ci, gi:gi + 1], scalar2=None, op0=ALU.mult),
                         reads=[bctmp, bgate], writes=[bCNb])
            c.op("dve", lambda e: e.tensor_reduce(out=lns[:, :, 0], in_=hacc[:], axis=AX.X, op=ALU.add), reads=[bhacc], writes=[blns])
            c.op("dve", lambda e: e.tensor_scalar(out=lns[:, :, 0], in0=lns[:, :, 0], scalar1=1.0 / 128, scalar2=None, op0=ALU.mult), reads=[blns], writes=[blns])
            c.op("dve", lambda e: e.tensor_tensor(out=lnw[:], in0=hacc[:], in1=lns[:, :, 0:1].to_broadcast([128, NTILE, 128]), op=ALU.subtract),
                 reads=[bhacc, blns], writes=[blnw])
            c.op("pool", lambda e: e.tensor_tensor(out=hacc[:], in0=lnw[:], in1=lnw[:], op=ALU.mult), reads=[blnw, bhacc], writes=[bhacc])
            c.op("dve", lambda e: e.tensor_reduce(out=lns[:, :, 1], in_=hacc[:], axis=AX.X, op=ALU.add), reads=[bhacc], writes=[blns])
            c.op("dve", lambda e: e.tensor_scalar(out=lns[:, :, 1], in0=lns[:, :, 1], scalar1=1.0 / 128, scalar2=EPS, op0=ALU.mult, op1=ALU.add), reads=[blns], writes=[blns])
            c.op("act", lambda e: e.activation(out=lns[:, :, 2], in_=lns[:, :, 1], func=AF.Sqrt), reads=[blns], writes=[blns])
            c.op("dve", lambda e: e.reciprocal(out=lns[:, :, 3], in_=lns[:, :, 2]), reads=[blns], writes=[blns])
            c.op("dve", lambda e: e.tensor_tensor(out=lnw[:], in0=lnw[:], in1=lns[:, :, 3:4].to_broadcast([128, NTILE, 128]), op=ALU.mult), reads=[blnw, blns], writes=[blnw])
            c.op("pool", lambda e, h=h: e.tensor_tensor(out=lnw[:], in0=lnw[:], in1=ng[:, h * 128:(h + 1) * 128].unsqueeze(1).to_broadcast([128, NTILE, 128]), op=ALU.mult),
                 reads=[blnw, bgb], writes=[blnw])
            c.op("dve", lambda e: e.tensor_tensor(out=lnw[:], in0=lnw[:], in1=sgob[:], op=ALU.mult), reads=[blnw, bsgob], writes=[blnw])
            for tq in range(4):
                for k in range(4):
                    t = tq * 4 + k
                    c.op("pe", lambda e, t=t, k=k: e.transpose(P.t[0][:, k * 128:(k + 1) * 128], lnw[:, t, :], K["ident_f"][:]), reads=[blnw, K["b"]], writes=[P.b[0]])
                c.op("dve", lambda e, tq=tq, h=h: e.tensor_copy(out=yT[:, 4 + h, tq * 512:(tq + 1) * 512], in_=P.t[0][:]), reads=[P.b[0]], writes=[byT[4 + h]])
        if stage == 2:
            continue
        for t in range(NTILE):
            i = xc[0] % 2; xc[0] += 1
            zi = i
            rr0 = r0 + t * 128
            c.dma("sp", lambda e, i=i, rr0=rr0: e.dma_start(out=xin[i][:], in_=x_dram[rr0:rr0 + 128, :]), writes=[bxin[i]])
            for hf in range(2):
                for k in range(8):
                    c.op("pe", lambda e, hf=hf, k=k, t=t: e.matmul(P.t[1 + hf][:], yT[:, k, t * 128:(t + 1) * 128], Wout[:, k, hf * 512:(hf + 1) * 512],
                                                                start=(k == 0), stop=(k == 7)), reads=[byT[k], bWout], writes=[P.b[1 + hf]])
                c.op("dve", lambda e, hf=hf, i=i, zi=zi: e.scalar_tensor_tensor(out=xin[zi][:, hf * 512:(hf + 1) * 512], in0=xin[i][:, hf * 512:(hf + 1) * 512], scalar=float(ALPHA),
                                                                       in1=P.t[1 + hf][:], op0=ALU.mult, op1=ALU.add), reads=[bxin[i], P.b[1 + hf]], writes=[bxin[zi]])
            emit_ln(c, xin[zi], bxin[zi], xin[zi], bxin[zi], gam, bet, bgb, lntmp)
            c.dma("sp", lambda e, zi=zi, rr0=rr0: e.dma_start(out=x1_dram[rr0:rr0 + 128, :], in_=xin[zi][:]), reads=[bxin[zi]])
            emit_router(c, K, P, xin[zi], bxin[zi], Wr, bWr, x1T, bx1T, rt, brt, rtm, affo[zi], baffo[zi], aff_dram, rr0)


def emit_router(c, K, P, z, bz, Wr, bWr, x1T, bx1T, rt, brt, rtm, affo, baffo, aff_dram, rr0):
    idf = K["ident_f"]
    for kh in range(2):
        for k in range(4):
            kk = kh * 4 + k
            c.op("pe", lambda e, k=k, kk=kk: e.transpose(P.t[0][:, k * 128:(k + 1) * 128], z[:, kk * 128:(kk + 1) * 128], idf[:]), reads=[bz, K["b"]], writes=[P.b[0]])
        c.op("dve", lambda e, kh=kh: e.tensor_copy(out=x1T[:, kh * 4:(kh + 1) * 4, :], in_=P.t[0][:].rearrange("p (k t) -> p k t", k=4)), reads=[P.b[0]], writes=[bx1T])
    for k in range(8):
        c.op("pe", lambda e, k=k: e.matmul(P.t[3][:, 0:16], x1T[:, k, :], Wr[:, k, :], start=(k == 0), stop=(k == 7)), reads=[bx1T, bWr], writes=[P.b[3]])
    c.op("dve", lambda e: e.tensor_reduce(out=rtm[:, 0:1], in_=P.t[3][:, 0:16], axis=AX.X, op=ALU.max), reads=[P.b[3]], writes=[brt])
    c.op("dve", lambda e: e.tensor_scalar(out=rtm[:, 1:2], in0=rtm[:, 0:1], scalar1=-1.0, scalar2=None, op0=ALU.mult), reads=[brt], writes=[brt])
    c.op("act", lambda e: e.activation(out=rt[:], in_=P.t[3][:, 0:16], func=AF.Exp, bias=rtm[:, 1:2], scale=1.0, accum_out=rtm[:, 2:3]), reads=[P.b[3], brt], writes=[brt])
    c.op("dve", lambda e: e.reciprocal(out=rtm[:, 3:4], in_=rtm[:, 2:3]), reads=[brt], writes=[brt])
    c.op("dve", lambda e: e.tensor_scalar(out=affo[:], in0=rt[:], scalar1=rtm[:, 3:4], scalar2=None, op0=ALU.mult), reads=[brt], writes=[baffo])
    c.dma("sp", lambda e: e.dma_start(out=aff_dram[rr0:rr0 + 128, :], in_=affo[:]), reads=[baffo])


def rope_tables_host():
    d = 64
    inv = (10000.0 ** (-np.arange(0, d, 2, dtype=np.float32) / d)).astype(np.float32)
    ang = np.arange(SEQ, dtype=np.float32)[:, None] * inv[None, :]
    cos, sin = np.cos(ang).astype(np.float32), np.sin(ang).astype(np.float32)
    COS = np.concatenate([cos, cos], 1).T
    SINS = np.concatenate([-sin, sin], 1).T
    COS = np.ascontiguousarray(np.concatenate([COS, COS], 0), dtype=np.float32)
    SINS = np.ascontiguousarray(np.concatenate([SINS, SINS], 0), dtype=np.float32)
    p = np.arange(128)[:, None]
    x = np.arange(3968)[None, :]
    dl = p - x + 1920
    m = (np.abs(dl) <= 64).astype(np.float32) + ((dl % 4 == 0) & (np.abs(dl) <= 256)) + ((dl % 16 == 0) & (np.abs(dl) <= 1024))
    return COS, SINS, np.ascontiguousarray(m.astype(np.float32))


def emit_mix_b(c, K, P, x_dram, w_in, w_out, lng, lnb, w_r, cos_d, sins_d, tab_d, x1_dram, aff_dram, nseq):
    Wout = c.sb("b_wout", [128, 8, D], BF16); bWout = Buf("wout")
    stg = [c.sb("b_stg%d" % i, [128, 8, 128], F32) for i in range(2)]; bstg = [Buf("stg0"), Buf("stg1")]
    st = [t[:].rearrange("p k n -> p (k n)") for t in stg]
    scnt = [0]
    load_cast_weight(c, Wout, bWout, w_out, D, st, bstg, scnt)
    WS = {n: c.sb("b_ws_" + n, [128, 8, 128], BF16) for n in ("q", "qs", "k", "ks", "v")}
    bWS = {n: Buf("ws_" + n) for n in WS}

    def load_w(name, col0, swap=False):
        dst, bd = WS[name], bWS[name]
        i = scnt[0] % 2; scnt[0] += 1
        src = w_in[:, col0:col0 + 128]
        if not swap:
            c.dma("sp", lambda e, i=i: e.dma_start(out=stg[i][:], in_=src.rearrange("(k p) n -> p k n", p=128)), writes=[bstg[i]])
        else:
            s5 = src.rearrange("(k p) (h two d) -> p k h two d", p=128, h=2, two=2)
            d5 = stg[i][:].rearrange("p k (h two d) -> p k h two d", h=2, two=2)
            for a in range(2):
                for hd in range(2):
                    c.dma("sp", lambda e, i=i, a=a, hd=hd: e.dma_start(out=d5[:, :, hd, 1 - a, :], in_=s5[:, :, hd, a, :]), writes=[bstg[i]])
        c.op("pool", lambda e, i=i: e.tensor_copy(out=dst[:], in_=stg[i][:]), reads=[bstg[i]], writes=[bd])

    Wr = c.sb("b_wr", [128, 8, 16], F32); bWr = Buf("wr")
    c.dma("sp", lambda e: e.dma_start(out=Wr[:], in_=w_r.rearrange("(k p) e -> p k e", p=128)), writes=[bWr])
    gam = c.sb("b_gam", [128, D], F32)
    bet = c.sb("b_bet", [128, D], F32)
    bgb = Buf("gb")
    c.dma("sp", lambda e: e.dma_start(out=gam[:], in_=lng.partition_broadcast(128)), writes=[bgb])
    c.dma("sp", lambda e: e.dma_start(out=bet[:], in_=lnb.partition_broadcast(128)), writes=[bgb])
    COS = c.sb("b_cos", [128, SEQ], F32); SINS = c.sb("b_sins", [128, SEQ], F32)
    TABf = c.sb("b_tabf", [128, 3968], F32); TAB = c.sb("b_tab", [128, 3968], BF16)
    btab = Buf("tab")
    c.dma("sp", lambda e: e.dma_start(out=COS[:], in_=cos_d), writes=[btab])
    c.dma("sp", lambda e: e.dma_start(out=SINS[:], in_=sins_d), writes=[btab])
    c.dma("sp", lambda e: e.dma_start(out=TABf[:], in_=tab_d), writes=[btab])
    c.op("pool", lambda e: e.tensor_copy(out=TAB[:], in_=TABf[:]), reads=[btab], writes=[btab])
    ones_f = K["ones_f"]
    ones_b = c.sb("b_onesb", [128, 128], BF16)
    c.op("pool", lambda e: e.tensor_copy(out=ones_b[:], in_=ones_f[:]), reads=[K["b"]], writes=[K["b"]])
    lntmp = {"st": c.sb("ln_st", [128, 2, 6], F32), "mv": c.sb("ln_mv", [128, 2], F32), "rs": c.sb("ln_rs", [128, 1], F32), "b": Buf("lnt")}

    xT = c.sb("b_xT", [128, 8, SEQ], BF16); bxT = Buf("xT")
    yT = c.sb("b_yT", [128, 8, SEQ], BF16); byT = [Buf("yT%d" % i) for i in range(8)]
    xin = [c.sb("b_xin%d" % i, [128, D], F32) for i in range(2)]; bxin = [Buf("xin0"), Buf("xin1")]
    QT = c.sb("b_QT", [128, SEQ], BF16); bQT = Buf("QT")
    KT = c.sb("b_KT", [128, SEQ], BF16); bKT = Buf("KT")
    Vv = c.sb("b_V", [128, NTILE, 128], BF16); bV = Buf("V")
    t1 = c.sb("b_t1", [128, 512], F32); bt1 = Buf("t1")
    t2 = c.sb("b_t2", [128, 512], F32); bt2 = Buf("t2")
    pexp = [c.sb("b_pexp%d" % i, [128, 512], BF16) for i in range(2)]; bpexp = [Buf("pexp0"), Buf("pexp1")]
    PT = [c.sb("b_PT%d" % i, [128, 512], BF16) for i in range(2)]; bPT = [Buf("PT0"), Buf("PT1")]
    rec = c.sb("b_rec", [128, 512], F32); brec = Buf("rec")
    x1T = c.sb("b_x1T", [128, 8, 128], F32); bx1T = Buf("x1T")
    rt = c.sb("b_rt", [128, 16], F32); brt = Buf("rt")
    rtm = c.sb("b_rtm", [128, 4], F32)
    affo = [c.sb("b_aff%d" % i, [128, 16], F32) for i in range(2)]; baffo = [Buf("aff0"), Buf("aff1")]
    xc = [0]; pc = [0]; sc = [0]

    def proj_rope(wa, wb, dst, bdst):
        for n0 in range(0, SEQ, 512):
            for k in range(8):
                c.op("pe", lambda e, k=k, n0=n0: e.matmul(P.t[1][:], WS[wa][:, k, :], xT[:, k, n0:n0 + 512], start=(k == 0), stop=(k == 7)),
                     reads=[bWS[wa], bxT], writes=[P.b[1]])
            for k in range(8):
                c.op("pe", lambda e, k=k, n0=n0: e.matmul(P.t[2][:], WS[wb][:, k, :], xT[:, k, n0:n0 + 512], start=(k == 0), stop=(k == 7)),
                     reads=[bWS[wb], bxT], writes=[P.b[2]])
            c.op("dve", lambda e, n0=n0: e.tensor_tensor(out=t1[:], in0=P.t[1][:], in1=COS[:, n0:n0 + 512], op=ALU.mult), reads=[P.b[1], btab], writes=[bt1])
            c.op("dve", lambda e, n0=n0: e.tensor_tensor(out=t2[:], in0=P.t[2][:], in1=SINS[:, n0:n0 + 512], op=ALU.mult), reads=[P.b[2], btab], writes=[bt2])
            c.op("pool", lambda e, n0=n0: e.tensor_tensor(out=dst[:, n0:n0 + 512], in0=t1[:], in1=t2[:], op=ALU.add), reads=[bt1, bt2], writes=[bdst])

    for s in range(nseq):
        r0 = s * SEQ
        emit_xT(c, K, P, x_dram, r0, NTILE, xT, bxT, xin, bxin, xc)
        for hp in range(8):
            load_w("q", hp * 128); load_w("qs", hp * 128, swap=True)
            load_w("k", 1024 + hp * 128); load_w("ks", 1024 + hp * 128, swap=True)
            load_w("v", 2048 + hp * 128)
            proj_rope("q", "qs", QT, bQT)
            proj_rope("k", "ks", KT, bKT)
            proj_tm(c, P, WS["v"], bWS["v"], xT, bxT, 0, 128, lambda t0, gg: Vv[:, t0:t0 + gg, :], bV, 1.0, NTILE, pc)
            for hh in range(2):
                pb = hh * 64
                for qb in range(4):
                    kts = []
                    for kt in range(NTILE):
                        lo = 128 * kt - 512 * qb - 511
                        hi = 128 * kt + 127 - 512 * qb
                        if lo > 1024 or hi < -1024:
                            continue
                        kts.append(kt)
                    for j, kt in enumerate(kts):
                        si = sc[0] % 2; sc[0] += 1
                        psS = P.t[3 + si]; bS = P.b[3 + si]
                        c.op("pe", lambda e, psS=psS, kt=kt, qb=qb, pb=pb: e.matmul(psS[:], KT[pb:pb + 64, kt * 128:(kt + 1) * 128], QT[pb:pb + 64, qb * 512:(qb + 1) * 512],
                                                                                 start=True, stop=True), reads=[bKT, bQT], writes=[bS])
                        c.op("act", lambda e, psS=psS, si=si: e.activation(out=pexp[si][:], in_=psS[:], func=AF.Exp, scale=0.125), reads=[bS], writes=[bpexp[si]])
                        x0 = 512 * qb - 128 * kt + 1920
                        c.op("dve", lambda e, si=si, x0=x0: e.tensor_tensor(out=PT[si][:], in0=pexp[si][:], in1=TAB[:, x0:x0 + 512], op=ALU.mult),
                             reads=[bpexp[si], btab], writes=[bPT[si]])
                        c.op("pe", lambda e, si=si, kt=kt, j=j, n=len(kts): e.matmul(P.t[5][:], Vv[:, kt, :], PT[si][:], start=(j == 0), stop=(j == n - 1)),
                             reads=[bV, bPT[si]], writes=[P.b[5]])
                        c.op("pe", lambda e, si=si, j=j, n=len(kts): e.matmul(P.t[6][:], ones_b[:], PT[si][:], start=(j == 0), stop=(j == n - 1)),
                             reads=[K["b"], bPT[si]], writes=[P.b[6]])
                    c.op("dve", lambda e: e.reciprocal(out=rec[:], in_=P.t[6][:]), reads=[P.b[6]], writes=[brec])
                    c.op("dve", lambda e, pb=pb, hp=hp, qb=qb: e.tensor_tensor(out=yT[pb:pb + 64, hp, qb * 512:(qb + 1) * 512], in0=P.t[5][pb:pb + 64, :],
                                                                            in1=rec[pb:pb + 64, :], op=ALU.mult), reads=[P.b[5], brec], writes=[byT[hp]])
        for t in range(NTILE):
            i = xc[0] % 2; xc[0] += 1
            rr0 = r0 + t * 128
            c.dma("sp", lambda e, i=i, rr0=rr0: e.dma_start(out=xin[i][:], in_=x_dram[rr0:rr0 + 128, :]), writes=[bxin[i]])
            for hf in range(2):
                for k in range(8):
                    c.op("pe", lambda e, hf=hf, k=k, t=t: e.matmul(P.t[1 + hf][:], yT[:, k, t * 128:(t + 1) * 128], Wout[:, k, hf * 512:(hf + 1) * 512],
                                                                start=(k == 0), stop=(k == 7)), reads=[byT[k], bWout], writes=[P.b[1 + hf]])
                c.op("dve", lambda e, hf=hf, i=i: e.scalar_tensor_tensor(out=xin[i][:, hf * 512:(hf + 1) * 512], in0=xin[i][:, hf * 512:(hf + 1) * 512], scalar=float(ALPHA),
                                                                 in1=P.t[1 + hf][:], op0=ALU.mult, op1=ALU.add), reads=[bxin[i], P.b[1 + hf]], writes=[bxin[i]])
            emit_ln(c, xin[i], bxin[i], xin[i], bxin[i], gam, bet, bgb, lntmp)
            c.dma("sp", lambda e, i=i, rr0=rr0: e.dma_start(out=x1_dram[rr0:rr0 + 128, :], in_=xin[i][:]), reads=[bxin[i]])
            emit_router(c, K, P, xin[i], bxin[i], Wr, bWr, x1T, bx1T, rt, brt, rtm, affo[i], baffo[i], aff_dram, rr0)


NCORES = 8
TILES_P, TILES_S = 32, 64
E_ = 16
DFF_ = 1408


def build_ffn_program():
    NT = TILES_P + TILES_S
    n_p, n_s = NCORES * TILES_P, NCORES * TILES_S
    cap_p, cap_s = 2 * n_p * 128 // E_, 2 * n_s * 128 // E_
    nc = bass.Bass("TRN2", target_bir_lowering=False)
    di = lambda n, s: nc.dram_tensor(n, s, F32, kind="ExternalInput").ap()
    x = di("x", [NT * 128, D]); aff = di("aff", [NT * 128, E_]); affT = di("affT", [128, E_, n_p + n_s])
    wg = di("wg", [E_, D, DFF_]); wu = di("wu", [E_, D, DFF_]); wd = di("wd", [E_, DFF_, D])
    lng = di("lng", [D]); lnb = di("lnb", [D])
    y = nc.dram_tensor("y", [NT * 128, D], F32, kind="ExternalOutput").ap()
    c = Ctx(nc)
    K = emit_consts(c)
    emit_ffn2(c, K, x, aff, affT, wg, wu, wd, lng, lnb, y, TILES_P, TILES_S, n_p, n_s, cap_p, cap_s, E=E_, DFF=DFF_, TB=8, CAP=256)
    c.finish()
    return nc


def build_a_program(nseq=6):
    nc = bass.Bass("TRN2", target_bir_lowering=False)
    di = lambda n, s: nc.dram_tensor(n, s, F32, kind="ExternalInput").ap()
    x = di("x", [nseq * 2048, D]); w_in = di("w_in", [D, 3600]); w_out = di("w_out", [D, D]); gbias = di("gbias", [16]); ng = di("ng", [512])
    lng = di("lng", [D]); lnb = di("lnb", [D]); wr = di("wr", [D, 16]); tab = di("tab", [8, 3, 128, 16, 64])
    x1 = nc.dram_tensor("x1", [nseq * 2048, D], F32, kind="ExternalOutput").ap()
    aff = nc.dram_tensor("aff", [nseq * 2048, 16], F32, kind="ExternalOutput").ap()
    c = Ctx(nc); K = emit_consts(c); P = PSB(c)
    emit_mix_a(c, K, P, x, w_in, w_out, gbias, ng, lng, lnb, wr, tab, x1, aff, nseq)
    c.finish()
    return nc


def build_b_program(nseq=6):
    nc = bass.Bass("TRN2", target_bir_lowering=False)
    di = lambda n, s: nc.dram_tensor(n, s, F32, kind="ExternalInput").ap()
    x = di("x", [nseq * 2048, D]); w_in = di("w_in", [D, 3072]); w_out = di("w_out", [D, D])
    lng = di("lng", [D]); lnb = di("lnb", [D]); wr = di("wr", [D, 16]); cos = di("cos", [128, 2048]); sins = di("sins", [128, 2048]); tab = di("tab", [128, 3968])
    x1 = nc.dram_tensor("x1", [nseq * 2048, D], F32, kind="ExternalOutput").ap()
    aff = nc.dram_tensor("aff", [nseq * 2048, 16], F32, kind="ExternalOutput").ap()
    c = Ctx(nc); K = emit_consts(c); P = PSB(c)
    emit_mix_b(c, K, P, x, w_in, w_out, lng, lnb, wr, cos, sins, tab, x1, aff, nseq)
    c.finish()
    return nc


def _aff_layout(affs):
    def toT(a):
        return a.reshape(-1, 128, E_).transpose(1, 2, 0)
    ap = np.concatenate([a[:TILES_P * 128] for a in affs], 0)
    as_ = np.concatenate([a[TILES_P * 128:] for a in affs], 0)
    return np.ascontiguousarray(np.concatenate([toT(ap), toT(as_)], axis=2), dtype=np.float32)


def kernel(x_prompt, x_sample, even_w_in, ml_gate_bias, na_rpb, ml_norm_g, even_w_out, da_w_in, da_w_out,
           ln_mix_g, ln_mix_b, ec_router, ec_w_gate, ec_w_up, ec_w_down, ln_ffn_g, ln_ffn_b):
    f32 = lambda a: np.ascontiguousarray(np.asarray(a), dtype=np.float32)
    x_prompt, x_sample = f32(x_prompt), f32(x_sample)
    cores = list(range(NCORES))
    xs = [np.concatenate([x_prompt[2 * c:2 * c + 2].reshape(-1, D), x_sample[4 * c:4 * c + 4].reshape(-1, D)], 0) for c in cores]
    tabA = na_table_host(f32(na_rpb)[0])
    feedA = {"w_in": f32(even_w_in)[0], "w_out": f32(even_w_out)[0], "gbias": f32(ml_gate_bias)[0], "ng": f32(ml_norm_g)[0],
             "lng": f32(ln_mix_g)[0], "lnb": f32(ln_mix_b)[0], "wr": f32(ec_router)[0], "tab": tabA}
    ncA = build_a_program()
    res = run_bass_kernel_spmd(ncA, [dict(feedA, x=xs[c]) for c in cores], core_ids=cores)
    x1 = [res.results[c]["x1"] for c in cores]; aff = [res.results[c]["aff"] for c in cores]
    del ncA
    ncF = build_ffn_program()
    def run_ffn(xl, affl, layer):
        affT = _aff_layout(affl)
        feed = {"affT": affT, "wg": f32(ec_w_gate)[layer], "wu": f32(ec_w_up)[layer], "wd": f32(ec_w_down)[layer],
                "lng": f32(ln_ffn_g)[layer], "lnb": f32(ln_ffn_b)[layer]}
        r = run_bass_kernel_spmd(ncF, [dict(feed, x=xl[c], aff=affl[c]) for c in cores], core_ids=cores)
        return [r.results[c]["y"] for c in cores]
    x2 = run_ffn(x1, aff, 0)
    COS, SINS, TAB = rope_tables_host()
    feedB = {"w_in": f32(da_w_in)[0], "w_out": f32(da_w_out)[0], "lng": f32(ln_mix_g)[1], "lnb": f32(ln_mix_b)[1],
             "wr": f32(ec_router)[1], "cos": COS, "sins": SINS, "tab": TAB}
    ncB = build_b_program()
    res = run_bass_kernel_spmd(ncB, [dict(feedB, x=x2[c]) for c in cores], core_ids=cores)
    x3 = [res.results[c]["x1"] for c in cores]; aff2 = [res.results[c]["aff"] for c in cores]
    del ncB
    y = run_ffn(x3, aff2, 1)
    y_prompt = np.stack([y[c][:TILES_P * 128].reshape(2, 2048, D) for c in cores], 0).reshape(16, 2048, D)
    y_sample = np.stack([y[c][TILES_P * 128:].reshape(4, 2048, D) for c in cores], 0).reshape(32, 2048, D)
    return (np.ascontiguousarray(y_prompt, dtype=np.float32), np.ascontiguousarray(y_sample, dtype=np.float32))
```

```python
from concourse.bass_utils import run_bass_kernel_spmd
import numpy as np
import concourse.bass as bass
import concourse.mybir as mybir

F32 = mybir.dt.float32
BF16 = mybir.dt.bfloat16
I32 = mybir.dt.int32
U32 = mybir.dt.uint32
ALU = mybir.AluOpType
AF = mybir.ActivationFunctionType
AX = mybir.AxisListType


class Buf:
    __slots__ = ("name", "writers", "readers")

    def __init__(self, name=""):
        self.name = name
        self.writers = {}
        self.readers = []


class EngS:
    def __init__(self, name, self_wait):
        self.name = name
        self.ops = []
        self.self_wait = self_wait
        self.known = {}
        self.n = 0


class Ctx:
    ENG = ("pe", "act", "dve", "pool", "sp")

    def __init__(self, nc, dma_pool=8):
        self.nc = nc
        self.E = {
            "pe": EngS("pe", False),
            "act": EngS("act", True),
            "dve": EngS("dve", True),
            "pool": EngS("pool", True),
            "sp": EngS("sp", False),
        }
        self.sems = {}
        self.stack = []
        cm = nc.semaphore("s_cc")
        self.sems["cc"] = cm.__enter__()
        self.stack.append(cm)
        self.cc_n = 0
        self.cc_scr = None
        for e in self.ENG:
            cm = nc.semaphore("s_" + e)
            self.sems[e] = cm.__enter__()
            self.stack.append(cm)
        self.dma_pool = {}
        self.dma_tot = {}
        self.dma_i = {}
        for q in ("sp", "pool", "act"):
            lst = []
            for i in range(dma_pool):
                cm = nc.semaphore("d_%s%d" % (q, i))
                lst.append(cm.__enter__())
                self.stack.append(cm)
            self.dma_pool[q] = lst
            self.dma_i[q] = 0
        self.ctxs = []

    def sb(self, name, shape, dt):
        cm = self.nc.sbuf_tensor(name, list(shape), dt)
        t = cm.__enter__()
        self.ctxs.append(cm)
        return t

    def ps(self, name, shape, dt):
        cm = self.nc.psum_tensor(name, list(shape), dt)
        t = cm.__enter__()
        self.ctxs.append(cm)
        return t

    def _need(self, es, ev, waits):
        key, val = ev
        if key == es.name and not es.self_wait:
            return
        if es.known.get(key, -1) >= val:
            return
        es.known[key] = val
        waits.append(ev)

    def _deps(self, es, reads, writes):
        waits = []
        for b in reads:
            for ev in b.writers.values():
                self._need(es, ev, waits)
        for b in writes:
            for ev in b.writers.values():
                self._need(es, ev, waits)
            for ev in b.readers:
                self._need(es, ev, waits)
        return waits

    def _mark(self, ev, reads, writes):
        for b in reads:
            b.readers.append(ev)
            if len(b.readers) > 64:
                b.readers = b.readers[-48:]
        for b in writes:
            b.writers = {ev[0]: ev}
            b.readers = []

    def op(self, eng, fn, reads=(), writes=()):
        es = self.E[eng]
        waits = self._deps(es, reads, writes)
        es.n += 1
        ev = (eng, es.n)
        es.ops.append((fn, waits, ("eng", None)))
        self._mark(ev, reads, writes)
        return ev

    def dma(self, q, fn, reads=(), writes=()):
        es = self.E[q]
        pool = self.dma_pool[q]
        i = self.dma_i[q]
        self.dma_i[q] = i + 1
        sem = pool[i % len(pool)]
        skey = ("dma", q, i % len(pool))
        prev = self.dma_tot.get(skey, 0)
        waits = self._deps(es, reads, writes)
        if prev > 0:
            self._need(es, (skey, prev), waits)
        tot = prev + 16
        self.dma_tot[skey] = tot
        es.n += 1
        es.ops.append((fn, waits, ("dma", sem)))
        ev = (skey, tot)
        self._mark(ev, reads, writes)
        return ev

    def cc(self, fn, reads=(), writes=()):
        es = self.E["pool"]
        if self.cc_scr is None:
            self.cc_scr = self.sb("cc_scr", [128, 8], F32)
        waits = self._deps(es, reads, writes)
        self.cc_n += 1
        es.n += 1
        es.ops.append((fn, waits, ("cc", self.sems["cc"])))
        es.n += 1
        scr = self.cc_scr
        es.ops.append((lambda e: e.memset(scr[:], 0.0), [("cc", self.cc_n)], ("eng", None)))
        ev = ("pool", es.n)
        es.known["pool"] = es.n
        self._mark(ev, reads, writes)
        return ev

    def _sem_of(self, key):
        if isinstance(key, tuple):
            return self.dma_pool[key[1]][key[2]]
        return self.sems[key]

    def finish(self):
        nc = self.nc
        es = self.E["sp"]
        waits = []
        for skey, tot in self.dma_tot.items():
            self._need(es, (skey, tot), waits)
        for e in ("pe", "act", "dve", "pool"):
            if self.E[e].n > 0:
                self._need(es, (e, self.E[e].n), waits)
        es.ops.append((None, waits, None))
        engmap = {"pe": "tensor", "act": "scalar", "dve": "vector", "pool": "gpsimd", "sp": "sync"}
        with nc.Block() as block:
            for e in self.ENG:
                st = self.E[e]

                def body(engine, st=st, e=e):
                    own = self.sems[e]
                    for fn, waits, inc in st.ops:
                        for key, val in waits:
                            engine.wait_ge(self._sem_of(key), val)
                        if fn is None:
                            continue
                        ins = fn(engine)
                        if inc[0] == "eng":
                            ins.then_inc(own, 1)
                        elif inc[0] == "cc":
                            ins.then_inc(inc[1])
                        else:
                            ins.then_inc(inc[1], 16)

                getattr(block, engmap[e])(body)
        for cm in reversed(self.ctxs):
            cm.__exit__(None, None, None)
        for cm in reversed(self.stack):
            cm.__exit__(None, None, None)


D = 1024
ALPHA = 4 ** 0.25
EPS = 1e-5


def emit_consts(c):
    nc = c.nc
    K = {}
    K["ident_f"] = c.sb("ident_f", [128, 128], F32)
    K["ident_b"] = c.sb("ident_b", [128, 128], BF16)
    K["ones_f"] = c.sb("ones_f", [128, 128], F32)
    K["b"] = Buf("consts")
    idf, idb, of = K["ident_f"], K["ident_b"], K["ones_f"]
    c.op("pool", lambda e: e.memset(of[:], 1.0), writes=[K["b"]])
    c.op("pool", lambda e: e.affine_select(out=idf[:], in_=of[:], pattern=[[-1, 128]], compare_op=ALU.is_equal,
                                           fill=0.0, base=0, channel_multiplier=1), reads=[K["b"]], writes=[K["b"]])
    c.op("pool", lambda e: e.tensor_copy(out=idb[:], in_=idf[:]), reads=[K["b"]], writes=[K["b"]])
    return K


def emit_ln(c, z, bz, out, bout, gam, bet, bgb, tmp):
    st, mv, rs, bt = tmp["st"], tmp["mv"], tmp["rs"], tmp["b"]
    c.op("dve", lambda e: e.bn_stats(out=st[:, 0, :], in_=z[:, 0:512]), reads=[bz], writes=[bt])
    c.op("dve", lambda e: e.bn_stats(out=st[:, 1, :], in_=z[:, 512:1024]), reads=[bz], writes=[bt])
    c.op("dve", lambda e: e.bn_aggr(out=mv[:], in_=st[:].rearrange("p a b -> p (a b)")), reads=[bt], writes=[bt])
    c.op("dve", lambda e: e.tensor_scalar(out=rs[:], in0=mv[:, 1:2], scalar1=EPS, scalar2=None, op0=ALU.add), reads=[bt], writes=[bt])
    c.op("act", lambda e: e.activation(out=rs[:], in_=rs[:], func=AF.Sqrt), reads=[bt], writes=[bt])
    c.op("dve", lambda e: e.reciprocal(out=rs[:], in_=rs[:]), reads=[bt], writes=[bt])
    c.op("dve", lambda e: e.tensor_scalar(out=out[:], in0=z[:], scalar1=mv[:, 0:1], scalar2=rs[:, 0:1],
                                          op0=ALU.subtract, op1=ALU.mult), reads=[bz, bt], writes=[bout])
    c.op("pool", lambda e: e.tensor_tensor(out=out[:], in0=out[:], in1=gam[:], op=ALU.mult), reads=[bout, bgb], writes=[bout])
    c.op("pool", lambda e: e.tensor_tensor(out=out[:], in0=out[:], in1=bet[:], op=ALU.add), reads=[bout, bgb], writes=[bout])


def emit_thresholds(c, K, affT_dram, n_p, n_s, cap_p, cap_s, E, iters=26, ret_A=False):
    A = c.sb("bis_A", [128, E, n_p + n_s], F32)
    bA = Buf("bisA")
    c.dma("sp", lambda e: e.dma_start(out=A[:], in_=affT_dram), writes=[bA])
    lo = c.sb("bis_lo", [128, 2 * E], F32)
    hi = c.sb("bis_hi", [128, 2 * E], F32)
    mid = c.sb("bis_mid", [128, 2 * E], F32)
    cnt = c.sb("bis_cnt", [128, 2 * E], F32)
    capv = c.sb("bis_cap", [128, 2 * E], F32)
    ge = c.sb("bis_ge", [128, 2 * E], U32)
    lt = c.sb("bis_lt", [128, 2 * E], U32)
    junk = c.sb("bis_junk", [128, max(n_p, n_s)], BF16)
    tot = c.ps("bis_tot", [128, 2 * E], F32)
    b = Buf("bis")
    bcnt = Buf("bcnt")
    btot = Buf("btot")
    bj = Buf("junk")
    c.op("dve", lambda e: e.memset(lo[:], 0.0), writes=[b])
    c.op("dve", lambda e: e.memset(hi[:], 1.0), writes=[b])
    c.op("dve", lambda e: e.memset(mid[:], 0.5), writes=[b])
    c.op("dve", lambda e: e.memset(capv[:, 0:E], float(cap_p)), writes=[b])
    c.op("dve", lambda e: e.memset(capv[:, E:2 * E], float(cap_s)), writes=[b])
    ones = K["ones_f"]
    for it in range(iters):
        for v in range(2 * E):
            g, ex = divmod(v, E)
            src = A[:, ex, 0:n_p] if g == 0 else A[:, ex, n_p:n_p + n_s]
            n = n_p if g == 0 else n_s
            c.op("dve", lambda e, src=src, v=v, n=n: e.tensor_scalar(
                out=junk[:, 0:n], in0=src, scalar1=mid[:, v:v + 1], scalar2=None,
                op0=ALU.is_ge, op1=ALU.add, accum_out=cnt[:, v:v + 1]), reads=[bA, b], writes=[bj, bcnt])
        c.op("pe", lambda e: e.matmul(tot[:], ones[:], cnt[:], start=True, stop=True), reads=[bcnt, K["b"]], writes=[btot])
        c.op("dve", lambda e: e.tensor_tensor(out=ge[:], in0=tot[:], in1=capv[:], op=ALU.is_ge), reads=[btot, b], writes=[b])
        c.op("dve", lambda e: e.tensor_tensor(out=lt[:], in0=tot[:], in1=capv[:], op=ALU.is_lt), reads=[btot, b], writes=[b])
        c.op("dve", lambda e: e.copy_predicated(out=lo[:], mask=ge[:], data=mid[:]), reads=[b], writes=[b])
        c.op("dve", lambda e: e.copy_predicated(out=hi[:], mask=lt[:], data=mid[:]), reads=[b], writes=[b])
        c.op("dve", lambda e: e.tensor_tensor(out=mid[:], in0=lo[:], in1=hi[:], op=ALU.add), reads=[b], writes=[b])
        c.op("dve", lambda e: e.tensor_scalar(out=mid[:], in0=mid[:], scalar1=0.5, scalar2=None, op0=ALU.mult), reads=[b], writes=[b])
    if ret_A:
        return lo, b, A, bA
    return lo, b


def emit_ffn(c, K, x_dram, aff_own_dram, affT_dram, wg, wu, wd, lng, lnb, y_dram,
             tiles_p, tiles_s, n_p, n_s, cap_p, cap_s, E=16, DFF=1408, TB=8, iters=26, dbg=None, stage=99):
    NT = tiles_p + tiles_s
    FC = DFF // 128
    theta, bth = emit_thresholds(c, K, affT_dram, n_p, n_s, cap_p, cap_s, E, iters)
    if stage == 1:
        c.dma("sp", lambda e: e.dma_start(out=dbg[:, 0:2 * E], in_=theta[:]), reads=[bth])
        return
    aff = c.sb("aff_own", [128, NT, E], F32)
    gp = c.sb("gprime", [128, NT, E], F32)
    bg = Buf("gp")
    c.dma("sp", lambda e: e.dma_start(out=aff[:], in_=aff_own_dram.rearrange("(j p) e -> p j e", p=128)), writes=[bg])
    for g, (t0, t1) in enumerate(((0, tiles_p), (tiles_p, NT))):
        if t1 == t0:
            continue
        th = theta[:, g * E:(g + 1) * E].unsqueeze(1).to_broadcast([128, t1 - t0, E])
        c.op("dve", lambda e, t0=t0, t1=t1, th=th: e.tensor_tensor(out=gp[:, t0:t1, :], in0=aff[:, t0:t1, :], in1=th, op=ALU.is_ge),
             reads=[bg, bth], writes=[bg])
        c.op("dve", lambda e, t0=t0, t1=t1: e.tensor_tensor(out=gp[:, t0:t1, :], in0=gp[:, t0:t1, :], in1=aff[:, t0:t1, :], op=ALU.mult),
             reads=[bg], writes=[bg])
    if stage == 2:
        c.dma("sp", lambda e: e.dma_start(out=dbg[:, 0:NT * E], in_=gp[:].rearrange("p a b -> p (a b)")), reads=[bg])
        return
    gam = c.sb("ffn_gam", [128, D], F32)
    bet = c.sb("ffn_bet", [128, D], F32)
    bgb = Buf("gb")
    c.dma("sp", lambda e: e.dma_start(out=gam[:], in_=lng.partition_broadcast(128)), writes=[bgb])
    c.dma("sp", lambda e: e.dma_start(out=bet[:], in_=lnb.partition_broadcast(128)), writes=[bgb])
    if stage == 30:
        return
    lntmp = {"st": c.sb("ln_st", [128, 2, 6], F32), "mv": c.sb("ln_mv", [128, 2], F32), "rs": c.sb("ln_rs", [128, 1], F32), "b": Buf("lnt")}

    TOK = TB * 128
    xT = c.sb("ffn_xT", [128, 8, TOK], BF16)
    hT = c.sb("ffn_hT", [128, FC, TOK], BF16)
    yacc = c.sb("ffn_yacc", [128, TB, D], F32)
    wgs = c.sb("ffn_wg", [128, 8, DFF], BF16)
    wus = c.sb("ffn_wu", [128, 8, DFF], BF16)
    wds = c.sb("ffn_wd", [128, FC, D], BF16)
    bwg, bwu, bwd = Buf("wg"), Buf("wu"), Buf("wd")
    bxT = Buf("xT")
    bhT = [Buf("hT%d" % i) for i in range(FC)]
    byacc = [Buf("yacc%d" % i) for i in range(TB)]
    xin = [c.sb("ffn_xin%d" % i, [128, D], F32) for i in range(2)]
    bxin = [Buf("xin0"), Buf("xin1")]
    sg = [c.sb("ffn_sg%d" % i, [128, 512], BF16) for i in range(2)]
    bsg = [Buf("sg0"), Buf("sg1")]
    zt = [c.sb("ffn_z%d" % i, [128, D], F32) for i in range(2)]
    bz = [Buf("z0"), Buf("z1")]
    ot, bo = zt, bz
    ps_t = c.ps("ps_t", [128, 512], F32); bps_t = Buf("ps_t")
    ps_g = [c.ps("ps_g%d" % i, [128, 512], F32) for i in range(2)]; bps_g = [Buf("psg0"), Buf("psg1")]
    ps_u = [c.ps("ps_u%d" % i, [128, 512], F32) for i in range(2)]; bps_u = [Buf("psu0"), Buf("psu1")]
    ps_d = [c.ps("ps_d%d" % i, [128, 512], F32) for i in range(2)]; bps_d = [Buf("psd0"), Buf("psd1")]
    idf = K["ident_f"]
    cnt = [0, 0, 0, 0]
    wst = [c.sb('ffn_wst%d' % i, [128, max(DFF, D)], F32) for i in range(2)]
    bwst = [Buf('wst%d' % i) for i in range(2)]
    nblk = (NT + TB - 1) // TB
    for blk in range(nblk):
        tl0 = blk * TB
        ntl = min(TB, NT - tl0)
        ntok = ntl * 128
        for t in range(ntl):
            i = cnt[0] % 2; cnt[0] += 1
            r0 = (tl0 + t) * 128
            c.dma("sp", lambda e, i=i, r0=r0: e.dma_start(out=xin[i][:], in_=x_dram[r0:r0 + 128, :]), writes=[bxin[i]])
            if stage == 31:
                continue
            for kh in range(2):
                for k in range(4):
                    kk = kh * 4 + k
                    c.op("pe", lambda e, i=i, k=k, kk=kk: e.transpose(ps_t[:, k * 128:(k + 1) * 128], xin[i][:, kk * 128:(kk + 1) * 128], idf[:]),
                         reads=[bxin[i], K["b"]], writes=[bps_t])
                if stage == 32:
                    continue
                c.op("dve", lambda e, t=t, kh=kh: e.tensor_copy(out=xT[:, kh * 4:(kh + 1) * 4, t * 128:(t + 1) * 128],
                                                      in_=ps_t[:].rearrange("p (k t) -> p k t", k=4)),
                     reads=[bps_t], writes=[bxT])
        if stage in (3, 31, 32):
            return
        for ex in range(E):
            if stage == 4 and ex == 1:
                return
            for (dst, src, nchunk, bw) in ((wgs, wg, 8, bwg), (wus, wu, 8, bwu), (wds, wd, FC, bwd)):
                wdt = src.shape[2]
                for k in range(nchunk):
                    i = cnt[3] % 2; cnt[3] += 1
                    c.dma("sp", lambda e, i=i, k=k, src=src, ex=ex, wdt=wdt: e.dma_start(out=wst[i][:, 0:wdt], in_=src[ex, k * 128:(k + 1) * 128, :]),
                          writes=[bwst[i]])
                    c.op("pool", lambda e, i=i, k=k, dst=dst, wdt=wdt: e.tensor_copy(out=dst[:, k, :], in_=wst[i][:, 0:wdt]),
                         reads=[bwst[i]], writes=[bw])
            if stage == 40:
                return
            for fc in range(FC):
                for n0 in range(0, ntok, 512):
                    nn = min(512, ntok - n0)
                    i = cnt[1] % 2; cnt[1] += 1
                    for k in range(8):
                        c.op("pe", lambda e, i=i, k=k, fc=fc, n0=n0, nn=nn: e.matmul(
                            ps_g[i][:, 0:nn], wgs[:, k, fc * 128:(fc + 1) * 128], xT[:, k, n0:n0 + nn], start=(k == 0), stop=(k == 7)),
                            reads=[bwg, bxT], writes=[bps_g[i]])
                    for k in range(8):
                        c.op("pe", lambda e, i=i, k=k, fc=fc, n0=n0, nn=nn: e.matmul(
                            ps_u[i][:, 0:nn], wus[:, k, fc * 128:(fc + 1) * 128], xT[:, k, n0:n0 + nn], start=(k == 0), stop=(k == 7)),
                            reads=[bwu, bxT], writes=[bps_u[i]])
                    if stage == 41:
                        continue
                    c.op("act", lambda e, i=i, nn=nn: e.activation(out=sg[i][:, 0:nn], in_=ps_g[i][:, 0:nn], func=AF.Silu),
                         reads=[bps_g[i]], writes=[bsg[i]])
                    c.op("dve", lambda e, i=i, fc=fc, n0=n0, nn=nn: e.tensor_tensor(
                        out=hT[:, fc, n0:n0 + nn], in0=ps_u[i][:, 0:nn], in1=sg[i][:, 0:nn], op=ALU.mult),
                        reads=[bps_u[i], bsg[i]], writes=[bhT[fc]])
            if stage in (5, 41):
                return
            for t in range(ntl):
                for h in range(2):
                    i = cnt[2] % 2; cnt[2] += 1
                    for fc in range(FC):
                        c.op("pe", lambda e, i=i, fc=fc, t=t, h=h: e.matmul(
                            ps_d[i][:], hT[:, fc, t * 128:(t + 1) * 128], wds[:, fc, h * 512:(h + 1) * 512], start=(fc == 0), stop=(fc == FC - 1)),
                            reads=[bwd, bhT[fc]], writes=[bps_d[i]])
                    gcol = gp[:, tl0 + t, ex:ex + 1]
                    if ex == 0:
                        c.op("dve", lambda e, i=i, t=t, h=h, gcol=gcol: e.tensor_scalar(
                            out=yacc[:, t, h * 512:(h + 1) * 512], in0=ps_d[i][:], scalar1=gcol, scalar2=None, op0=ALU.mult),
                            reads=[bps_d[i], bg], writes=[byacc[t]])
                    else:
                        c.op("dve", lambda e, i=i, t=t, h=h, gcol=gcol: e.scalar_tensor_tensor(
                            out=yacc[:, t, h * 512:(h + 1) * 512], in0=ps_d[i][:], scalar=gcol, in1=yacc[:, t, h * 512:(h + 1) * 512],
                            op0=ALU.mult, op1=ALU.add), reads=[bps_d[i], bg, byacc[t]], writes=[byacc[t]])
        if stage == 6:
            return
        for t in range(ntl):
            i = cnt[0] % 2; cnt[0] += 1
            r0 = (tl0 + t) * 128
            c.dma("sp", lambda e, i=i, r0=r0: e.dma_start(out=xin[i][:], in_=x_dram[r0:r0 + 128, :]), writes=[bxin[i]])
            c.op("dve", lambda e, i=i, t=t: e.scalar_tensor_tensor(out=zt[i][:], in0=xin[i][:], scalar=float(ALPHA), in1=yacc[:, t, :],
                                                                 op0=ALU.mult, op1=ALU.add), reads=[bxin[i], byacc[t]], writes=[bz[i]])
            emit_ln(c, zt[i], bz[i], ot[i], bo[i], gam, bet, bgb, lntmp)
            c.dma("sp", lambda e, i=i, r0=r0: e.dma_start(out=y_dram[r0:r0 + 128, :], in_=ot[i][:]), reads=[bo[i]])


def emit_ffn2(c, K, x_dram, aff_own_dram, affT_dram, wg, wu, wd, lng, lnb, y_dram,
              tiles_p, tiles_s, n_p, n_s, cap_p, cap_s, E=16, DFF=1408, TB=8, CAP=256, iters=26):
    NT = tiles_p + tiles_s
    FC = DFF // 128
    NST = CAP // 128
    theta, bth, A, bA = emit_thresholds(c, K, affT_dram, n_p, n_s, cap_p, cap_s, E, iters, ret_A=True)
    aff = c.sb("aff_own", [128, NT, E], F32)
    gp = c.sb("gprime", [128, NT, E], F32)
    self_ = c.sb("self", [128, NT, E], F32)
    selb = c.sb("selb", [128, NT, E], BF16)
    bg = Buf("gp")
    c.dma("sp", lambda e: e.dma_start(out=aff[:], in_=aff_own_dram.rearrange("(j p) e -> p j e", p=128)), writes=[bg])
    for g, (t0, t1) in enumerate(((0, tiles_p), (tiles_p, NT))):
        if t1 == t0:
            continue
        th = theta[:, g * E:(g + 1) * E].unsqueeze(1).to_broadcast([128, t1 - t0, E])
        c.op("dve", lambda e, t0=t0, t1=t1, th=th: e.tensor_tensor(out=self_[:, t0:t1, :], in0=aff[:, t0:t1, :], in1=th, op=ALU.is_ge),
             reads=[bg, bth], writes=[bg])
        c.op("dve", lambda e, t0=t0, t1=t1: e.tensor_tensor(out=gp[:, t0:t1, :], in0=self_[:, t0:t1, :], in1=aff[:, t0:t1, :], op=ALU.mult),
             reads=[bg], writes=[bg])
    c.op("dve", lambda e: e.tensor_copy(out=selb[:], in_=self_[:]), reads=[bg], writes=[bg])
    gam = c.sb("ffn_gam", [128, D], F32)
    bet = c.sb("ffn_bet", [128, D], F32)
    bgb = Buf("gb")
    c.dma("sp", lambda e: e.dma_start(out=gam[:], in_=lng.partition_broadcast(128)), writes=[bgb])
    c.dma("sp", lambda e: e.dma_start(out=bet[:], in_=lnb.partition_broadcast(128)), writes=[bgb])
    lntmp = {"st": c.sb("ln_st", [128, 2, 6], F32), "mv": c.sb("ln_mv", [128, 2], F32), "rs": c.sb("ln_rs", [128, 1], F32), "b": Buf("lnt")}
    stri = c.sb("f_stri", [128, 128], BF16)
    ones_b = c.sb("f_onesb", [128, 128], BF16)
    iot = c.sb("f_iota", [128, CAP], F32)
    trif = c.sb("f_trif", [128, 128], F32)
    c.op("pool", lambda e: e.affine_select(out=trif[:], in_=K["ones_f"][:], pattern=[[1, 128]], compare_op=ALU.is_gt, fill=0.0, base=0, channel_multiplier=-1),
         reads=[K["b"]], writes=[K["b"]])
    c.op("pool", lambda e: e.tensor_copy(out=stri[:], in_=trif[:]), reads=[K["b"]], writes=[K["b"]])
    c.op("pool", lambda e: e.tensor_copy(out=ones_b[:], in_=K["ones_f"][:]), reads=[K["b"]], writes=[K["b"]])
    c.op("pool", lambda e: e.iota(iot[:], pattern=[[1, CAP]], base=0, channel_multiplier=0, allow_small_or_imprecise_dtypes=True), writes=[K["b"]])

    asz = E * (n_p + n_s)
    need = 8 * DFF
    if asz >= need:
        Ab = A[:].rearrange("p e n -> p (e n)")
        wgs = Ab[:, 0:4 * DFF].bitcast(BF16).rearrange("p (k f) -> p k f", k=8)
        wus = Ab[:, 4 * DFF:8 * DFF].bitcast(BF16).rearrange("p (k f) -> p k f", k=8)
        alias = [bA]
    else:
        wgs = c.sb("ffn_wg", [128, 8, DFF], BF16)[:]
        wus = c.sb("ffn_wu", [128, 8, DFF], BF16)[:]
        alias = []
    wds = c.sb("ffn_wd", [128, FC, D], BF16)
    bwg, bwu, bwd = Buf("wg"), Buf("wu"), Buf("wd")
    xtok = c.sb("f_xtok", [128, TB, D], BF16); bxtok = Buf("xtok")
    xeT = c.sb("f_xeT", [128, 8, CAP], BF16); bxeT = Buf("xeT")
    hT = c.sb("f_hT", [128, FC, CAP], BF16); bhT = [Buf("hT%d" % i) for i in range(FC)]
    osb = c.sb("f_os", [128, NST, D], BF16); bos = Buf("os")
    OH = c.sb("f_OH", [128, TB, CAP], BF16); bOH = Buf("OH")
    OHT = c.sb("f_OHT", [128, NST, TB * 128], BF16); bOHT = Buf("OHT")
    yacc = c.sb("ffn_yacc", [128, TB, D], F32); byacc = [Buf("yacc%d" % i) for i in range(TB)]
    csel = c.sb("f_csel", [128, TB, E], BF16); bcsel = Buf("csel")
    rank = c.sb("f_rank", [128, TB, E], F32); brank = Buf("rank")
    xin = [c.sb("ffn_xin%d" % i, [128, D], F32) for i in range(2)]; bxin = [Buf("xin0"), Buf("xin1")]
    sg = [c.sb("ffn_sg%d" % i, [128, CAP], BF16) for i in range(2)]; bsg = [Buf("sg0"), Buf("sg1")]
    wst = [c.sb("ffn_wst%d" % i, [128, max(DFF, D)], F32) for i in range(3)]; bwst = [Buf("wst%d" % i) for i in range(3)]
    ps_rank = c.ps("ps_rank", [128, 512], F32); bps_rank = Buf("psrank")
    ps_tp = c.ps("ps_tp", [128, 1024], BF16); bps_tp = Buf("pstp")
    ps_ga = c.ps("ps_ga", [128, 512], F32); bps_ga = Buf("psga")
    ps_g = c.ps("ps_g", [128, 512], F32); bps_g = Buf("psg")
    ps_u = c.ps("ps_u", [128, 512], F32); bps_u = Buf("psu")
    ps_d = c.ps("ps_d", [128, 512], F32); bps_d = Buf("psd")
    ps_s0 = c.ps("ps_s0", [128, 512], F32); ps_s = [ps_s0, ps_s0]; _b = Buf("pss0"); bps_s = [_b, _b]
    idb = K["ident_b"]
    cnt = [0, 0, 0, 0]
    nblk = (NT + TB - 1) // TB
    first_w = [True]
    for blk in range(nblk):
        tl0 = blk * TB
        ntl = min(TB, NT - tl0)
        for t in range(ntl):
            i = cnt[0] % 2; cnt[0] += 1
            r0 = (tl0 + t) * 128
            c.dma("sp", lambda e, i=i, r0=r0: e.dma_start(out=xin[i][:], in_=x_dram[r0:r0 + 128, :]), writes=[bxin[i]])
            c.op("dve", lambda e, i=i, t=t: e.tensor_copy(out=xtok[:, t, :], in_=xin[i][:]), reads=[bxin[i]], writes=[bxtok])
        c.op("dve", lambda e: e.memset(csel[:, 0, :], 0.0), writes=[bcsel])
        for t in range(1, ntl):
            c.op("dve", lambda e, t=t, tl0=tl0: e.tensor_tensor(out=csel[:, t, :], in0=csel[:, t - 1, :], in1=selb[:, tl0 + t - 1, :], op=ALU.add),
                 reads=[bg, bcsel], writes=[bcsel])
        c.op("pe", lambda e, ntl=ntl, tl0=tl0: e.matmul(ps_rank[:, 0:ntl * E], stri[:], selb[:, tl0:tl0 + ntl, :].rearrange("p t e -> p (t e)"), start=True, stop=False),
             reads=[bg, K["b"]], writes=[bps_rank])
        c.op("pe", lambda e, ntl=ntl: e.matmul(ps_rank[:, 0:ntl * E], ones_b[:], csel[:, 0:ntl, :].rearrange("p t e -> p (t e)"), start=False, stop=True),
             reads=[bcsel, K["b"]], writes=[bps_rank])
        c.op("dve", lambda e, ntl=ntl: e.tensor_copy(out=rank[:, 0:ntl, :].rearrange("p t e -> p (t e)"), in_=ps_rank[:, 0:ntl * E]), reads=[bps_rank], writes=[brank])
        for ex in range(E):
            for (dst, src, nchunk, bw, al) in ((wgs, wg, 8, bwg, alias), (wus, wu, 8, bwu, alias), (wds[:], wd, FC, bwd, [])):
                wdt = src.shape[2]
                for k in range(nchunk):
                    i = cnt[3] % 3; cnt[3] += 1
                    c.dma("sp", lambda e, i=i, k=k, src=src, ex=ex, wdt=wdt: e.dma_start(out=wst[i][:, 0:wdt], in_=src[ex, k * 128:(k + 1) * 128, :]),
                          writes=[bwst[i]])
                    ce = ("pool", "act", "pool", "act", "dve")[cnt[3] % 5]
                    if ce == "act":
                        c.op("act", lambda e, i=i, k=k, dst=dst, wdt=wdt: e.activation(out=dst[:, k, :], in_=wst[i][:, 0:wdt], func=AF.Identity),
                             reads=[bwst[i]], writes=[bw] + (al if first_w[0] else []))
                    else:
                        c.op(ce, lambda e, i=i, k=k, dst=dst, wdt=wdt: e.tensor_copy(out=dst[:, k, :], in_=wst[i][:, 0:wdt]),
                             reads=[bwst[i]], writes=[bw] + (al if first_w[0] else []))
            first_w[0] = False
            for t in range(ntl):
                c.op("dve", lambda e, t=t, ex=ex, tl0=tl0: e.tensor_scalar(out=OH[:, t, :], in0=iot[:], scalar1=rank[:, t, ex:ex + 1], scalar2=self_[:, tl0 + t, ex:ex + 1],
                                                               op0=ALU.is_equal, op1=ALU.mult), reads=[brank, bg, K["b"]], writes=[bOH])
            kper = 512 // CAP
            for k0 in range(0, 8, kper):
                for kk in range(kper):
                    k = k0 + kk
                    for t in range(ntl):
                        c.op("pe", lambda e, k=k, kk=kk, t=t, ntl=ntl: e.matmul(ps_ga[:, kk * CAP:(kk + 1) * CAP], xtok[:, t, k * 128:(k + 1) * 128], OH[:, t, :],
                                                                               start=(t == 0), stop=(t == ntl - 1)), reads=[bxtok, bOH], writes=[bps_ga])
                c.op("dve", lambda e, k0=k0: e.tensor_copy(out=xeT[:, k0:k0 + kper, :], in_=ps_ga[:, 0:kper * CAP].rearrange("p (k n) -> p k n", k=kper)),
                     reads=[bps_ga], writes=[bxeT])
            for st_ in range(NST):
                for t in range(ntl):
                    c.op("pe", lambda e, st_=st_, t=t: e.transpose(ps_tp[:, t * 128:(t + 1) * 128], OH[:, t, st_ * 128:(st_ + 1) * 128], idb[:]),
                         reads=[bOH, K["b"]], writes=[bps_tp])
                c.op("dve", lambda e, st_=st_, ntl=ntl: e.tensor_copy(out=OHT[:, st_, 0:ntl * 128], in_=ps_tp[:, 0:ntl * 128]), reads=[bps_tp], writes=[bOHT])
            for fc in range(FC):
                i = cnt[1] % 2; cnt[1] += 1
                for k in range(8):
                    c.op("pe", lambda e, k=k, fc=fc: e.matmul(ps_g[:, 0:CAP], wgs[:, k, fc * 128:(fc + 1) * 128], xeT[:, k, :], start=(k == 0), stop=(k == 7)),
                         reads=[bwg, bxeT], writes=[bps_g])
                for k in range(8):
                    c.op("pe", lambda e, k=k, fc=fc: e.matmul(ps_u[:, 0:CAP], wus[:, k, fc * 128:(fc + 1) * 128], xeT[:, k, :], start=(k == 0), stop=(k == 7)),
                         reads=[bwu, bxeT], writes=[bps_u])
                c.op("act", lambda e, i=i: e.activation(out=sg[i][:], in_=ps_g[:, 0:CAP], func=AF.Silu), reads=[bps_g], writes=[bsg[i]])
                c.op("dve", lambda e, i=i, fc=fc: e.tensor_tensor(out=hT[:, fc, :], in0=ps_u[:, 0:CAP], in1=sg[i][:], op=ALU.mult),
                     reads=[bps_u, bsg[i]], writes=[bhT[fc]])
            for st_ in range(NST):
                for h in range(2):
                    for fc in range(FC):
                        c.op("pe", lambda e, st_=st_, h=h, fc=fc: e.matmul(ps_d[:], hT[:, fc, st_ * 128:(st_ + 1) * 128], wds[:, fc, h * 512:(h + 1) * 512],
                                                                        start=(fc == 0), stop=(fc == FC - 1)), reads=[bwd, bhT[fc]], writes=[bps_d])
                    c.op("dve", lambda e, st_=st_, h=h: e.tensor_copy(out=osb[:, st_, h * 512:(h + 1) * 512], in_=ps_d[:]), reads=[bps_d], writes=[bos])
            for t in range(ntl):
                for h in range(2):
                    i = cnt[2] % 2; cnt[2] += 1
                    for st_ in range(NST):
                        c.op("pe", lambda e, i=i, st_=st_, t=t, h=h: e.matmul(ps_s[i][:], OHT[:, st_, t * 128:(t + 1) * 128], osb[:, st_, h * 512:(h + 1) * 512],
                                                                           start=(st_ == 0), stop=(st_ == NST - 1)), reads=[bOHT, bos], writes=[bps_s[i]])
                    gcol = gp[:, tl0 + t, ex:ex + 1]
                    if ex == 0:
                        c.op("dve", lambda e, i=i, t=t, h=h, gcol=gcol: e.tensor_scalar(
                            out=yacc[:, t, h * 512:(h + 1) * 512], in0=ps_s[i][:], scalar1=gcol, scalar2=None, op0=ALU.mult),
                            reads=[bps_s[i], bg], writes=[byacc[t]])
                    else:
                        c.op("dve", lambda e, i=i, t=t, h=h, gcol=gcol: e.scalar_tensor_tensor(
                            out=yacc[:, t, h * 512:(h + 1) * 512], in0=ps_s[i][:], scalar=gcol, in1=yacc[:, t, h * 512:(h + 1) * 512],
                            op0=ALU.mult, op1=ALU.add), reads=[bps_s[i], bg, byacc[t]], writes=[byacc[t]])
        for t in range(ntl):
            i = cnt[0] % 2; cnt[0] += 1
            r0 = (tl0 + t) * 128
            c.dma("sp", lambda e, i=i, r0=r0: e.dma_start(out=xin[i][:], in_=x_dram[r0:r0 + 128, :]), writes=[bxin[i]])
            c.op("dve", lambda e, i=i, t=t: e.scalar_tensor_tensor(out=xin[i][:], in0=xin[i][:], scalar=float(ALPHA), in1=yacc[:, t, :],
                                                                 op0=ALU.mult, op1=ALU.add), reads=[bxin[i], byacc[t]], writes=[bxin[i]])
            emit_ln(c, xin[i], bxin[i], xin[i], bxin[i], gam, bet, bgb, lntmp)
            c.dma("sp", lambda e, i=i, r0=r0: e.dma_start(out=y_dram[r0:r0 + 128, :], in_=xin[i][:]), reads=[bxin[i]])


NAW, MLW = 512, 512
EVEN_IN = 3600
SEQ = 2048
NTILE = 16


def na_table_host(rpb):
    NEG = -30000.0
    qc = np.arange(64)
    kc = np.arange(64)
    c0 = np.clip(qc - 8, 0, 48)
    colvalid = (kc[:, None] >= c0[None, :]) & (kc[:, None] < c0[None, :] + 16)
    dc = np.clip(kc[:, None] - qc[None, :] + 15, 0, 30)
    T = np.full((8, 3, 128, 16, 64), NEG, np.float32)
    for i in range(16):
        for half in range(2):
            dr = i - 1 + half
            if dr < 0 or dr > 14:
                continue
            vals = np.where(colvalid[None], rpb[:, dr][:, dc], NEG).astype(np.float32)
            sl = slice(half * 64, half * 64 + 64)
            T[:, 0, sl, i, :] = vals
            if half == 1:
                T[:, 1, sl, i, :] = vals
            else:
                T[:, 2, sl, i, :] = vals
    return T


class PSB:
    def __init__(self, c, n=8):
        self.t = [c.ps("psb%d" % i, [128, 512], F32) for i in range(n)]
        self.b = [Buf("psb%d" % i) for i in range(n)]


def load_cast_weight(c, dst, bdst, src, ncol, st, bst, cnt):
    for k in range(8):
        for c0 in range(0, ncol, 1024):
            cw = min(1024, ncol - c0)
            i = cnt[0] % len(st); cnt[0] += 1
            c.dma("sp", lambda e, i=i, k=k, c0=c0, cw=cw: e.dma_start(out=st[i][:, 0:cw], in_=src[k * 128:(k + 1) * 128, c0:c0 + cw]), writes=[bst[i]])
            c.op("pool", lambda e, i=i, k=k, c0=c0, cw=cw: e.tensor_copy(out=dst[:, k, c0:c0 + cw], in_=st[i][:, 0:cw]), reads=[bst[i]], writes=[bdst])


def emit_xT(c, K, P, x_dram, r0, ntiles, xT, bxT, xin, bxin, cnt):
    idf = K["ident_f"]
    for t in range(ntiles):
        i = cnt[0] % 2; cnt[0] += 1
        c.dma("sp", lambda e, i=i, t=t: e.dma_start(out=xin[i][:], in_=x_dram[r0 + t * 128:r0 + (t + 1) * 128, :]), writes=[bxin[i]])
        for kh in range(2):
            for k in range(4):
                kk = kh * 4 + k
                c.op("pe", lambda e, i=i, k=k, kk=kk: e.transpose(P.t[0][:, k * 128:(k + 1) * 128], xin[i][:, kk * 128:(kk + 1) * 128], idf[:]),
                     reads=[bxin[i], K["b"]], writes=[P.b[0]])
            c.op("dve", lambda e, t=t, kh=kh: e.tensor_copy(out=xT[:, kh * 4:(kh + 1) * 4, t * 128:(t + 1) * 128],
                                                         in_=P.t[0][:].rearrange("p (k t) -> p k t", k=4)), reads=[P.b[0]], writes=[bxT])


def proj_fm(c, P, W, bW, xT, bxT, col0, dst, bdst, scale, ntok, pc):
    for n0 in range(0, ntok, 512):
        j = 1 + (pc[0] % 2); pc[0] += 1
        for k in range(8):
            c.op("pe", lambda e, j=j, k=k, n0=n0: e.matmul(P.t[j][:], W[:, k, col0:col0 + 128], xT[:, k, n0:n0 + 512], start=(k == 0), stop=(k == 7)),
                 reads=[bW, bxT], writes=[P.b[j]])
        c.op("dve", lambda e, j=j, n0=n0: e.tensor_scalar(out=dst[:, n0:n0 + 512], in0=P.t[j][:], scalar1=float(scale), scalar2=None, op0=ALU.mult),
             reads=[P.b[j]], writes=[bdst])


def proj_tm(c, P, W, bW, xT, bxT, col0, ncol, dstfn, bdst, scale, ntile, pc):
    g = max(1, 512 // ncol)
    g = min(g, 4)
    for t0 in range(0, ntile, g):
        j = 1 + (pc[0] % 2); pc[0] += 1
        gg = min(g, ntile - t0)
        for ti in range(gg):
            t = t0 + ti
            for k in range(8):
                c.op("pe", lambda e, j=j, k=k, t=t, ti=ti: e.matmul(P.t[j][:, ti * ncol:(ti + 1) * ncol], xT[:, k, t * 128:(t + 1) * 128], W[:, k, col0:col0 + ncol],
                                                                     start=(k == 0), stop=(k == 7)), reads=[bW, bxT], writes=[P.b[j]])
        c.op("dve", lambda e, j=j, t0=t0, gg=gg: e.tensor_scalar(out=dstfn(t0, gg), in0=P.t[j][:, 0:gg * ncol].rearrange("p (g n) -> p g n", g=gg),
                                                                scalar1=float(scale), scalar2=None, op0=ALU.mult), reads=[P.b[j]], writes=[bdst])


def emit_mix_a(c, K, P, x_dram, w_in, w_out, gate_bias, norm_g, lng, lnb, w_r, na_tab, x1_dram, aff_dram, nseq, stage=99, dbg=None):
    Wout = c.sb("a_wout", [128, 8, D], BF16); bWout = Buf("wout")
    stg = [c.sb("a_stg%d" % i, [128, 8, 128], F32) for i in range(2)]; bstg = [Buf("stg0"), Buf("stg1")]
    st = [t[:].rearrange("p k n -> p (k n)") for t in stg]; bst = bstg
    scnt = [0]
    load_cast_weight(c, Wout, bWout, w_out, D, st, bst, scnt)
    WS = {n: c.sb("a_ws_" + n, [128, 8, 128], BF16) for n in ("q", "k", "v", "o")}
    bWS = {n: Buf("ws_" + n) for n in WS}
    WG = c.sb("a_wg16", [128, 8, 16], BF16); bWG = Buf("wg16")

    def load_w(name, col0, ncol=128):
        dst, bd = (WS[name], bWS[name]) if name != "g" else (WG, bWG)
        i = scnt[0] % 2; scnt[0] += 1
        c.dma("sp", lambda e, i=i: e.dma_start(out=stg[i][:, :, 0:ncol], in_=w_in[:, col0:col0 + ncol].rearrange("(k p) n -> p k n", p=128)), writes=[bstg[i]])
        c.op("pool", lambda e, i=i: e.tensor_copy(out=dst[:, :, 0:ncol], in_=stg[i][:, :, 0:ncol]), reads=[bstg[i]], writes=[bd])
    Wr = c.sb("a_wr", [128, 8, 16], F32); bWr = Buf("wr")
    c.dma("sp", lambda e: e.dma_start(out=Wr[:], in_=w_r.rearrange("(k p) e -> p k e", p=128)), writes=[bWr])
    gb = c.sb("a_gb", [128, 16], F32)
    ng = c.sb("a_ng", [128, 512], F32)
    gam = c.sb("a_gam", [128, D], F32)
    bet = c.sb("a_bet", [128, D], F32)
    bgb = Buf("gb")
    c.dma("sp", lambda e: e.dma_start(out=gb[:], in_=gate_bias.partition_broadcast(128)), writes=[bgb])
    c.dma("sp", lambda e: e.dma_start(out=ng[:], in_=norm_g.partition_broadcast(128)), writes=[bgb])
    c.dma("sp", lambda e: e.dma_start(out=gam[:], in_=lng.partition_broadcast(128)), writes=[bgb])
    c.dma("sp", lambda e: e.dma_start(out=bet[:], in_=lnb.partition_broadcast(128)), writes=[bgb])
    ones_f = K["ones_f"]
    tri = c.sb("a_tri", [128, 128], F32)
    trir = c.sb("a_trir", [128, 128], F32)
    ones_b = c.sb("a_onesb", [128, 128], BF16)
    c.op("pool", lambda e: e.affine_select(out=tri[:], in_=ones_f[:], pattern=[[1, 128]], compare_op=ALU.is_ge, fill=0.0, base=0, channel_multiplier=-1),
         reads=[K["b"]], writes=[K["b"]])
    c.op("pool", lambda e: e.affine_select(out=trir[:], in_=ones_f[:], pattern=[[-1, 128]], compare_op=ALU.is_ge, fill=0.0, base=0, channel_multiplier=1),
         reads=[K["b"]], writes=[K["b"]])
    c.op("pool", lambda e: e.tensor_copy(out=ones_b[:], in_=ones_f[:]), reads=[K["b"]], writes=[K["b"]])
    lntmp = {"st": c.sb("ln_st", [128, 2, 6], F32), "mv": c.sb("ln_mv", [128, 2], F32), "rs": c.sb("ln_rs", [128, 1], F32), "b": Buf("lnt")}

    xT = c.sb("a_xT", [128, 8, SEQ], BF16); bxT = Buf("xT")
    yT = c.sb("a_yT", [128, 8, SEQ], BF16); byT = [Buf("yT%d" % i) for i in range(8)]
    xin = [c.sb("a_xin%d" % i, [128, D], F32) for i in range(2)]; bxin = [Buf("xin0"), Buf("xin1")]
    QT = c.sb("a_QT", [128, SEQ], BF16); bQT = Buf("QT")
    KT = c.sb("a_KT", [128, SEQ], BF16); bKT = Buf("KT")
    Vb = c.sb("a_V", [128, NTILE * 132], BF16); bV = Buf("V")
    Ktm = c.sb("a_Ktm", [128, NTILE, 128], BF16); bKtm = Buf("Ktm")
    sgob = c.sb("a_sgob", [128, NTILE, 128], F32); bsgob = Buf("sgob")
    hacc = c.sb("a_hacc", [128, NTILE, 128], F32); bhacc = Buf("hacc")
    EB = c.sb("a_EB", [128, 3, 16, 64], F32); bEB = Buf("EB")
    pexp = [c.sb("a_pexp%d" % i, [128, 5 * 64], F32) for i in range(3)]; bpexp = [Buf("pexp%d" % i) for i in range(3)]
    PT = [c.sb("a_PT%d" % i, [128, 5, 64], BF16) for i in range(3)]; bPT = [Buf("PT%d" % i) for i in range(3)]
    rec = c.sb("a_rec", [128, 512], F32); brec = Buf("rec")
    G = c.sb("a_G", [128, NTILE, 16], F32); bG = Buf("G")
    nlf = c.sb("a_nlf", [128, NTILE, 8], F32)
    uu = c.sb("a_u", [128, NTILE, 8], F32)
    vv = c.sb("a_v", [128, NTILE, 8], F32)
    eL = c.sb("a_eL", [128, NTILE, 8], F32)
    bgate = Buf("gate")
    CN = c.sb("a_CN", [128, 129], F32); bCN = Buf("CN")
    CNb = c.sb("a_CNb", [128, 129], BF16); bCNb = Buf("CNb")
    ctmp = c.sb("a_ctmp", [128, 129], F32); bctmp = Buf("ctmp")
    St = [c.sb("a_St%d" % i, [128, 128], BF16) for i in range(2)]; bSt = [Buf("St0"), Buf("St1")]
    Kt = [c.sb("a_Kt%d" % i, [128, 128], BF16) for i in range(2)]; bKt = [Buf("Kt0"), Buf("Kt1")]
    sm = c.sb("a_sm", [128, 8], F32); bsm = Buf("sm")
    lnw = c.sb("a_lnw", [128, NTILE, 128], F32); blnw = Buf("lnw")
    lns = c.sb("a_lns", [128, NTILE, 4], F32); blns = Buf("lns")
    x1T = c.sb("a_x1T", [128, 8, 128], F32); bx1T = Buf("x1T")
    rt = c.sb("a_rt", [128, 16], F32); brt = Buf("rt")
    rtm = c.sb("a_rtm", [128, 4], F32)
    affo = [c.sb("a_aff%d" % i, [128, 16], F32) for i in range(2)]; baffo = [Buf("aff0"), Buf("aff1")]
    xc = [0]; pc = [0]; sc = [0]; hc = [0]; zc = [0]

    for s in range(nseq):
        r0 = s * SEQ
        emit_xT(c, K, P, x_dram, r0, NTILE, xT, bxT, xin, bxin, xc)
        for hp in range(4):
            load_w("q", hp * 128); load_w("k", 512 + hp * 128); load_w("v", 1024 + hp * 128)
            proj_fm(c, P, WS["q"], bWS["q"], xT, bxT, 0, QT, bQT, 0.125, SEQ, pc)
            proj_fm(c, P, WS["k"], bWS["k"], xT, bxT, 0, KT, bKT, 1.0, SEQ, pc)
            Vv = Vb[:, 0:NTILE * 128].rearrange("p (t n) -> p t n", t=NTILE)
            proj_tm(c, P, WS["v"], bWS["v"], xT, bxT, 0, 128, lambda t0, gg: Vv[:, t0:t0 + gg, :], bV, 1.0, NTILE, pc)
            for hh in range(2):
                pb = hh * 64
                c.dma("sp", lambda e, hp=hp, hh=hh: e.dma_start(out=EB[:].rearrange("p v i q -> p v (i q)"),
                                                         in_=na_tab[2 * hp + hh].rearrange("v p i q -> p v (i q)")), writes=[bEB])
                c.op("act", lambda e: e.activation(out=EB[:].rearrange("p v i q -> p (v i q)"), in_=EB[:].rearrange("p v i q -> p (v i q)"), func=AF.Exp),
                     reads=[bEB], writes=[bEB])
                SB = (3, 4, 7)

                def na_front(r, hh=hh, pb=pb):
                    rs = min(max(r - 4, 0), 24)
                    a0, a1 = rs // 2, (rs + 7) // 2
                    nt = a1 - a0 + 1
                    si = r % 3
                    psS = P.t[SB[si]]; bS = P.b[SB[si]]
                    for j in range(nt):
                        a = a0 + j
                        c.op("pe", lambda e, j=j, a=a, psS=psS, pb=pb, r=r: e.matmul(psS[:, j * 64:(j + 1) * 64], KT[pb:pb + 64, a * 128:(a + 1) * 128],
                                                                                    QT[pb:pb + 64, r * 64:(r + 1) * 64], start=True, stop=True),
                             reads=[bKT, bQT], writes=[bS])
                    c.op("act", lambda e, si=si, psS=psS, nt=nt: e.activation(out=pexp[si][:, 0:nt * 64], in_=psS[:, 0:nt * 64], func=AF.Exp),
                         reads=[bS], writes=[bpexp[si]])
                    for j in range(nt):
                        a = a0 + j
                        var = 1 if 2 * a < rs else (2 if 2 * a + 1 >= rs + 8 else 0)
                        ii = 2 * a - r + 7 + 1
                        c.op("dve", lambda e, si=si, j=j, var=var, ii=ii: e.tensor_tensor(out=PT[si][:, j, :], in0=pexp[si][:, j * 64:(j + 1) * 64],
                                                                                   in1=EB[:, var, ii, :], op=ALU.mult),
                             reads=[bpexp[si], bEB], writes=[bPT[si]])

                def na_back(r, hh=hh, pb=pb, hp=hp):
                    rs = min(max(r - 4, 0), 24)
                    a0, a1 = rs // 2, (rs + 7) // 2
                    nt = a1 - a0 + 1
                    si = r % 3
                    rr = r % 8
                    for j in range(nt):
                        a = a0 + j
                        c.op("pe", lambda e, si=si, j=j, a=a, rr=rr, nt=nt: e.matmul(P.t[5][:, rr * 64:(rr + 1) * 64], Vv[:, a, :], PT[si][:, j, :],
                                                                                  start=(j == 0), stop=(j == nt - 1)), reads=[bV, bPT[si]], writes=[P.b[5]])
                    for j in range(nt):
                        c.op("pe", lambda e, si=si, j=j, rr=rr, nt=nt: e.matmul(P.t[6][:, rr * 64:(rr + 1) * 64], ones_b[:], PT[si][:, j, :],
                                                                             start=(j == 0), stop=(j == nt - 1)), reads=[K["b"], bPT[si]], writes=[P.b[6]])
                    if rr == 7:
                        q0 = (r - 7) * 64
                        c.op("dve", lambda e: e.reciprocal(out=rec[:], in_=P.t[6][:]), reads=[P.b[6]], writes=[brec])
                        c.op("dve", lambda e, pb=pb, hp=hp, q0=q0: e.tensor_tensor(out=yT[pb:pb + 64, hp, q0:q0 + 512], in0=P.t[5][pb:pb + 64, :],
                                                                                in1=rec[pb:pb + 64, :], op=ALU.mult),
                             reads=[P.b[5], brec], writes=[byT[hp]])
                LOOK = 2
                for r in range(32 + LOOK):
                    if r < 32:
                        na_front(r)
                    if r >= LOOK:
                        na_back(r - LOOK)
        if stage == 1:
            continue
        load_w("g", 3584, 16)
        Gps = P.t[1][:, 0:NTILE * 16].rearrange("p (t n) -> p t n", t=NTILE)
        for t in range(NTILE):
            for k in range(8):
                c.op("pe", lambda e, t=t, k=k: e.matmul(Gps[:, t, :], xT[:, k, t * 128:(t + 1) * 128], WG[:, k, :], start=(k == 0), stop=(k == 7)),
                     reads=[bxT, bWG], writes=[P.b[1]])
        c.op("dve", lambda e: e.tensor_tensor(out=G[:], in0=Gps, in1=gb[:].unsqueeze(1).to_broadcast([128, NTILE, 16]), op=ALU.add),
             reads=[P.b[1], bgb], writes=[bG])
        c.op("act", lambda e: e.activation(out=nlf[:], in_=G[:, :, 8:16], func=AF.Exp, scale=-1.0), reads=[bG], writes=[bgate])
        c.op("act", lambda e: e.activation(out=nlf[:], in_=nlf[:], func=AF.Ln, bias=1.0, scale=1.0), reads=[bgate], writes=[bgate])
        cum = P.t[2][:, 0:NTILE * 8].rearrange("p (t n) -> p t n", t=NTILE)
        tot = P.t[2][:, 256:256 + NTILE * 8].rearrange("p (t n) -> p t n", t=NTILE)
        for t in range(NTILE):
            c.op("pe", lambda e, t=t: e.matmul(cum[:, t, 0:4], tri[:], nlf[:, t, 0:4], start=True, stop=True), reads=[bgate, K["b"]], writes=[P.b[2]])
            c.op("pe", lambda e, t=t: e.matmul(cum[:, t, 4:8], trir[:], nlf[:, t, 4:8], start=True, stop=True), reads=[bgate, K["b"]], writes=[P.b[2]])
            c.op("pe", lambda e, t=t: e.matmul(tot[:, t, :], ones_f[:], nlf[:, t, :], start=True, stop=True), reads=[bgate, K["b"]], writes=[P.b[2]])
        c.op("act", lambda e: e.activation(out=uu[:], in_=cum, func=AF.Exp, scale=-1.0), reads=[P.b[2]], writes=[bgate])
        c.op("act", lambda e: e.activation(out=eL[:], in_=tot, func=AF.Exp, scale=-1.0), reads=[P.b[2]], writes=[bgate])
        c.op("dve", lambda e: e.tensor_tensor(out=vv[:], in0=cum, in1=G[:, :, 0:8], op=ALU.add), reads=[P.b[2], bG], writes=[bgate])
        c.op("act", lambda e: e.activation(out=vv[:], in_=vv[:], func=AF.Exp), reads=[bgate], writes=[bgate])
        for h in range(4):
            load_w("q", 1536 + h * 128); load_w("k", 2048 + h * 128); load_w("v", 2560 + h * 128); load_w("o", 3072 + h * 128)
            proj_fm(c, P, WS["q"], bWS["q"], xT, bxT, 0, QT, bQT, 1.0, SEQ, pc)
            proj_fm(c, P, WS["k"], bWS["k"], xT, bxT, 0, KT, bKT, 128 ** -0.5, SEQ, pc)
            proj_tm(c, P, WS["k"], bWS["k"], xT, bxT, 0, 128, lambda t0, gg: Ktm[:, t0:t0 + gg, :], bKtm, 128 ** -0.5, NTILE, pc)
            Va = Vb[:, 0:NTILE * 129].rearrange("p (t n) -> p t n", t=NTILE)
            proj_tm(c, P, WS["v"], bWS["v"], xT, bxT, 0, 128, lambda t0, gg: Va[:, t0:t0 + gg, 0:128], bV, 1.0, NTILE, pc)
            c.op("pool", lambda e: e.memset(Va[:, :, 128:129], 1.0), reads=[], writes=[bV])
            proj_tm(c, P, WS["o"], bWS["o"], xT, bxT, 0, 128, lambda t0, gg: sgob[:, t0:t0 + gg, :], bsgob, 1.0, NTILE, pc)
            c.op("act", lambda e: e.activation(out=sgob[:].rearrange("p t n -> p (t n)"), in_=sgob[:].rearrange("p t n -> p (t n)"), func=AF.Sigmoid),
                 reads=[bsgob], writes=[bsgob])
            for dr in range(2):
                gi = dr * 4 + h
                mask = tri if dr == 0 else trir
                c.op("pool", lambda e: e.memset(CN[:], 0.0), writes=[bCN])
                c.op("pool", lambda e: e.memset(CNb[:], 0.0), writes=[bCNb])
                order = range(NTILE) if dr == 0 else range(NTILE - 1, -1, -1)
                for ci in order:
                    si = sc[0] % 2; sc[0] += 1
                    hi_ = hc[0] % 2; hc[0] += 1
                    psS = P.t[3 + si]; bS = P.b[3 + si]
                    psH = P.t[5] if hi_ == 0 else P.t[7]; bH = P.b[5] if hi_ == 0 else P.b[7]
                    cs = slice(ci * 128, (ci + 1) * 128)
                    c.op("pe", lambda e, psS=psS, cs=cs: e.matmul(psS[:, 0:128], KT[:, cs], QT[:, cs], start=True, stop=True), reads=[bKT, bQT], writes=[bS])
                    c.op("dve", lambda e, psS=psS, si=si, ci=ci, gi=gi, mask=mask: e.scalar_tensor_tensor(
                        out=St[si][:], in0=psS[:, 0:128], scalar=vv[:, ci, gi:gi + 1], in1=mask[:], op0=ALU.mult, op1=ALU.mult),
                        reads=[bS, bgate, K["b"]], writes=[bSt[si]])
                    c.op("pe", lambda e, psH=psH, si=si, ci=ci: e.matmul(psH[:, 0:129], St[si][:], Va[:, ci, :], start=True, stop=False),
                         reads=[bSt[si], bV], writes=[bH])
                    c.op("pe", lambda e, psH=psH, cs=cs: e.matmul(psH[:, 0:129], QT[:, cs], CNb[:], start=False, stop=True),
                         reads=[bQT, bCNb], writes=[bH])
                    c.op("dve", lambda e, psH=psH, ci=ci, gi=gi: e.tensor_tensor(out=sm[:, 0:1], in0=psH[:, 128:129], in1=uu[:, ci, gi:gi + 1], op=ALU.mult),
                         reads=[bH, bgate], writes=[bsm])
                    c.op("dve", lambda e: e.tensor_scalar(out=sm[:, 4:5], in0=sm[:, 0:1], scalar1=-1.0, scalar2=None, op0=ALU.mult), reads=[bsm], writes=[bsm])
                    c.op("dve", lambda e: e.tensor_tensor(out=sm[:, 5:6], in0=sm[:, 0:1], in1=sm[:, 4:5], op=ALU.max), reads=[bsm], writes=[bsm])
                    c.op("dve", lambda e: e.tensor_scalar(out=sm[:, 1:2], in0=sm[:, 5:6], scalar1=1.0, scalar2=None, op0=ALU.max), reads=[bsm], writes=[bsm])
                    c.op("dve", lambda e: e.reciprocal(out=sm[:, 2:3], in_=sm[:, 1:2]), reads=[bsm], writes=[bsm])
                    c.op("dve", lambda e, ci=ci, gi=gi: e.tensor_tensor(out=sm[:, 3:4], in0=sm[:, 2:3], in1=uu[:, ci, gi:gi + 1], op=ALU.mult),
                         reads=[bsm, bgate], writes=[bsm])
                    if dr == 0:
                        c.op("dve", lambda e, psH=psH, ci=ci: e.tensor_scalar(out=hacc[:, ci, :], in0=psH[:, 0:128], scalar1=sm[:, 3:4], scalar2=None, op0=ALU.mult),
                             reads=[bH, bsm], writes=[bhacc])
                    else:
                        c.op("dve", lambda e, psH=psH, ci=ci: e.scalar_tensor_tensor(out=hacc[:, ci, :], in0=psH[:, 0:128], scalar=sm[:, 3:4], in1=hacc[:, ci, :],
                                                                                   op0=ALU.mult, op1=ALU.add), reads=[bH, bsm, bhacc], writes=[bhacc])
                    c.op("pool", lambda e, si=si, ci=ci, gi=gi: e.tensor_scalar(out=Kt[si][:], in0=Ktm[:, ci, :], scalar1=vv[:, ci, gi:gi + 1], scalar2=None, op0=ALU.mult),
                         reads=[bKtm, bgate], writes=[bKt[si]])
                    c.op("pe", lambda e, si=si, ci=ci: e.matmul(P.t[6][:, 0:129], Kt[si][:], Va[:, ci, :], start=True, stop=True),
                         reads=[bKt[si], bV], writes=[P.b[6]])
                    c.op("dve", lambda e: e.tensor_tensor(out=ctmp[:], in0=P.t[6][:, 0:129], in1=CN[:], op=ALU.add), reads=[P.b[6], bCN], writes=[bctmp])
                    c.op("dve", lambda e, ci=ci, gi=gi: e.tensor_scalar(out=CN[:], in0=ctmp[:], scalar1=eL[:, ci, gi:gi + 1], scalar2=None, op0=ALU.mult),
                         reads=[bctmp, bgate], writes=[bCN])
                    c.op("pool", lambda e, ci=ci, gi=gi: e.tensor_scalar(out=CNb[:], in0=ctmp[:], scalar1=eL[:, ci, gi:gi + 1], scalar2=None, op0=ALU.mult),
                         reads=[bctmp, bgate], writes=[bCNb])
            c.op("dve", lambda e: e.tensor_reduce(out=lns[:, :, 0], in_=hacc[:], axis=AX.X, op=ALU.add), reads=[bhacc], writes=[blns])
            c.op("dve", lambda e: e.tensor_scalar(out=lns[:, :, 0], in0=lns[:, :, 0], scalar1=1.0 / 128, scalar2=None, op0=ALU.mult), reads=[blns], writes=[blns])
            c.op("dve", lambda e: e.tensor_tensor(out=lnw[:], in0=hacc[:], in1=lns[:, :, 0:1].to_broadcast([128, NTILE, 128]), op=ALU.subtract),
                 reads=[bhacc, blns], writes=[blnw])
            c.op("pool", lambda e: e.tensor_tensor(out=hacc[:], in0=lnw[:], in1=lnw[:], op=ALU.mult), reads=[blnw, bhacc], writes=[bhacc])
            c.op("dve", lambda e: e.tensor_reduce(out=lns[:, :, 1], in_=hacc[:], axis=AX.X, op=ALU.add), reads=[bhacc], writes=[blns])
            c.op("dve", lambda e: e.tensor_scalar(out=lns[:, :, 1], in0=lns[:, :, 1], scalar1=1.0 / 128, scalar2=EPS, op0=ALU.mult, op1=ALU.add), reads=[blns], writes=[blns])
            c.op("act", lambda e: e.activation(out=lns[:, :, 2], in_=lns[:, :, 1], func=AF.Sqrt), reads=[blns], writes=[blns])
            c.op("dve", lambda e: e.reciprocal(out=lns[:, :, 3], in_=lns[:, :, 2]), reads=[blns], writes=[blns])
            c.op("dve", lambda e: e.tensor_tensor(out=lnw[:], in0=lnw[:], in1=lns[:, :, 3:4].to_broadcast([128, NTILE, 128]), op=ALU.mult), reads=[blnw, blns], writes=[blnw])
            c.op("pool", lambda e, h=h: e.tensor_tensor(out=lnw[:], in0=lnw[:], in1=ng[:, h * 128:(h + 1) * 128].unsqueeze(1).to_broadcast([128, NTILE, 128]), op=ALU.mult),
                 reads=[blnw, bgb], writes=[blnw])
            c.op("dve", lambda e: e.tensor_tensor(out=lnw[:], in0=lnw[:], in1=sgob[:], op=ALU.mult), reads=[blnw, bsgob], writes=[blnw])
            for tq in range(4):
                for k in range(4):
                    t = tq * 4 + k
                    c.op("pe", lambda e, t=t, k=k: e.transpose(P.t[0][:, k * 128:(k + 1) * 128], lnw[:, t, :], K["ident_f"][:]), reads=[blnw, K["b"]], writes=[P.b[0]])
                c.op("dve", lambda e, tq=tq, h=h: e.tensor_copy(out=yT[:, 4 + h, tq * 512:(tq + 1) * 512], in_=P.t[0][:]), reads=[P.b[0]], writes=[byT[4 + h]])
        if stage == 2:
            continue
        for t in range(NTILE):
            i = xc[0] % 2; xc[0] += 1
            zi = i
            rr0 = r0 + t * 128
            c.dma("sp", lambda e, i=i, rr0=rr0: e.dma_start(out=xin[i][:], in_=x_dram[rr0:rr0 + 128, :]), writes=[bxin[i]])
            for hf in range(2):
                for k in range(8):
                    c.op("pe", lambda e, hf=hf, k=k, t=t: e.matmul(P.t[1 + hf][:], yT[:, k, t * 128:(t + 1) * 128], Wout[:, k, hf * 512:(hf + 1) * 512],
                                                                start=(k == 0), stop=(k == 7)), reads=[byT[k], bWout], writes=[P.b[1 + hf]])
                c.op("dve", lambda e, hf=hf, i=i, zi=zi: e.scalar_tensor_tensor(out=xin[zi][:, hf * 512:(hf + 1) * 512], in0=xin[i][:, hf * 512:(hf + 1) * 512], scalar=float(ALPHA),
                                                                       in1=P.t[1 + hf][:], op0=ALU.mult, op1=ALU.add), reads=[bxin[i], P.b[1 + hf]], writes=[bxin[zi]])
            emit_ln(c, xin[zi], bxin[zi], xin[zi], bxin[zi], gam, bet, bgb, lntmp)
            c.dma("sp", lambda e, zi=zi, rr0=rr0: e.dma_start(out=x1_dram[rr0:rr0 + 128, :], in_=xin[zi][:]), reads=[bxin[zi]])
            emit_router(c, K, P, xin[zi], bxin[zi], Wr, bWr, x1T, bx1T, rt, brt, rtm, affo[zi], baffo[zi], aff_dram, rr0)


def emit_router(c, K, P, z, bz, Wr, bWr, x1T, bx1T, rt, brt, rtm, affo, baffo, aff_dram, rr0):
    idf = K["ident_f"]
    for kh in range(2):
        for k in range(4):
            kk = kh * 4 + k
            c.op("pe", lambda e, k=k, kk=kk: e.transpose(P.t[0][:, k * 128:(k + 1) * 128], z[:, kk * 128:(kk + 1) * 128], idf[:]), reads=[bz, K["b"]], writes=[P.b[0]])
        c.op("dve", lambda e, kh=kh: e.tensor_copy(out=x1T[:, kh * 4:(kh + 1) * 4, :], in_=P.t[0][:].rearrange("p (k t) -> p k t", k=4)), reads=[P.b[0]], writes=[bx1T])
    for k in range(8):
        c.op("pe", lambda e, k=k: e.matmul(P.t[3][:, 0:16], x1T[:, k, :], Wr[:, k, :], start=(k == 0), stop=(k == 7)), reads=[bx1T, bWr], writes=[P.b[3]])
    c.op("dve", lambda e: e.tensor_reduce(out=rtm[:, 0:1], in_=P.t[3][:, 0:16], axis=AX.X, op=ALU.max), reads=[P.b[3]], writes=[brt])
    c.op("dve", lambda e: e.tensor_scalar(out=rtm[:, 1:2], in0=rtm[:, 0:1], scalar1=-1.0, scalar2=None, op0=ALU.mult), reads=[brt], writes=[brt])
    c.op("act", lambda e: e.activation(out=rt[:], in_=P.t[3][:, 0:16], func=AF.Exp, bias=rtm[:, 1:2], scale=1.0, accum_out=rtm[:, 2:3]), reads=[P.b[3], brt], writes=[brt])
    c.op("dve", lambda e: e.reciprocal(out=rtm[:, 3:4], in_=rtm[:, 2:3]), reads=[brt], writes=[brt])
    c.op("dve", lambda e: e.tensor_scalar(out=affo[:], in0=rt[:], scalar1=rtm[:, 3:4], scalar2=None, op0=ALU.mult), reads=[brt], writes=[baffo])
    c.dma("sp", lambda e: e.dma_start(out=aff_dram[rr0:rr0 + 128, :], in_=affo[:]), reads=[baffo])


def rope_tables_host():
    d = 64
    inv = (10000.0 ** (-np.arange(0, d, 2, dtype=np.float32) / d)).astype(np.float32)
    ang = np.arange(SEQ, dtype=np.float32)[:, None] * inv[None, :]
    cos, sin = np.cos(ang).astype(np.float32), np.sin(ang).astype(np.float32)
    COS = np.concatenate([cos, cos], 1).T
    SINS = np.concatenate([-sin, sin], 1).T
    COS = np.ascontiguousarray(np.concatenate([COS, COS], 0), dtype=np.float32)
    SINS = np.ascontiguousarray(np.concatenate([SINS, SINS], 0), dtype=np.float32)
    p = np.arange(128)[:, None]
    x = np.arange(3968)[None, :]
    dl = p - x + 1920
    m = (np.abs(dl) <= 64).astype(np.float32) + ((dl % 4 == 0) & (np.abs(dl) <= 256)) + ((dl % 16 == 0) & (np.abs(dl) <= 1024))
    return COS, SINS, np.ascontiguousarray(m.astype(np.float32))


def emit_mix_b(c, K, P, x_dram, w_in, w_out, lng, lnb, w_r, cos_d, sins_d, tab_d, x1_dram, aff_dram, nseq):
    Wout = c.sb("b_wout", [128, 8, D], BF16); bWout = Buf("wout")
    stg = [c.sb("b_stg%d" % i, [128, 8, 128], F32) for i in range(2)]; bstg = [Buf("stg0"), Buf("stg1")]
    st = [t[:].rearrange("p k n -> p (k n)") for t in stg]
    scnt = [0]
    load_cast_weight(c, Wout, bWout, w_out, D, st, bstg, scnt)
    WS = {n: c.sb("b_ws_" + n, [128, 8, 128], BF16) for n in ("q", "qs", "k", "ks", "v")}
    bWS = {n: Buf("ws_" + n) for n in WS}

    def load_w(name, col0, swap=False):
        dst, bd = WS[name], bWS[name]
        i = scnt[0] % 2; scnt[0] += 1
        src = w_in[:, col0:col0 + 128]
        if not swap:
            c.dma("sp", lambda e, i=i: e.dma_start(out=stg[i][:], in_=src.rearrange("(k p) n -> p k n", p=128)), writes=[bstg[i]])
        else:
            s5 = src.rearrange("(k p) (h two d) -> p k h two d", p=128, h=2, two=2)
            d5 = stg[i][:].rearrange("p k (h two d) -> p k h two d", h=2, two=2)
            for a in range(2):
                for hd in range(2):
                    c.dma("sp", lambda e, i=i, a=a, hd=hd: e.dma_start(out=d5[:, :, hd, 1 - a, :], in_=s5[:, :, hd, a, :]), writes=[bstg[i]])
        c.op("pool", lambda e, i=i: e.tensor_copy(out=dst[:], in_=stg[i][:]), reads=[bstg[i]], writes=[bd])

    Wr = c.sb("b_wr", [128, 8, 16], F32); bWr = Buf("wr")
    c.dma("sp", lambda e: e.dma_start(out=Wr[:], in_=w_r.rearrange("(k p) e -> p k e", p=128)), writes=[bWr])
    gam = c.sb("b_gam", [128, D], F32)
    bet = c.sb("b_bet", [128, D], F32)
    bgb = Buf("gb")
    c.dma("sp", lambda e: e.dma_start(out=gam[:], in_=lng.partition_broadcast(128)), writes=[bgb])
    c.dma("sp", lambda e: e.dma_start(out=bet[:], in_=lnb.partition_broadcast(128)), writes=[bgb])
    COS = c.sb("b_cos", [128, SEQ], F32); SINS = c.sb("b_sins", [128, SEQ], F32)
    TABf = c.sb("b_tabf", [128, 3968], F32); TAB = c.sb("b_tab", [128, 3968], BF16)
    btab = Buf("tab")
    c.dma("sp", lambda e: e.dma_start(out=COS[:], in_=cos_d), writes=[btab])
    c.dma("sp", lambda e: e.dma_start(out=SINS[:], in_=sins_d), writes=[btab])
    c.dma("sp", lambda e: e.dma_start(out=TABf[:], in_=tab_d), writes=[btab])
    c.op("pool", lambda e: e.tensor_copy(out=TAB[:], in_=TABf[:]), reads=[btab], writes=[btab])
    ones_f = K["ones_f"]
    ones_b = c.sb("b_onesb", [128, 128], BF16)
    c.op("pool", lambda e: e.tensor_copy(out=ones_b[:], in_=ones_f[:]), reads=[K["b"]], writes=[K["b"]])
    lntmp = {"st": c.sb("ln_st", [128, 2, 6], F32), "mv": c.sb("ln_mv", [128, 2], F32), "rs": c.sb("ln_rs", [128, 1], F32), "b": Buf("lnt")}

    xT = c.sb("b_xT", [128, 8, SEQ], BF16); bxT = Buf("xT")
    yT = c.sb("b_yT", [128, 8, SEQ], BF16); byT = [Buf("yT%d" % i) for i in range(8)]
    xin = [c.sb("b_xin%d" % i, [128, D], F32) for i in range(2)]; bxin = [Buf("xin0"), Buf("xin1")]
    QT = c.sb("b_QT", [128, SEQ], BF16); bQT = Buf("QT")
    KT = c.sb("b_KT", [128, SEQ], BF16); bKT = Buf("KT")
    Vv = c.sb("b_V", [128, NTILE, 128], BF16); bV = Buf("V")
    t1 = c.sb("b_t1", [128, 512], F32); bt1 = Buf("t1")
    t2 = c.sb("b_t2", [128, 512], F32); bt2 = Buf("t2")
    pexp = [c.sb("b_pexp%d" % i, [128, 512], BF16) for i in range(3)]; bpexp = [Buf("pexp%d" % i) for i in range(3)]
    PT = [c.sb("b_PT%d" % i, [128, 512], BF16) for i in range(3)]; bPT = [Buf("PT%d" % i) for i in range(3)]
    rec = c.sb("b_rec", [128, 512], F32); brec = Buf("rec")
    x1T = c.sb("b_x1T", [128, 8, 128], F32); bx1T = Buf("x1T")
    rt = c.sb("b_rt", [128, 16], F32); brt = Buf("rt")
    rtm = c.sb("b_rtm", [128, 4], F32)
    affo = [c.sb("b_aff%d" % i, [128, 16], F32) for i in range(2)]; baffo = [Buf("aff0"), Buf("aff1")]
    xc = [0]; pc = [0]; sc = [0]

    def proj_rope(wa, wb, dst, bdst):
        for n0 in range(0, SEQ, 512):
            for k in range(8):
                c.op("pe", lambda e, k=k, n0=n0: e.matmul(P.t[1][:], WS[wa][:, k, :], xT[:, k, n0:n0 + 512], start=(k == 0), stop=(k == 7)),
                     reads=[bWS[wa], bxT], writes=[P.b[1]])
            for k in range(8):
                c.op("pe", lambda e, k=k, n0=n0: e.matmul(P.t[2][:], WS[wb][:, k, :], xT[:, k, n0:n0 + 512], start=(k == 0), stop=(k == 7)),
                     reads=[bWS[wb], bxT], writes=[P.b[2]])
            c.op("dve", lambda e, n0=n0: e.tensor_tensor(out=t1[:], in0=P.t[1][:], in1=COS[:, n0:n0 + 512], op=ALU.mult), reads=[P.b[1], btab], writes=[bt1])
            c.op("dve", lambda e, n0=n0: e.tensor_tensor(out=t2[:], in0=P.t[2][:], in1=SINS[:, n0:n0 + 512], op=ALU.mult), reads=[P.b[2], btab], writes=[bt2])
            c.op("pool", lambda e, n0=n0: e.tensor_tensor(out=dst[:, n0:n0 + 512], in0=t1[:], in1=t2[:], op=ALU.add), reads=[bt1, bt2], writes=[bdst])

    for s in range(nseq):
        r0 = s * SEQ
        emit_xT(c, K, P, x_dram, r0, NTILE, xT, bxT, xin, bxin, xc)
        for hp in range(8):
            load_w("q", hp * 128); load_w("qs", hp * 128, swap=True)
            load_w("k", 1024 + hp * 128); load_w("ks", 1024 + hp * 128, swap=True)
            load_w("v", 2048 + hp * 128)
            proj_rope("q", "qs", QT, bQT)
            proj_rope("k", "ks", KT, bKT)
            proj_tm(c, P, WS["v"], bWS["v"], xT, bxT, 0, 128, lambda t0, gg: Vv[:, t0:t0 + gg, :], bV, 1.0, NTILE, pc)
            blocks = []
            for hh in range(2):
                for qb in range(4):
                    kts = [kt for kt in range(NTILE) if not (128 * kt - 512 * qb - 511 > 1024 or 128 * kt + 127 - 512 * qb < -1024)]
                    for j, kt in enumerate(kts):
                        blocks.append((hh, qb, j, kt, len(kts)))
            SB = (3, 4, 7)

            def front(bi, blocks=blocks):
                hh, qb, j, kt, n = blocks[bi]
                pb = hh * 64
                si = bi % 3
                psS = P.t[SB[si]]; bS = P.b[SB[si]]
                c.op("pe", lambda e, psS=psS, kt=kt, qb=qb, pb=pb: e.matmul(psS[:], KT[pb:pb + 64, kt * 128:(kt + 1) * 128], QT[pb:pb + 64, qb * 512:(qb + 1) * 512],
                                                                         start=True, stop=True), reads=[bKT, bQT], writes=[bS])
                c.op("act", lambda e, psS=psS, si=si: e.activation(out=pexp[si][:], in_=psS[:], func=AF.Exp, scale=0.125), reads=[bS], writes=[bpexp[si]])
                x0 = 512 * qb - 128 * kt + 1920
                c.op("dve", lambda e, si=si, x0=x0: e.tensor_tensor(out=PT[si][:], in0=pexp[si][:], in1=TAB[:, x0:x0 + 512], op=ALU.mult),
                     reads=[bpexp[si], btab], writes=[bPT[si]])

            def back(bi, hp=hp, blocks=blocks):
                hh, qb, j, kt, n = blocks[bi]
                pb = hh * 64
                si = bi % 3
                c.op("pe", lambda e, si=si, kt=kt, j=j, n=n: e.matmul(P.t[5][:], Vv[:, kt, :], PT[si][:], start=(j == 0), stop=(j == n - 1)),
                     reads=[bV, bPT[si]], writes=[P.b[5]])
                c.op("pe", lambda e, si=si, j=j, n=n: e.matmul(P.t[6][:], ones_b[:], PT[si][:], start=(j == 0), stop=(j == n - 1)),
                     reads=[K["b"], bPT[si]], writes=[P.b[6]])
                if j == n - 1:
                    c.op("dve", lambda e: e.reciprocal(out=rec[:], in_=P.t[6][:]), reads=[P.b[6]], writes=[brec])
                    c.op("dve", lambda e, pb=pb, hp=hp, qb=qb: e.tensor_tensor(out=yT[pb:pb + 64, hp, qb * 512:(qb + 1) * 512], in0=P.t[5][pb:pb + 64, :],
                                                                            in1=rec[pb:pb + 64, :], op=ALU.mult), reads=[P.b[5], brec], writes=[byT[hp]])
            LOOK = 2
            for bi in range(len(blocks) + LOOK):
                if bi < len(blocks):
                    front(bi)
                if bi >= LOOK:
                    back(bi - LOOK)
        for t in range(NTILE):
            i = xc[0] % 2; xc[0] += 1
            rr0 = r0 + t * 128
            c.dma("sp", lambda e, i=i, rr0=rr0: e.dma_start(out=xin[i][:], in_=x_dram[rr0:rr0 + 128, :]), writes=[bxin[i]])
            for hf in range(2):
                for k in range(8):
                    c.op("pe", lambda e, hf=hf, k=k, t=t: e.matmul(P.t[1 + hf][:], yT[:, k, t * 128:(t + 1) * 128], Wout[:, k, hf * 512:(hf + 1) * 512],
                                                                start=(k == 0), stop=(k == 7)), reads=[byT[k], bWout], writes=[P.b[1 + hf]])
                c.op("dve", lambda e, hf=hf, i=i: e.scalar_tensor_tensor(out=xin[i][:, hf * 512:(hf + 1) * 512], in0=xin[i][:, hf * 512:(hf + 1) * 512], scalar=float(ALPHA),
                                                                 in1=P.t[1 + hf][:], op0=ALU.mult, op1=ALU.add), reads=[bxin[i], P.b[1 + hf]], writes=[bxin[i]])
            emit_ln(c, xin[i], bxin[i], xin[i], bxin[i], gam, bet, bgb, lntmp)
            c.dma("sp", lambda e, i=i, rr0=rr0: e.dma_start(out=x1_dram[rr0:rr0 + 128, :], in_=xin[i][:]), reads=[bxin[i]])
            emit_router(c, K, P, xin[i], bxin[i], Wr, bWr, x1T, bx1T, rt, brt, rtm, affo[i], baffo[i], aff_dram, rr0)


NCORES = 8
TILES_P, TILES_S = 32, 64
E_ = 16
DFF_ = 1408


def build_ffn_program():
    NT = TILES_P + TILES_S
    n_p, n_s = NCORES * TILES_P, NCORES * TILES_S
    cap_p, cap_s = 2 * n_p * 128 // E_, 2 * n_s * 128 // E_
    nc = bass.Bass("TRN2", target_bir_lowering=False)
    di = lambda n, s: nc.dram_tensor(n, s, F32, kind="ExternalInput").ap()
    x = di("x", [NT * 128, D]); aff = di("aff", [NT * 128, E_]); affT = di("affT", [128, E_, n_p + n_s])
    wg = di("wg", [E_, D, DFF_]); wu = di("wu", [E_, D, DFF_]); wd = di("wd", [E_, DFF_, D])
    lng = di("lng", [D]); lnb = di("lnb", [D])
    y = nc.dram_tensor("y", [NT * 128, D], F32, kind="ExternalOutput").ap()
    c = Ctx(nc)
    K = emit_consts(c)
    emit_ffn2(c, K, x, aff, affT, wg, wu, wd, lng, lnb, y, TILES_P, TILES_S, n_p, n_s, cap_p, cap_s, E=E_, DFF=DFF_, TB=8, CAP=256)
    c.finish()
    return nc


def build_a_program(nseq=6):
    nc = bass.Bass("TRN2", target_bir_lowering=False)
    di = lambda n, s: nc.dram_tensor(n, s, F32, kind="ExternalInput").ap()
    x = di("x", [nseq * 2048, D]); w_in = di("w_in", [D, 3600]); w_out = di("w_out", [D, D]); gbias = di("gbias", [16]); ng = di("ng", [512])
    lng = di("lng", [D]); lnb = di("lnb", [D]); wr = di("wr", [D, 16]); tab = di("tab", [8, 3, 128, 16, 64])
    x1 = nc.dram_tensor("x1", [nseq * 2048, D], F32, kind="ExternalOutput").ap()
    aff = nc.dram_tensor("aff", [nseq * 2048, 16], F32, kind="ExternalOutput").ap()
    c = Ctx(nc); K = emit_consts(c); P = PSB(c)
    emit_mix_a(c, K, P, x, w_in, w_out, gbias, ng, lng, lnb, wr, tab, x1, aff, nseq)
    c.finish()
    return nc


def build_b_program(nseq=6):
    nc = bass.Bass("TRN2", target_bir_lowering=False)
    di = lambda n, s: nc.dram_tensor(n, s, F32, kind="ExternalInput").ap()
    x = di("x", [nseq * 2048, D]); w_in = di("w_in", [D, 3072]); w_out = di("w_out", [D, D])
    lng = di("lng", [D]); lnb = di("lnb", [D]); wr = di("wr", [D, 16]); cos = di("cos", [128, 2048]); sins = di("sins", [128, 2048]); tab = di("tab", [128, 3968])
    x1 = nc.dram_tensor("x1", [nseq * 2048, D], F32, kind="ExternalOutput").ap()
    aff = nc.dram_tensor("aff", [nseq * 2048, 16], F32, kind="ExternalOutput").ap()
    c = Ctx(nc); K = emit_consts(c); P = PSB(c)
    emit_mix_b(c, K, P, x, w_in, w_out, lng, lnb, wr, cos, sins, tab, x1, aff, nseq)
    c.finish()
    return nc


def _aff_layout(affs):
    def toT(a):
        return a.reshape(-1, 128, E_).transpose(1, 2, 0)
    ap = np.concatenate([a[:TILES_P * 128] for a in affs], 0)
    as_ = np.concatenate([a[TILES_P * 128:] for a in affs], 0)
    return np.ascontiguousarray(np.concatenate([toT(ap), toT(as_)], axis=2), dtype=np.float32)


def kernel(x_prompt, x_sample, even_w_in, ml_gate_bias, na_rpb, ml_norm_g, even_w_out, da_w_in, da_w_out,
           ln_mix_g, ln_mix_b, ec_router, ec_w_gate, ec_w_up, ec_w_down, ln_ffn_g, ln_ffn_b):
    f32 = lambda a: np.ascontiguousarray(np.asarray(a), dtype=np.float32)
    x_prompt, x_sample = f32(x_prompt), f32(x_sample)
    cores = list(range(NCORES))
    xs = [np.concatenate([x_prompt[2 * c:2 * c + 2].reshape(-1, D), x_sample[4 * c:4 * c + 4].reshape(-1, D)], 0) for c in cores]
    tabA = na_table_host(f32(na_rpb)[0])
    feedA = {"w_in": f32(even_w_in)[0], "w_out": f32(even_w_out)[0], "gbias": f32(ml_gate_bias)[0], "ng": f32(ml_norm_g)[0],
             "lng": f32(ln_mix_g)[0], "lnb": f32(ln_mix_b)[0], "wr": f32(ec_router)[0], "tab": tabA}
    ncA = build_a_program()
    res = run_bass_kernel_spmd(ncA, [dict(feedA, x=xs[c]) for c in cores], core_ids=cores)
    x1 = [res.results[c]["x1"] for c in cores]; aff = [res.results[c]["aff"] for c in cores]
    del ncA
    ncF = build_ffn_program()
    def run_ffn(xl, affl, layer):
        affT = _aff_layout(affl)
        feed = {"affT": affT, "wg": f32(ec_w_gate)[layer], "wu": f32(ec_w_up)[layer], "wd": f32(ec_w_down)[layer],
                "lng": f32(ln_ffn_g)[layer], "lnb": f32(ln_ffn_b)[layer]}
        r = run_bass_kernel_spmd(ncF, [dict(feed, x=xl[c], aff=affl[c]) for c in cores], core_ids=cores)
        return [r.results[c]["y"] for c in cores]
    x2 = run_ffn(x1, aff, 0)
    COS, SINS, TAB = rope_tables_host()
    feedB = {"w_in": f32(da_w_in)[0], "w_out": f32(da_w_out)[0], "lng": f32(ln_mix_g)[1], "lnb": f32(ln_mix_b)[1],
             "wr": f32(ec_router)[1], "cos": COS, "sins": SINS, "tab": TAB}
    ncB = build_b_program()
    res = run_bass_kernel_spmd(ncB, [dict(feedB, x=x2[c]) for c in cores], core_ids=cores)
    x3 = [res.results[c]["x1"] for c in cores]; aff2 = [res.results[c]["aff"] for c in cores]
    del ncB
    y = run_ffn(x3, aff2, 1)
    y_prompt = np.stack([y[c][:TILES_P * 128].reshape(2, 2048, D) for c in cores], 0).reshape(16, 2048, D)
    y_sample = np.stack([y[c][TILES_P * 128:].reshape(4, 2048, D) for c in cores], 0).reshape(32, 2048, D)
    return (np.ascontiguousarray(y_prompt, dtype=np.float32), np.ascontiguousarray(y_sample, dtype=np.float32))
```

```python
from concourse.bass_utils import run_bass_kernel_spmd
import numpy as np
import concourse.bass as bass
import concourse.mybir as mybir

F32 = mybir.dt.float32
BF16 = mybir.dt.bfloat16
I32 = mybir.dt.int32
U32 = mybir.dt.uint32
ALU = mybir.AluOpType
AF = mybir.ActivationFunctionType
AX = mybir.AxisListType


class Buf:
    __slots__ = ("name", "writers", "readers")

    def __init__(self, name=""):
        self.name = name
        self.writers = {}
        self.readers = []


class EngS:
    def __init__(self, name, self_wait):
        self.name = name
        self.ops = []
        self.self_wait = self_wait
        self.known = {}
        self.n = 0


class Ctx:
    ENG = ("pe", "act", "dve", "pool", "sp")

    def __init__(self, nc, dma_pool=8):
        self.nc = nc
        self.E = {
            "pe": EngS("pe", False),
            "act": EngS("act", True),
            "dve": EngS("dve", True),
            "pool": EngS("pool", True),
            "sp": EngS("sp", False),
        }
        self.sems = {}
        self.stack = []
        cm = nc.semaphore("s_cc")
        self.sems["cc"] = cm.__enter__()
        self.stack.append(cm)
        self.cc_n = 0
        self.cc_scr = None
        for e in self.ENG:
            cm = nc.semaphore("s_" + e)
            self.sems[e] = cm.__enter__()
            self.stack.append(cm)
        self.dma_pool = {}
        self.dma_tot = {}
        self.dma_i = {}
        for q in ("sp", "pool", "act"):
            lst = []
            for i in range(dma_pool):
                cm = nc.semaphore("d_%s%d" % (q, i))
                lst.append(cm.__enter__())
                self.stack.append(cm)
            self.dma_pool[q] = lst
            self.dma_i[q] = 0
        self.ctxs = []

    def sb(self, name, shape, dt):
        cm = self.nc.sbuf_tensor(name, list(shape), dt)
        t = cm.__enter__()
        self.ctxs.append(cm)
        return t

    def ps(self, name, shape, dt):
        cm = self.nc.psum_tensor(name, list(shape), dt)
        t = cm.__enter__()
        self.ctxs.append(cm)
        return t

    def _need(self, es, ev, waits):
        key, val = ev
        if key == es.name and not es.self_wait:
            return
        if es.known.get(key, -1) >= val:
            return
        es.known[key] = val
        waits.append(ev)

    def _deps(self, es, reads, writes):
        waits = []
        for b in reads:
            for ev in b.writers.values():
                self._need(es, ev, waits)
        for b in writes:
            for ev in b.writers.values():
                self._need(es, ev, waits)
            for ev in b.readers:
                self._need(es, ev, waits)
        return waits

    def _mark(self, ev, reads, writes):
        for b in reads:
            b.readers.append(ev)
            if len(b.readers) > 64:
                b.readers = b.readers[-48:]
        for b in writes:
            b.writers = {ev[0]: ev}
            b.readers = []

    def op(self, eng, fn, reads=(), writes=()):
        es = self.E[eng]
        waits = self._deps(es, reads, writes)
        es.n += 1
        ev = (eng, es.n)
        es.ops.append((fn, waits, ("eng", None)))
        self._mark(ev, reads, writes)
        return ev

    def dma(self, q, fn, reads=(), writes=()):
        es = self.E[q]
        pool = self.dma_pool[q]
        i = self.dma_i[q]
        self.dma_i[q] = i + 1
        sem = pool[i % len(pool)]
        skey = ("dma", q, i % len(pool))
        prev = self.dma_tot.get(skey, 0)
        waits = self._deps(es, reads, writes)
        if prev > 0:
            self._need(es, (skey, prev), waits)
        tot = prev + 16
        self.dma_tot[skey] = tot
        es.n += 1
        es.ops.append((fn, waits, ("dma", sem)))
        ev = (skey, tot)
        self._mark(ev, reads, writes)
        return ev

    def cc(self, fn, reads=(), writes=()):
        es = self.E["pool"]
        if self.cc_scr is None:
            self.cc_scr = self.sb("cc_scr", [128, 8], F32)
        waits = self._deps(es, reads, writes)
        self.cc_n += 1
        es.n += 1
        es.ops.append((fn, waits, ("cc", self.sems["cc"])))
        es.n += 1
        scr = self.cc_scr
        es.ops.append((lambda e: e.memset(scr[:], 0.0), [("cc", self.cc_n)], ("eng", None)))
        ev = ("pool", es.n)
        es.known["pool"] = es.n
        self._mark(ev, reads, writes)
        return ev

    def _sem_of(self, key):
        if isinstance(key, tuple):
            return self.dma_pool[key[1]][key[2]]
        return self.sems[key]

    def finish(self):
        nc = self.nc
        es = self.E["sp"]
        waits = []
        for skey, tot in self.dma_tot.items():
            self._need(es, (skey, tot), waits)
        for e in ("pe", "act", "dve", "pool"):
            if self.E[e].n > 0:
                self._need(es, (e, self.E[e].n), waits)
        es.ops.append((None, waits, None))
        engmap = {"pe": "tensor", "act": "scalar", "dve": "vector", "pool": "gpsimd", "sp": "sync"}
        with nc.Block() as block:
            for e in self.ENG:
                st = self.E[e]

                def body(engine, st=st, e=e):
                    own = self.sems[e]
                    for fn, waits, inc in st.ops:
                        for key, val in waits:
                            engine.wait_ge(self._sem_of(key), val)
                        if fn is None:
                            continue
                        ins = fn(engine)
                        if inc[0] == "eng":
                            ins.then_inc(own, 1)
                        elif inc[0] == "cc":
                            ins.then_inc(inc[1])
                        else:
                            ins.then_inc(inc[1], 16)

                getattr(block, engmap[e])(body)
        for cm in reversed(self.ctxs):
            cm.__exit__(None, None, None)
        for cm in reversed(self.stack):
            cm.__exit__(None, None, None)


D = 1024
ALPHA = 4 ** 0.25
EPS = 1e-5


def emit_consts(c):
    nc = c.nc
    K = {}
    K["ident_f"] = c.sb("ident_f", [128, 128], F32)
    K["ident_b"] = c.sb("ident_b", [128, 128], BF16)
    K["ones_f"] = c.sb("ones_f", [128, 128], F32)
    K["b"] = Buf("consts")
    idf, idb, of = K["ident_f"], K["ident_b"], K["ones_f"]
    c.op("pool", lambda e: e.memset(of[:], 1.0), writes=[K["b"]])
    c.op("pool", lambda e: e.affine_select(out=idf[:], in_=of[:], pattern=[[-1, 128]], compare_op=ALU.is_equal,
                                           fill=0.0, base=0, channel_multiplier=1), reads=[K["b"]], writes=[K["b"]])
    c.op("pool", lambda e: e.tensor_copy(out=idb[:], in_=idf[:]), reads=[K["b"]], writes=[K["b"]])
    return K


def emit_ln(c, z, bz, out, bout, gam, bet, bgb, tmp):
    st, mv, rs, bt = tmp["st"], tmp["mv"], tmp["rs"], tmp["b"]
    c.op("dve", lambda e: e.bn_stats(out=st[:, 0, :], in_=z[:, 0:512]), reads=[bz], writes=[bt])
    c.op("dve", lambda e: e.bn_stats(out=st[:, 1, :], in_=z[:, 512:1024]), reads=[bz], writes=[bt])
    c.op("dve", lambda e: e.bn_aggr(out=mv[:], in_=st[:].rearrange("p a b -> p (a b)")), reads=[bt], writes=[bt])
    c.op("dve", lambda e: e.tensor_scalar(out=rs[:], in0=mv[:, 1:2], scalar1=EPS, scalar2=None, op0=ALU.add), reads=[bt], writes=[bt])
    c.op("act", lambda e: e.activation(out=rs[:], in_=rs[:], func=AF.Sqrt), reads=[bt], writes=[bt])
    c.op("dve", lambda e: e.reciprocal(out=rs[:], in_=rs[:]), reads=[bt], writes=[bt])
    c.op("dve", lambda e: e.tensor_scalar(out=out[:], in0=z[:], scalar1=mv[:, 0:1], scalar2=rs[:, 0:1],
                                          op0=ALU.subtract, op1=ALU.mult), reads=[bz, bt], writes=[bout])
    c.op("pool", lambda e: e.tensor_tensor(out=out[:], in0=out[:], in1=gam[:], op=ALU.mult), reads=[bout, bgb], writes=[bout])
    c.op("pool", lambda e: e.tensor_tensor(out=out[:], in0=out[:], in1=bet[:], op=ALU.add), reads=[bout, bgb], writes=[bout])


def emit_thresholds(c, K, affT_dram, n_p, n_s, cap_p, cap_s, E, iters=26, ret_A=False):
    A = c.sb("bis_A", [128, E, n_p + n_s], F32)
    bA = Buf("bisA")
    c.dma("sp", lambda e: e.dma_start(out=A[:], in_=affT_dram), writes=[bA])
    lo = c.sb("bis_lo", [128, 2 * E], F32)
    hi = c.sb("bis_hi", [128, 2 * E], F32)
    mid = c.sb("bis_mid", [128, 2 * E], F32)
    cnt = c.sb("bis_cnt", [128, 2 * E], F32)
    capv = c.sb("bis_cap", [128, 2 * E], F32)
    ge = c.sb("bis_ge", [128, 2 * E], U32)
    lt = c.sb("bis_lt", [128, 2 * E], U32)
    junk = c.sb("bis_junk", [128, max(n_p, n_s)], BF16)
    tot = c.ps("bis_tot", [128, 2 * E], F32)
    b = Buf("bis")
    bcnt = Buf("bcnt")
    btot = Buf("btot")
    bj = Buf("junk")
    c.op("dve", lambda e: e.memset(lo[:], 0.0), writes=[b])
    c.op("dve", lambda e: e.memset(hi[:], 1.0), writes=[b])
    c.op("dve", lambda e: e.memset(mid[:], 0.5), writes=[b])
    c.op("dve", lambda e: e.memset(capv[:, 0:E], float(cap_p)), writes=[b])
    c.op("dve", lambda e: e.memset(capv[:, E:2 * E], float(cap_s)), writes=[b])
    ones = K["ones_f"]
    for it in range(iters):
        for v in range(2 * E):
            g, ex = divmod(v, E)
            src = A[:, ex, 0:n_p] if g == 0 else A[:, ex, n_p:n_p + n_s]
            n = n_p if g == 0 else n_s
            c.op("dve", lambda e, src=src, v=v, n=n: e.tensor_scalar(
                out=junk[:, 0:n], in0=src, scalar1=mid[:, v:v + 1], scalar2=None,
                op0=ALU.is_ge, op1=ALU.add, accum_out=cnt[:, v:v + 1]), reads=[bA, b], writes=[bj, bcnt])
        c.op("pe", lambda e: e.matmul(tot[:], ones[:], cnt[:], start=True, stop=True), reads=[bcnt, K["b"]], writes=[btot])
        c.op("dve", lambda e: e.tensor_tensor(out=ge[:], in0=tot[:], in1=capv[:], op=ALU.is_ge), reads=[btot, b], writes=[b])
        c.op("dve", lambda e: e.tensor_tensor(out=lt[:], in0=tot[:], in1=capv[:], op=ALU.is_lt), reads=[btot, b], writes=[b])
        c.op("dve", lambda e: e.copy_predicated(out=lo[:], mask=ge[:], data=mid[:]), reads=[b], writes=[b])
        c.op("dve", lambda e: e.copy_predicated(out=hi[:], mask=lt[:], data=mid[:]), reads=[b], writes=[b])
        c.op("dve", lambda e: e.tensor_tensor(out=mid[:], in0=lo[:], in1=hi[:], op=ALU.add), reads=[b], writes=[b])
        c.op("dve", lambda e: e.tensor_scalar(out=mid[:], in0=mid[:], scalar1=0.5, scalar2=None, op0=ALU.mult), reads=[b], writes=[b])
    if ret_A:
        return lo, b, A, bA
    return lo, b


def emit_ffn(c, K, x_dram, aff_own_dram, affT_dram, wg, wu, wd, lng, lnb, y_dram,
             tiles_p, tiles_s, n_p, n_s, cap_p, cap_s, E=16, DFF=1408, TB=8, iters=26, dbg=None, stage=99):
    NT = tiles_p + tiles_s
    FC = DFF // 128
    theta, bth = emit_thresholds(c, K, affT_dram, n_p, n_s, cap_p, cap_s, E, iters)
    if stage == 1:
        c.dma("sp", lambda e: e.dma_start(out=dbg[:, 0:2 * E], in_=theta[:]), reads=[bth])
        return
    aff = c.sb("aff_own", [128, NT, E], F32)
    gp = c.sb("gprime", [128, NT, E], F32)
    bg = Buf("gp")
    c.dma("sp", lambda e: e.dma_start(out=aff[:], in_=aff_own_dram.rearrange("(j p) e -> p j e", p=128)), writes=[bg])
    for g, (t0, t1) in enumerate(((0, tiles_p), (tiles_p, NT))):
        if t1 == t0:
            continue
        th = theta[:, g * E:(g + 1) * E].unsqueeze(1).to_broadcast([128, t1 - t0, E])
        c.op("dve", lambda e, t0=t0, t1=t1, th=th: e.tensor_tensor(out=gp[:, t0:t1, :], in0=aff[:, t0:t1, :], in1=th, op=ALU.is_ge),
             reads=[bg, bth], writes=[bg])
        c.op("dve", lambda e, t0=t0, t1=t1: e.tensor_tensor(out=gp[:, t0:t1, :], in0=gp[:, t0:t1, :], in1=aff[:, t0:t1, :], op=ALU.mult),
             reads=[bg], writes=[bg])
    if stage == 2:
        c.dma("sp", lambda e: e.dma_start(out=dbg[:, 0:NT * E], in_=gp[:].rearrange("p a b -> p (a b)")), reads=[bg])
        return
    gam = c.sb("ffn_gam", [128, D], F32)
    bet = c.sb("ffn_bet", [128, D], F32)
    bgb = Buf("gb")
    c.dma("sp", lambda e: e.dma_start(out=gam[:], in_=lng.partition_broadcast(128)), writes=[bgb])
    c.dma("sp", lambda e: e.dma_start(out=bet[:], in_=lnb.partition_broadcast(128)), writes=[bgb])
    if stage == 30:
        return
    lntmp = {"st": c.sb("ln_st", [128, 2, 6], F32), "mv": c.sb("ln_mv", [128, 2], F32), "rs": c.sb("ln_rs", [128, 1], F32), "b": Buf("lnt")}

    TOK = TB * 128
    xT = c.sb("ffn_xT", [128, 8, TOK], BF16)
    hT = c.sb("ffn_hT", [128, FC, TOK], BF16)
    yacc = c.sb("ffn_yacc", [128, TB, D], F32)
    wgs = c.sb("ffn_wg", [128, 8, DFF], BF16)
    wus = c.sb("ffn_wu", [128, 8, DFF], BF16)
    wds = c.sb("ffn_wd", [128, FC, D], BF16)
    bwg, bwu, bwd = Buf("wg"), Buf("wu"), Buf("wd")
    bxT = Buf("xT")
    bhT = [Buf("hT%d" % i) for i in range(FC)]
    byacc = [Buf("yacc%d" % i) for i in range(TB)]
    xin = [c.sb("ffn_xin%d" % i, [128, D], F32) for i in range(2)]
    bxin = [Buf("xin0"), Buf("xin1")]
    sg = [c.sb("ffn_sg%d" % i, [128, 512], BF16) for i in range(2)]
    bsg = [Buf("sg0"), Buf("sg1")]
    zt = [c.sb("ffn_z%d" % i, [128, D], F32) for i in range(2)]
    bz = [Buf("z0"), Buf("z1")]
    ot, bo = zt, bz
    ps_t = c.ps("ps_t", [128, 512], F32); bps_t = Buf("ps_t")
    ps_g = [c.ps("ps_g%d" % i, [128, 512], F32) for i in range(2)]; bps_g = [Buf("psg0"), Buf("psg1")]
    ps_u = [c.ps("ps_u%d" % i, [128, 512], F32) for i in range(2)]; bps_u = [Buf("psu0"), Buf("psu1")]
    ps_d = [c.ps("ps_d%d" % i, [128, 512], F32) for i in range(2)]; bps_d = [Buf("psd0"), Buf("psd1")]
    idf = K["ident_f"]
    cnt = [0, 0, 0, 0]
    wst = [c.sb('ffn_wst%d' % i, [128, max(DFF, D)], F32) for i in range(2)]
    bwst = [Buf('wst%d' % i) for i in range(2)]
    nblk = (NT + TB - 1) // TB
    for blk in range(nblk):
        tl0 = blk * TB
        ntl = min(TB, NT - tl0)
        ntok = ntl * 128
        for t in range(ntl):
            i = cnt[0] % 2; cnt[0] += 1
            r0 = (tl0 + t) * 128
            c.dma("sp", lambda e, i=i, r0=r0: e.dma_start(out=xin[i][:], in_=x_dram[r0:r0 + 128, :]), writes=[bxin[i]])
            if stage == 31:
                continue
            for kh in range(2):
                for k in range(4):
                    kk = kh * 4 + k
                    c.op("pe", lambda e, i=i, k=k, kk=kk: e.transpose(ps_t[:, k * 128:(k + 1) * 128], xin[i][:, kk * 128:(kk + 1) * 128], idf[:]),
                         reads=[bxin[i], K["b"]], writes=[bps_t])
                if stage == 32:
                    continue
                c.op("dve", lambda e, t=t, kh=kh: e.tensor_copy(out=xT[:, kh * 4:(kh + 1) * 4, t * 128:(t + 1) * 128],
                                                      in_=ps_t[:].rearrange("p (k t) -> p k t", k=4)),
                     reads=[bps_t], writes=[bxT])
        if stage in (3, 31, 32):
            return
        for ex in range(E):
            if stage == 4 and ex == 1:
                return
            for (dst, src, nchunk, bw) in ((wgs, wg, 8, bwg), (wus, wu, 8, bwu), (wds, wd, FC, bwd)):
                wdt = src.shape[2]
                for k in range(nchunk):
                    i = cnt[3] % 2; cnt[3] += 1
                    c.dma("sp", lambda e, i=i, k=k, src=src, ex=ex, wdt=wdt: e.dma_start(out=wst[i][:, 0:wdt], in_=src[ex, k * 128:(k + 1) * 128, :]),
                          writes=[bwst[i]])
                    c.op("pool", lambda e, i=i, k=k, dst=dst, wdt=wdt: e.tensor_copy(out=dst[:, k, :], in_=wst[i][:, 0:wdt]),
                         reads=[bwst[i]], writes=[bw])
            if stage == 40:
                return
            for fc in range(FC):
                for n0 in range(0, ntok, 512):
                    nn = min(512, ntok - n0)
                    i = cnt[1] % 2; cnt[1] += 1
                    for k in range(8):
                        c.op("pe", lambda e, i=i, k=k, fc=fc, n0=n0, nn=nn: e.matmul(
                            ps_g[i][:, 0:nn], wgs[:, k, fc * 128:(fc + 1) * 128], xT[:, k, n0:n0 + nn], start=(k == 0), stop=(k == 7)),
                            reads=[bwg, bxT], writes=[bps_g[i]])
                    for k in range(8):
                        c.op("pe", lambda e, i=i, k=k, fc=fc, n0=n0, nn=nn: e.matmul(
                            ps_u[i][:, 0:nn], wus[:, k, fc * 128:(fc + 1) * 128], xT[:, k, n0:n0 + nn], start=(k == 0), stop=(k == 7)),
                            reads=[bwu, bxT], writes=[bps_u[i]])
                    if stage == 41:
                        continue
                    c.op("act", lambda e, i=i, nn=nn: e.activation(out=sg[i][:, 0:nn], in_=ps_g[i][:, 0:nn], func=AF.Silu),
                         reads=[bps_g[i]], writes=[bsg[i]])
                    c.op("dve", lambda e, i=i, fc=fc, n0=n0, nn=nn: e.tensor_tensor(
                        out=hT[:, fc, n0:n0 + nn], in0=ps_u[i][:, 0:nn], in1=sg[i][:, 0:nn], op=ALU.mult),
                        reads=[bps_u[i], bsg[i]], writes=[bhT[fc]])
            if stage in (5, 41):
                return
            for t in range(ntl):
                for h in range(2):
                    i = cnt[2] % 2; cnt[2] += 1
                    for fc in range(FC):
                        c.op("pe", lambda e, i=i, fc=fc, t=t, h=h: e.matmul(
                            ps_d[i][:], hT[:, fc, t * 128:(t + 1) * 128], wds[:, fc, h * 512:(h + 1) * 512], start=(fc == 0), stop=(fc == FC - 1)),
                            reads=[bwd, bhT[fc]], writes=[bps_d[i]])
                    gcol = gp[:, tl0 + t, ex:ex + 1]
                    if ex == 0:
                        c.op("dve", lambda e, i=i, t=t, h=h, gcol=gcol: e.tensor_scalar(
                            out=yacc[:, t, h * 512:(h + 1) * 512], in0=ps_d[i][:], scalar1=gcol, scalar2=None, op0=ALU.mult),
                            reads=[bps_d[i], bg], writes=[byacc[t]])
                    else:
                        c.op("dve", lambda e, i=i, t=t, h=h, gcol=gcol: e.scalar_tensor_tensor(
                            out=yacc[:, t, h * 512:(h + 1) * 512], in0=ps_d[i][:], scalar=gcol, in1=yacc[:, t, h * 512:(h + 1) * 512],
                            op0=ALU.mult, op1=ALU.add), reads=[bps_d[i], bg, byacc[t]], writes=[byacc[t]])
        if stage == 6:
            return
        for t in range(ntl):
            i = cnt[0] % 2; cnt[0] += 1
            r0 = (tl0 + t) * 128
            c.dma("sp", lambda e, i=i, r0=r0: e.dma_start(out=xin[i][:], in_=x_dram[r0:r0 + 128, :]), writes=[bxin[i]])
            c.op("dve", lambda e, i=i, t=t: e.scalar_tensor_tensor(out=zt[i][:], in0=xin[i][:], scalar=float(ALPHA), in1=yacc[:, t, :],
                                                                 op0=ALU.mult, op1=ALU.add), reads=[bxin[i], byacc[t]], writes=[bz[i]])
            emit_ln(c, zt[i], bz[i], ot[i], bo[i], gam, bet, bgb, lntmp)
            c.dma("sp", lambda e, i=i, r0=r0: e.dma_start(out=y_dram[r0:r0 + 128, :], in_=ot[i][:]), reads=[bo[i]])


def emit_ffn2(c, K, x_dram, aff_own_dram, affT_dram, wg, wu, wd, lng, lnb, y_dram,
              tiles_p, tiles_s, n_p, n_s, cap_p, cap_s, E=16, DFF=1408, TB=8, CAP=256, iters=26):
    NT = tiles_p + tiles_s
    FC = DFF // 128
    NST = CAP // 128
    theta, bth, A, bA = emit_thresholds(c, K, affT_dram, n_p, n_s, cap_p, cap_s, E, iters, ret_A=True)
    aff = c.sb("aff_own", [128, NT, E], F32)
    gp = c.sb("gprime", [128, NT, E], F32)
    self_ = c.sb("self", [128, NT, E], F32)
    selb = c.sb("selb", [128, NT, E], BF16)
    bg = Buf("gp")
    c.dma("sp", lambda e: e.dma_start(out=aff[:], in_=aff_own_dram.rearrange("(j p) e -> p j e", p=128)), writes=[bg])
    for g, (t0, t1) in enumerate(((0, tiles_p), (tiles_p, NT))):
        if t1 == t0:
            continue
        th = theta[:, g * E:(g + 1) * E].unsqueeze(1).to_broadcast([128, t1 - t0, E])
        c.op("dve", lambda e, t0=t0, t1=t1, th=th: e.tensor_tensor(out=self_[:, t0:t1, :], in0=aff[:, t0:t1, :], in1=th, op=ALU.is_ge),
             reads=[bg, bth], writes=[bg])
        c.op("dve", lambda e, t0=t0, t1=t1: e.tensor_tensor(out=gp[:, t0:t1, :], in0=self_[:, t0:t1, :], in1=aff[:, t0:t1, :], op=ALU.mult),
             reads=[bg], writes=[bg])
    c.op("dve", lambda e: e.tensor_copy(out=selb[:], in_=self_[:]), reads=[bg], writes=[bg])
    gam = c.sb("ffn_gam", [128, D], F32)
    bet = c.sb("ffn_bet", [128, D], F32)
    bgb = Buf("gb")
    c.dma("sp", lambda e: e.dma_start(out=gam[:], in_=lng.partition_broadcast(128)), writes=[bgb])
    c.dma("sp", lambda e: e.dma_start(out=bet[:], in_=lnb.partition_broadcast(128)), writes=[bgb])
    lntmp = {"st": c.sb("ln_st", [128, 2, 6], F32), "mv": c.sb("ln_mv", [128, 2], F32), "rs": c.sb("ln_rs", [128, 1], F32), "b": Buf("lnt")}
    stri = c.sb("f_stri", [128, 128], BF16)
    ones_b = c.sb("f_onesb", [128, 128], BF16)
    iot = c.sb("f_iota", [128, CAP], F32)
    trif = c.sb("f_trif", [128, 128], F32)
    c.op("pool", lambda e: e.affine_select(out=trif[:], in_=K["ones_f"][:], pattern=[[1, 128]], compare_op=ALU.is_gt, fill=0.0, base=0, channel_multiplier=-1),
         reads=[K["b"]], writes=[K["b"]])
    c.op("pool", lambda e: e.tensor_copy(out=stri[:], in_=trif[:]), reads=[K["b"]], writes=[K["b"]])
    c.op("pool", lambda e: e.tensor_copy(out=ones_b[:], in_=K["ones_f"][:]), reads=[K["b"]], writes=[K["b"]])
    c.op("pool", lambda e: e.iota(iot[:], pattern=[[1, CAP]], base=0, channel_multiplier=0, allow_small_or_imprecise_dtypes=True), writes=[K["b"]])

    asz = E * (n_p + n_s)
    need = 8 * DFF
    if asz >= need:
        Ab = A[:].rearrange("p e n -> p (e n)")
        wgs = Ab[:, 0:4 * DFF].bitcast(BF16).rearrange("p (k f) -> p k f", k=8)
        wus = Ab[:, 4 * DFF:8 * DFF].bitcast(BF16).rearrange("p (k f) -> p k f", k=8)
        alias = [bA]
    else:
        wgs = c.sb("ffn_wg", [128, 8, DFF], BF16)[:]
        wus = c.sb("ffn_wu", [128, 8, DFF], BF16)[:]
        alias = []
    wds = c.sb("ffn_wd", [128, FC, D], BF16)
    bwg, bwu, bwd = Buf("wg"), Buf("wu"), Buf("wd")
    xtok = c.sb("f_xtok", [128, TB, D], BF16); bxtok = Buf("xtok")
    xeT = c.sb("f_xeT", [128, 8, CAP], BF16); bxeT = Buf("xeT")
    hT = c.sb("f_hT", [128, FC, CAP], BF16); bhT = [Buf("hT%d" % i) for i in range(FC)]
    osb = c.sb("f_os", [128, NST, D], BF16); bos = Buf("os")
    OH = c.sb("f_OH", [128, TB, CAP], BF16); bOH = Buf("OH")
    OHT = c.sb("f_OHT", [128, NST, TB * 128], BF16); bOHT = Buf("OHT")
    yacc = c.sb("ffn_yacc", [128, TB, D], F32); byacc = [Buf("yacc%d" % i) for i in range(TB)]
    csel = c.sb("f_csel", [128, TB, E], BF16); bcsel = Buf("csel")
    rank = c.sb("f_rank", [128, TB, E], F32); brank = Buf("rank")
    xin = [c.sb("ffn_xin%d" % i, [128, D], F32) for i in range(2)]; bxin = [Buf("xin0"), Buf("xin1")]
    sg = [c.sb("ffn_sg%d" % i, [128, CAP], BF16) for i in range(2)]; bsg = [Buf("sg0"), Buf("sg1")]
    wst = [c.sb("ffn_wst%d" % i, [128, max(DFF, D)], F32) for i in range(3)]; bwst = [Buf("wst%d" % i) for i in range(3)]
    ps_rank = c.ps("ps_rank", [128, 512], F32); bps_rank = Buf("psrank")
    ps_tp = c.ps("ps_tp", [128, 1024], BF16); bps_tp = Buf("pstp")
    ps_ga = c.ps("ps_ga", [128, 512], F32); bps_ga = Buf("psga")
    ps_g = c.ps("ps_g", [128, 512], F32); bps_g = Buf("psg")
    ps_u = c.ps("ps_u", [128, 512], F32); bps_u = Buf("psu")
    ps_d = c.ps("ps_d", [128, 512], F32); bps_d = Buf("psd")
    ps_s0 = c.ps("ps_s0", [128, 512], F32); ps_s = [ps_s0, ps_s0]; _b = Buf("pss0"); bps_s = [_b, _b]
    idb = K["ident_b"]
    cnt = [0, 0, 0, 0]
    nblk = (NT + TB - 1) // TB
    first_w = [True]
    for blk in range(nblk):
        tl0 = blk * TB
        ntl = min(TB, NT - tl0)
        for t in range(ntl):
            i = cnt[0] % 2; cnt[0] += 1
            r0 = (tl0 + t) * 128
            c.dma("sp", lambda e, i=i, r0=r0: e.dma_start(out=xin[i][:], in_=x_dram[r0:r0 + 128, :]), writes=[bxin[i]])
            c.op("dve", lambda e, i=i, t=t: e.tensor_copy(out=xtok[:, t, :], in_=xin[i][:]), reads=[bxin[i]], writes=[bxtok])
        c.op("dve", lambda e: e.memset(csel[:, 0, :], 0.0), writes=[bcsel])
        for t in range(1, ntl):
            c.op("dve", lambda e, t=t, tl0=tl0: e.tensor_tensor(out=csel[:, t, :], in0=csel[:, t - 1, :], in1=selb[:, tl0 + t - 1, :], op=ALU.add),
                 reads=[bg, bcsel], writes=[bcsel])
        c.op("pe", lambda e, ntl=ntl, tl0=tl0: e.matmul(ps_rank[:, 0:ntl * E], stri[:], selb[:, tl0:tl0 + ntl, :].rearrange("p t e -> p (t e)"), start=True, stop=False),
             reads=[bg, K["b"]], writes=[bps_rank])
        c.op("pe", lambda e, ntl=ntl: e.matmul(ps_rank[:, 0:ntl * E], ones_b[:], csel[:, 0:ntl, :].rearrange("p t e -> p (t e)"), start=False, stop=True),
             reads=[bcsel, K["b"]], writes=[bps_rank])
        c.op("dve", lambda e, ntl=ntl: e.tensor_copy(out=rank[:, 0:ntl, :].rearrange("p t e -> p (t e)"), in_=ps_rank[:, 0:ntl * E]), reads=[bps_rank], writes=[brank])
        for ex in range(E):
            for (dst, src, nchunk, bw, al) in ((wgs, wg, 8, bwg, alias), (wus, wu, 8, bwu, alias), (wds[:], wd, FC, bwd, [])):
                wdt = src.shape[2]
                for k in range(nchunk):
                    i = cnt[3] % 3; cnt[3] += 1
                    c.dma("sp", lambda e, i=i, k=k, src=src, ex=ex, wdt=wdt: e.dma_start(out=wst[i][:, 0:wdt], in_=src[ex, k * 128:(k + 1) * 128, :]),
                          writes=[bwst[i]])
                    ce = ("act", "act", "pool")[cnt[3] % 3]
                    if ce == "act":
                        c.op("act", lambda e, i=i, k=k, dst=dst, wdt=wdt: e.activation(out=dst[:, k, :], in_=wst[i][:, 0:wdt], func=AF.Identity),
                             reads=[bwst[i]], writes=[bw] + (al if first_w[0] else []))
                    else:
                        c.op(ce, lambda e, i=i, k=k, dst=dst, wdt=wdt: e.tensor_copy(out=dst[:, k, :], in_=wst[i][:, 0:wdt]),
                             reads=[bwst[i]], writes=[bw] + (al if first_w[0] else []))
            first_w[0] = False
            for t in range(ntl):
                c.op("dve", lambda e, t=t, ex=ex, tl0=tl0: e.tensor_scalar(out=OH[:, t, :], in0=iot[:], scalar1=rank[:, t, ex:ex + 1], scalar2=self_[:, tl0 + t, ex:ex + 1],
                                                               op0=ALU.is_equal, op1=ALU.mult), reads=[brank, bg, K["b"]], writes=[bOH])
            kper = 512 // CAP
            for k0 in range(0, 8, kper):
                for kk in range(kper):
                    k = k0 + kk
                    for t in range(ntl):
                        c.op("pe", lambda e, k=k, kk=kk, t=t, ntl=ntl: e.matmul(ps_ga[:, kk * CAP:(kk + 1) * CAP], xtok[:, t, k * 128:(k + 1) * 128], OH[:, t, :],
                                                                               start=(t == 0), stop=(t == ntl - 1)), reads=[bxtok, bOH], writes=[bps_ga])
                c.op("dve", lambda e, k0=k0: e.tensor_copy(out=xeT[:, k0:k0 + kper, :], in_=ps_ga[:, 0:kper * CAP].rearrange("p (k n) -> p k n", k=kper)),
                     reads=[bps_ga], writes=[bxeT])
            for st_ in range(NST):
                for t in range(ntl):
                    c.op("pe", lambda e, st_=st_, t=t: e.transpose(ps_tp[:, t * 128:(t + 1) * 128], OH[:, t, st_ * 128:(st_ + 1) * 128], idb[:]),
                         reads=[bOH, K["b"]], writes=[bps_tp])
                c.op("dve", lambda e, st_=st_, ntl=ntl: e.tensor_copy(out=OHT[:, st_, 0:ntl * 128], in_=ps_tp[:, 0:ntl * 128]), reads=[bps_tp], writes=[bOHT])
            for fc in range(FC):
                i = cnt[1] % 2; cnt[1] += 1
                for k in range(8):
                    c.op("pe", lambda e, k=k, fc=fc: e.matmul(ps_g[:, 0:CAP], wgs[:, k, fc * 128:(fc + 1) * 128], xeT[:, k, :], start=(k == 0), stop=(k == 7)),
                         reads=[bwg, bxeT], writes=[bps_g])
                for k in range(8):
                    c.op("pe", lambda e, k=k, fc=fc: e.matmul(ps_u[:, 0:CAP], wus[:, k, fc * 128:(fc + 1) * 128], xeT[:, k, :], start=(k == 0), stop=(k == 7)),
                         reads=[bwu, bxeT], writes=[bps_u])
                c.op("act", lambda e, i=i: e.activation(out=sg[i][:], in_=ps_g[:, 0:CAP], func=AF.Silu), reads=[bps_g], writes=[bsg[i]])
                c.op("dve", lambda e, i=i, fc=fc: e.tensor_tensor(out=hT[:, fc, :], in0=ps_u[:, 0:CAP], in1=sg[i][:], op=ALU.mult),
                     reads=[bps_u, bsg[i]], writes=[bhT[fc]])
            for st_ in range(NST):
                for h in range(2):
                    for fc in range(FC):
                        c.op("pe", lambda e, st_=st_, h=h, fc=fc: e.matmul(ps_d[:], hT[:, fc, st_ * 128:(st_ + 1) * 128], wds[:, fc, h * 512:(h + 1) * 512],
                                                                        start=(fc == 0), stop=(fc == FC - 1)), reads=[bwd, bhT[fc]], writes=[bps_d])
                    c.op("dve", lambda e, st_=st_, h=h: e.tensor_copy(out=osb[:, st_, h * 512:(h + 1) * 512], in_=ps_d[:]), reads=[bps_d], writes=[bos])
            for t in range(ntl):
                for h in range(2):
                    i = cnt[2] % 2; cnt[2] += 1
                    for st_ in range(NST):
                        c.op("pe", lambda e, i=i, st_=st_, t=t, h=h: e.matmul(ps_s[i][:], OHT[:, st_, t * 128:(t + 1) * 128], osb[:, st_, h * 512:(h + 1) * 512],
                                                                           start=(st_ == 0), stop=(st_ == NST - 1)), reads=[bOHT, bos], writes=[bps_s[i]])
                    gcol = gp[:, tl0 + t, ex:ex + 1]
                    if ex == 0:
                        c.op("dve", lambda e, i=i, t=t, h=h, gcol=gcol: e.tensor_scalar(
                            out=yacc[:, t, h * 512:(h + 1) * 512], in0=ps_s[i][:], scalar1=gcol, scalar2=None, op0=ALU.mult),
                            reads=[bps_s[i], bg], writes=[byacc[t]])
                    else:
                        c.op("dve", lambda e, i=i, t=t, h=h, gcol=gcol: e.scalar_tensor_tensor(
                            out=yacc[:, t, h * 512:(h + 1) * 512], in0=ps_s[i][:], scalar=gcol, in1=yacc[:, t, h * 512:(h + 1) * 512],
                            op0=ALU.mult, op1=ALU.add), reads=[bps_s[i], bg, byacc[t]], writes=[byacc[t]])
        for t in range(ntl):
            i = cnt[0] % 2; cnt[0] += 1
            r0 = (tl0 + t) * 128
            c.dma("sp", lambda e, i=i, r0=r0: e.dma_start(out=xin[i][:], in_=x_dram[r0:r0 + 128, :]), writes=[bxin[i]])
            c.op("dve", lambda e, i=i, t=t: e.scalar_tensor_tensor(out=xin[i][:], in0=xin[i][:], scalar=float(ALPHA), in1=yacc[:, t, :],
                                                                 op0=ALU.mult, op1=ALU.add), reads=[bxin[i], byacc[t]], writes=[bxin[i]])
            emit_ln(c, xin[i], bxin[i], xin[i], bxin[i], gam, bet, bgb, lntmp)
            c.dma("sp", lambda e, i=i, r0=r0: e.dma_start(out=y_dram[r0:r0 + 128, :], in_=xin[i][:]), reads=[bxin[i]])


NAW, MLW = 512, 512
EVEN_IN = 3600
SEQ = 2048
NTILE = 16


def na_table_host(rpb):
    NEG = -30000.0
    qc = np.arange(64)
    kc = np.arange(64)
    c0 = np.clip(qc - 8, 0, 48)
    colvalid = (kc[:, None] >= c0[None, :]) & (kc[:, None] < c0[None, :] + 16)
    dc = np.clip(kc[:, None] - qc[None, :] + 15, 0, 30)
    T = np.full((8, 3, 128, 16, 64), NEG, np.float32)
    for i in range(16):
        for half in range(2):
            dr = i - 1 + half
            if dr < 0 or dr > 14:
                continue
            vals = np.where(colvalid[None], rpb[:, dr][:, dc], NEG).astype(np.float32)
            sl = slice(half * 64, half * 64 + 64)
            T[:, 0, sl, i, :] = vals
            if half == 1:
                T[:, 1, sl, i, :] = vals
            else:
                T[:, 2, sl, i, :] = vals
    return T


class PSB:
    def __init__(self, c, n=8):
        self.t = [c.ps("psb%d" % i, [128, 512], F32) for i in range(n)]
        self.b = [Buf("psb%d" % i) for i in range(n)]


def load_cast_weight(c, dst, bdst, src, ncol, st, bst, cnt):
    for k in range(8):
        for c0 in range(0, ncol, 1024):
            cw = min(1024, ncol - c0)
            i = cnt[0] % len(st); cnt[0] += 1
            c.dma("sp", lambda e, i=i, k=k, c0=c0, cw=cw: e.dma_start(out=st[i][:, 0:cw], in_=src[k * 128:(k + 1) * 128, c0:c0 + cw]), writes=[bst[i]])
            c.op("pool", lambda e, i=i, k=k, c0=c0, cw=cw: e.tensor_copy(out=dst[:, k, c0:c0 + cw], in_=st[i][:, 0:cw]), reads=[bst[i]], writes=[bdst])


def emit_xT(c, K, P, x_dram, r0, ntiles, xT, bxT, xin, bxin, cnt):
    idf = K["ident_f"]
    for t in range(ntiles):
        i = cnt[0] % 2; cnt[0] += 1
        c.dma("sp", lambda e, i=i, t=t: e.dma_start(out=xin[i][:], in_=x_dram[r0 + t * 128:r0 + (t + 1) * 128, :]), writes=[bxin[i]])
        for kh in range(2):
            for k in range(4):
                kk = kh * 4 + k
                c.op("pe", lambda e, i=i, k=k, kk=kk: e.transpose(P.t[0][:, k * 128:(k + 1) * 128], xin[i][:, kk * 128:(kk + 1) * 128], idf[:]),
                     reads=[bxin[i], K["b"]], writes=[P.b[0]])
            c.op("dve", lambda e, t=t, kh=kh: e.tensor_copy(out=xT[:, kh * 4:(kh + 1) * 4, t * 128:(t + 1) * 128],
                                                         in_=P.t[0][:].rearrange("p (k t) -> p k t", k=4)), reads=[P.b[0]], writes=[bxT])


def proj_fm(c, P, W, bW, xT, bxT, col0, dst, bdst, scale, ntok, pc):
    for n0 in range(0, ntok, 512):
        j = 1 + (pc[0] % 2); pc[0] += 1
        for k in range(8):
            c.op("pe", lambda e, j=j, k=k, n0=n0: e.matmul(P.t[j][:], W[:, k, col0:col0 + 128], xT[:, k, n0:n0 + 512], start=(k == 0), stop=(k == 7)),
                 reads=[bW, bxT], writes=[P.b[j]])
        c.op("dve", lambda e, j=j, n0=n0: e.tensor_scalar(out=dst[:, n0:n0 + 512], in0=P.t[j][:], scalar1=float(scale), scalar2=None, op0=ALU.mult),
             reads=[P.b[j]], writes=[bdst])


def proj_tm(c, P, W, bW, xT, bxT, col0, ncol, dstfn, bdst, scale, ntile, pc):
    g = max(1, 512 // ncol)
    g = min(g, 4)
    for t0 in range(0, ntile, g):
        j = 1 + (pc[0] % 2); pc[0] += 1
        gg = min(g, ntile - t0)
        for ti in range(gg):
            t = t0 + ti
            for k in range(8):
                c.op("pe", lambda e, j=j, k=k, t=t, ti=ti: e.matmul(P.t[j][:, ti * ncol:(ti + 1) * ncol], xT[:, k, t * 128:(t + 1) * 128], W[:, k, col0:col0 + ncol],
                                                                     start=(k == 0), stop=(k == 7)), reads=[bW, bxT], writes=[P.b[j]])
        c.op("dve", lambda e, j=j, t0=t0, gg=gg: e.tensor_scalar(out=dstfn(t0, gg), in0=P.t[j][:, 0:gg * ncol].rearrange("p (g n) -> p g n", g=gg),
                                                                scalar1=float(scale), scalar2=None, op0=ALU.mult), reads=[P.b[j]], writes=[bdst])


def emit_mix_a(c, K, P, x_dram, w_in, w_out, gate_bias, norm_g, lng, lnb, w_r, na_tab, x1_dram, aff_dram, nseq, stage=99, dbg=None):
    Wout = c.sb("a_wout", [128, 8, D], BF16); bWout = Buf("wout")
    stg = [c.sb("a_stg%d" % i, [128, 8, 128], F32) for i in range(2)]; bstg = [Buf("stg0"), Buf("stg1")]
    st = [t[:].rearrange("p k n -> p (k n)") for t in stg]; bst = bstg
    scnt = [0]
    load_cast_weight(c, Wout, bWout, w_out, D, st, bst, scnt)
    WS = {n: c.sb("a_ws_" + n, [128, 8, 128], BF16) for n in ("q", "k", "v", "o")}
    bWS = {n: Buf("ws_" + n) for n in WS}
    WG = c.sb("a_wg16", [128, 8, 16], BF16); bWG = Buf("wg16")

    def load_w(name, col0, ncol=128):
        dst, bd = (WS[name], bWS[name]) if name != "g" else (WG, bWG)
        i = scnt[0] % 2; scnt[0] += 1
        c.dma("sp", lambda e, i=i: e.dma_start(out=stg[i][:, :, 0:ncol], in_=w_in[:, col0:col0 + ncol].rearrange("(k p) n -> p k n", p=128)), writes=[bstg[i]])
        c.op("pool", lambda e, i=i: e.tensor_copy(out=dst[:, :, 0:ncol], in_=stg[i][:, :, 0:ncol]), reads=[bstg[i]], writes=[bd])
    Wr = c.sb("a_wr", [128, 8, 16], F32); bWr = Buf("wr")
    c.dma("sp", lambda e: e.dma_start(out=Wr[:], in_=w_r.rearrange("(k p) e -> p k e", p=128)), writes=[bWr])
    gb = c.sb("a_gb", [128, 16], F32)
    ng = c.sb("a_ng", [128, 512], F32)
    gam = c.sb("a_gam", [128, D], F32)
    bet = c.sb("a_bet", [128, D], F32)
    bgb = Buf("gb")
    c.dma("sp", lambda e: e.dma_start(out=gb[:], in_=gate_bias.partition_broadcast(128)), writes=[bgb])
    c.dma("sp", lambda e: e.dma_start(out=ng[:], in_=norm_g.partition_broadcast(128)), writes=[bgb])
    c.dma("sp", lambda e: e.dma_start(out=gam[:], in_=lng.partition_broadcast(128)), writes=[bgb])
    c.dma("sp", lambda e: e.dma_start(out=bet[:], in_=lnb.partition_broadcast(128)), writes=[bgb])
    ones_f = K["ones_f"]
    tri = c.sb("a_tri", [128, 128], F32)
    trir = c.sb("a_trir", [128, 128], F32)
    ones_b = c.sb("a_onesb", [128, 128], BF16)
    c.op("pool", lambda e: e.affine_select(out=tri[:], in_=ones_f[:], pattern=[[1, 128]], compare_op=ALU.is_ge, fill=0.0, base=0, channel_multiplier=-1),
         reads=[K["b"]], writes=[K["b"]])
    c.op("pool", lambda e: e.affine_select(out=trir[:], in_=ones_f[:], pattern=[[-1, 128]], compare_op=ALU.is_ge, fill=0.0, base=0, channel_multiplier=1),
         reads=[K["b"]], writes=[K["b"]])
    c.op("pool", lambda e: e.tensor_copy(out=ones_b[:], in_=ones_f[:]), reads=[K["b"]], writes=[K["b"]])
    lntmp = {"st": c.sb("ln_st", [128, 2, 6], F32), "mv": c.sb("ln_mv", [128, 2], F32), "rs": c.sb("ln_rs", [128, 1], F32), "b": Buf("lnt")}

    xT = c.sb("a_xT", [128, 8, SEQ], BF16); bxT = Buf("xT")
    yT = c.sb("a_yT", [128, 8, SEQ], BF16); byT = [Buf("yT%d" % i) for i in range(8)]
    xin = [c.sb("a_xin%d" % i, [128, D], F32) for i in range(2)]; bxin = [Buf("xin0"), Buf("xin1")]
    QT = c.sb("a_QT", [128, SEQ], BF16); bQT = Buf("QT")
    KT = c.sb("a_KT", [128, SEQ], BF16); bKT = Buf("KT")
    Vb = c.sb("a_V", [128, NTILE * 132], BF16); bV = Buf("V")
    Ktm = c.sb("a_Ktm", [128, NTILE, 128], BF16); bKtm = Buf("Ktm")
    sgob = c.sb("a_sgob", [128, NTILE, 128], F32); bsgob = Buf("sgob")
    hacc = c.sb("a_hacc", [128, NTILE, 128], F32); bhacc = Buf("hacc")
    EB = c.sb("a_EB", [128, 3, 16, 64], F32); bEB = Buf("EB")
    pexp = [c.sb("a_pexp%d" % i, [128, 5 * 64], F32) for i in range(3)]; bpexp = [Buf("pexp%d" % i) for i in range(3)]
    PT = [c.sb("a_PT%d" % i, [128, 5, 64], BF16) for i in range(3)]; bPT = [Buf("PT%d" % i) for i in range(3)]
    rec = c.sb("a_rec", [128, 512], F32); brec = Buf("rec")
    G = c.sb("a_G", [128, NTILE, 16], F32); bG = Buf("G")
    nlf = c.sb("a_nlf", [128, NTILE, 8], F32)
    uu = c.sb("a_u", [128, NTILE, 8], F32)
    vv = c.sb("a_v", [128, NTILE, 8], F32)
    eL = c.sb("a_eL", [128, NTILE, 8], F32)
    bgate = Buf("gate")
    CN = c.sb("a_CN", [128, 129], F32); bCN = Buf("CN")
    CNb = c.sb("a_CNb", [128, 129], BF16); bCNb = Buf("CNb")
    ctmp = c.sb("a_ctmp", [128, 129], F32); bctmp = Buf("ctmp")
    St = [c.sb("a_St%d" % i, [128, 128], BF16) for i in range(2)]; bSt = [Buf("St0"), Buf("St1")]
    Kt = [c.sb("a_Kt%d" % i, [128, 128], BF16) for i in range(2)]; bKt = [Buf("Kt0"), Buf("Kt1")]
    sm = c.sb("a_sm", [128, 8], F32); bsm = Buf("sm")
    lnw = c.sb("a_lnw", [128, NTILE, 128], F32); blnw = Buf("lnw")
    lns = c.sb("a_lns", [128, NTILE, 4], F32); blns = Buf("lns")
    x1T = c.sb("a_x1T", [128, 8, 128], F32); bx1T = Buf("x1T")
    rt = c.sb("a_rt", [128, 16], F32); brt = Buf("rt")
    rtm = c.sb("a_rtm", [128, 4], F32)
    affo = [c.sb("a_aff%d" % i, [128, 16], F32) for i in range(2)]; baffo = [Buf("aff0"), Buf("aff1")]
    xc = [0]; pc = [0]; sc = [0]; hc = [0]; zc = [0]

    for s in range(nseq):
        r0 = s * SEQ
        emit_xT(c, K, P, x_dram, r0, NTILE, xT, bxT, xin, bxin, xc)
        for hp in range(4):
            load_w("q", hp * 128); load_w("k", 512 + hp * 128); load_w("v", 1024 + hp * 128)
            proj_fm(c, P, WS["q"], bWS["q"], xT, bxT, 0, QT, bQT, 0.125, SEQ, pc)
            proj_fm(c, P, WS["k"], bWS["k"], xT, bxT, 0, KT, bKT, 1.0, SEQ, pc)
            Vv = Vb[:, 0:NTILE * 128].rearrange("p (t n) -> p t n", t=NTILE)
            proj_tm(c, P, WS["v"], bWS["v"], xT, bxT, 0, 128, lambda t0, gg: Vv[:, t0:t0 + gg, :], bV, 1.0, NTILE, pc)
            for hh in range(2):
                pb = hh * 64
                c.dma("sp", lambda e, hp=hp, hh=hh: e.dma_start(out=EB[:].rearrange("p v i q -> p v (i q)"),
                                                         in_=na_tab[2 * hp + hh].rearrange("v p i q -> p v (i q)")), writes=[bEB])
                c.op("act", lambda e: e.activation(out=EB[:].rearrange("p v i q -> p (v i q)"), in_=EB[:].rearrange("p v i q -> p (v i q)"), func=AF.Exp),
                     reads=[bEB], writes=[bEB])
                SB = (3, 4, 7)

                def na_front(r, hh=hh, pb=pb):
                    rs = min(max(r - 4, 0), 24)
                    a0, a1 = rs // 2, (rs + 7) // 2
                    nt = a1 - a0 + 1
                    si = r % 3
                    psS = P.t[SB[si]]; bS = P.b[SB[si]]
                    for j in range(nt):
                        a = a0 + j
                        c.op("pe", lambda e, j=j, a=a, psS=psS, pb=pb, r=r: e.matmul(psS[:, j * 64:(j + 1) * 64], KT[pb:pb + 64, a * 128:(a + 1) * 128],
                                                                                    QT[pb:pb + 64, r * 64:(r + 1) * 64], start=True, stop=True),
                             reads=[bKT, bQT], writes=[bS])
                    c.op("act", lambda e, si=si, psS=psS, nt=nt: e.activation(out=pexp[si][:, 0:nt * 64], in_=psS[:, 0:nt * 64], func=AF.Exp),
                         reads=[bS], writes=[bpexp[si]])
                    for j in range(nt):
                        a = a0 + j
                        var = 1 if 2 * a < rs else (2 if 2 * a + 1 >= rs + 8 else 0)
                        ii = 2 * a - r + 7 + 1
                        c.op("dve", lambda e, si=si, j=j, var=var, ii=ii: e.tensor_tensor(out=PT[si][:, j, :], in0=pexp[si][:, j * 64:(j + 1) * 64],
                                                                                   in1=EB[:, var, ii, :], op=ALU.mult),
                             reads=[bpexp[si], bEB], writes=[bPT[si]])

                def na_back(r, hh=hh, pb=pb, hp=hp):
                    rs = min(max(r - 4, 0), 24)
                    a0, a1 = rs // 2, (rs + 7) // 2
                    nt = a1 - a0 + 1
                    si = r % 3
                    rr = r % 8
                    for j in range(nt):
                        a = a0 + j
                        c.op("pe", lambda e, si=si, j=j, a=a, rr=rr, nt=nt: e.matmul(P.t[5][:, rr * 64:(rr + 1) * 64], Vv[:, a, :], PT[si][:, j, :],
                                                                                  start=(j == 0), stop=(j == nt - 1)), reads=[bV, bPT[si]], writes=[P.b[5]])
                    for j in range(nt):
                        c.op("pe", lambda e, si=si, j=j, rr=rr, nt=nt: e.matmul(P.t[6][:, rr * 64:(rr + 1) * 64], ones_b[:], PT[si][:, j, :],
                                                                             start=(j == 0), stop=(j == nt - 1)), reads=[K["b"], bPT[si]], writes=[P.b[6]])
                    if rr == 7:
                        q0 = (r - 7) * 64
                        c.op("dve", lambda e: e.reciprocal(out=rec[:], in_=P.t[6][:]), reads=[P.b[6]], writes=[brec])
                        c.op("dve", lambda e, pb=pb, hp=hp, q0=q0: e.tensor_tensor(out=yT[pb:pb + 64, hp, q0:q0 + 512], in0=P.t[5][pb:pb + 64, :],
                                                                                in1=rec[pb:pb + 64, :], op=ALU.mult),
                             reads=[P.b[5], brec], writes=[byT[hp]])
                LOOK = 2
                for r in range(32 + LOOK):
                    if r < 32:
                        na_front(r)
                    if r >= LOOK:
                        na_back(r - LOOK)
        if stage == 1:
            continue
        load_w("g", 3584, 16)
        Gps = P.t[1][:, 0:NTILE * 16].rearrange("p (t n) -> p t n", t=NTILE)
        for t in range(NTILE):
            for k in range(8):
                c.op("pe", lambda e, t=t, k=k: e.matmul(Gps[:, t, :], xT[:, k, t * 128:(t + 1) * 128], WG[:, k, :], start=(k == 0), stop=(k == 7)),
                     reads=[bxT, bWG], writes=[P.b[1]])
        c.op("dve", lambda e: e.tensor_tensor(out=G[:], in0=Gps, in1=gb[:].unsqueeze(1).to_broadcast([128, NTILE, 16]), op=ALU.add),
             reads=[P.b[1], bgb], writes=[bG])
        c.op("act", lambda e: e.activation(out=nlf[:], in_=G[:, :, 8:16], func=AF.Exp, scale=-1.0), reads=[bG], writes=[bgate])
        c.op("act", lambda e: e.activation(out=nlf[:], in_=nlf[:], func=AF.Ln, bias=1.0, scale=1.0), reads=[bgate], writes=[bgate])
        cum = P.t[2][:, 0:NTILE * 8].rearrange("p (t n) -> p t n", t=NTILE)
        tot = P.t[2][:, 256:256 + NTILE * 8].rearrange("p (t n) -> p t n", t=NTILE)
        for t in range(NTILE):
            c.op("pe", lambda e, t=t: e.matmul(cum[:, t, 0:4], tri[:], nlf[:, t, 0:4], start=True, stop=True), reads=[bgate, K["b"]], writes=[P.b[2]])
            c.op("pe", lambda e, t=t: e.matmul(cum[:, t, 4:8], trir[:], nlf[:, t, 4:8], start=True, stop=True), reads=[bgate, K["b"]], writes=[P.b[2]])
            c.op("pe", lambda e, t=t: e.matmul(tot[:, t, :], ones_f[:], nlf[:, t, :], start=True, stop=True), reads=[bgate, K["b"]], writes=[P.b[2]])
        c.op("act", lambda e: e.activation(out=uu[:], in_=cum, func=AF.Exp, scale=-1.0), reads=[P.b[2]], writes=[bgate])
        c.op("act", lambda e: e.activation(out=eL[:], in_=tot, func=AF.Exp, scale=-1.0), reads=[P.b[2]], writes=[bgate])
        c.op("dve", lambda e: e.tensor_tensor(out=vv[:], in0=cum, in1=G[:, :, 0:8], op=ALU.add), reads=[P.b[2], bG], writes=[bgate])
        c.op("act", lambda e: e.activation(out=vv[:], in_=vv[:], func=AF.Exp), reads=[bgate], writes=[bgate])
        for h in range(4):
            load_w("q", 1536 + h * 128); load_w("k", 2048 + h * 128); load_w("v", 2560 + h * 128); load_w("o", 3072 + h * 128)
            proj_fm(c, P, WS["q"], bWS["q"], xT, bxT, 0, QT, bQT, 1.0, SEQ, pc)
            proj_fm(c, P, WS["k"], bWS["k"], xT, bxT, 0, KT, bKT, 128 ** -0.5, SEQ, pc)
            proj_tm(c, P, WS["k"], bWS["k"], xT, bxT, 0, 128, lambda t0, gg: Ktm[:, t0:t0 + gg, :], bKtm, 128 ** -0.5, NTILE, pc)
            Va = Vb[:, 0:NTILE * 129].rearrange("p (t n) -> p t n", t=NTILE)
            proj_tm(c, P, WS["v"], bWS["v"], xT, bxT, 0, 128, lambda t0, gg: Va[:, t0:t0 + gg, 0:128], bV, 1.0, NTILE, pc)
            c.op("pool", lambda e: e.memset(Va[:, :, 128:129], 1.0), reads=[], writes=[bV])
            proj_tm(c, P, WS["o"], bWS["o"], xT, bxT, 0, 128, lambda t0, gg: sgob[:, t0:t0 + gg, :], bsgob, 1.0, NTILE, pc)
            c.op("act", lambda e: e.activation(out=sgob[:].rearrange("p t n -> p (t n)"), in_=sgob[:].rearrange("p t n -> p (t n)"), func=AF.Sigmoid),
                 reads=[bsgob], writes=[bsgob])
            for dr in range(2):
                gi = dr * 4 + h
                mask = tri if dr == 0 else trir
                c.op("pool", lambda e: e.memset(CN[:], 0.0), writes=[bCN])
                c.op("pool", lambda e: e.memset(CNb[:], 0.0), writes=[bCNb])
                order = range(NTILE) if dr == 0 else range(NTILE - 1, -1, -1)
                for ci in order:
                    si = sc[0] % 2; sc[0] += 1
                    hi_ = hc[0] % 2; hc[0] += 1
                    psS = P.t[3 + si]; bS = P.b[3 + si]
                    psH = P.t[5] if hi_ == 0 else P.t[7]; bH = P.b[5] if hi_ == 0 else P.b[7]
                    cs = slice(ci * 128, (ci + 1) * 128)
                    c.op("pe", lambda e, psS=psS, cs=cs: e.matmul(psS[:, 0:128], KT[:, cs], QT[:, cs], start=True, stop=True), reads=[bKT, bQT], writes=[bS])
                    c.op("dve", lambda e, psS=psS, si=si, ci=ci, gi=gi, mask=mask: e.scalar_tensor_tensor(
                        out=St[si][:], in0=psS[:, 0:128], scalar=vv[:, ci, gi:gi + 1], in1=mask[:], op0=ALU.mult, op1=ALU.mult),
                        reads=[bS, bgate, K["b"]], writes=[bSt[si]])
                    c.op("pe", lambda e, psH=psH, si=si, ci=ci: e.matmul(psH[:, 0:129], St[si][:], Va[:, ci, :], start=True, stop=False),
                         reads=[bSt[si], bV], writes=[bH])
                    c.op("pe", lambda e, psH=psH, cs=cs: e.matmul(psH[:, 0:129], QT[:, cs], CNb[:], start=False, stop=True),
                         reads=[bQT, bCNb], writes=[bH])
                    c.op("dve", lambda e, psH=psH, ci=ci, gi=gi: e.tensor_tensor(out=sm[:, 0:1], in0=psH[:, 128:129], in1=uu[:, ci, gi:gi + 1], op=ALU.mult),
                         reads=[bH, bgate], writes=[bsm])
                    c.op("dve", lambda e: e.tensor_scalar(out=sm[:, 4:5], in0=sm[:, 0:1], scalar1=-1.0, scalar2=None, op0=ALU.mult), reads=[bsm], writes=[bsm])
                    c.op("dve", lambda e: e.tensor_tensor(out=sm[:, 5:6], in0=sm[:, 0:1], in1=sm[:, 4:5], op=ALU.max), reads=[bsm], writes=[bsm])
                    c.op("dve", lambda e: e.tensor_scalar(out=sm[:, 1:2], in0=sm[:, 5:6], scalar1=1.0, scalar2=None, op0=ALU.max), reads=[bsm], writes=[bsm])
                    c.op("dve", lambda e: e.reciprocal(out=sm[:, 2:3], in_=sm[:, 1:2]), reads=[bsm], writes=[bsm])
                    c.op("dve", lambda e, ci=ci, gi=gi: e.tensor_tensor(out=sm[:, 3:4], in0=sm[:, 2:3], in1=uu[:, ci, gi:gi + 1], op=ALU.mult),
                         reads=[bsm, bgate], writes=[bsm])
                    if dr == 0:
                        c.op("dve", lambda e, psH=psH, ci=ci: e.tensor_scalar(out=hacc[:, ci, :], in0=psH[:, 0:128], scalar1=sm[:, 3:4], scalar2=None, op0=ALU.mult),
                             reads=[bH, bsm], writes=[bhacc])
                    else:
                        c.op("dve", lambda e, psH=psH, ci=ci: e.scalar_tensor_tensor(out=hacc[:, ci, :], in0=psH[:, 0:128], scalar=sm[:, 3:4], in1=hacc[:, ci, :],
                                                                                   op0=ALU.mult, op1=ALU.add), reads=[bH, bsm, bhacc], writes=[bhacc])
                    c.op("pool", lambda e, si=si, ci=ci, gi=gi: e.tensor_scalar(out=Kt[si][:], in0=Ktm[:, ci, :], scalar1=vv[:, ci, gi:gi + 1], scalar2=None, op0=ALU.mult),
                         reads=[bKtm, bgate], writes=[bKt[si]])
                    c.op("pe", lambda e, si=si, ci=ci: e.matmul(P.t[6][:, 0:129], Kt[si][:], Va[:, ci, :], start=True, stop=True),
                         reads=[bKt[si], bV], writes=[P.b[6]])
                    c.op("dve", lambda e: e.tensor_tensor(out=ctmp[:], in0=P.t[6][:, 0:129], in1=CN[:], op=ALU.add), reads=[P.b[6], bCN], writes=[bctmp])
                    c.op("dve", lambda e, ci=ci, gi=gi: e.tensor_scalar(out=CN[:], in0=ctmp[:], scalar1=eL[:, ci, gi:gi + 1], scalar2=None, op0=ALU.mult),
                         reads=[bctmp, bgate], writes=[bCN])
                    c.op("pool", lambda e, ci=ci, gi=gi: e.tensor_scalar(out=CNb[:], in0=ctmp[:], scalar1=eL[:, ci, gi:gi + 1], scalar2=None, op0=ALU.mult),
                         reads=[bctmp, bgate], writes=[bCNb])
            c.op("dve", lambda e: e.tensor_reduce(out=lns[:, :, 0], in_=hacc[:], axis=AX.X, op=ALU.add), reads=[bhacc], writes=[blns])
            c.op("dve", lambda e: e.tensor_scalar(out=lns[:, :, 0], in0=lns[:, :, 0], scalar1=1.0 / 128, scalar2=None, op0=ALU.mult), reads=[blns], writes=[blns])
            c.op("dve", lambda e: e.tensor_tensor(out=lnw[:], in0=hacc[:], in1=lns[:, :, 0:1].to_broadcast([128, NTILE, 128]), op=ALU.subtract),
                 reads=[bhacc, blns], writes=[blnw])
            c.op("pool", lambda e: e.tensor_tensor(out=hacc[:], in0=lnw[:], in1=lnw[:], op=ALU.mult), reads=[blnw, bhacc], writes=[bhacc])
            c.op("dve", lambda e: e.tensor_reduce(out=lns[:, :, 1], in_=hacc[:], axis=AX.X, op=ALU.add), reads=[bhacc], writes=[blns])
            c.op("dve", lambda e: e.tensor_scalar(out=lns[:, :, 1], in0=lns[:, :, 1], scalar1=1.0 / 128, scalar2=EPS, op0=ALU.mult, op1=ALU.add), reads=[blns], writes=[blns])
            c.op("act", lambda e: e.activation(out=lns[:, :, 2], in_=lns[:, :, 1], func=AF.Sqrt), reads=[blns], writes=[blns])
            c.op("dve", lambda e: e.reciprocal(out=lns[:, :, 3], in_=lns[:, :, 2]), reads=[blns], writes=[blns])
            c.op("dve", lambda e: e.tensor_tensor(out=lnw[:], in0=lnw[:], in1=lns[:, :, 3:4].to_broadcast([128, NTILE, 128]), op=ALU.mult), reads=[blnw, blns], writes=[blnw])
            c.op("pool", lambda e, h=h: e.tensor_tensor(out=lnw[:], in0=lnw[:], in1=ng[:, h * 128:(h + 1) * 128].unsqueeze(1).to_broadcast([128, NTILE, 128]), op=ALU.mult),
                 reads=[blnw, bgb], writes=[blnw])
            c.op("dve", lambda e: e.tensor_tensor(out=lnw[:], in0=lnw[:], in1=sgob[:], op=ALU.mult), reads=[blnw, bsgob], writes=[blnw])
            for tq in range(4):
                for k in range(4):
                    t = tq * 4 + k
                    c.op("pe", lambda e, t=t, k=k: e.transpose(P.t[0][:, k * 128:(k + 1) * 128], lnw[:, t, :], K["ident_f"][:]), reads=[blnw, K["b"]], writes=[P.b[0]])
                c.op("dve", lambda e, tq=tq, h=h: e.tensor_copy(out=yT[:, 4 + h, tq * 512:(tq + 1) * 512], in_=P.t[0][:]), reads=[P.b[0]], writes=[byT[4 + h]])
        if stage == 2:
            continue
        for t in range(NTILE):
            i = xc[0] % 2; xc[0] += 1
            zi = i
            rr0 = r0 + t * 128
            c.dma("sp", lambda e, i=i, rr0=rr0: e.dma_start(out=xin[i][:], in_=x_dram[rr0:rr0 + 128, :]), writes=[bxin[i]])
            for hf in range(2):
                for k in range(8):
                    c.op("pe", lambda e, hf=hf, k=k, t=t: e.matmul(P.t[1 + hf][:], yT[:, k, t * 128:(t + 1) * 128], Wout[:, k, hf * 512:(hf + 1) * 512],
                                                                start=(k == 0), stop=(k == 7)), reads=[byT[k], bWout], writes=[P.b[1 + hf]])
                c.op("dve", lambda e, hf=hf, i=i, zi=zi: e.scalar_tensor_tensor(out=xin[zi][:, hf * 512:(hf + 1) * 512], in0=xin[i][:, hf * 512:(hf + 1) * 512], scalar=float(ALPHA),
                                                                       in1=P.t[1 + hf][:], op0=ALU.mult, op1=ALU.add), reads=[bxin[i], P.b[1 + hf]], writes=[bxin[zi]])
            emit_ln(c, xin[zi], bxin[zi], xin[zi], bxin[zi], gam, bet, bgb, lntmp)
            c.dma("sp", lambda e, zi=zi, rr0=rr0: e.dma_start(out=x1_dram[rr0:rr0 + 128, :], in_=xin[zi][:]), reads=[bxin[zi]])
            emit_router(c, K, P, xin[zi], bxin[zi], Wr, bWr, x1T, bx1T, rt, brt, rtm, affo[zi], baffo[zi], aff_dram, rr0)


def emit_router(c, K, P, z, bz, Wr, bWr, x1T, bx1T, rt, brt, rtm, affo, baffo, aff_dram, rr0):
    idf = K["ident_f"]
    for kh in range(2):
        for k in range(4):
            kk = kh * 4 + k
            c.op("pe", lambda e, k=k, kk=kk: e.transpose(P.t[0][:, k * 128:(k + 1) * 128], z[:, kk * 128:(kk + 1) * 128], idf[:]), reads=[bz, K["b"]], writes=[P.b[0]])
        c.op("dve", lambda e, kh=kh: e.tensor_copy(out=x1T[:, kh * 4:(kh + 1) * 4, :], in_=P.t[0][:].rearrange("p (k t) -> p k t", k=4)), reads=[P.b[0]], writes=[bx1T])
    for k in range(8):
        c.op("pe", lambda e, k=k: e.matmul(P.t[3][:, 0:16], x1T[:, k, :], Wr[:, k, :], start=(k == 0), stop=(k == 7)), reads=[bx1T, bWr], writes=[P.b[3]])
    c.op("dve", lambda e: e.tensor_reduce(out=rtm[:, 0:1], in_=P.t[3][:, 0:16], axis=AX.X, op=ALU.max), reads=[P.b[3]], writes=[brt])
    c.op("dve", lambda e: e.tensor_scalar(out=rtm[:, 1:2], in0=rtm[:, 0:1], scalar1=-1.0, scalar2=None, op0=ALU.mult), reads=[brt], writes=[brt])
    c.op("act", lambda e: e.activation(out=rt[:], in_=P.t[3][:, 0:16], func=AF.Exp, bias=rtm[:, 1:2], scale=1.0, accum_out=rtm[:, 2:3]), reads=[P.b[3], brt], writes=[brt])
    c.op("dve", lambda e: e.reciprocal(out=rtm[:, 3:4], in_=rtm[:, 2:3]), reads=[brt], writes=[brt])
    c.op("dve", lambda e: e.tensor_scalar(out=affo[:], in0=rt[:], scalar1=rtm[:, 3:4], scalar2=None, op0=ALU.mult), reads=[brt], writes=[baffo])
    c.dma("sp", lambda e: e.dma_start(out=aff_dram[rr0:rr0 + 128, :], in_=affo[:]), reads=[baffo])


def rope_tables_host():
    d = 64
    inv = (10000.0 ** (-np.arange(0, d, 2, dtype=np.float32) / d)).astype(np.float32)
    ang = np.arange(SEQ, dtype=np.float32)[:, None] * inv[None, :]
    cos, sin = np.cos(ang).astype(np.float32), np.sin(ang).astype(np.float32)
    COS = np.concatenate([cos, cos], 1).T
    SINS = np.concatenate([-sin, sin], 1).T
    COS = np.ascontiguousarray(np.concatenate([COS, COS], 0), dtype=np.float32)
    SINS = np.ascontiguousarray(np.concatenate([SINS, SINS], 0), dtype=np.float32)
    p = np.arange(128)[:, None]
    x = np.arange(3968)[None, :]
    dl = p - x + 1920
    m = (np.abs(dl) <= 64).astype(np.float32) + ((dl % 4 == 0) & (np.abs(dl) <= 256)) + ((dl % 16 == 0) & (np.abs(dl) <= 1024))
    return COS, SINS, np.ascontiguousarray(m.astype(np.float32))


def emit_mix_b(c, K, P, x_dram, w_in, w_out, lng, lnb, w_r, cos_d, sins_d, tab_d, x1_dram, aff_dram, nseq):
    Wout = c.sb("b_wout", [128, 8, D], BF16); bWout = Buf("wout")
    stg = [c.sb("b_stg%d" % i, [128, 8, 128], F32) for i in range(2)]; bstg = [Buf("stg0"), Buf("stg1")]
    st = [t[:].rearrange("p k n -> p (k n)") for t in stg]
    scnt = [0]
    load_cast_weight(c, Wout, bWout, w_out, D, st, bstg, scnt)
    WS = {n: c.sb("b_ws_" + n, [128, 8, 128], BF16) for n in ("q", "qs", "k", "ks", "v")}
    bWS = {n: Buf("ws_" + n) for n in WS}

    def load_w(name, col0, swap=False):
        dst, bd = WS[name], bWS[name]
        i = scnt[0] % 2; scnt[0] += 1
        src = w_in[:, col0:col0 + 128]
        if not swap:
            c.dma("sp", lambda e, i=i: e.dma_start(out=stg[i][:], in_=src.rearrange("(k p) n -> p k n", p=128)), writes=[bstg[i]])
        else:
            s5 = src.rearrange("(k p) (h two d) -> p k h two d", p=128, h=2, two=2)
            d5 = stg[i][:].rearrange("p k (h two d) -> p k h two d", h=2, two=2)
            for a in range(2):
                for hd in range(2):
                    c.dma("sp", lambda e, i=i, a=a, hd=hd: e.dma_start(out=d5[:, :, hd, 1 - a, :], in_=s5[:, :, hd, a, :]), writes=[bstg[i]])
        c.op("pool", lambda e, i=i: e.tensor_copy(out=dst[:], in_=stg[i][:]), reads=[bstg[i]], writes=[bd])

    Wr = c.sb("b_wr", [128, 8, 16], F32); bWr = Buf("wr")
    c.dma("sp", lambda e: e.dma_start(out=Wr[:], in_=w_r.rearrange("(k p) e -> p k e", p=128)), writes=[bWr])
    gam = c.sb("b_gam", [128, D], F32)
    bet = c.sb("b_bet", [128, D], F32)
    bgb = Buf("gb")
    c.dma("sp", lambda e: e.dma_start(out=gam[:], in_=lng.partition_broadcast(128)), writes=[bgb])
    c.dma("sp", lambda e: e.dma_start(out=bet[:], in_=lnb.partition_broadcast(128)), writes=[bgb])
    COS = c.sb("b_cos", [128, SEQ], F32); SINS = c.sb("b_sins", [128, SEQ], F32)
    TABf = c.sb("b_tabf", [128, 3968], F32); TAB = c.sb("b_tab", [128, 3968], BF16)
    btab = Buf("tab")
    c.dma("sp", lambda e: e.dma_start(out=COS[:], in_=cos_d), writes=[btab])
    c.dma("sp", lambda e: e.dma_start(out=SINS[:], in_=sins_d), writes=[btab])
    c.dma("sp", lambda e: e.dma_start(out=TABf[:], in_=tab_d), writes=[btab])
    c.op("pool", lambda e: e.tensor_copy(out=TAB[:], in_=TABf[:]), reads=[btab], writes=[btab])
    ones_f = K["ones_f"]
    ones_b = c.sb("b_onesb", [128, 128], BF16)
    c.op("pool", lambda e: e.tensor_copy(out=ones_b[:], in_=ones_f[:]), reads=[K["b"]], writes=[K["b"]])
    lntmp = {"st": c.sb("ln_st", [128, 2, 6], F32), "mv": c.sb("ln_mv", [128, 2], F32), "rs": c.sb("ln_rs", [128, 1], F32), "b": Buf("lnt")}

    xT = c.sb("b_xT", [128, 8, SEQ], BF16); bxT = Buf("xT")
    yT = c.sb("b_yT", [128, 8, SEQ], BF16); byT = [Buf("yT%d" % i) for i in range(8)]
    xin = [c.sb("b_xin%d" % i, [128, D], F32) for i in range(2)]; bxin = [Buf("xin0"), Buf("xin1")]
    QT = c.sb("b_QT", [128, SEQ], BF16); bQT = Buf("QT")
    KT = c.sb("b_KT", [128, SEQ], BF16); bKT = Buf("KT")
    Vv = c.sb("b_V", [128, NTILE, 128], BF16); bV = Buf("V")
    t1 = c.sb("b_t1", [128, 512], F32); bt1 = Buf("t1")
    t2 = c.sb("b_t2", [128, 512], F32); bt2 = Buf("t2")
    pexp = [c.sb("b_pexp%d" % i, [128, 512], BF16) for i in range(3)]; bpexp = [Buf("pexp%d" % i) for i in range(3)]
    PT = [c.sb("b_PT%d" % i, [128, 512], BF16) for i in range(3)]; bPT = [Buf("PT%d" % i) for i in range(3)]
    rec = c.sb("b_rec", [128, 512], F32); brec = Buf("rec")
    x1T = c.sb("b_x1T", [128, 8, 128], F32); bx1T = Buf("x1T")
    rt = c.sb("b_rt", [128, 16], F32); brt = Buf("rt")
    rtm = c.sb("b_rtm", [128, 4], F32)
    affo = [c.sb("b_aff%d" % i, [128, 16], F32) for i in range(2)]; baffo = [Buf("aff0"), Buf("aff1")]
    xc = [0]; pc = [0]; sc = [0]

    def proj_rope(wa, wb, dst, bdst):
        for n0 in range(0, SEQ, 512):
            for k in range(8):
                c.op("pe", lambda e, k=k, n0=n0: e.matmul(P.t[1][:], WS[wa][:, k, :], xT[:, k, n0:n0 + 512], start=(k == 0), stop=(k == 7)),
                     reads=[bWS[wa], bxT], writes=[P.b[1]])
            for k in range(8):
                c.op("pe", lambda e, k=k, n0=n0: e.matmul(P.t[2][:], WS[wb][:, k, :], xT[:, k, n0:n0 + 512], start=(k == 0), stop=(k == 7)),
                     reads=[bWS[wb], bxT], writes=[P.b[2]])
            c.op("dve", lambda e, n0=n0: e.tensor_tensor(out=t1[:], in0=P.t[1][:], in1=COS[:, n0:n0 + 512], op=ALU.mult), reads=[P.b[1], btab], writes=[bt1])
            c.op("dve", lambda e, n0=n0: e.tensor_tensor(out=t2[:], in0=P.t[2][:], in1=SINS[:, n0:n0 + 512], op=ALU.mult), reads=[P.b[2], btab], writes=[bt2])
            c.op("pool", lambda e, n0=n0: e.tensor_tensor(out=dst[:, n0:n0 + 512], in0=t1[:], in1=t2[:], op=ALU.add), reads=[bt1, bt2], writes=[bdst])

    for s in range(nseq):
        r0 = s * SEQ
        emit_xT(c, K, P, x_dram, r0, NTILE, xT, bxT, xin, bxin, xc)
        for hp in range(8):
            load_w("q", hp * 128); load_w("qs", hp * 128, swap=True)
            load_w("k", 1024 + hp * 128); load_w("ks", 1024 + hp * 128, swap=True)
            load_w("v", 2048 + hp * 128)
            proj_rope("q", "qs", QT, bQT)
            proj_rope("k", "ks", KT, bKT)
            proj_tm(c, P, WS["v"], bWS["v"], xT, bxT, 0, 128, lambda t0, gg: Vv[:, t0:t0 + gg, :], bV, 1.0, NTILE, pc)
            blocks = []
            for hh in range(2):
                for qb in range(4):
                    kts = [kt for kt in range(NTILE) if not (128 * kt - 512 * qb - 511 > 1024 or 128 * kt + 127 - 512 * qb < -1024)]
                    for j, kt in enumerate(kts):
                        blocks.append((hh, qb, j, kt, len(kts)))
            SB = (3, 4, 7)

            def front(bi, blocks=blocks):
                hh, qb, j, kt, n = blocks[bi]
                pb = hh * 64
                si = bi % 3
                psS = P.t[SB[si]]; bS = P.b[SB[si]]
                c.op("pe", lambda e, psS=psS, kt=kt, qb=qb, pb=pb: e.matmul(psS[:], KT[pb:pb + 64, kt * 128:(kt + 1) * 128], QT[pb:pb + 64, qb * 512:(qb + 1) * 512],
                                                                         start=True, stop=True), reads=[bKT, bQT], writes=[bS])
                c.op("act", lambda e, psS=psS, si=si: e.activation(out=pexp[si][:], in_=psS[:], func=AF.Exp, scale=0.125), reads=[bS], writes=[bpexp[si]])
                x0 = 512 * qb - 128 * kt + 1920
                c.op("dve", lambda e, si=si, x0=x0: e.tensor_tensor(out=PT[si][:], in0=pexp[si][:], in1=TAB[:, x0:x0 + 512], op=ALU.mult),
                     reads=[bpexp[si], btab], writes=[bPT[si]])

            def back(bi, hp=hp, blocks=blocks):
                hh, qb, j, kt, n = blocks[bi]
                pb = hh * 64
                si = bi % 3
                c.op("pe", lambda e, si=si, kt=kt, j=j, n=n: e.matmul(P.t[5][:], Vv[:, kt, :], PT[si][:], start=(j == 0), stop=(j == n - 1)),
                     reads=[bV, bPT[si]], writes=[P.b[5]])
                c.op("pe", lambda e, si=si, j=j, n=n: e.matmul(P.t[6][:], ones_b[:], PT[si][:], start=(j == 0), stop=(j == n - 1)),
                     reads=[K["b"], bPT[si]], writes=[P.b[6]])
                if j == n - 1:
                    c.op("dve", lambda e: e.reciprocal(out=rec[:], in_=P.t[6][:]), reads=[P.b[6]], writes=[brec])
                    c.op("dve", lambda e, pb=pb, hp=hp, qb=qb: e.tensor_tensor(out=yT[pb:pb + 64, hp, qb * 512:(qb + 1) * 512], in0=P.t[5][pb:pb + 64, :],
                                                                            in1=rec[pb:pb + 64, :], op=ALU.mult), reads=[P.b[5], brec], writes=[byT[hp]])
            LOOK = 2
            for bi in range(len(blocks) + LOOK):
                if bi < len(blocks):
                    front(bi)
                if bi >= LOOK:
                    back(bi - LOOK)
        for t in range(NTILE):
            i = xc[0] % 2; xc[0] += 1
            rr0 = r0 + t * 128
            c.dma("sp", lambda e, i=i, rr0=rr0: e.dma_start(out=xin[i][:], in_=x_dram[rr0:rr0 + 128, :]), writes=[bxin[i]])
            for hf in range(2):
                for k in range(8):
                    c.op("pe", lambda e, hf=hf, k=k, t=t: e.matmul(P.t[1 + hf][:], yT[:, k, t * 128:(t + 1) * 128], Wout[:, k, hf * 512:(hf + 1) * 512],
                                                                start=(k == 0), stop=(k == 7)), reads=[byT[k], bWout], writes=[P.b[1 + hf]])
                c.op("dve", lambda e, hf=hf, i=i: e.scalar_tensor_tensor(out=xin[i][:, hf * 512:(hf + 1) * 512], in0=xin[i][:, hf * 512:(hf + 1) * 512], scalar=float(ALPHA),
                                                                 in1=P.t[1 + hf][:], op0=ALU.mult, op1=ALU.add), reads=[bxin[i], P.b[1 + hf]], writes=[bxin[i]])
            emit_ln(c, xin[i], bxin[i], xin[i], bxin[i], gam, bet, bgb, lntmp)
            c.dma("sp", lambda e, i=i, rr0=rr0: e.dma_start(out=x1_dram[rr0:rr0 + 128, :], in_=xin[i][:]), reads=[bxin[i]])
            emit_router(c, K, P, xin[i], bxin[i], Wr, bWr, x1T, bx1T, rt, brt, rtm, affo[i], baffo[i], aff_dram, rr0)


NCORES = 8
TILES_P, TILES_S = 32, 64
E_ = 16
DFF_ = 1408


def build_ffn_program():
    NT = TILES_P + TILES_S
    n_p, n_s = NCORES * TILES_P, NCORES * TILES_S
    cap_p, cap_s = 2 * n_p * 128 // E_, 2 * n_s * 128 // E_
    nc = bass.Bass("TRN2", target_bir_lowering=False)
    di = lambda n, s: nc.dram_tensor(n, s, F32, kind="ExternalInput").ap()
    x = di("x", [NT * 128, D]); aff = di("aff", [NT * 128, E_]); affT = di("affT", [128, E_, n_p + n_s])
    wg = di("wg", [E_, D, DFF_]); wu = di("wu", [E_, D, DFF_]); wd = di("wd", [E_, DFF_, D])
    lng = di("lng", [D]); lnb = di("lnb", [D])
    y = nc.dram_tensor("y", [NT * 128, D], F32, kind="ExternalOutput").ap()
    c = Ctx(nc)
    K = emit_consts(c)
    emit_ffn2(c, K, x, aff, affT, wg, wu, wd, lng, lnb, y, TILES_P, TILES_S, n_p, n_s, cap_p, cap_s, E=E_, DFF=DFF_, TB=8, CAP=256)
    c.finish()
    return nc


def build_a_program(nseq=6):
    nc = bass.Bass("TRN2", target_bir_lowering=False)
    di = lambda n, s: nc.dram_tensor(n, s, F32, kind="ExternalInput").ap()
    x = di("x", [nseq * 2048, D]); w_in = di("w_in", [D, 3600]); w_out = di("w_out", [D, D]); gbias = di("gbias", [16]); ng = di("ng", [512])
    lng = di("lng", [D]); lnb = di("lnb", [D]); wr = di("wr", [D, 16]); tab = di("tab", [8, 3, 128, 16, 64])
    x1 = nc.dram_tensor("x1", [nseq * 2048, D], F32, kind="ExternalOutput").ap()
    aff = nc.dram_tensor("aff", [nseq * 2048, 16], F32, kind="ExternalOutput").ap()
    c = Ctx(nc); K = emit_consts(c); P = PSB(c)
    emit_mix_a(c, K, P, x, w_in, w_out, gbias, ng, lng, lnb, wr, tab, x1, aff, nseq)
    c.finish()
    return nc


def build_b_program(nseq=6):
    nc = bass.Bass("TRN2", target_bir_lowering=False)
    di = lambda n, s: nc.dram_tensor(n, s, F32, kind="ExternalInput").ap()
    x = di("x", [nseq * 2048, D]); w_in = di("w_in", [D, 3072]); w_out = di("w_out", [D, D])
    lng = di("lng", [D]); lnb = di("lnb", [D]); wr = di("wr", [D, 16]); cos = di("cos", [128, 2048]); sins = di("sins", [128, 2048]); tab = di("tab", [128, 3968])
    x1 = nc.dram_tensor("x1", [nseq * 2048, D], F32, kind="ExternalOutput").ap()
    aff = nc.dram_tensor("aff", [nseq * 2048, 16], F32, kind="ExternalOutput").ap()
    c = Ctx(nc); K = emit_consts(c); P = PSB(c)
    emit_mix_b(c, K, P, x, w_in, w_out, lng, lnb, wr, cos, sins, tab, x1, aff, nseq)
    c.finish()
    return nc


def _aff_layout(affs):
    def toT(a):
        return a.reshape(-1, 128, E_).transpose(1, 2, 0)
    ap = np.concatenate([a[:TILES_P * 128] for a in affs], 0)
    as_ = np.concatenate([a[TILES_P * 128:] for a in affs], 0)
    return np.ascontiguousarray(np.concatenate([toT(ap), toT(as_)], axis=2), dtype=np.float32)


def kernel(x_prompt, x_sample, even_w_in, ml_gate_bias, na_rpb, ml_norm_g, even_w_out, da_w_in, da_w_out,
           ln_mix_g, ln_mix_b, ec_router, ec_w_gate, ec_w_up, ec_w_down, ln_ffn_g, ln_ffn_b):
    f32 = lambda a: np.ascontiguousarray(np.asarray(a), dtype=np.float32)
    x_prompt, x_sample = f32(x_prompt), f32(x_sample)
    cores = list(range(NCORES))
    xs = [np.concatenate([x_prompt[2 * c:2 * c + 2].reshape(-1, D), x_sample[4 * c:4 * c + 4].reshape(-1, D)], 0) for c in cores]
    tabA = na_table_host(f32(na_rpb)[0])
    feedA = {"w_in": f32(even_w_in)[0], "w_out": f32(even_w_out)[0], "gbias": f32(ml_gate_bias)[0], "ng": f32(ml_norm_g)[0],
             "lng": f32(ln_mix_g)[0], "lnb": f32(ln_mix_b)[0], "wr": f32(ec_router)[0], "tab": tabA}
    ncA = build_a_program()
    res = run_bass_kernel_spmd(ncA, [dict(feedA, x=xs[c]) for c in cores], core_ids=cores)
    x1 = [res.results[c]["x1"] for c in cores]; aff = [res.results[c]["aff"] for c in cores]
    del ncA
    ncF = build_ffn_program()
    def run_ffn(xl, affl, layer):
        affT = _aff_layout(affl)
        feed = {"affT": affT, "wg": f32(ec_w_gate)[layer], "wu": f32(ec_w_up)[layer], "wd": f32(ec_w_down)[layer],
                "lng": f32(ln_ffn_g)[layer], "lnb": f32(ln_ffn_b)[layer]}
        r = run_bass_kernel_spmd(ncF, [dict(feed, x=xl[c], aff=affl[c]) for c in cores], core_ids=cores)
        return [r.results[c]["y"] for c in cores]
    x2 = run_ffn(x1, aff, 0)
    COS, SINS, TAB = rope_tables_host()
    feedB = {"w_in": f32(da_w_in)[0], "w_out": f32(da_w_out)[0], "lng": f32(ln_mix_g)[1], "lnb": f32(ln_mix_b)[1],
             "wr": f32(ec_router)[1], "cos": COS, "sins": SINS, "tab": TAB}
    ncB = build_b_program()
    res = run_bass_kernel_spmd(ncB, [dict(feedB, x=x2[c]) for c in cores], core_ids=cores)
    x3 = [res.results[c]["x1"] for c in cores]; aff2 = [res.results[c]["aff"] for c in cores]
    del ncB
    y = run_ffn(x3, aff2, 1)
    y_prompt = np.stack([y[c][:TILES_P * 128].reshape(2, 2048, D) for c in cores], 0).reshape(16, 2048, D)
    y_sample = np.stack([y[c][TILES_P * 128:].reshape(4, 2048, D) for c in cores], 0).reshape(32, 2048, D)
    return (np.ascontiguousarray(y_prompt, dtype=np.float32), np.ascontiguousarray(y_sample, dtype=np.float32))
```

```python
from concourse.bass_utils import run_bass_kernel_spmd
import numpy as np
import concourse.bass as bass
import concourse.mybir as mybir

F32 = mybir.dt.float32
BF16 = mybir.dt.bfloat16
I32 = mybir.dt.int32
U32 = mybir.dt.uint32
ALU = mybir.AluOpType
AF = mybir.ActivationFunctionType
AX = mybir.AxisListType


class Buf:
    __slots__ = ("name", "writers", "readers")

    def __init__(self, name=""):
        self.name = name
        self.writers = {}
        self.readers = []


class EngS:
    def __init__(self, name, self_wait):
        self.name = name
        self.ops = []
        self.self_wait = self_wait
        self.known = {}
        self.n = 0


class Ctx:
    ENG = ("pe", "act", "dve", "pool", "sp")

    def __init__(self, nc, dma_pool=8):
        self.nc = nc
        self.E = {
            "pe": EngS("pe", False),
            "act": EngS("act", True),
            "dve": EngS("dve", True),
            "pool": EngS("pool", True),
            "sp": EngS("sp", False),
        }
        self.sems = {}
        self.stack = []
        cm = nc.semaphore("s_cc")
        self.sems["cc"] = cm.__enter__()
        self.stack.append(cm)
        self.cc_n = 0
        self.cc_scr = None
        for e in self.ENG:
            cm = nc.semaphore("s_" + e)
            self.sems[e] = cm.__enter__()
            self.stack.append(cm)
        self.dma_pool = {}
        self.dma_tot = {}
        self.dma_i = {}
        for q in ("sp", "pool", "act"):
            lst = []
            for i in range(dma_pool):
                cm = nc.semaphore("d_%s%d" % (q, i))
                lst.append(cm.__enter__())
                self.stack.append(cm)
            self.dma_pool[q] = lst
            self.dma_i[q] = 0
        self.ctxs = []

    def sb(self, name, shape, dt):
        cm = self.nc.sbuf_tensor(name, list(shape), dt)
        t = cm.__enter__()
        self.ctxs.append(cm)
        return t

    def ps(self, name, shape, dt):
        cm = self.nc.psum_tensor(name, list(shape), dt)
        t = cm.__enter__()
        self.ctxs.append(cm)
        return t

    def _need(self, es, ev, waits):
        key, val = ev
        if key == es.name and not es.self_wait:
            return
        if es.known.get(key, -1) >= val:
            return
        es.known[key] = val
        waits.append(ev)

    def _deps(self, es, reads, writes):
        waits = []
        for b in reads:
            for ev in b.writers.values():
                self._need(es, ev, waits)
        for b in writes:
            for ev in b.writers.values():
                self._need(es, ev, waits)
            for ev in b.readers:
                self._need(es, ev, waits)
        return waits

    def _mark(self, ev, reads, writes):
        for b in reads:
            b.readers.append(ev)
            if len(b.readers) > 64:
                b.readers = b.readers[-48:]
        for b in writes:
            b.writers = {ev[0]: ev}
            b.readers = []

    def op(self, eng, fn, reads=(), writes=()):
        es = self.E[eng]
        waits = self._deps(es, reads, writes)
        es.n += 1
        ev = (eng, es.n)
        es.ops.append((fn, waits, ("eng", None)))
        self._mark(ev, reads, writes)
        return ev

    def dma(self, q, fn, reads=(), writes=()):
        es = self.E[q]
        pool = self.dma_pool[q]
        i = self.dma_i[q]
        self.dma_i[q] = i + 1
        sem = pool[i % len(pool)]
        skey = ("dma", q, i % len(pool))
        prev = self.dma_tot.get(skey, 0)
        waits = self._deps(es, reads, writes)
        if prev > 0:
            self._need(es, (skey, prev), waits)
        tot = prev + 16
        self.dma_tot[skey] = tot
        es.n += 1
        es.ops.append((fn, waits, ("dma", sem)))
        ev = (skey, tot)
        self._mark(ev, reads, writes)
        return ev

    def cc(self, fn, reads=(), writes=()):
        es = self.E["pool"]
        if self.cc_scr is None:
            self.cc_scr = self.sb("cc_scr", [128, 8], F32)
        waits = self._deps(es, reads, writes)
        self.cc_n += 1
        es.n += 1
        es.ops.append((fn, waits, ("cc", self.sems["cc"])))
        es.n += 1
        scr = self.cc_scr
        es.ops.append((lambda e: e.memset(scr[:], 0.0), [("cc", self.cc_n)], ("eng", None)))
        ev = ("pool", es.n)
        es.known["pool"] = es.n
        self._mark(ev, reads, writes)
        return ev

    def _sem_of(self, key):
        if isinstance(key, tuple):
            return self.dma_pool[key[1]][key[2]]
        return self.sems[key]

    def finish(self):
        nc = self.nc
        es = self.E["sp"]
        waits = []
        for skey, tot in self.dma_tot.items():
            self._need(es, (skey, tot), waits)
        for e in ("pe", "act", "dve", "pool"):
            if self.E[e].n > 0:
                self._need(es, (e, self.E[e].n), waits)
        es.ops.append((None, waits, None))
        engmap = {"pe": "tensor", "act": "scalar", "dve": "vector", "pool": "gpsimd", "sp": "sync"}
        with nc.Block() as block:
            for e in self.ENG:
                st = self.E[e]

                def body(engine, st=st, e=e):
                    own = self.sems[e]
                    for fn, waits, inc in st.ops:
                        for key, val in waits:
                            engine.wait_ge(self._sem_of(key), val)
                        if fn is None:
                            continue
                        ins = fn(engine)
                        if inc[0] == "eng":
                            ins.then_inc(own, 1)
                        elif inc[0] == "cc":
                            ins.then_inc(inc[1])
                        else:
                            ins.then_inc(inc[1], 16)

                getattr(block, engmap[e])(body)
        for cm in reversed(self.ctxs):
            cm.__exit__(None, None, None)
        for cm in reversed(self.stack):
            cm.__exit__(None, None, None)


D = 1024
ALPHA = 4 ** 0.25
EPS = 1e-5


def emit_consts(c):
    nc = c.nc
    K = {}
    K["ident_f"] = c.sb("ident_f", [128, 128], F32)
    K["ident_b"] = c.sb("ident_b", [128, 128], BF16)
    K["ones_f"] = c.sb("ones_f", [128, 128], F32)
    K["b"] = Buf("consts")
    idf, idb, of = K["ident_f"], K["ident_b"], K["ones_f"]
    c.op("pool", lambda e: e.memset(of[:], 1.0), writes=[K["b"]])
    c.op("pool", lambda e: e.affine_select(out=idf[:], in_=of[:], pattern=[[-1, 128]], compare_op=ALU.is_equal,
                                           fill=0.0, base=0, channel_multiplier=1), reads=[K["b"]], writes=[K["b"]])
    c.op("pool", lambda e: e.tensor_copy(out=idb[:], in_=idf[:]), reads=[K["b"]], writes=[K["b"]])
    return K


def emit_ln(c, z, bz, out, bout, gam, bet, bgb, tmp):
    st, mv, rs, bt = tmp["st"], tmp["mv"], tmp["rs"], tmp["b"]
    c.op("dve", lambda e: e.bn_stats(out=st[:, 0, :], in_=z[:, 0:512]), reads=[bz], writes=[bt])
    c.op("dve", lambda e: e.bn_stats(out=st[:, 1, :], in_=z[:, 512:1024]), reads=[bz], writes=[bt])
    c.op("dve", lambda e: e.bn_aggr(out=mv[:], in_=st[:].rearrange("p a b -> p (a b)")), reads=[bt], writes=[bt])
    c.op("dve", lambda e: e.tensor_scalar(out=rs[:], in0=mv[:, 1:2], scalar1=EPS, scalar2=None, op0=ALU.add), reads=[bt], writes=[bt])
    c.op("act", lambda e: e.activation(out=rs[:], in_=rs[:], func=AF.Sqrt), reads=[bt], writes=[bt])
    c.op("dve", lambda e: e.reciprocal(out=rs[:], in_=rs[:]), reads=[bt], writes=[bt])
    c.op("dve", lambda e: e.tensor_scalar(out=out[:], in0=z[:], scalar1=mv[:, 0:1], scalar2=rs[:, 0:1],
                                          op0=ALU.subtract, op1=ALU.mult), reads=[bz, bt], writes=[bout])
    c.op("pool", lambda e: e.tensor_tensor(out=out[:], in0=out[:], in1=gam[:], op=ALU.mult), reads=[bout, bgb], writes=[bout])
    c.op("pool", lambda e: e.tensor_tensor(out=out[:], in0=out[:], in1=bet[:], op=ALU.add), reads=[bout, bgb], writes=[bout])


def emit_thresholds(c, K, affT_dram, n_p, n_s, cap_p, cap_s, E, iters=26, ret_A=False):
    A = c.sb("bis_A", [128, E, n_p + n_s], F32)
    bA = Buf("bisA")
    c.dma("sp", lambda e: e.dma_start(out=A[:], in_=affT_dram), writes=[bA])
    lo = c.sb("bis_lo", [128, 2 * E], F32)
    hi = c.sb("bis_hi", [128, 2 * E], F32)
    mid = c.sb("bis_mid", [128, 2 * E], F32)
    cnt = c.sb("bis_cnt", [128, 2 * E], F32)
    capv = c.sb("bis_cap", [128, 2 * E], F32)
    ge = c.sb("bis_ge", [128, 2 * E], U32)
    lt = c.sb("bis_lt", [128, 2 * E], U32)
    junk = c.sb("bis_junk", [128, max(n_p, n_s)], BF16)
    tot = c.ps("bis_tot", [128, 2 * E], F32)
    b = Buf("bis")
    bcnt = Buf("bcnt")
    btot = Buf("btot")
    bj = Buf("junk")
    c.op("dve", lambda e: e.memset(lo[:], 0.0), writes=[b])
    c.op("dve", lambda e: e.memset(hi[:], 1.0), writes=[b])
    c.op("dve", lambda e: e.memset(mid[:], 0.5), writes=[b])
    c.op("dve", lambda e: e.memset(capv[:, 0:E], float(cap_p)), writes=[b])
    c.op("dve", lambda e: e.memset(capv[:, E:2 * E], float(cap_s)), writes=[b])
    ones = K["ones_f"]
    for it in range(iters):
        for v in range(2 * E):
            g, ex = divmod(v, E)
            src = A[:, ex, 0:n_p] if g == 0 else A[:, ex, n_p:n_p + n_s]
            n = n_p if g == 0 else n_s
            c.op("dve", lambda e, src=src, v=v, n=n: e.tensor_scalar(
                out=junk[:, 0:n], in0=src, scalar1=mid[:, v:v + 1], scalar2=None,
                op0=ALU.is_ge, op1=ALU.add, accum_out=cnt[:, v:v + 1]), reads=[bA, b], writes=[bj, bcnt])
        c.op("pe", lambda e: e.matmul(tot[:], ones[:], cnt[:], start=True, stop=True), reads=[bcnt, K["b"]], writes=[btot])
        c.op("dve", lambda e: e.tensor_tensor(out=ge[:], in0=tot[:], in1=capv[:], op=ALU.is_ge), reads=[btot, b], writes=[b])
        c.op("dve", lambda e: e.tensor_tensor(out=lt[:], in0=tot[:], in1=capv[:], op=ALU.is_lt), reads=[btot, b], writes=[b])
        c.op("dve", lambda e: e.copy_predicated(out=lo[:], mask=ge[:], data=mid[:]), reads=[b], writes=[b])
        c.op("dve", lambda e: e.copy_predicated(out=hi[:], mask=lt[:], data=mid[:]), reads=[b], writes=[b])
        c.op("dve", lambda e: e.tensor_tensor(out=mid[:], in0=lo[:], in1=hi[:], op=ALU.add), reads=[b], writes=[b])
        c.op("dve", lambda e: e.tensor_scalar(out=mid[:], in0=mid[:], scalar1=0.5, scalar2=None, op0=ALU.mult), reads=[b], writes=[b])
    if ret_A:
        return lo, b, A, bA
    return lo, b


def emit_ffn(c, K, x_dram, aff_own_dram, affT_dram, wg, wu, wd, lng, lnb, y_dram,
             tiles_p, tiles_s, n_p, n_s, cap_p, cap_s, E=16, DFF=1408, TB=8, iters=26, dbg=None, stage=99):
    NT = tiles_p + tiles_s
    FC = DFF // 128
    theta, bth = emit_thresholds(c, K, affT_dram, n_p, n_s, cap_p, cap_s, E, iters)
    if stage == 1:
        c.dma("sp", lambda e: e.dma_start(out=dbg[:, 0:2 * E], in_=theta[:]), reads=[bth])
        return
    aff = c.sb("aff_own", [128, NT, E], F32)
    gp = c.sb("gprime", [128, NT, E], F32)
    bg = Buf("gp")
    c.dma("sp", lambda e: e.dma_start(out=aff[:], in_=aff_own_dram.rearrange("(j p) e -> p j e", p=128)), writes=[bg])
    for g, (t0, t1) in enumerate(((0, tiles_p), (tiles_p, NT))):
        if t1 == t0:
            continue
        th = theta[:, g * E:(g + 1) * E].unsqueeze(1).to_broadcast([128, t1 - t0, E])
        c.op("dve", lambda e, t0=t0, t1=t1, th=th: e.tensor_tensor(out=gp[:, t0:t1, :], in0=aff[:, t0:t1, :], in1=th, op=ALU.is_ge),
             reads=[bg, bth], writes=[bg])
        c.op("dve", lambda e, t0=t0, t1=t1: e.tensor_tensor(out=gp[:, t0:t1, :], in0=gp[:, t0:t1, :], in1=aff[:, t0:t1, :], op=ALU.mult),
             reads=[bg], writes=[bg])
    if stage == 2:
        c.dma("sp", lambda e: e.dma_start(out=dbg[:, 0:NT * E], in_=gp[:].rearrange("p a b -> p (a b)")), reads=[bg])
        return
    gam = c.sb("ffn_gam", [128, D], F32)
    bet = c.sb("ffn_bet", [128, D], F32)
    bgb = Buf("gb")
    c.dma("sp", lambda e: e.dma_start(out=gam[:], in_=lng.partition_broadcast(128)), writes=[bgb])
    c.dma("sp", lambda e: e.dma_start(out=bet[:], in_=lnb.partition_broadcast(128)), writes=[bgb])
    if stage == 30:
        return
    lntmp = {"st": c.sb("ln_st", [128, 2, 6], F32), "mv": c.sb("ln_mv", [128, 2], F32), "rs": c.sb("ln_rs", [128, 1], F32), "b": Buf("lnt")}

    TOK = TB * 128
    xT = c.sb("ffn_xT", [128, 8, TOK], BF16)
    hT = c.sb("ffn_hT", [128, FC, TOK], BF16)
    yacc = c.sb("ffn_yacc", [128, TB, D], F32)
    wgs = c.sb("ffn_wg", [128, 8, DFF], BF16)
    wus = c.sb("ffn_wu", [128, 8, DFF], BF16)
    wds = c.sb("ffn_wd", [128, FC, D], BF16)
    bwg, bwu, bwd = Buf("wg"), Buf("wu"), Buf("wd")
    bxT = Buf("xT")
    bhT = [Buf("hT%d" % i) for i in range(FC)]
    byacc = [Buf("yacc%d" % i) for i in range(TB)]
    xin = [c.sb("ffn_xin%d" % i, [128, D], F32) for i in range(2)]
    bxin = [Buf("xin0"), Buf("xin1")]
    sg = [c.sb("ffn_sg%d" % i, [128, 512], BF16) for i in range(2)]
    bsg = [Buf("sg0"), Buf("sg1")]
    zt = [c.sb("ffn_z%d" % i, [128, D], F32) for i in range(2)]
    bz = [Buf("z0"), Buf("z1")]
    ot, bo = zt, bz
    ps_t = c.ps("ps_t", [128, 512], F32); bps_t = Buf("ps_t")
    ps_g = [c.ps("ps_g%d" % i, [128, 512], F32) for i in range(2)]; bps_g = [Buf("psg0"), Buf("psg1")]
    ps_u = [c.ps("ps_u%d" % i, [128, 512], F32) for i in range(2)]; bps_u = [Buf("psu0"), Buf("psu1")]
    ps_d = [c.ps("ps_d%d" % i, [128, 512], F32) for i in range(2)]; bps_d = [Buf("psd0"), Buf("psd1")]
    idf = K["ident_f"]
    cnt = [0, 0, 0, 0]
    wst = [c.sb('ffn_wst%d' % i, [128, max(DFF, D)], F32) for i in range(2)]
    bwst = [Buf('wst%d' % i) for i in range(2)]
    nblk = (NT + TB - 1) // TB
    for blk in range(nblk):
        tl0 = blk * TB
        ntl = min(TB, NT - tl0)
        ntok = ntl * 128
        for t in range(ntl):
            i = cnt[0] % 2; cnt[0] += 1
            r0 = (tl0 + t) * 128
            c.dma("sp", lambda e, i=i, r0=r0: e.dma_start(out=xin[i][:], in_=x_dram[r0:r0 + 128, :]), writes=[bxin[i]])
            if stage == 31:
                continue
            for kh in range(2):
                for k in range(4):
                    kk = kh * 4 + k
                    c.op("pe", lambda e, i=i, k=k, kk=kk: e.transpose(ps_t[:, k * 128:(k + 1) * 128], xin[i][:, kk * 128:(kk + 1) * 128], idf[:]),
                         reads=[bxin[i], K["b"]], writes=[bps_t])
                if stage == 32:
                    continue
                c.op("dve", lambda e, t=t, kh=kh: e.tensor_copy(out=xT[:, kh * 4:(kh + 1) * 4, t * 128:(t + 1) * 128],
                                                      in_=ps_t[:].rearrange("p (k t) -> p k t", k=4)),
                     reads=[bps_t], writes=[bxT])
        if stage in (3, 31, 32):
            return
        for ex in range(E):
            if stage == 4 and ex == 1:
                return
            for (dst, src, nchunk, bw) in ((wgs, wg, 8, bwg), (wus, wu, 8, bwu), (wds, wd, FC, bwd)):
                wdt = src.shape[2]
                for k in range(nchunk):
                    i = cnt[3] % 2; cnt[3] += 1
                    c.dma("sp", lambda e, i=i, k=k, src=src, ex=ex, wdt=wdt: e.dma_start(out=wst[i][:, 0:wdt], in_=src[ex, k * 128:(k + 1) * 128, :]),
                          writes=[bwst[i]])
                    c.op("pool", lambda e, i=i, k=k, dst=dst, wdt=wdt: e.tensor_copy(out=dst[:, k, :], in_=wst[i][:, 0:wdt]),
                         reads=[bwst[i]], writes=[bw])
            if stage == 40:
                return
            for fc in range(FC):
                for n0 in range(0, ntok, 512):
                    nn = min(512, ntok - n0)
                    i = cnt[1] % 2; cnt[1] += 1
                    for k in range(8):
                        c.op("pe", lambda e, i=i, k=k, fc=fc, n0=n0, nn=nn: e.matmul(
                            ps_g[i][:, 0:nn], wgs[:, k, fc * 128:(fc + 1) * 128], xT[:, k, n0:n0 + nn], start=(k == 0), stop=(k == 7)),
                            reads=[bwg, bxT], writes=[bps_g[i]])
                    for k in range(8):
                        c.op("pe", lambda e, i=i, k=k, fc=fc, n0=n0, nn=nn: e.matmul(
                            ps_u[i][:, 0:nn], wus[:, k, fc * 128:(fc + 1) * 128], xT[:, k, n0:n0 + nn], start=(k == 0), stop=(k == 7)),
                            reads=[bwu, bxT], writes=[bps_u[i]])
                    if stage == 41:
                        continue
                    c.op("act", lambda e, i=i, nn=nn: e.activation(out=sg[i][:, 0:nn], in_=ps_g[i][:, 0:nn], func=AF.Silu),
                         reads=[bps_g[i]], writes=[bsg[i]])
                    c.op("dve", lambda e, i=i, fc=fc, n0=n0, nn=nn: e.tensor_tensor(
                        out=hT[:, fc, n0:n0 + nn], in0=ps_u[i][:, 0:nn], in1=sg[i][:, 0:nn], op=ALU.mult),
                        reads=[bps_u[i], bsg[i]], writes=[bhT[fc]])
            if stage in (5, 41):
                return
            for t in range(ntl):
                for h in range(2):
                    i = cnt[2] % 2; cnt[2] += 1
                    for fc in range(FC):
                        c.op("pe", lambda e, i=i, fc=fc, t=t, h=h: e.matmul(
                            ps_d[i][:], hT[:, fc, t * 128:(t + 1) * 128], wds[:, fc, h * 512:(h + 1) * 512], start=(fc == 0), stop=(fc == FC - 1)),
                            reads=[bwd, bhT[fc]], writes=[bps_d[i]])
                    gcol = gp[:, tl0 + t, ex:ex + 1]
                    if ex == 0:
                        c.op("dve", lambda e, i=i, t=t, h=h, gcol=gcol: e.tensor_scalar(
                            out=yacc[:, t, h * 512:(h + 1) * 512], in0=ps_d[i][:], scalar1=gcol, scalar2=None, op0=ALU.mult),
                            reads=[bps_d[i], bg], writes=[byacc[t]])
                    else:
                        c.op("dve", lambda e, i=i, t=t, h=h, gcol=gcol: e.scalar_tensor_tensor(
                            out=yacc[:, t, h * 512:(h + 1) * 512], in0=ps_d[i][:], scalar=gcol, in1=yacc[:, t, h * 512:(h + 1) * 512],
                            op0=ALU.mult, op1=ALU.add), reads=[bps_d[i], bg, byacc[t]], writes=[byacc[t]])
        if stage == 6:
            return
        for t in range(ntl):
            i = cnt[0] % 2; cnt[0] += 1
            r0 = (tl0 + t) * 128
            c.dma("sp", lambda e, i=i, r0=r0: e.dma_start(out=xin[i][:], in_=x_dram[r0:r0 + 128, :]), writes=[bxin[i]])
            c.op("dve", lambda e, i=i, t=t: e.scalar_tensor_tensor(out=zt[i][:], in0=xin[i][:], scalar=float(ALPHA), in1=yacc[:, t, :],
                                                                 op0=ALU.mult, op1=ALU.add), reads=[bxin[i], byacc[t]], writes=[bz[i]])
            emit_ln(c, zt[i], bz[i], ot[i], bo[i], gam, bet, bgb, lntmp)
            c.dma("sp", lambda e, i=i, r0=r0: e.dma_start(out=y_dram[r0:r0 + 128, :], in_=ot[i][:]), reads=[bo[i]])


def emit_ffn2(c, K, x_dram, aff_own_dram, affT_dram, wg, wu, wd, lng, lnb, y_dram,
              tiles_p, tiles_s, n_p, n_s, cap_p, cap_s, E=16, DFF=1408, TB=8, CAP=256, iters=26, NWST=4):
    NT = tiles_p + tiles_s
    FC = DFF // 128
    NST = CAP // 128
    theta, bth, A, bA = emit_thresholds(c, K, affT_dram, n_p, n_s, cap_p, cap_s, E, iters, ret_A=True)
    aff = c.sb("aff_own", [128, NT, E], F32)
    gp = c.sb("gprime", [128, NT, E], F32)
    self_ = c.sb("self", [128, NT, E], F32)
    selb = c.sb("selb", [128, NT, E], BF16)
    bg = Buf("gp")
    c.dma("sp", lambda e: e.dma_start(out=aff[:], in_=aff_own_dram.rearrange("(j p) e -> p j e", p=128)), writes=[bg])
    for g, (t0, t1) in enumerate(((0, tiles_p), (tiles_p, NT))):
        if t1 == t0:
            continue
        th = theta[:, g * E:(g + 1) * E].unsqueeze(1).to_broadcast([128, t1 - t0, E])
        c.op("dve", lambda e, t0=t0, t1=t1, th=th: e.tensor_tensor(out=self_[:, t0:t1, :], in0=aff[:, t0:t1, :], in1=th, op=ALU.is_ge),
             reads=[bg, bth], writes=[bg])
        c.op("dve", lambda e, t0=t0, t1=t1: e.tensor_tensor(out=gp[:, t0:t1, :], in0=self_[:, t0:t1, :], in1=aff[:, t0:t1, :], op=ALU.mult),
             reads=[bg], writes=[bg])
    c.op("dve", lambda e: e.tensor_copy(out=selb[:], in_=self_[:]), reads=[bg], writes=[bg])
    gam = c.sb("ffn_gam", [128, D], F32)
    bet = c.sb("ffn_bet", [128, D], F32)
    bgb = Buf("gb")
    c.dma("sp", lambda e: e.dma_start(out=gam[:], in_=lng.partition_broadcast(128)), writes=[bgb])
    c.dma("sp", lambda e: e.dma_start(out=bet[:], in_=lnb.partition_broadcast(128)), writes=[bgb])
    lntmp = {"st": c.sb("ln_st", [128, 2, 6], F32), "mv": c.sb("ln_mv", [128, 2], F32), "rs": c.sb("ln_rs", [128, 1], F32), "b": Buf("lnt")}
    stri = c.sb("f_stri", [128, 128], BF16)
    ones_b = c.sb("f_onesb", [128, 128], BF16)
    iot = c.sb("f_iota", [128, CAP], F32)
    trif = c.sb("f_trif", [128, 128], F32)
    c.op("pool", lambda e: e.affine_select(out=trif[:], in_=K["ones_f"][:], pattern=[[1, 128]], compare_op=ALU.is_gt, fill=0.0, base=0, channel_multiplier=-1),
         reads=[K["b"]], writes=[K["b"]])
    c.op("pool", lambda e: e.tensor_copy(out=stri[:], in_=trif[:]), reads=[K["b"]], writes=[K["b"]])
    c.op("pool", lambda e: e.tensor_copy(out=ones_b[:], in_=K["ones_f"][:]), reads=[K["b"]], writes=[K["b"]])
    c.op("pool", lambda e: e.iota(iot[:], pattern=[[1, CAP]], base=0, channel_multiplier=0, allow_small_or_imprecise_dtypes=True), writes=[K["b"]])

    asz = E * (n_p + n_s)
    need = 8 * DFF
    if asz >= need:
        Ab = A[:].rearrange("p e n -> p (e n)")
        wgs = Ab[:, 0:4 * DFF].bitcast(BF16).rearrange("p (k f) -> p k f", k=8)
        wus = Ab[:, 4 * DFF:8 * DFF].bitcast(BF16).rearrange("p (k f) -> p k f", k=8)
        alias = [bA]
    else:
        wgs = c.sb("ffn_wg", [128, 8, DFF], BF16)[:]
        wus = c.sb("ffn_wu", [128, 8, DFF], BF16)[:]
        alias = []
    wds = c.sb("ffn_wd", [128, FC, D], BF16)
    bwg, bwu, bwd = Buf("wg"), Buf("wu"), Buf("wd")
    xtok = c.sb("f_xtok", [128, TB, D], BF16); bxtok = Buf("xtok")
    xeT = c.sb("f_xeT", [128, 8, CAP], BF16); bxeT = Buf("xeT")
    hT = c.sb("f_hT", [128, FC, CAP], BF16); bhT = [Buf("hT%d" % i) for i in range(FC)]
    osb = c.sb("f_os", [128, NST, D], BF16); bos = Buf("os")
    OH = c.sb("f_OH", [128, TB, CAP], BF16); bOH = Buf("OH")
    OHT = c.sb("f_OHT", [128, NST, TB * 128], BF16); bOHT = Buf("OHT")
    yacc = c.sb("ffn_yacc", [128, TB, D], F32); byacc = [Buf("yacc%d" % i) for i in range(TB)]
    csel = c.sb("f_csel", [128, TB, E], BF16); bcsel = Buf("csel")
    rank = c.sb("f_rank", [128, TB, E], F32); brank = Buf("rank")
    xin = [c.sb("ffn_xin%d" % i, [128, D], F32) for i in range(2)]; bxin = [Buf("xin0"), Buf("xin1")]
    sg = [c.sb("ffn_sg%d" % i, [128, CAP], BF16) for i in range(2)]; bsg = [Buf("sg0"), Buf("sg1")]
    wst = [c.sb("ffn_wst%d" % i, [128, max(DFF, D)], F32) for i in range(NWST)]; bwst = [Buf("wst%d" % i) for i in range(NWST)]
    ps_rank = c.ps("ps_rank", [128, 512], F32); bps_rank = Buf("psrank")
    ps_tp = c.ps("ps_tp", [128, 1024], BF16); bps_tp = Buf("pstp")
    ps_ga = c.ps("ps_ga", [128, 512], F32); bps_ga = Buf("psga")
    ps_g = c.ps("ps_g", [128, 512], F32); bps_g = Buf("psg")
    ps_u = c.ps("ps_u", [128, 512], F32); bps_u = Buf("psu")
    ps_d = c.ps("ps_d", [128, 512], F32); bps_d = Buf("psd")
    ps_s0 = c.ps("ps_s0", [128, 512], F32); ps_s = [ps_s0, ps_s0]; _b = Buf("pss0"); bps_s = [_b, _b]
    idb = K["ident_b"]
    cnt = [0, 0, 0, 0]
    nblk = (NT + TB - 1) // TB
    first_w = [True]
    for blk in range(nblk):
        tl0 = blk * TB
        ntl = min(TB, NT - tl0)
        for t in range(ntl):
            i = cnt[0] % 2; cnt[0] += 1
            r0 = (tl0 + t) * 128
            c.dma("sp", lambda e, i=i, r0=r0: e.dma_start(out=xin[i][:], in_=x_dram[r0:r0 + 128, :]), writes=[bxin[i]])
            c.op("dve", lambda e, i=i, t=t: e.tensor_copy(out=xtok[:, t, :], in_=xin[i][:]), reads=[bxin[i]], writes=[bxtok])
        c.op("dve", lambda e: e.memset(csel[:, 0, :], 0.0), writes=[bcsel])
        for t in range(1, ntl):
            c.op("dve", lambda e, t=t, tl0=tl0: e.tensor_tensor(out=csel[:, t, :], in0=csel[:, t - 1, :], in1=selb[:, tl0 + t - 1, :], op=ALU.add),
                 reads=[bg, bcsel], writes=[bcsel])
        c.op("pe", lambda e, ntl=ntl, tl0=tl0: e.matmul(ps_rank[:, 0:ntl * E], stri[:], selb[:, tl0:tl0 + ntl, :].rearrange("p t e -> p (t e)"), start=True, stop=False),
             reads=[bg, K["b"]], writes=[bps_rank])
        c.op("pe", lambda e, ntl=ntl: e.matmul(ps_rank[:, 0:ntl * E], ones_b[:], csel[:, 0:ntl, :].rearrange("p t e -> p (t e)"), start=False, stop=True),
             reads=[bcsel, K["b"]], writes=[bps_rank])
        c.op("dve", lambda e, ntl=ntl: e.tensor_copy(out=rank[:, 0:ntl, :].rearrange("p t e -> p (t e)"), in_=ps_rank[:, 0:ntl * E]), reads=[bps_rank], writes=[brank])
        for ex in range(E):
            for (dst, src, nchunk, bw, al) in ((wgs, wg, 8, bwg, alias), (wus, wu, 8, bwu, alias), (wds[:], wd, FC, bwd, [])):
                wdt = src.shape[2]
                for k in range(nchunk):
                    i = cnt[3] % NWST; cnt[3] += 1
                    c.dma("sp", lambda e, i=i, k=k, src=src, ex=ex, wdt=wdt: e.dma_start(out=wst[i][:, 0:wdt], in_=src[ex, k * 128:(k + 1) * 128, :]),
                          writes=[bwst[i]])
                    ce = ("act", "act", "pool", "act", "act")[cnt[3] % 5]
                    if ce == "act":
                        c.op("act", lambda e, i=i, k=k, dst=dst, wdt=wdt: e.activation(out=dst[:, k, :], in_=wst[i][:, 0:wdt], func=AF.Identity),
                             reads=[bwst[i]], writes=[bw] + (al if first_w[0] else []))
                    else:
                        c.op(ce, lambda e, i=i, k=k, dst=dst, wdt=wdt: e.tensor_copy(out=dst[:, k, :], in_=wst[i][:, 0:wdt]),
                             reads=[bwst[i]], writes=[bw] + (al if first_w[0] else []))
            first_w[0] = False
            for t in range(ntl):
                c.op("dve", lambda e, t=t, ex=ex, tl0=tl0: e.tensor_scalar(out=OH[:, t, :], in0=iot[:], scalar1=rank[:, t, ex:ex + 1], scalar2=self_[:, tl0 + t, ex:ex + 1],
                                                               op0=ALU.is_equal, op1=ALU.mult), reads=[brank, bg, K["b"]], writes=[bOH])
            kper = 512 // CAP
            for k0 in range(0, 8, kper):
                for kk in range(kper):
                    k = k0 + kk
                    for t in range(ntl):
                        c.op("pe", lambda e, k=k, kk=kk, t=t, ntl=ntl: e.matmul(ps_ga[:, kk * CAP:(kk + 1) * CAP], xtok[:, t, k * 128:(k + 1) * 128], OH[:, t, :],
                                                                               start=(t == 0), stop=(t == ntl - 1)), reads=[bxtok, bOH], writes=[bps_ga])
                c.op("dve", lambda e, k0=k0: e.tensor_copy(out=xeT[:, k0:k0 + kper, :], in_=ps_ga[:, 0:kper * CAP].rearrange("p (k n) -> p k n", k=kper)),
                     reads=[bps_ga], writes=[bxeT])
            for st_ in range(NST):
                for t in range(ntl):
                    c.op("pe", lambda e, st_=st_, t=t: e.transpose(ps_tp[:, t * 128:(t + 1) * 128], OH[:, t, st_ * 128:(st_ + 1) * 128], idb[:]),
                         reads=[bOH, K["b"]], writes=[bps_tp])
                c.op("dve", lambda e, st_=st_, ntl=ntl: e.tensor_copy(out=OHT[:, st_, 0:ntl * 128], in_=ps_tp[:, 0:ntl * 128]), reads=[bps_tp], writes=[bOHT])
            for fc in range(FC):
                i = cnt[1] % 2; cnt[1] += 1
                for k in range(8):
                    c.op("pe", lambda e, k=k, fc=fc: e.matmul(ps_g[:, 0:CAP], wgs[:, k, fc * 128:(fc + 1) * 128], xeT[:, k, :], start=(k == 0), stop=(k == 7)),
                         reads=[bwg, bxeT], writes=[bps_g])
                for k in range(8):
                    c.op("pe", lambda e, k=k, fc=fc: e.matmul(ps_u[:, 0:CAP], wus[:, k, fc * 128:(fc + 1) * 128], xeT[:, k, :], start=(k == 0), stop=(k == 7)),
                         reads=[bwu, bxeT], writes=[bps_u])
                c.op("act", lambda e, i=i: e.activation(out=sg[i][:], in_=ps_g[:, 0:CAP], func=AF.Silu), reads=[bps_g], writes=[bsg[i]])
                c.op("dve", lambda e, i=i, fc=fc: e.tensor_tensor(out=hT[:, fc, :], in0=ps_u[:, 0:CAP], in1=sg[i][:], op=ALU.mult),
                     reads=[bps_u, bsg[i]], writes=[bhT[fc]])
            for st_ in range(NST):
                for h in range(2):
                    for fc in range(FC):
                        c.op("pe", lambda e, st_=st_, h=h, fc=fc: e.matmul(ps_d[:], hT[:, fc, st_ * 128:(st_ + 1) * 128], wds[:, fc, h * 512:(h + 1) * 512],
                                                                        start=(fc == 0), stop=(fc == FC - 1)), reads=[bwd, bhT[fc]], writes=[bps_d])
                    c.op("dve", lambda e, st_=st_, h=h: e.tensor_copy(out=osb[:, st_, h * 512:(h + 1) * 512], in_=ps_d[:]), reads=[bps_d], writes=[bos])
            for t in range(ntl):
                for h in range(2):
                    i = cnt[2] % 2; cnt[2] += 1
                    for st_ in range(NST):
                        c.op("pe", lambda e, i=i, st_=st_, t=t, h=h: e.matmul(ps_s[i][:], OHT[:, st_, t * 128:(t + 1) * 128], osb[:, st_, h * 512:(h + 1) * 512],
                                                                           start=(st_ == 0), stop=(st_ == NST - 1)), reads=[bOHT, bos], writes=[bps_s[i]])
                    gcol = gp[:, tl0 + t, ex:ex + 1]
                    if ex == 0:
                        c.op("dve", lambda e, i=i, t=t, h=h, gcol=gcol: e.tensor_scalar(
                            out=yacc[:, t, h * 512:(h + 1) * 512], in0=ps_s[i][:], scalar1=gcol, scalar2=None, op0=ALU.mult),
                            reads=[bps_s[i], bg], writes=[byacc[t]])
                    else:
                        c.op("dve", lambda e, i=i, t=t, h=h, gcol=gcol: e.scalar_tensor_tensor(
                            out=yacc[:, t, h * 512:(h + 1) * 512], in0=ps_s[i][:], scalar=gcol, in1=yacc[:, t, h * 512:(h + 1) * 512],
                            op0=ALU.mult, op1=ALU.add), reads=[bps_s[i], bg, byacc[t]], writes=[byacc[t]])
        for t in range(ntl):
            i = cnt[0] % 2; cnt[0] += 1
            r0 = (tl0 + t) * 128
            c.dma("sp", lambda e, i=i, r0=r0: e.dma_start(out=xin[i][:], in_=x_dram[r0:r0 + 128, :]), writes=[bxin[i]])
            c.op("dve", lambda e, i=i, t=t: e.scalar_tensor_tensor(out=xin[i][:], in0=xin[i][:], scalar=float(ALPHA), in1=yacc[:, t, :],
                                                                 op0=ALU.mult, op1=ALU.add), reads=[bxin[i], byacc[t]], writes=[bxin[i]])
            emit_ln(c, xin[i], bxin[i], xin[i], bxin[i], gam, bet, bgb, lntmp)
            c.dma("sp", lambda e, i=i, r0=r0: e.dma_start(out=y_dram[r0:r0 + 128, :], in_=xin[i][:]), reads=[bxin[i]])


NAW, MLW = 512, 512
EVEN_IN = 3600
SEQ = 2048
NTILE = 16


def na_table_host(rpb):
    NEG = -30000.0
    qc = np.arange(64)
    kc = np.arange(64)
    c0 = np.clip(qc - 8, 0, 48)
    colvalid = (kc[:, None] >= c0[None, :]) & (kc[:, None] < c0[None, :] + 16)
    dc = np.clip(kc[:, None] - qc[None, :] + 15, 0, 30)
    T = np.full((8, 3, 128, 16, 64), NEG, np.float32)
    for i in range(16):
        for half in range(2):
            dr = i - 1 + half
            if dr < 0 or dr > 14:
                continue
            vals = np.where(colvalid[None], rpb[:, dr][:, dc], NEG).astype(np.float32)
            sl = slice(half * 64, half * 64 + 64)
            T[:, 0, sl, i, :] = vals
            if half == 1:
                T[:, 1, sl, i, :] = vals
            else:
                T[:, 2, sl, i, :] = vals
    return T


class PSB:
    def __init__(self, c, n=8):
        self.t = [c.ps("psb%d" % i, [128, 512], F32) for i in range(n)]
        self.b = [Buf("psb%d" % i) for i in range(n)]


def load_cast_weight(c, dst, bdst, src, ncol, st, bst, cnt):
    for k in range(8):
        for c0 in range(0, ncol, 1024):
            cw = min(1024, ncol - c0)
            i = cnt[0] % len(st); cnt[0] += 1
            c.dma("sp", lambda e, i=i, k=k, c0=c0, cw=cw: e.dma_start(out=st[i][:, 0:cw], in_=src[k * 128:(k + 1) * 128, c0:c0 + cw]), writes=[bst[i]])
            c.op("pool", lambda e, i=i, k=k, c0=c0, cw=cw: e.tensor_copy(out=dst[:, k, c0:c0 + cw], in_=st[i][:, 0:cw]), reads=[bst[i]], writes=[bdst])


def emit_xT(c, K, P, x_dram, r0, ntiles, xT, bxT, xin, bxin, cnt):
    idf = K["ident_f"]
    for t in range(ntiles):
        i = cnt[0] % 2; cnt[0] += 1
        c.dma("sp", lambda e, i=i, t=t: e.dma_start(out=xin[i][:], in_=x_dram[r0 + t * 128:r0 + (t + 1) * 128, :]), writes=[bxin[i]])
        for kh in range(2):
            for k in range(4):
                kk = kh * 4 + k
                c.op("pe", lambda e, i=i, k=k, kk=kk: e.transpose(P.t[0][:, k * 128:(k + 1) * 128], xin[i][:, kk * 128:(kk + 1) * 128], idf[:]),
                     reads=[bxin[i], K["b"]], writes=[P.b[0]])
            c.op("dve", lambda e, t=t, kh=kh: e.tensor_copy(out=xT[:, kh * 4:(kh + 1) * 4, t * 128:(t + 1) * 128],
                                                         in_=P.t[0][:].rearrange("p (k t) -> p k t", k=4)), reads=[P.b[0]], writes=[bxT])


def proj_fm(c, P, W, bW, xT, bxT, col0, dst, bdst, scale, ntok, pc):
    for n0 in range(0, ntok, 512):
        j = 1 + (pc[0] % 2); pc[0] += 1
        for k in range(8):
            c.op("pe", lambda e, j=j, k=k, n0=n0: e.matmul(P.t[j][:], W[:, k, col0:col0 + 128], xT[:, k, n0:n0 + 512], start=(k == 0), stop=(k == 7)),
                 reads=[bW, bxT], writes=[P.b[j]])
        c.op("dve", lambda e, j=j, n0=n0: e.tensor_scalar(out=dst[:, n0:n0 + 512], in0=P.t[j][:], scalar1=float(scale), scalar2=None, op0=ALU.mult),
             reads=[P.b[j]], writes=[bdst])


def proj_tm(c, P, W, bW, xT, bxT, col0, ncol, dstfn, bdst, scale, ntile, pc):
    g = max(1, 512 // ncol)
    g = min(g, 4)
    for t0 in range(0, ntile, g):
        j = 1 + (pc[0] % 2); pc[0] += 1
        gg = min(g, ntile - t0)
        for ti in range(gg):
            t = t0 + ti
            for k in range(8):
                c.op("pe", lambda e, j=j, k=k, t=t, ti=ti: e.matmul(P.t[j][:, ti * ncol:(ti + 1) * ncol], xT[:, k, t * 128:(t + 1) * 128], W[:, k, col0:col0 + ncol],
                                                                     start=(k == 0), stop=(k == 7)), reads=[bW, bxT], writes=[P.b[j]])
        c.op("dve", lambda e, j=j, t0=t0, gg=gg: e.tensor_scalar(out=dstfn(t0, gg), in0=P.t[j][:, 0:gg * ncol].rearrange("p (g n) -> p g n", g=gg),
                                                                scalar1=float(scale), scalar2=None, op0=ALU.mult), reads=[P.b[j]], writes=[bdst])


def emit_mix_a(c, K, P, x_dram, w_in, w_out, gate_bias, norm_g, lng, lnb, w_r, na_tab, x1_dram, aff_dram, nseq, stage=99, dbg=None):
    Wout = c.sb("a_wout", [128, 8, D], BF16); bWout = Buf("wout")
    stg = [c.sb("a_stg%d" % i, [128, 8, 128], F32) for i in range(2)]; bstg = [Buf("stg0"), Buf("stg1")]
    st = [t[:].rearrange("p k n -> p (k n)") for t in stg]; bst = bstg
    scnt = [0]
    load_cast_weight(c, Wout, bWout, w_out, D, st, bst, scnt)
    WS = {n: c.sb("a_ws_" + n, [128, 8, 128], BF16) for n in ("q", "k", "v", "o")}
    bWS = {n: Buf("ws_" + n) for n in WS}
    WG = c.sb("a_wg16", [128, 8, 16], BF16); bWG = Buf("wg16")

    def load_w(name, col0, ncol=128):
        dst, bd = (WS[name], bWS[name]) if name != "g" else (WG, bWG)
        i = scnt[0] % 2; scnt[0] += 1
        c.dma("sp", lambda e, i=i: e.dma_start(out=stg[i][:, :, 0:ncol], in_=w_in[:, col0:col0 + ncol].rearrange("(k p) n -> p k n", p=128)), writes=[bstg[i]])
        c.op("pool", lambda e, i=i: e.tensor_copy(out=dst[:, :, 0:ncol], in_=stg[i][:, :, 0:ncol]), reads=[bstg[i]], writes=[bd])
    Wr = c.sb("a_wr", [128, 8, 16], F32); bWr = Buf("wr")
    c.dma("sp", lambda e: e.dma_start(out=Wr[:], in_=w_r.rearrange("(k p) e -> p k e", p=128)), writes=[bWr])
    gb = c.sb("a_gb", [128, 16], F32)
    ng = c.sb("a_ng", [128, 512], F32)
    gam = c.sb("a_gam", [128, D], F32)
    bet = c.sb("a_bet", [128, D], F32)
    bgb = Buf("gb")
    c.dma("sp", lambda e: e.dma_start(out=gb[:], in_=gate_bias.partition_broadcast(128)), writes=[bgb])
    c.dma("sp", lambda e: e.dma_start(out=ng[:], in_=norm_g.partition_broadcast(128)), writes=[bgb])
    c.dma("sp", lambda e: e.dma_start(out=gam[:], in_=lng.partition_broadcast(128)), writes=[bgb])
    c.dma("sp", lambda e: e.dma_start(out=bet[:], in_=lnb.partition_broadcast(128)), writes=[bgb])
    ones_f = K["ones_f"]
    tri = c.sb("a_tri", [128, 128], F32)
    trir = c.sb("a_trir", [128, 128], F32)
    ones_b = c.sb("a_onesb", [128, 128], BF16)
    c.op("pool", lambda e: e.affine_select(out=tri[:], in_=ones_f[:], pattern=[[1, 128]], compare_op=ALU.is_ge, fill=0.0, base=0, channel_multiplier=-1),
         reads=[K["b"]], writes=[K["b"]])
    c.op("pool", lambda e: e.affine_select(out=trir[:], in_=ones_f[:], pattern=[[-1, 128]], compare_op=ALU.is_ge, fill=0.0, base=0, channel_multiplier=1),
         reads=[K["b"]], writes=[K["b"]])
    c.op("pool", lambda e: e.tensor_copy(out=ones_b[:], in_=ones_f[:]), reads=[K["b"]], writes=[K["b"]])
    lntmp = {"st": c.sb("ln_st", [128, 2, 6], F32), "mv": c.sb("ln_mv", [128, 2], F32), "rs": c.sb("ln_rs", [128, 1], F32), "b": Buf("lnt")}

    xT = c.sb("a_xT", [128, 8, SEQ], BF16); bxT = Buf("xT")
    yT = c.sb("a_yT", [128, 8, SEQ], BF16); byT = [Buf("yT%d" % i) for i in range(8)]
    xin = [c.sb("a_xin%d" % i, [128, D], F32) for i in range(2)]; bxin = [Buf("xin0"), Buf("xin1")]
    QT = c.sb("a_QT", [128, SEQ], BF16); bQT = Buf("QT")
    KT = c.sb("a_KT", [128, SEQ], BF16); bKT = Buf("KT")
    Vb = c.sb("a_V", [128, NTILE * 132], BF16); bV = Buf("V")
    Ktm = c.sb("a_Ktm", [128, NTILE, 128], BF16); bKtm = Buf("Ktm")
    sgob = c.sb("a_sgob", [128, NTILE, 128], F32); bsgob = Buf("sgob")
    hacc = c.sb("a_hacc", [128, NTILE, 128], F32); bhacc = Buf("hacc")
    EB = c.sb("a_EB", [128, 3, 16, 64], F32); bEB = Buf("EB")
    pexp = [c.sb("a_pexp%d" % i, [128, 5 * 64], F32) for i in range(3)]; bpexp = [Buf("pexp%d" % i) for i in range(3)]
    PT = [c.sb("a_PT%d" % i, [128, 5, 64], BF16) for i in range(3)]; bPT = [Buf("PT%d" % i) for i in range(3)]
    rec = c.sb("a_rec", [128, 512], F32); brec = Buf("rec")
    G = c.sb("a_G", [128, NTILE, 16], F32); bG = Buf("G")
    nlf = c.sb("a_nlf", [128, NTILE, 8], F32)
    uu = c.sb("a_u", [128, NTILE, 8], F32)
    vv = c.sb("a_v", [128, NTILE, 8], F32)
    eL = c.sb("a_eL", [128, NTILE, 8], F32)
    bgate = Buf("gate")
    CN = [c.sb("a_CN%d" % i, [128, 129], F32) for i in range(2)]; bCN = [Buf("CN0"), Buf("CN1")]
    CNb = [c.sb("a_CNb%d" % i, [128, 129], BF16) for i in range(2)]; bCNb = [Buf("CNb0"), Buf("CNb1")]
    ctmp = [c.sb("a_ctmp%d" % i, [128, 129], F32) for i in range(2)]; bctmp = [Buf("ctmp0"), Buf("ctmp1")]
    St = [c.sb("a_St%d" % i, [128, 128], BF16) for i in range(2)]; bSt = [Buf("St0"), Buf("St1")]
    Kt = [c.sb("a_Kt%d" % i, [128, 128], BF16) for i in range(2)]; bKt = [Buf("Kt0"), Buf("Kt1")]
    sm = [c.sb("a_sm%d" % i, [128, 8], F32) for i in range(2)]; bsm = [Buf("sm0"), Buf("sm1")]
    lnw = c.sb("a_lnw", [128, NTILE, 128], F32); blnw = Buf("lnw")
    lns = c.sb("a_lns", [128, NTILE, 4], F32); blns = Buf("lns")
    x1T = c.sb("a_x1T", [128, 8, 128], F32); bx1T = Buf("x1T")
    rt = c.sb("a_rt", [128, 16], F32); brt = Buf("rt")
    rtm = c.sb("a_rtm", [128, 4], F32)
    affo = [c.sb("a_aff%d" % i, [128, 16], F32) for i in range(2)]; baffo = [Buf("aff0"), Buf("aff1")]
    xc = [0]; pc = [0]; sc = [0]; hc = [0]; zc = [0]

    for s in range(nseq):
        r0 = s * SEQ
        emit_xT(c, K, P, x_dram, r0, NTILE, xT, bxT, xin, bxin, xc)
        for hp in range(4):
            load_w("q", hp * 128); load_w("k", 512 + hp * 128); load_w("v", 1024 + hp * 128)
            proj_fm(c, P, WS["q"], bWS["q"], xT, bxT, 0, QT, bQT, 0.125, SEQ, pc)
            proj_fm(c, P, WS["k"], bWS["k"], xT, bxT, 0, KT, bKT, 1.0, SEQ, pc)
            Vv = Vb[:, 0:NTILE * 128].rearrange("p (t n) -> p t n", t=NTILE)
            proj_tm(c, P, WS["v"], bWS["v"], xT, bxT, 0, 128, lambda t0, gg: Vv[:, t0:t0 + gg, :], bV, 1.0, NTILE, pc)
            for hh in range(2):
                pb = hh * 64
                c.dma("sp", lambda e, hp=hp, hh=hh: e.dma_start(out=EB[:].rearrange("p v i q -> p v (i q)"),
                                                         in_=na_tab[2 * hp + hh].rearrange("v p i q -> p v (i q)")), writes=[bEB])
                c.op("act", lambda e: e.activation(out=EB[:].rearrange("p v i q -> p (v i q)"), in_=EB[:].rearrange("p v i q -> p (v i q)"), func=AF.Exp),
                     reads=[bEB], writes=[bEB])
                SB = (3, 4, 7)

                def na_front(r, hh=hh, pb=pb):
                    rs = min(max(r - 4, 0), 24)
                    a0, a1 = rs // 2, (rs + 7) // 2
                    nt = a1 - a0 + 1
                    si = r % 3
                    psS = P.t[SB[si]]; bS = P.b[SB[si]]
                    for j in range(nt):
                        a = a0 + j
                        c.op("pe", lambda e, j=j, a=a, psS=psS, pb=pb, r=r: e.matmul(psS[:, j * 64:(j + 1) * 64], KT[pb:pb + 64, a * 128:(a + 1) * 128],
                                                                                    QT[pb:pb + 64, r * 64:(r + 1) * 64], start=True, stop=True),
                             reads=[bKT, bQT], writes=[bS])
                    c.op("act", lambda e, si=si, psS=psS, nt=nt: e.activation(out=pexp[si][:, 0:nt * 64], in_=psS[:, 0:nt * 64], func=AF.Exp),
                         reads=[bS], writes=[bpexp[si]])
                    for j in range(nt):
                        a = a0 + j
                        var = 1 if 2 * a < rs else (2 if 2 * a + 1 >= rs + 8 else 0)
                        ii = 2 * a - r + 7 + 1
                        c.op("dve", lambda e, si=si, j=j, var=var, ii=ii: e.tensor_tensor(out=PT[si][:, j, :], in0=pexp[si][:, j * 64:(j + 1) * 64],
                                                                                   in1=EB[:, var, ii, :], op=ALU.mult),
                             reads=[bpexp[si], bEB], writes=[bPT[si]])

                def na_back(r, hh=hh, pb=pb, hp=hp):
                    rs = min(max(r - 4, 0), 24)
                    a0, a1 = rs // 2, (rs + 7) // 2
                    nt = a1 - a0 + 1
                    si = r % 3
                    rr = r % 8
                    for j in range(nt):
                        a = a0 + j
                        c.op("pe", lambda e, si=si, j=j, a=a, rr=rr, nt=nt: e.matmul(P.t[5][:, rr * 64:(rr + 1) * 64], Vv[:, a, :], PT[si][:, j, :],
                                                                                  start=(j == 0), stop=(j == nt - 1)), reads=[bV, bPT[si]], writes=[P.b[5]])
                    for j in range(nt):
                        c.op("pe", lambda e, si=si, j=j, rr=rr, nt=nt: e.matmul(P.t[6][:, rr * 64:(rr + 1) * 64], ones_b[:], PT[si][:, j, :],
                                                                             start=(j == 0), stop=(j == nt - 1)), reads=[K["b"], bPT[si]], writes=[P.b[6]])
                    if rr == 7:
                        q0 = (r - 7) * 64
                        c.op("dve", lambda e: e.reciprocal(out=rec[:], in_=P.t[6][:]), reads=[P.b[6]], writes=[brec])
                        c.op("dve", lambda e, pb=pb, hp=hp, q0=q0: e.tensor_tensor(out=yT[pb:pb + 64, hp, q0:q0 + 512], in0=P.t[5][pb:pb + 64, :],
                                                                                in1=rec[pb:pb + 64, :], op=ALU.mult),
                             reads=[P.b[5], brec], writes=[byT[hp]])
                LOOK = 2
                for r in range(32 + LOOK):
                    if r < 32:
                        na_front(r)
                    if r >= LOOK:
                        na_back(r - LOOK)
        if stage == 1:
            continue
        load_w("g", 3584, 16)
        Gps = P.t[1][:, 0:NTILE * 16].rearrange("p (t n) -> p t n", t=NTILE)
        for t in range(NTILE):
            for k in range(8):
                c.op("pe", lambda e, t=t, k=k: e.matmul(Gps[:, t, :], xT[:, k, t * 128:(t + 1) * 128], WG[:, k, :], start=(k == 0), stop=(k == 7)),
                     reads=[bxT, bWG], writes=[P.b[1]])
        c.op("dve", lambda e: e.tensor_tensor(out=G[:], in0=Gps, in1=gb[:].unsqueeze(1).to_broadcast([128, NTILE, 16]), op=ALU.add),
             reads=[P.b[1], bgb], writes=[bG])
        c.op("act", lambda e: e.activation(out=nlf[:], in_=G[:, :, 8:16], func=AF.Exp, scale=-1.0), reads=[bG], writes=[bgate])
        c.op("act", lambda e: e.activation(out=nlf[:], in_=nlf[:], func=AF.Ln, bias=1.0, scale=1.0), reads=[bgate], writes=[bgate])
        cum = P.t[2][:, 0:NTILE * 8].rearrange("p (t n) -> p t n", t=NTILE)
        tot = P.t[2][:, 256:256 + NTILE * 8].rearrange("p (t n) -> p t n", t=NTILE)
        for t in range(NTILE):
            c.op("pe", lambda e, t=t: e.matmul(cum[:, t, 0:4], tri[:], nlf[:, t, 0:4], start=True, stop=True), reads=[bgate, K["b"]], writes=[P.b[2]])
            c.op("pe", lambda e, t=t: e.matmul(cum[:, t, 4:8], trir[:], nlf[:, t, 4:8], start=True, stop=True), reads=[bgate, K["b"]], writes=[P.b[2]])
            c.op("pe", lambda e, t=t: e.matmul(tot[:, t, :], ones_f[:], nlf[:, t, :], start=True, stop=True), reads=[bgate, K["b"]], writes=[P.b[2]])
        c.op("act", lambda e: e.activation(out=uu[:], in_=cum, func=AF.Exp, scale=-1.0), reads=[P.b[2]], writes=[bgate])
        c.op("act", lambda e: e.activation(out=eL[:], in_=tot, func=AF.Exp, scale=-1.0), reads=[P.b[2]], writes=[bgate])
        c.op("dve", lambda e: e.tensor_tensor(out=vv[:], in0=cum, in1=G[:, :, 0:8], op=ALU.add), reads=[P.b[2], bG], writes=[bgate])
        c.op("act", lambda e: e.activation(out=vv[:], in_=vv[:], func=AF.Exp), reads=[bgate], writes=[bgate])
        for h in range(4):
            load_w("q", 1536 + h * 128); load_w("k", 2048 + h * 128); load_w("v", 2560 + h * 128); load_w("o", 3072 + h * 128)
            proj_fm(c, P, WS["q"], bWS["q"], xT, bxT, 0, QT, bQT, 1.0, SEQ, pc)
            proj_fm(c, P, WS["k"], bWS["k"], xT, bxT, 0, KT, bKT, 128 ** -0.5, SEQ, pc)
            proj_tm(c, P, WS["k"], bWS["k"], xT, bxT, 0, 128, lambda t0, gg: Ktm[:, t0:t0 + gg, :], bKtm, 128 ** -0.5, NTILE, pc)
            Va = Vb[:, 0:NTILE * 129].rearrange("p (t n) -> p t n", t=NTILE)
            proj_tm(c, P, WS["v"], bWS["v"], xT, bxT, 0, 128, lambda t0, gg: Va[:, t0:t0 + gg, 0:128], bV, 1.0, NTILE, pc)
            c.op("pool", lambda e: e.memset(Va[:, :, 128:129], 1.0), reads=[], writes=[bV])
            proj_tm(c, P, WS["o"], bWS["o"], xT, bxT, 0, 128, lambda t0, gg: sgob[:, t0:t0 + gg, :], bsgob, 1.0, NTILE, pc)
            c.op("act", lambda e: e.activation(out=sgob[:].rearrange("p t n -> p (t n)"), in_=sgob[:].rearrange("p t n -> p (t n)"), func=AF.Sigmoid),
                 reads=[bsgob], writes=[bsgob])
            c.op("pool", lambda e: e.memset(hacc[:], 0.0), writes=[bhacc])
            for dr in range(2):
                c.op("pool", lambda e, dr=dr: e.memset(CN[dr][:], 0.0), writes=[bCN[dr]])
                c.op("pool", lambda e, dr=dr: e.memset(CNb[dr][:], 0.0), writes=[bCNb[dr]])
            for step in range(NTILE):
              for dr in range(2):
                    gi = dr * 4 + h
                    mask = tri if dr == 0 else trir
                    ci = step if dr == 0 else NTILE - 1 - step
                    si = sc[0] % 2; sc[0] += 1
                    psS = P.t[3 + si]; bS = P.b[3 + si]
                    psH = P.t[5] if dr == 0 else P.t[7]; bH = P.b[5] if dr == 0 else P.b[7]
                    smd = sm[dr]; bsmd = bsm[dr]
                    cs = slice(ci * 128, (ci + 1) * 128)
                    c.op("pe", lambda e, psS=psS, cs=cs: e.matmul(psS[:, 0:128], KT[:, cs], QT[:, cs], start=True, stop=True), reads=[bKT, bQT], writes=[bS])
                    c.op("dve", lambda e, psS=psS, si=si, ci=ci, gi=gi, mask=mask: e.scalar_tensor_tensor(
                        out=St[si][:], in0=psS[:, 0:128], scalar=vv[:, ci, gi:gi + 1], in1=mask[:], op0=ALU.mult, op1=ALU.mult),
                        reads=[bS, bgate, K["b"]], writes=[bSt[si]])
                    c.op("pe", lambda e, psH=psH, si=si, ci=ci: e.matmul(psH[:, 0:129], St[si][:], Va[:, ci, :], start=True, stop=False),
                         reads=[bSt[si], bV], writes=[bH])
                    c.op("pe", lambda e, psH=psH, cs=cs, dr=dr: e.matmul(psH[:, 0:129], QT[:, cs], CNb[dr][:], start=False, stop=True),
                         reads=[bQT, bCNb[dr]], writes=[bH])
                    c.op("dve", lambda e, psH=psH, ci=ci, gi=gi, smd=smd: e.tensor_tensor(out=smd[:, 0:1], in0=psH[:, 128:129], in1=uu[:, ci, gi:gi + 1], op=ALU.mult),
                         reads=[bH, bgate], writes=[bsmd])
                    c.op("dve", lambda e, smd=smd: e.tensor_scalar(out=smd[:, 4:5], in0=smd[:, 0:1], scalar1=-1.0, scalar2=None, op0=ALU.mult), reads=[bsmd], writes=[bsmd])
                    c.op("dve", lambda e, smd=smd: e.tensor_tensor(out=smd[:, 5:6], in0=smd[:, 0:1], in1=smd[:, 4:5], op=ALU.max), reads=[bsmd], writes=[bsmd])
                    c.op("dve", lambda e, smd=smd: e.tensor_scalar(out=smd[:, 1:2], in0=smd[:, 5:6], scalar1=1.0, scalar2=None, op0=ALU.max), reads=[bsmd], writes=[bsmd])
                    c.op("dve", lambda e, smd=smd: e.reciprocal(out=smd[:, 2:3], in_=smd[:, 1:2]), reads=[bsmd], writes=[bsmd])
                    c.op("dve", lambda e, ci=ci, gi=gi, smd=smd: e.tensor_tensor(out=smd[:, 3:4], in0=smd[:, 2:3], in1=uu[:, ci, gi:gi + 1], op=ALU.mult),
                         reads=[bsmd, bgate], writes=[bsmd])
                    c.op("dve", lambda e, psH=psH, ci=ci, smd=smd: e.scalar_tensor_tensor(out=hacc[:, ci, :], in0=psH[:, 0:128], scalar=smd[:, 3:4], in1=hacc[:, ci, :],
                                                                               op0=ALU.mult, op1=ALU.add), reads=[bH, bsmd, bhacc], writes=[bhacc])
                    c.op("pool", lambda e, si=si, ci=ci, gi=gi: e.tensor_scalar(out=Kt[si][:], in0=Ktm[:, ci, :], scalar1=vv[:, ci, gi:gi + 1], scalar2=None, op0=ALU.mult),
                         reads=[bKtm, bgate], writes=[bKt[si]])
                    c.op("pe", lambda e, si=si, ci=ci: e.matmul(P.t[6][:, 0:129], Kt[si][:], Va[:, ci, :], start=True, stop=True),
                         reads=[bKt[si], bV], writes=[P.b[6]])
                    c.op("dve", lambda e, dr=dr: e.tensor_tensor(out=ctmp[dr][:], in0=P.t[6][:, 0:129], in1=CN[dr][:], op=ALU.add), reads=[P.b[6], bCN[dr]], writes=[bctmp[dr]])
                    c.op("dve", lambda e, ci=ci, gi=gi, dr=dr: e.tensor_scalar(out=CN[dr][:], in0=ctmp[dr][:], scalar1=eL[:, ci, gi:gi + 1], scalar2=None, op0=ALU.mult),
                         reads=[bctmp[dr], bgate], writes=[bCN[dr]])
                    c.op("pool", lambda e, ci=ci, gi=gi, dr=dr: e.tensor_scalar(out=CNb[dr][:], in0=ctmp[dr][:], scalar1=eL[:, ci, gi:gi + 1], scalar2=None, op0=ALU.mult),
                         reads=[bctmp[dr], bgate], writes=[bCNb[dr]])
            c.op("dve", lambda e: e.tensor_reduce(out=lns[:, :, 0], in_=hacc[:], axis=AX.X, op=ALU.add), reads=[bhacc], writes=[blns])
            c.op("dve", lambda e: e.tensor_scalar(out=lns[:, :, 0], in0=lns[:, :, 0], scalar1=1.0 / 128, scalar2=None, op0=ALU.mult), reads=[blns], writes=[blns])
            c.op("dve", lambda e: e.tensor_tensor(out=lnw[:], in0=hacc[:], in1=lns[:, :, 0:1].to_broadcast([128, NTILE, 128]), op=ALU.subtract),
                 reads=[bhacc, blns], writes=[blnw])
            c.op("pool", lambda e: e.tensor_tensor(out=hacc[:], in0=lnw[:], in1=lnw[:], op=ALU.mult), reads=[blnw, bhacc], writes=[bhacc])
            c.op("dve", lambda e: e.tensor_reduce(out=lns[:, :, 1], in_=hacc[:], axis=AX.X, op=ALU.add), reads=[bhacc], writes=[blns])
            c.op("dve", lambda e: e.tensor_scalar(out=lns[:, :, 1], in0=lns[:, :, 1], scalar1=1.0 / 128, scalar2=EPS, op0=ALU.mult, op1=ALU.add), reads=[blns], writes=[blns])
            c.op("act", lambda e: e.activation(out=lns[:, :, 2], in_=lns[:, :, 1], func=AF.Sqrt), reads=[blns], writes=[blns])
            c.op("dve", lambda e: e.reciprocal(out=lns[:, :, 3], in_=lns[:, :, 2]), reads=[blns], writes=[blns])
            c.op("dve", lambda e: e.tensor_tensor(out=lnw[:], in0=lnw[:], in1=lns[:, :, 3:4].to_broadcast([128, NTILE, 128]), op=ALU.mult), reads=[blnw, blns], writes=[blnw])
            c.op("pool", lambda e, h=h: e.tensor_tensor(out=lnw[:], in0=lnw[:], in1=ng[:, h * 128:(h + 1) * 128].unsqueeze(1).to_broadcast([128, NTILE, 128]), op=ALU.mult),
                 reads=[blnw, bgb], writes=[blnw])
            c.op("dve", lambda e: e.tensor_tensor(out=lnw[:], in0=lnw[:], in1=sgob[:], op=ALU.mult), reads=[blnw, bsgob], writes=[blnw])
            for tq in range(4):
                for k in range(4):
                    t = tq * 4 + k
                    c.op("pe", lambda e, t=t, k=k: e.transpose(P.t[0][:, k * 128:(k + 1) * 128], lnw[:, t, :], K["ident_f"][:]), reads=[blnw, K["b"]], writes=[P.b[0]])
                c.op("dve", lambda e, tq=tq, h=h: e.tensor_copy(out=yT[:, 4 + h, tq * 512:(tq + 1) * 512], in_=P.t[0][:]), reads=[P.b[0]], writes=[byT[4 + h]])
        if stage == 2:
            continue
        for t in range(NTILE):
            i = xc[0] % 2; xc[0] += 1
            zi = i
            rr0 = r0 + t * 128
            c.dma("sp", lambda e, i=i, rr0=rr0: e.dma_start(out=xin[i][:], in_=x_dram[rr0:rr0 + 128, :]), writes=[bxin[i]])
            for hf in range(2):
                for k in range(8):
                    c.op("pe", lambda e, hf=hf, k=k, t=t: e.matmul(P.t[1 + hf][:], yT[:, k, t * 128:(t + 1) * 128], Wout[:, k, hf * 512:(hf + 1) * 512],
                                                                start=(k == 0), stop=(k == 7)), reads=[byT[k], bWout], writes=[P.b[1 + hf]])
                c.op("dve", lambda e, hf=hf, i=i, zi=zi: e.scalar_tensor_tensor(out=xin[zi][:, hf * 512:(hf + 1) * 512], in0=xin[i][:, hf * 512:(hf + 1) * 512], scalar=float(ALPHA),
                                                                       in1=P.t[1 + hf][:], op0=ALU.mult, op1=ALU.add), reads=[bxin[i], P.b[1 + hf]], writes=[bxin[zi]])
            emit_ln(c, xin[zi], bxin[zi], xin[zi], bxin[zi], gam, bet, bgb, lntmp)
            c.dma("sp", lambda e, zi=zi, rr0=rr0: e.dma_start(out=x1_dram[rr0:rr0 + 128, :], in_=xin[zi][:]), reads=[bxin[zi]])
            emit_router(c, K, P, xin[zi], bxin[zi], Wr, bWr, x1T, bx1T, rt, brt, rtm, affo[zi], baffo[zi], aff_dram, rr0)


def emit_router(c, K, P, z, bz, Wr, bWr, x1T, bx1T, rt, brt, rtm, affo, baffo, aff_dram, rr0):
    idf = K["ident_f"]
    for kh in range(2):
        for k in range(4):
            kk = kh * 4 + k
            c.op("pe", lambda e, k=k, kk=kk: e.transpose(P.t[0][:, k * 128:(k + 1) * 128], z[:, kk * 128:(kk + 1) * 128], idf[:]), reads=[bz, K["b"]], writes=[P.b[0]])
        c.op("dve", lambda e, kh=kh: e.tensor_copy(out=x1T[:, kh * 4:(kh + 1) * 4, :], in_=P.t[0][:].rearrange("p (k t) -> p k t", k=4)), reads=[P.b[0]], writes=[bx1T])
    for k in range(8):
        c.op("pe", lambda e, k=k: e.matmul(P.t[3][:, 0:16], x1T[:, k, :], Wr[:, k, :], start=(k == 0), stop=(k == 7)), reads=[bx1T, bWr], writes=[P.b[3]])
    c.op("dve", lambda e: e.tensor_reduce(out=rtm[:, 0:1], in_=P.t[3][:, 0:16], axis=AX.X, op=ALU.max), reads=[P.b[3]], writes=[brt])
    c.op("dve", lambda e: e.tensor_scalar(out=rtm[:, 1:2], in0=rtm[:, 0:1], scalar1=-1.0, scalar2=None, op0=ALU.mult), reads=[brt], writes=[brt])
    c.op("act", lambda e: e.activation(out=rt[:], in_=P.t[3][:, 0:16], func=AF.Exp, bias=rtm[:, 1:2], scale=1.0, accum_out=rtm[:, 2:3]), reads=[P.b[3], brt], writes=[brt])
    c.op("dve", lambda e: e.reciprocal(out=rtm[:, 3:4], in_=rtm[:, 2:3]), reads=[brt], writes=[brt])
    c.op("dve", lambda e: e.tensor_scalar(out=affo[:], in0=rt[:], scalar1=rtm[:, 3:4], scalar2=None, op0=ALU.mult), reads=[brt], writes=[baffo])
    c.dma("sp", lambda e: e.dma_start(out=aff_dram[rr0:rr0 + 128, :], in_=affo[:]), reads=[baffo])


def rope_tables_host():
    d = 64
    inv = (10000.0 ** (-np.arange(0, d, 2, dtype=np.float32) / d)).astype(np.float32)
    ang = np.arange(SEQ, dtype=np.float32)[:, None] * inv[None, :]
    cos, sin = np.cos(ang).astype(np.float32), np.sin(ang).astype(np.float32)
    COS = np.concatenate([cos, cos], 1).T
    SINS = np.concatenate([-sin, sin], 1).T
    COS = np.ascontiguousarray(np.concatenate([COS, COS], 0), dtype=np.float32)
    SINS = np.ascontiguousarray(np.concatenate([SINS, SINS], 0), dtype=np.float32)
    p = np.arange(128)[:, None]
    x = np.arange(3968)[None, :]
    dl = p - x + 1920
    m = (np.abs(dl) <= 64).astype(np.float32) + ((dl % 4 == 0) & (np.abs(dl) <= 256)) + ((dl % 16 == 0) & (np.abs(dl) <= 1024))
    return COS, SINS, np.ascontiguousarray(m.astype(np.float32))


def emit_mix_b(c, K, P, x_dram, w_in, w_out, lng, lnb, w_r, cos_d, sins_d, tab_d, x1_dram, aff_dram, nseq):
    Wout = c.sb("b_wout", [128, 8, D], BF16); bWout = Buf("wout")
    stg = [c.sb("b_stg%d" % i, [128, 8, 128], F32) for i in range(2)]; bstg = [Buf("stg0"), Buf("stg1")]
    st = [t[:].rearrange("p k n -> p (k n)") for t in stg]
    scnt = [0]
    load_cast_weight(c, Wout, bWout, w_out, D, st, bstg, scnt)
    WS = {n: c.sb("b_ws_" + n, [128, 8, 128], BF16) for n in ("q", "qs", "k", "ks", "v")}
    bWS = {n: Buf("ws_" + n) for n in WS}

    def load_w(name, col0, swap=False):
        dst, bd = WS[name], bWS[name]
        i = scnt[0] % 2; scnt[0] += 1
        src = w_in[:, col0:col0 + 128]
        if not swap:
            c.dma("sp", lambda e, i=i: e.dma_start(out=stg[i][:], in_=src.rearrange("(k p) n -> p k n", p=128)), writes=[bstg[i]])
        else:
            s5 = src.rearrange("(k p) (h two d) -> p k h two d", p=128, h=2, two=2)
            d5 = stg[i][:].rearrange("p k (h two d) -> p k h two d", h=2, two=2)
            for a in range(2):
                for hd in range(2):
                    c.dma("sp", lambda e, i=i, a=a, hd=hd: e.dma_start(out=d5[:, :, hd, 1 - a, :], in_=s5[:, :, hd, a, :]), writes=[bstg[i]])
        c.op("pool", lambda e, i=i: e.tensor_copy(out=dst[:], in_=stg[i][:]), reads=[bstg[i]], writes=[bd])

    Wr = c.sb("b_wr", [128, 8, 16], F32); bWr = Buf("wr")
    c.dma("sp", lambda e: e.dma_start(out=Wr[:], in_=w_r.rearrange("(k p) e -> p k e", p=128)), writes=[bWr])
    gam = c.sb("b_gam", [128, D], F32)
    bet = c.sb("b_bet", [128, D], F32)
    bgb = Buf("gb")
    c.dma("sp", lambda e: e.dma_start(out=gam[:], in_=lng.partition_broadcast(128)), writes=[bgb])
    c.dma("sp", lambda e: e.dma_start(out=bet[:], in_=lnb.partition_broadcast(128)), writes=[bgb])
    COS = c.sb("b_cos", [128, SEQ], F32); SINS = c.sb("b_sins", [128, SEQ], F32)
    TABf = c.sb("b_tabf", [128, 3968], F32); TAB = c.sb("b_tab", [128, 3968], BF16)
    btab = Buf("tab")
    c.dma("sp", lambda e: e.dma_start(out=COS[:], in_=cos_d), writes=[btab])
    c.dma("sp", lambda e: e.dma_start(out=SINS[:], in_=sins_d), writes=[btab])
    c.dma("sp", lambda e: e.dma_start(out=TABf[:], in_=tab_d), writes=[btab])
    c.op("pool", lambda e: e.tensor_copy(out=TAB[:], in_=TABf[:]), reads=[btab], writes=[btab])
    ones_f = K["ones_f"]
    ones_b = c.sb("b_onesb", [128, 128], BF16)
    c.op("pool", lambda e: e.tensor_copy(out=ones_b[:], in_=ones_f[:]), reads=[K["b"]], writes=[K["b"]])
    lntmp = {"st": c.sb("ln_st", [128, 2, 6], F32), "mv": c.sb("ln_mv", [128, 2], F32), "rs": c.sb("ln_rs", [128, 1], F32), "b": Buf("lnt")}

    xT = c.sb("b_xT", [128, 8, SEQ], BF16); bxT = Buf("xT")
    yT = c.sb("b_yT", [128, 8, SEQ], BF16); byT = [Buf("yT%d" % i) for i in range(8)]
    xin = [c.sb("b_xin%d" % i, [128, D], F32) for i in range(2)]; bxin = [Buf("xin0"), Buf("xin1")]
    QT = c.sb("b_QT", [128, SEQ], BF16); bQT = Buf("QT")
    KT = c.sb("b_KT", [128, SEQ], BF16); bKT = Buf("KT")
    Vv = c.sb("b_V", [128, NTILE, 128], BF16); bV = Buf("V")
    t1 = c.sb("b_t1", [128, 512], F32); bt1 = Buf("t1")
    t2 = c.sb("b_t2", [128, 512], F32); bt2 = Buf("t2")
    pexp = [c.sb("b_pexp%d" % i, [128, 512], BF16) for i in range(3)]; bpexp = [Buf("pexp%d" % i) for i in range(3)]
    PT = [c.sb("b_PT%d" % i, [128, 512], BF16) for i in range(3)]; bPT = [Buf("PT%d" % i) for i in range(3)]
    rec = c.sb("b_rec", [128, 512], F32); brec = Buf("rec")
    x1T = c.sb("b_x1T", [128, 8, 128], F32); bx1T = Buf("x1T")
    rt = c.sb("b_rt", [128, 16], F32); brt = Buf("rt")
    rtm = c.sb("b_rtm", [128, 4], F32)
    affo = [c.sb("b_aff%d" % i, [128, 16], F32) for i in range(2)]; baffo = [Buf("aff0"), Buf("aff1")]
    xc = [0]; pc = [0]; sc = [0]

    def proj_rope(wa, wb, dst, bdst):
        for n0 in range(0, SEQ, 512):
            for k in range(8):
                c.op("pe", lambda e, k=k, n0=n0: e.matmul(P.t[1][:], WS[wa][:, k, :], xT[:, k, n0:n0 + 512], start=(k == 0), stop=(k == 7)),
                     reads=[bWS[wa], bxT], writes=[P.b[1]])
            for k in range(8):
                c.op("pe", lambda e, k=k, n0=n0: e.matmul(P.t[2][:], WS[wb][:, k, :], xT[:, k, n0:n0 + 512], start=(k == 0), stop=(k == 7)),
                     reads=[bWS[wb], bxT], writes=[P.b[2]])
            c.op("dve", lambda e, n0=n0: e.tensor_tensor(out=t1[:], in0=P.t[1][:], in1=COS[:, n0:n0 + 512], op=ALU.mult), reads=[P.b[1], btab], writes=[bt1])
            c.op("dve", lambda e, n0=n0: e.tensor_tensor(out=t2[:], in0=P.t[2][:], in1=SINS[:, n0:n0 + 512], op=ALU.mult), reads=[P.b[2], btab], writes=[bt2])
            c.op("pool", lambda e, n0=n0: e.tensor_tensor(out=dst[:, n0:n0 + 512], in0=t1[:], in1=t2[:], op=ALU.add), reads=[bt1, bt2], writes=[bdst])

    for s in range(nseq):
        r0 = s * SEQ
        emit_xT(c, K, P, x_dram, r0, NTILE, xT, bxT, xin, bxin, xc)
        for hp in range(8):
            load_w("q", hp * 128); load_w("qs", hp * 128, swap=True)
            load_w("k", 1024 + hp * 128); load_w("ks", 1024 + hp * 128, swap=True)
            load_w("v", 2048 + hp * 128)
            proj_rope("q", "qs", QT, bQT)
            proj_rope("k", "ks", KT, bKT)
            proj_tm(c, P, WS["v"], bWS["v"], xT, bxT, 0, 128, lambda t0, gg: Vv[:, t0:t0 + gg, :], bV, 1.0, NTILE, pc)
            blocks = []
            for hh in range(2):
                for qb in range(4):
                    kts = [kt for kt in range(NTILE) if not (128 * kt - 512 * qb - 511 > 1024 or 128 * kt + 127 - 512 * qb < -1024)]
                    for j, kt in enumerate(kts):
                        blocks.append((hh, qb, j, kt, len(kts)))
            SB = (3, 4, 7)

            def front(bi, blocks=blocks):
                hh, qb, j, kt, n = blocks[bi]
                pb = hh * 64
                si = bi % 3
                psS = P.t[SB[si]]; bS = P.b[SB[si]]
                c.op("pe", lambda e, psS=psS, kt=kt, qb=qb, pb=pb: e.matmul(psS[:], KT[pb:pb + 64, kt * 128:(kt + 1) * 128], QT[pb:pb + 64, qb * 512:(qb + 1) * 512],
                                                                         start=True, stop=True), reads=[bKT, bQT], writes=[bS])
                c.op("act", lambda e, psS=psS, si=si: e.activation(out=pexp[si][:], in_=psS[:], func=AF.Exp, scale=0.125), reads=[bS], writes=[bpexp[si]])
                x0 = 512 * qb - 128 * kt + 1920
                c.op("dve", lambda e, si=si, x0=x0: e.tensor_tensor(out=PT[si][:], in0=pexp[si][:], in1=TAB[:, x0:x0 + 512], op=ALU.mult),
                     reads=[bpexp[si], btab], writes=[bPT[si]])

            def back(bi, hp=hp, blocks=blocks):
                hh, qb, j, kt, n = blocks[bi]
                pb = hh * 64
                si = bi % 3
                c.op("pe", lambda e, si=si, kt=kt, j=j, n=n: e.matmul(P.t[5][:], Vv[:, kt, :], PT[si][:], start=(j == 0), stop=(j == n - 1)),
                     reads=[bV, bPT[si]], writes=[P.b[5]])
                c.op("pe", lambda e, si=si, j=j, n=n: e.matmul(P.t[6][:], ones_b[:], PT[si][:], start=(j == 0), stop=(j == n - 1)),
                     reads=[K["b"], bPT[si]], writes=[P.b[6]])
                if j == n - 1:
                    c.op("dve", lambda e: e.reciprocal(out=rec[:], in_=P.t[6][:]), reads=[P.b[6]], writes=[brec])
                    c.op("dve", lambda e, pb=pb, hp=hp, qb=qb: e.tensor_tensor(out=yT[pb:pb + 64, hp, qb * 512:(qb + 1) * 512], in0=P.t[5][pb:pb + 64, :],
                                                                            in1=rec[pb:pb + 64, :], op=ALU.mult), reads=[P.b[5], brec], writes=[byT[hp]])
            LOOK = 2
            for bi in range(len(blocks) + LOOK):
                if bi < len(blocks):
                    front(bi)
                if bi >= LOOK:
                    back(bi - LOOK)
        for t in range(NTILE):
            i = xc[0] % 2; xc[0] += 1
            rr0 = r0 + t * 128
            c.dma("sp", lambda e, i=i, rr0=rr0: e.dma_start(out=xin[i][:], in_=x_dram[rr0:rr0 + 128, :]), writes=[bxin[i]])
            for hf in range(2):
                for k in range(8):
                    c.op("pe", lambda e, hf=hf, k=k, t=t: e.matmul(P.t[1 + hf][:], yT[:, k, t * 128:(t + 1) * 128], Wout[:, k, hf * 512:(hf + 1) * 512],
                                                                start=(k == 0), stop=(k == 7)), reads=[byT[k], bWout], writes=[P.b[1 + hf]])
                c.op("dve", lambda e, hf=hf, i=i: e.scalar_tensor_tensor(out=xin[i][:, hf * 512:(hf + 1) * 512], in0=xin[i][:, hf * 512:(hf + 1) * 512], scalar=float(ALPHA),
                                                                 in1=P.t[1 + hf][:], op0=ALU.mult, op1=ALU.add), reads=[bxin[i], P.b[1 + hf]], writes=[bxin[i]])
            emit_ln(c, xin[i], bxin[i], xin[i], bxin[i], gam, bet, bgb, lntmp)
            c.dma("sp", lambda e, i=i, rr0=rr0: e.dma_start(out=x1_dram[rr0:rr0 + 128, :], in_=xin[i][:]), reads=[bxin[i]])
            emit_router(c, K, P, xin[i], bxin[i], Wr, bWr, x1T, bx1T, rt, brt, rtm, affo[i], baffo[i], aff_dram, rr0)


NCORES = 8
TILES_P, TILES_S = 32, 64
E_ = 16
DFF_ = 1408


def build_ffn_program():
    NT = TILES_P + TILES_S
    n_p, n_s = NCORES * TILES_P, NCORES * TILES_S
    cap_p, cap_s = 2 * n_p * 128 // E_, 2 * n_s * 128 // E_
    nc = bass.Bass("TRN2", target_bir_lowering=False)
    di = lambda n, s: nc.dram_tensor(n, s, F32, kind="ExternalInput").ap()
    x = di("x", [NT * 128, D]); aff = di("aff", [NT * 128, E_]); affT = di("affT", [128, E_, n_p + n_s])
    wg = di("wg", [E_, D, DFF_]); wu = di("wu", [E_, D, DFF_]); wd = di("wd", [E_, DFF_, D])
    lng = di("lng", [D]); lnb = di("lnb", [D])
    y = nc.dram_tensor("y", [NT * 128, D], F32, kind="ExternalOutput").ap()
    c = Ctx(nc)
    K = emit_consts(c)
    emit_ffn2(c, K, x, aff, affT, wg, wu, wd, lng, lnb, y, TILES_P, TILES_S, n_p, n_s, cap_p, cap_s, E=E_, DFF=DFF_, TB=8, CAP=256)
    c.finish()
    return nc


def build_a_program(nseq=6):
    nc = bass.Bass("TRN2", target_bir_lowering=False)
    di = lambda n, s: nc.dram_tensor(n, s, F32, kind="ExternalInput").ap()
    x = di("x", [nseq * 2048, D]); w_in = di("w_in", [D, 3600]); w_out = di("w_out", [D, D]); gbias = di("gbias", [16]); ng = di("ng", [512])
    lng = di("lng", [D]); lnb = di("lnb", [D]); wr = di("wr", [D, 16]); tab = di("tab", [8, 3, 128, 16, 64])
    x1 = nc.dram_tensor("x1", [nseq * 2048, D], F32, kind="ExternalOutput").ap()
    aff = nc.dram_tensor("aff", [nseq * 2048, 16], F32, kind="ExternalOutput").ap()
    c = Ctx(nc); K = emit_consts(c); P = PSB(c)
    emit_mix_a(c, K, P, x, w_in, w_out, gbias, ng, lng, lnb, wr, tab, x1, aff, nseq)
    c.finish()
    return nc


def build_b_program(nseq=6):
    nc = bass.Bass("TRN2", target_bir_lowering=False)
    di = lambda n, s: nc.dram_tensor(n, s, F32, kind="ExternalInput").ap()
    x = di("x", [nseq * 2048, D]); w_in = di("w_in", [D, 3072]); w_out = di("w_out", [D, D])
    lng = di("lng", [D]); lnb = di("lnb", [D]); wr = di("wr", [D, 16]); cos = di("cos", [128, 2048]); sins = di("sins", [128, 2048]); tab = di("tab", [128, 3968])
    x1 = nc.dram_tensor("x1", [nseq * 2048, D], F32, kind="ExternalOutput").ap()
    aff = nc.dram_tensor("aff", [nseq * 2048, 16], F32, kind="ExternalOutput").ap()
    c = Ctx(nc); K = emit_consts(c); P = PSB(c)
    emit_mix_b(c, K, P, x, w_in, w_out, lng, lnb, wr, cos, sins, tab, x1, aff, nseq)
    c.finish()
    return nc


def _aff_layout(affs):
    def toT(a):
        return a.reshape(-1, 128, E_).transpose(1, 2, 0)
    ap = np.concatenate([a[:TILES_P * 128] for a in affs], 0)
    as_ = np.concatenate([a[TILES_P * 128:] for a in affs], 0)
    return np.ascontiguousarray(np.concatenate([toT(ap), toT(as_)], axis=2), dtype=np.float32)


def kernel(x_prompt, x_sample, even_w_in, ml_gate_bias, na_rpb, ml_norm_g, even_w_out, da_w_in, da_w_out,
           ln_mix_g, ln_mix_b, ec_router, ec_w_gate, ec_w_up, ec_w_down, ln_ffn_g, ln_ffn_b):
    f32 = lambda a: np.ascontiguousarray(np.asarray(a), dtype=np.float32)
    x_prompt, x_sample = f32(x_prompt), f32(x_sample)
    cores = list(range(NCORES))
    xs = [np.concatenate([x_prompt[2 * c:2 * c + 2].reshape(-1, D), x_sample[4 * c:4 * c + 4].reshape(-1, D)], 0) for c in cores]
    tabA = na_table_host(f32(na_rpb)[0])
    feedA = {"w_in": f32(even_w_in)[0], "w_out": f32(even_w_out)[0], "gbias": f32(ml_gate_bias)[0], "ng": f32(ml_norm_g)[0],
             "lng": f32(ln_mix_g)[0], "lnb": f32(ln_mix_b)[0], "wr": f32(ec_router)[0], "tab": tabA}
    ncA = build_a_program()
    res = run_bass_kernel_spmd(ncA, [dict(feedA, x=xs[c]) for c in cores], core_ids=cores)
    x1 = [res.results[c]["x1"] for c in cores]; aff = [res.results[c]["aff"] for c in cores]
    del ncA
    ncF = build_ffn_program()
    def run_ffn(xl, affl, layer):
        affT = _aff_layout(affl)
        feed = {"affT": affT, "wg": f32(ec_w_gate)[layer], "wu": f32(ec_w_up)[layer], "wd": f32(ec_w_down)[layer],
                "lng": f32(ln_ffn_g)[layer], "lnb": f32(ln_ffn_b)[layer]}
        r = run_bass_kernel_spmd(ncF, [dict(feed, x=xl[c], aff=affl[c]) for c in cores], core_ids=cores)
        return [r.results[c]["y"] for c in cores]
    x2 = run_ffn(x1, aff, 0)
    COS, SINS, TAB = rope_tables_host()
    feedB = {"w_in": f32(da_w_in)[0], "w_out": f32(da_w_out)[0], "lng": f32(ln_mix_g)[1], "lnb": f32(ln_mix_b)[1],
             "wr": f32(ec_router)[1], "cos": COS, "sins": SINS, "tab": TAB}
    ncB = build_b_program()
    res = run_bass_kernel_spmd(ncB, [dict(feedB, x=x2[c]) for c in cores], core_ids=cores)
    x3 = [res.results[c]["x1"] for c in cores]; aff2 = [res.results[c]["aff"] for c in cores]
    del ncB
    y = run_ffn(x3, aff2, 1)
    y_prompt = np.stack([y[c][:TILES_P * 128].reshape(2, 2048, D) for c in cores], 0).reshape(16, 2048, D)
    y_sample = np.stack([y[c][TILES_P * 128:].reshape(4, 2048, D) for c in cores], 0).reshape(32, 2048, D)
    return (np.ascontiguousarray(y_prompt, dtype=np.float32), np.ascontiguousarray(y_sample, dtype=np.float32))
```

```python
from concourse.bass_utils import run_bass_kernel_spmd
import numpy as np
import concourse.bass as bass
import concourse.mybir as mybir

F32 = mybir.dt.float32
BF16 = mybir.dt.bfloat16
I32 = mybir.dt.int32
U32 = mybir.dt.uint32
ALU = mybir.AluOpType
AF = mybir.ActivationFunctionType
AX = mybir.AxisListType


class Buf:
    __slots__ = ("name", "writers", "readers")

    def __init__(self, name=""):
        self.name = name
        self.writers = {}
        self.readers = []


class EngS:
    def __init__(self, name, self_wait):
        self.name = name
        self.ops = []
        self.self_wait = self_wait
        self.known = {}
        self.n = 0


class Ctx:
    ENG = ("pe", "act", "dve", "pool", "sp")

    def __init__(self, nc, dma_pool=8):
        self.nc = nc
        self.E = {
            "pe": EngS("pe", False),
            "act": EngS("act", True),
            "dve": EngS("dve", True),
            "pool": EngS("pool", True),
            "sp": EngS("sp", False),
        }
        self.sems = {}
        self.stack = []
        cm = nc.semaphore("s_cc")
        self.sems["cc"] = cm.__enter__()
        self.stack.append(cm)
        self.cc_n = 0
        self.cc_scr = None
        for e in self.ENG:
            cm = nc.semaphore("s_" + e)
            self.sems[e] = cm.__enter__()
            self.stack.append(cm)
        self.dma_pool = {}
        self.dma_tot = {}
        self.dma_i = {}
        for q in ("sp", "pool", "act"):
            lst = []
            for i in range(dma_pool):
                cm = nc.semaphore("d_%s%d" % (q, i))
                lst.append(cm.__enter__())
                self.stack.append(cm)
            self.dma_pool[q] = lst
            self.dma_i[q] = 0
        self.ctxs = []

    def sb(self, name, shape, dt):
        cm = self.nc.sbuf_tensor(name, list(shape), dt)
        t = cm.__enter__()
        self.ctxs.append(cm)
        return t

    def ps(self, name, shape, dt):
        cm = self.nc.psum_tensor(name, list(shape), dt)
        t = cm.__enter__()
        self.ctxs.append(cm)
        return t

    def _need(self, es, ev, waits):
        key, val = ev
        if key == es.name and not es.self_wait:
            return
        if es.known.get(key, -1) >= val:
            return
        es.known[key] = val
        waits.append(ev)

    def _deps(self, es, reads, writes):
        waits = []
        for b in reads:
            for ev in b.writers.values():
                self._need(es, ev, waits)
        for b in writes:
            for ev in b.writers.values():
                self._need(es, ev, waits)
            for ev in b.readers:
                self._need(es, ev, waits)
        return waits

    def _mark(self, ev, reads, writes):
        for b in reads:
            b.readers.append(ev)
            if len(b.readers) > 64:
                b.readers = b.readers[-48:]
        for b in writes:
            b.writers = {ev[0]: ev}
            b.readers = []

    def op(self, eng, fn, reads=(), writes=()):
        es = self.E[eng]
        waits = self._deps(es, reads, writes)
        es.n += 1
        ev = (eng, es.n)
        es.ops.append((fn, waits, ("eng", None)))
        self._mark(ev, reads, writes)
        return ev

    def dma(self, q, fn, reads=(), writes=()):
        es = self.E[q]
        pool = self.dma_pool[q]
        i = self.dma_i[q]
        self.dma_i[q] = i + 1
        sem = pool[i % len(pool)]
        skey = ("dma", q, i % len(pool))
        prev = self.dma_tot.get(skey, 0)
        waits = self._deps(es, reads, writes)
        if prev > 0:
            self._need(es, (skey, prev), waits)
        tot = prev + 16
        self.dma_tot[skey] = tot
        es.n += 1
        es.ops.append((fn, waits, ("dma", sem)))
        ev = (skey, tot)
        self._mark(ev, reads, writes)
        return ev

    def cc(self, fn, reads=(), writes=()):
        es = self.E["pool"]
        if self.cc_scr is None:
            self.cc_scr = self.sb("cc_scr", [128, 8], F32)
        waits = self._deps(es, reads, writes)
        self.cc_n += 1
        es.n += 1
        es.ops.append((fn, waits, ("cc", self.sems["cc"])))
        es.n += 1
        scr = self.cc_scr
        es.ops.append((lambda e: e.memset(scr[:], 0.0), [("cc", self.cc_n)], ("eng", None)))
        ev = ("pool", es.n)
        es.known["pool"] = es.n
        self._mark(ev, reads, writes)
        return ev

    def _sem_of(self, key):
        if isinstance(key, tuple):
            return self.dma_pool[key[1]][key[2]]
        return self.sems[key]

    def finish(self):
        nc = self.nc
        es = self.E["sp"]
        waits = []
        for skey, tot in self.dma_tot.items():
            self._need(es, (skey, tot), waits)
        for e in ("pe", "act", "dve", "pool"):
            if self.E[e].n > 0:
                self._need(es, (e, self.E[e].n), waits)
        es.ops.append((None, waits, None))
        engmap = {"pe": "tensor", "act": "scalar", "dve": "vector", "pool": "gpsimd", "sp": "sync"}
        with nc.Block() as block:
            for e in self.ENG:
                st = self.E[e]

                def body(engine, st=st, e=e):
                    own = self.sems[e]
                    for fn, waits, inc in st.ops:
                        for key, val in waits:
                            engine.wait_ge(self._sem_of(key), val)
                        if fn is None:
                            continue
                        ins = fn(engine)
                        if inc[0] == "eng":
                            ins.then_inc(own, 1)
                        elif inc[0] == "cc":
                            ins.then_inc(inc[1])
                        else:
                            ins.then_inc(inc[1], 16)

                getattr(block, engmap[e])(body)
        for cm in reversed(self.ctxs):
            cm.__exit__(None, None, None)
        for cm in reversed(self.stack):
            cm.__exit__(None, None, None)


D = 1024
ALPHA = 4 ** 0.25
EPS = 1e-5


def emit_consts(c):
    nc = c.nc
    K = {}
    K["ident_f"] = c.sb("ident_f", [128, 128], F32)
    K["ident_b"] = c.sb("ident_b", [128, 128], BF16)
    K["ones_f"] = c.sb("ones_f", [128, 128], F32)
    K["b"] = Buf("consts")
    idf, idb, of = K["ident_f"], K["ident_b"], K["ones_f"]
    c.op("pool", lambda e: e.memset(of[:], 1.0), writes=[K["b"]])
    c.op("pool", lambda e: e.affine_select(out=idf[:], in_=of[:], pattern=[[-1, 128]], compare_op=ALU.is_equal,
                                           fill=0.0, base=0, channel_multiplier=1), reads=[K["b"]], writes=[K["b"]])
    c.op("pool", lambda e: e.tensor_copy(out=idb[:], in_=idf[:]), reads=[K["b"]], writes=[K["b"]])
    return K


def emit_ln(c, z, bz, out, bout, gam, bet, bgb, tmp):
    st, mv, rs, bt = tmp["st"], tmp["mv"], tmp["rs"], tmp["b"]
    c.op("dve", lambda e: e.bn_stats(out=st[:, 0, :], in_=z[:, 0:512]), reads=[bz], writes=[bt])
    c.op("dve", lambda e: e.bn_stats(out=st[:, 1, :], in_=z[:, 512:1024]), reads=[bz], writes=[bt])
    c.op("dve", lambda e: e.bn_aggr(out=mv[:], in_=st[:].rearrange("p a b -> p (a b)")), reads=[bt], writes=[bt])
    c.op("dve", lambda e: e.tensor_scalar(out=rs[:], in0=mv[:, 1:2], scalar1=EPS, scalar2=None, op0=ALU.add), reads=[bt], writes=[bt])
    c.op("act", lambda e: e.activation(out=rs[:], in_=rs[:], func=AF.Sqrt), reads=[bt], writes=[bt])
    c.op("dve", lambda e: e.reciprocal(out=rs[:], in_=rs[:]), reads=[bt], writes=[bt])
    c.op("dve", lambda e: e.tensor_scalar(out=out[:], in0=z[:], scalar1=mv[:, 0:1], scalar2=rs[:, 0:1],
                                          op0=ALU.subtract, op1=ALU.mult), reads=[bz, bt], writes=[bout])
    c.op("pool", lambda e: e.tensor_tensor(out=out[:], in0=out[:], in1=gam[:], op=ALU.mult), reads=[bout, bgb], writes=[bout])
    c.op("pool", lambda e: e.tensor_tensor(out=out[:], in0=out[:], in1=bet[:], op=ALU.add), reads=[bout, bgb], writes=[bout])


def emit_thresholds(c, K, affT_dram, n_p, n_s, cap_p, cap_s, E, iters=26, ret_A=False):
    A = c.sb("bis_A", [128, E, n_p + n_s], F32)
    bA = Buf("bisA")
    c.dma("sp", lambda e: e.dma_start(out=A[:], in_=affT_dram), writes=[bA])
    lo = c.sb("bis_lo", [128, 2 * E], F32)
    hi = c.sb("bis_hi", [128, 2 * E], F32)
    mid = c.sb("bis_mid", [128, 2 * E], F32)
    cnt = c.sb("bis_cnt", [128, 2 * E], F32)
    capv = c.sb("bis_cap", [128, 2 * E], F32)
    ge = c.sb("bis_ge", [128, 2 * E], U32)
    lt = c.sb("bis_lt", [128, 2 * E], U32)
    junk = c.sb("bis_junk", [128, max(n_p, n_s)], BF16)
    tot_full = c.ps("bis_tot", [128, 512], F32)
    tot = tot_full[:, 0:2 * E]
    b = Buf("bis")
    bcnt = Buf("bcnt")
    btot = Buf("btot")
    bj = Buf("junk")
    c.op("dve", lambda e: e.memset(lo[:], 0.0), writes=[b])
    c.op("dve", lambda e: e.memset(hi[:], 1.0), writes=[b])
    c.op("dve", lambda e: e.memset(mid[:], 0.5), writes=[b])
    c.op("dve", lambda e: e.memset(capv[:, 0:E], float(cap_p)), writes=[b])
    c.op("dve", lambda e: e.memset(capv[:, E:2 * E], float(cap_s)), writes=[b])
    ones = K["ones_f"]
    for it in range(iters):
        for v in range(2 * E):
            g, ex = divmod(v, E)
            src = A[:, ex, 0:n_p] if g == 0 else A[:, ex, n_p:n_p + n_s]
            n = n_p if g == 0 else n_s
            c.op("dve", lambda e, src=src, v=v, n=n: e.tensor_scalar(
                out=junk[:, 0:n], in0=src, scalar1=mid[:, v:v + 1], scalar2=None,
                op0=ALU.is_ge, op1=ALU.add, accum_out=cnt[:, v:v + 1]), reads=[bA, b], writes=[bj, bcnt])
        c.op("pe", lambda e: e.matmul(tot, ones[:], cnt[:], start=True, stop=True), reads=[bcnt, K["b"]], writes=[btot])
        c.op("dve", lambda e: e.tensor_tensor(out=ge[:], in0=tot, in1=capv[:], op=ALU.is_ge), reads=[btot, b], writes=[b])
        c.op("dve", lambda e: e.tensor_tensor(out=lt[:], in0=tot, in1=capv[:], op=ALU.is_lt), reads=[btot, b], writes=[b])
        c.op("dve", lambda e: e.copy_predicated(out=lo[:], mask=ge[:], data=mid[:]), reads=[b], writes=[b])
        c.op("dve", lambda e: e.copy_predicated(out=hi[:], mask=lt[:], data=mid[:]), reads=[b], writes=[b])
        c.op("dve", lambda e: e.tensor_tensor(out=mid[:], in0=lo[:], in1=hi[:], op=ALU.add), reads=[b], writes=[b])
        c.op("dve", lambda e: e.tensor_scalar(out=mid[:], in0=mid[:], scalar1=0.5, scalar2=None, op0=ALU.mult), reads=[b], writes=[b])
    if ret_A:
        return lo, b, A, bA, tot_full, btot
    return lo, b


def emit_ffn(c, K, x_dram, aff_own_dram, affT_dram, wg, wu, wd, lng, lnb, y_dram,
             tiles_p, tiles_s, n_p, n_s, cap_p, cap_s, E=16, DFF=1408, TB=8, iters=26, dbg=None, stage=99):
    NT = tiles_p + tiles_s
    FC = DFF // 128
    theta, bth = emit_thresholds(c, K, affT_dram, n_p, n_s, cap_p, cap_s, E, iters)
    if stage == 1:
        c.dma("sp", lambda e: e.dma_start(out=dbg[:, 0:2 * E], in_=theta[:]), reads=[bth])
        return
    aff = c.sb("aff_own", [128, NT, E], F32)
    gp = c.sb("gprime", [128, NT, E], F32)
    bg = Buf("gp")
    c.dma("sp", lambda e: e.dma_start(out=aff[:], in_=aff_own_dram.rearrange("(j p) e -> p j e", p=128)), writes=[bg])
    for g, (t0, t1) in enumerate(((0, tiles_p), (tiles_p, NT))):
        if t1 == t0:
            continue
        th = theta[:, g * E:(g + 1) * E].unsqueeze(1).to_broadcast([128, t1 - t0, E])
        c.op("dve", lambda e, t0=t0, t1=t1, th=th: e.tensor_tensor(out=gp[:, t0:t1, :], in0=aff[:, t0:t1, :], in1=th, op=ALU.is_ge),
             reads=[bg, bth], writes=[bg])
        c.op("dve", lambda e, t0=t0, t1=t1: e.tensor_tensor(out=gp[:, t0:t1, :], in0=gp[:, t0:t1, :], in1=aff[:, t0:t1, :], op=ALU.mult),
             reads=[bg], writes=[bg])
    if stage == 2:
        c.dma("sp", lambda e: e.dma_start(out=dbg[:, 0:NT * E], in_=gp[:].rearrange("p a b -> p (a b)")), reads=[bg])
        return
    gam = c.sb("ffn_gam", [128, D], F32)
    bet = c.sb("ffn_bet", [128, D], F32)
    bgb = Buf("gb")
    c.dma("sp", lambda e: e.dma_start(out=gam[:], in_=lng.partition_broadcast(128)), writes=[bgb])
    c.dma("sp", lambda e: e.dma_start(out=bet[:], in_=lnb.partition_broadcast(128)), writes=[bgb])
    if stage == 30:
        return
    lntmp = {"st": c.sb("ln_st", [128, 2, 6], F32), "mv": c.sb("ln_mv", [128, 2], F32), "rs": c.sb("ln_rs", [128, 1], F32), "b": Buf("lnt")}

    TOK = TB * 128
    xT = c.sb("ffn_xT", [128, 8, TOK], BF16)
    hT = c.sb("ffn_hT", [128, FC, TOK], BF16)
    yacc = c.sb("ffn_yacc", [128, TB, D], F32)
    wgs = c.sb("ffn_wg", [128, 8, DFF], BF16)
    wus = c.sb("ffn_wu", [128, 8, DFF], BF16)
    wds = c.sb("ffn_wd", [128, FC, D], BF16)
    bwg, bwu, bwd = Buf("wg"), Buf("wu"), Buf("wd")
    bxT = Buf("xT")
    bhT = [Buf("hT%d" % i) for i in range(FC)]
    byacc = [Buf("yacc%d" % i) for i in range(TB)]
    xin = [c.sb("ffn_xin%d" % i, [128, D], F32) for i in range(2)]
    bxin = [Buf("xin0"), Buf("xin1")]
    sg = [c.sb("ffn_sg%d" % i, [128, 512], BF16) for i in range(2)]
    bsg = [Buf("sg0"), Buf("sg1")]
    zt = [c.sb("ffn_z%d" % i, [128, D], F32) for i in range(2)]
    bz = [Buf("z0"), Buf("z1")]
    ot, bo = zt, bz
    ps_t = c.ps("ps_t", [128, 512], F32); bps_t = Buf("ps_t")
    ps_g = [c.ps("ps_g%d" % i, [128, 512], F32) for i in range(2)]; bps_g = [Buf("psg0"), Buf("psg1")]
    ps_u = [c.ps("ps_u%d" % i, [128, 512], F32) for i in range(2)]; bps_u = [Buf("psu0"), Buf("psu1")]
    ps_d = [c.ps("ps_d%d" % i, [128, 512], F32) for i in range(2)]; bps_d = [Buf("psd0"), Buf("psd1")]
    idf = K["ident_f"]
    cnt = [0, 0, 0, 0]
    wst = [c.sb('ffn_wst%d' % i, [128, max(DFF, D)], F32) for i in range(2)]
    bwst = [Buf('wst%d' % i) for i in range(2)]
    nblk = (NT + TB - 1) // TB
    for blk in range(nblk):
        tl0 = blk * TB
        ntl = min(TB, NT - tl0)
        ntok = ntl * 128
        for t in range(ntl):
            i = cnt[0] % 2; cnt[0] += 1
            r0 = (tl0 + t) * 128
            c.dma("sp", lambda e, i=i, r0=r0: e.dma_start(out=xin[i][:], in_=x_dram[r0:r0 + 128, :]), writes=[bxin[i]])
            if stage == 31:
                continue
            for kh in range(2):
                for k in range(4):
                    kk = kh * 4 + k
                    c.op("pe", lambda e, i=i, k=k, kk=kk: e.transpose(ps_t[:, k * 128:(k + 1) * 128], xin[i][:, kk * 128:(kk + 1) * 128], idf[:]),
                         reads=[bxin[i], K["b"]], writes=[bps_t])
                if stage == 32:
                    continue
                c.op("dve", lambda e, t=t, kh=kh: e.tensor_copy(out=xT[:, kh * 4:(kh + 1) * 4, t * 128:(t + 1) * 128],
                                                      in_=ps_t[:].rearrange("p (k t) -> p k t", k=4)),
                     reads=[bps_t], writes=[bxT])
        if stage in (3, 31, 32):
            return
        for ex in range(E):
            if stage == 4 and ex == 1:
                return
            for (dst, src, nchunk, bw) in ((wgs, wg, 8, bwg), (wus, wu, 8, bwu), (wds, wd, FC, bwd)):
                wdt = src.shape[2]
                for k in range(nchunk):
                    i = cnt[3] % 2; cnt[3] += 1
                    c.dma("sp", lambda e, i=i, k=k, src=src, ex=ex, wdt=wdt: e.dma_start(out=wst[i][:, 0:wdt], in_=src[ex, k * 128:(k + 1) * 128, :]),
                          writes=[bwst[i]])
                    c.op("pool", lambda e, i=i, k=k, dst=dst, wdt=wdt: e.tensor_copy(out=dst[:, k, :], in_=wst[i][:, 0:wdt]),
                         reads=[bwst[i]], writes=[bw])
            if stage == 40:
                return
            for fc in range(FC):
                for n0 in range(0, ntok, 512):
                    nn = min(512, ntok - n0)
                    i = cnt[1] % 2; cnt[1] += 1
                    for k in range(8):
                        c.op("pe", lambda e, i=i, k=k, fc=fc, n0=n0, nn=nn: e.matmul(
                            ps_g[i][:, 0:nn], wgs[:, k, fc * 128:(fc + 1) * 128], xT[:, k, n0:n0 + nn], start=(k == 0), stop=(k == 7)),
                            reads=[bwg, bxT], writes=[bps_g[i]])
                    for k in range(8):
                        c.op("pe", lambda e, i=i, k=k, fc=fc, n0=n0, nn=nn: e.matmul(
                            ps_u[i][:, 0:nn], wus[:, k, fc * 128:(fc + 1) * 128], xT[:, k, n0:n0 + nn], start=(k == 0), stop=(k == 7)),
                            reads=[bwu, bxT], writes=[bps_u[i]])
                    if stage == 41:
                        continue
                    c.op("act", lambda e, i=i, nn=nn: e.activation(out=sg[i][:, 0:nn], in_=ps_g[i][:, 0:nn], func=AF.Silu),
                         reads=[bps_g[i]], writes=[bsg[i]])
                    c.op("dve", lambda e, i=i, fc=fc, n0=n0, nn=nn: e.tensor_tensor(
                        out=hT[:, fc, n0:n0 + nn], in0=ps_u[i][:, 0:nn], in1=sg[i][:, 0:nn], op=ALU.mult),
                        reads=[bps_u[i], bsg[i]], writes=[bhT[fc]])
            if stage in (5, 41):
                return
            for t in range(ntl):
                for h in range(2):
                    i = cnt[2] % 2; cnt[2] += 1
                    for fc in range(FC):
                        c.op("pe", lambda e, i=i, fc=fc, t=t, h=h: e.matmul(
                            ps_d[i][:], hT[:, fc, t * 128:(t + 1) * 128], wds[:, fc, h * 512:(h + 1) * 512], start=(fc == 0), stop=(fc == FC - 1)),
                            reads=[bwd, bhT[fc]], writes=[bps_d[i]])
                    gcol = gp[:, tl0 + t, ex:ex + 1]
                    if ex == 0:
                        c.op("dve", lambda e, i=i, t=t, h=h, gcol=gcol: e.tensor_scalar(
                            out=yacc[:, t, h * 512:(h + 1) * 512], in0=ps_d[i][:], scalar1=gcol, scalar2=None, op0=ALU.mult),
                            reads=[bps_d[i], bg], writes=[byacc[t]])
                    else:
                        c.op("dve", lambda e, i=i, t=t, h=h, gcol=gcol: e.scalar_tensor_tensor(
                            out=yacc[:, t, h * 512:(h + 1) * 512], in0=ps_d[i][:], scalar=gcol, in1=yacc[:, t, h * 512:(h + 1) * 512],
                            op0=ALU.mult, op1=ALU.add), reads=[bps_d[i], bg, byacc[t]], writes=[byacc[t]])
        if stage == 6:
            return
        for t in range(ntl):
            i = cnt[0] % 2; cnt[0] += 1
            r0 = (tl0 + t) * 128
            c.dma("sp", lambda e, i=i, r0=r0: e.dma_start(out=xin[i][:], in_=x_dram[r0:r0 + 128, :]), writes=[bxin[i]])
            c.op("dve", lambda e, i=i, t=t: e.scalar_tensor_tensor(out=zt[i][:], in0=xin[i][:], scalar=float(ALPHA), in1=yacc[:, t, :],
                                                                 op0=ALU.mult, op1=ALU.add), reads=[bxin[i], byacc[t]], writes=[bz[i]])
            emit_ln(c, zt[i], bz[i], ot[i], bo[i], gam, bet, bgb, lntmp)
            c.dma("sp", lambda e, i=i, r0=r0: e.dma_start(out=y_dram[r0:r0 + 128, :], in_=ot[i][:]), reads=[bo[i]])


def emit_ffn2(c, K, x_dram, aff_own_dram, affT_dram, wg, wu, wd, lng, lnb, y_dram,
              tiles_p, tiles_s, n_p, n_s, cap_p, cap_s, E=16, DFF=1408, TB=8, CAP=256, iters=26, NWST=4):
    NT = tiles_p + tiles_s
    FC = DFF // 128
    NST = CAP // 128
    theta, bth, A, bA, ps_rank0, bps_rank = emit_thresholds(c, K, affT_dram, n_p, n_s, cap_p, cap_s, E, iters, ret_A=True)
    aff = c.sb("aff_own", [128, NT, E], F32)
    gp = c.sb("gprime", [128, NT, E], F32)
    self_ = c.sb("self", [128, NT, E], F32)
    selb = c.sb("selb", [128, NT, E], BF16)
    bg = Buf("gp")
    c.dma("sp", lambda e: e.dma_start(out=aff[:], in_=aff_own_dram.rearrange("(j p) e -> p j e", p=128)), writes=[bg])
    for g, (t0, t1) in enumerate(((0, tiles_p), (tiles_p, NT))):
        if t1 == t0:
            continue
        th = theta[:, g * E:(g + 1) * E].unsqueeze(1).to_broadcast([128, t1 - t0, E])
        c.op("dve", lambda e, t0=t0, t1=t1, th=th: e.tensor_tensor(out=self_[:, t0:t1, :], in0=aff[:, t0:t1, :], in1=th, op=ALU.is_ge),
             reads=[bg, bth], writes=[bg])
        c.op("dve", lambda e, t0=t0, t1=t1: e.tensor_tensor(out=gp[:, t0:t1, :], in0=self_[:, t0:t1, :], in1=aff[:, t0:t1, :], op=ALU.mult),
             reads=[bg], writes=[bg])
    c.op("dve", lambda e: e.tensor_copy(out=selb[:], in_=self_[:]), reads=[bg], writes=[bg])
    gam = c.sb("ffn_gam", [128, D], F32)
    bet = c.sb("ffn_bet", [128, D], F32)
    bgb = Buf("gb")
    c.dma("sp", lambda e: e.dma_start(out=gam[:], in_=lng.partition_broadcast(128)), writes=[bgb])
    c.dma("sp", lambda e: e.dma_start(out=bet[:], in_=lnb.partition_broadcast(128)), writes=[bgb])
    lntmp = {"st": c.sb("ln_st", [128, 2, 6], F32), "mv": c.sb("ln_mv", [128, 2], F32), "rs": c.sb("ln_rs", [128, 1], F32), "b": Buf("lnt")}
    stri = c.sb("f_stri", [128, 128], BF16)
    ones_b = c.sb("f_onesb", [128, 128], BF16)
    iot = c.sb("f_iota", [128, CAP], F32)
    trif = c.sb("f_trif", [128, 128], F32)
    c.op("pool", lambda e: e.affine_select(out=trif[:], in_=K["ones_f"][:], pattern=[[1, 128]], compare_op=ALU.is_gt, fill=0.0, base=0, channel_multiplier=-1),
         reads=[K["b"]], writes=[K["b"]])
    c.op("pool", lambda e: e.tensor_copy(out=stri[:], in_=trif[:]), reads=[K["b"]], writes=[K["b"]])
    c.op("pool", lambda e: e.tensor_copy(out=ones_b[:], in_=K["ones_f"][:]), reads=[K["b"]], writes=[K["b"]])
    c.op("pool", lambda e: e.iota(iot[:], pattern=[[1, CAP]], base=0, channel_multiplier=0, allow_small_or_imprecise_dtypes=True), writes=[K["b"]])

    asz = E * (n_p + n_s)
    need = 8 * DFF
    if asz >= need:
        Ab = A[:].rearrange("p e n -> p (e n)")
        wgs = Ab[:, 0:4 * DFF].bitcast(BF16).rearrange("p (k f) -> p k f", k=8)
        wus = Ab[:, 4 * DFF:8 * DFF].bitcast(BF16).rearrange("p (k f) -> p k f", k=8)
        alias = [bA]
    else:
        wgs = c.sb("ffn_wg", [128, 8, DFF], BF16)[:]
        wus = c.sb("ffn_wu", [128, 8, DFF], BF16)[:]
        alias = []
    wds = c.sb("ffn_wd", [128, FC, D], BF16)
    bwg, bwu, bwd = Buf("wg"), Buf("wu"), Buf("wd")
    xtok = c.sb("f_xtok", [128, TB, D], BF16); bxtok = Buf("xtok")
    xeT = c.sb("f_xeT", [128, 8, CAP], BF16); bxeT = Buf("xeT")
    hT = c.sb("f_hT", [128, FC, CAP], BF16); bhT = [Buf("hT%d" % i) for i in range(FC)]
    osb = c.sb("f_os", [128, NST, D], BF16); bos = Buf("os")
    OH = c.sb("f_OH", [128, TB, CAP], BF16); bOH = Buf("OH")
    OHT = c.sb("f_OHT", [128, NST, TB * 128], BF16); bOHT = Buf("OHT")
    yacc = c.sb("ffn_yacc", [128, TB, D], F32); byacc = [Buf("yacc%d" % i) for i in range(TB)]
    csel = c.sb("f_csel", [128, TB, E], BF16); bcsel = Buf("csel")
    rank = c.sb("f_rank", [128, TB, E], F32); brank = Buf("rank")
    xin = [c.sb("ffn_xin%d" % i, [128, D], F32) for i in range(2)]; bxin = [Buf("xin0"), Buf("xin1")]
    sg = [c.sb("ffn_sg%d" % i, [128, CAP], BF16) for i in range(2)]; bsg = [Buf("sg0"), Buf("sg1")]
    wst = [c.sb("ffn_wst%d" % i, [128, max(DFF, D)], F32) for i in range(NWST)]; bwst = [Buf("wst%d" % i) for i in range(NWST)]
    ps_rank = ps_rank0
    ps_tp = c.ps("ps_tp", [128, 1024], BF16); bps_tp = Buf("pstp")
    ps_ga = c.ps("ps_ga", [128, 512], F32); bps_ga = Buf("psga")
    ps_g = c.ps("ps_g", [128, 512], F32); bps_g = Buf("psg")
    ps_u = c.ps("ps_u", [128, 512], F32); bps_u = Buf("psu")
    ps_d = c.ps("ps_d", [128, 512], F32); bps_d = Buf("psd")
    ps_s = [c.ps("ps_s%d" % i, [128, 512], F32) for i in range(2)]; bps_s = [Buf("pss%d" % i) for i in range(2)]
    idb = K["ident_b"]
    cnt = [0, 0, 0, 0]
    nblk = (NT + TB - 1) // TB
    first_w = [True]
    for blk in range(nblk):
        tl0 = blk * TB
        ntl = min(TB, NT - tl0)
        for t in range(ntl):
            i = cnt[0] % 2; cnt[0] += 1
            r0 = (tl0 + t) * 128
            c.dma("sp", lambda e, i=i, r0=r0: e.dma_start(out=xin[i][:], in_=x_dram[r0:r0 + 128, :]), writes=[bxin[i]])
            c.op("dve", lambda e, i=i, t=t: e.tensor_copy(out=xtok[:, t, :], in_=xin[i][:]), reads=[bxin[i]], writes=[bxtok])
        c.op("dve", lambda e: e.memset(csel[:, 0, :], 0.0), writes=[bcsel])
        for t in range(1, ntl):
            c.op("dve", lambda e, t=t, tl0=tl0: e.tensor_tensor(out=csel[:, t, :], in0=csel[:, t - 1, :], in1=selb[:, tl0 + t - 1, :], op=ALU.add),
                 reads=[bg, bcsel], writes=[bcsel])
        c.op("pe", lambda e, ntl=ntl, tl0=tl0: e.matmul(ps_rank[:, 0:ntl * E], stri[:], selb[:, tl0:tl0 + ntl, :].rearrange("p t e -> p (t e)"), start=True, stop=False),
             reads=[bg, K["b"]], writes=[bps_rank])
        c.op("pe", lambda e, ntl=ntl: e.matmul(ps_rank[:, 0:ntl * E], ones_b[:], csel[:, 0:ntl, :].rearrange("p t e -> p (t e)"), start=False, stop=True),
             reads=[bcsel, K["b"]], writes=[bps_rank])
        c.op("dve", lambda e, ntl=ntl: e.tensor_copy(out=rank[:, 0:ntl, :].rearrange("p t e -> p (t e)"), in_=ps_rank[:, 0:ntl * E]), reads=[bps_rank], writes=[brank])
        for ex in range(E):
            for (dst, src, nchunk, bw, al) in ((wgs, wg, 8, bwg, alias), (wus, wu, 8, bwu, alias), (wds[:], wd, FC, bwd, [])):
                wdt = src.shape[2]
                for k in range(nchunk):
                    i = cnt[3] % NWST; cnt[3] += 1
                    c.dma("sp", lambda e, i=i, k=k, src=src, ex=ex, wdt=wdt: e.dma_start(out=wst[i][:, 0:wdt], in_=src[ex, k * 128:(k + 1) * 128, :]),
                          writes=[bwst[i]])
                    ce = ("act", "act", "pool", "act", "act")[cnt[3] % 5]
                    if ce == "act":
                        c.op("act", lambda e, i=i, k=k, dst=dst, wdt=wdt: e.activation(out=dst[:, k, :], in_=wst[i][:, 0:wdt], func=AF.Identity),
                             reads=[bwst[i]], writes=[bw] + (al if first_w[0] else []))
                    else:
                        c.op(ce, lambda e, i=i, k=k, dst=dst, wdt=wdt: e.tensor_copy(out=dst[:, k, :], in_=wst[i][:, 0:wdt]),
                             reads=[bwst[i]], writes=[bw] + (al if first_w[0] else []))
            first_w[0] = False
            for t in range(ntl):
                c.op("dve", lambda e, t=t, ex=ex, tl0=tl0: e.tensor_scalar(out=OH[:, t, :], in0=iot[:], scalar1=rank[:, t, ex:ex + 1], scalar2=self_[:, tl0 + t, ex:ex + 1],
                                                               op0=ALU.is_equal, op1=ALU.mult), reads=[brank, bg, K["b"]], writes=[bOH])
            kper = 512 // CAP
            for k0 in range(0, 8, kper):
                for kk in range(kper):
                    k = k0 + kk
                    for t in range(ntl):
                        c.op("pe", lambda e, k=k, kk=kk, t=t, ntl=ntl: e.matmul(ps_ga[:, kk * CAP:(kk + 1) * CAP], xtok[:, t, k * 128:(k + 1) * 128], OH[:, t, :],
                                                                               start=(t == 0), stop=(t == ntl - 1)), reads=[bxtok, bOH], writes=[bps_ga])
                c.op("dve", lambda e, k0=k0: e.tensor_copy(out=xeT[:, k0:k0 + kper, :], in_=ps_ga[:, 0:kper * CAP].rearrange("p (k n) -> p k n", k=kper)),
                     reads=[bps_ga], writes=[bxeT])
            for st_ in range(NST):
                for t in range(ntl):
                    c.op("pe", lambda e, st_=st_, t=t: e.transpose(ps_tp[:, t * 128:(t + 1) * 128], OH[:, t, st_ * 128:(st_ + 1) * 128], idb[:]),
                         reads=[bOH, K["b"]], writes=[bps_tp])
                c.op("dve", lambda e, st_=st_, ntl=ntl: e.tensor_copy(out=OHT[:, st_, 0:ntl * 128], in_=ps_tp[:, 0:ntl * 128]), reads=[bps_tp], writes=[bOHT])
            for fc in range(FC):
                i = cnt[1] % 2; cnt[1] += 1
                for k in range(8):
                    c.op("pe", lambda e, k=k, fc=fc: e.matmul(ps_g[:, 0:CAP], wgs[:, k, fc * 128:(fc + 1) * 128], xeT[:, k, :], start=(k == 0), stop=(k == 7)),
                         reads=[bwg, bxeT], writes=[bps_g])
                for k in range(8):
                    c.op("pe", lambda e, k=k, fc=fc: e.matmul(ps_u[:, 0:CAP], wus[:, k, fc * 128:(fc + 1) * 128], xeT[:, k, :], start=(k == 0), stop=(k == 7)),
                         reads=[bwu, bxeT], writes=[bps_u])
                c.op("act", lambda e, i=i: e.activation(out=sg[i][:], in_=ps_g[:, 0:CAP], func=AF.Silu), reads=[bps_g], writes=[bsg[i]])
                c.op("dve", lambda e, i=i, fc=fc: e.tensor_tensor(out=hT[:, fc, :], in0=ps_u[:, 0:CAP], in1=sg[i][:], op=ALU.mult),
                     reads=[bps_u, bsg[i]], writes=[bhT[fc]])
            for st_ in range(NST):
                for h in range(2):
                    for fc in range(FC):
                        c.op("pe", lambda e, st_=st_, h=h, fc=fc: e.matmul(ps_d[:], hT[:, fc, st_ * 128:(st_ + 1) * 128], wds[:, fc, h * 512:(h + 1) * 512],
                                                                        start=(fc == 0), stop=(fc == FC - 1)), reads=[bwd, bhT[fc]], writes=[bps_d])
                    c.op("dve", lambda e, st_=st_, h=h: e.tensor_copy(out=osb[:, st_, h * 512:(h + 1) * 512], in_=ps_d[:]), reads=[bps_d], writes=[bos])
            for t in range(ntl):
                for h in range(2):
                    i = cnt[2] % 2; cnt[2] += 1
                    for st_ in range(NST):
                        c.op("pe", lambda e, i=i, st_=st_, t=t, h=h: e.matmul(ps_s[i][:], OHT[:, st_, t * 128:(t + 1) * 128], osb[:, st_, h * 512:(h + 1) * 512],
                                                                           start=(st_ == 0), stop=(st_ == NST - 1)), reads=[bOHT, bos], writes=[bps_s[i]])
                    gcol = gp[:, tl0 + t, ex:ex + 1]
                    if ex == 0:
                        c.op("dve", lambda e, i=i, t=t, h=h, gcol=gcol: e.tensor_scalar(
                            out=yacc[:, t, h * 512:(h + 1) * 512], in0=ps_s[i][:], scalar1=gcol, scalar2=None, op0=ALU.mult),
                            reads=[bps_s[i], bg], writes=[byacc[t]])
                    else:
                        c.op("dve", lambda e, i=i, t=t, h=h, gcol=gcol: e.scalar_tensor_tensor(
                            out=yacc[:, t, h * 512:(h + 1) * 512], in0=ps_s[i][:], scalar=gcol, in1=yacc[:, t, h * 512:(h + 1) * 512],
                            op0=ALU.mult, op1=ALU.add), reads=[bps_s[i], bg, byacc[t]], writes=[byacc[t]])
        for t in range(ntl):
            i = cnt[0] % 2; cnt[0] += 1
            r0 = (tl0 + t) * 128
            c.dma("sp", lambda e, i=i, r0=r0: e.dma_start(out=xin[i][:], in_=x_dram[r0:r0 + 128, :]), writes=[bxin[i]])
            c.op("dve", lambda e, i=i, t=t: e.scalar_tensor_tensor(out=xin[i][:], in0=xin[i][:], scalar=float(ALPHA), in1=yacc[:, t, :],
                                                                 op0=ALU.mult, op1=ALU.add), reads=[bxin[i], byacc[t]], writes=[bxin[i]])
            emit_ln(c, xin[i], bxin[i], xin[i], bxin[i], gam, bet, bgb, lntmp)
            c.dma("sp", lambda e, i=i, r0=r0: e.dma_start(out=y_dram[r0:r0 + 128, :], in_=xin[i][:]), reads=[bxin[i]])


NAW, MLW = 512, 512
EVEN_IN = 3600
SEQ = 2048
NTILE = 16


def na_table_host(rpb):
    NEG = -30000.0
    qc = np.arange(64)
    kc = np.arange(64)
    c0 = np.clip(qc - 8, 0, 48)
    colvalid = (kc[:, None] >= c0[None, :]) & (kc[:, None] < c0[None, :] + 16)
    dc = np.clip(kc[:, None] - qc[None, :] + 15, 0, 30)
    T = np.full((8, 3, 128, 16, 64), NEG, np.float32)
    for i in range(16):
        for half in range(2):
            dr = i - 1 + half
            if dr < 0 or dr > 14:
                continue
            vals = np.where(colvalid[None], rpb[:, dr][:, dc], NEG).astype(np.float32)
            sl = slice(half * 64, half * 64 + 64)
            T[:, 0, sl, i, :] = vals
            if half == 1:
                T[:, 1, sl, i, :] = vals
            else:
                T[:, 2, sl, i, :] = vals
    return T


class PSB:
    def __init__(self, c, n=8):
        self.t = [c.ps("psb%d" % i, [128, 512], F32) for i in range(n)]
        self.b = [Buf("psb%d" % i) for i in range(n)]


def load_cast_weight(c, dst, bdst, src, ncol, st, bst, cnt):
    for k in range(8):
        for c0 in range(0, ncol, 1024):
            cw = min(1024, ncol - c0)
            i = cnt[0] % len(st); cnt[0] += 1
            c.dma("sp", lambda e, i=i, k=k, c0=c0, cw=cw: e.dma_start(out=st[i][:, 0:cw], in_=src[k * 128:(k + 1) * 128, c0:c0 + cw]), writes=[bst[i]])
            c.op("pool", lambda e, i=i, k=k, c0=c0, cw=cw: e.tensor_copy(out=dst[:, k, c0:c0 + cw], in_=st[i][:, 0:cw]), reads=[bst[i]], writes=[bdst])


def emit_xT(c, K, P, x_dram, r0, ntiles, xT, bxT, xin, bxin, cnt):
    idf = K["ident_f"]
    for t in range(ntiles):
        i = cnt[0] % 2; cnt[0] += 1
        c.dma("sp", lambda e, i=i, t=t: e.dma_start(out=xin[i][:], in_=x_dram[r0 + t * 128:r0 + (t + 1) * 128, :]), writes=[bxin[i]])
        for kh in range(2):
            for k in range(4):
                kk = kh * 4 + k
                c.op("pe", lambda e, i=i, k=k, kk=kk: e.transpose(P.t[0][:, k * 128:(k + 1) * 128], xin[i][:, kk * 128:(kk + 1) * 128], idf[:]),
                     reads=[bxin[i], K["b"]], writes=[P.b[0]])
            c.op("dve", lambda e, t=t, kh=kh: e.tensor_copy(out=xT[:, kh * 4:(kh + 1) * 4, t * 128:(t + 1) * 128],
                                                         in_=P.t[0][:].rearrange("p (k t) -> p k t", k=4)), reads=[P.b[0]], writes=[bxT])


def proj_fm(c, P, W, bW, xT, bxT, col0, dst, bdst, scale, ntok, pc):
    for n0 in range(0, ntok, 512):
        j = 1 + (pc[0] % 2); pc[0] += 1
        for k in range(8):
            c.op("pe", lambda e, j=j, k=k, n0=n0: e.matmul(P.t[j][:], W[:, k, col0:col0 + 128], xT[:, k, n0:n0 + 512], start=(k == 0), stop=(k == 7)),
                 reads=[bW, bxT], writes=[P.b[j]])
        c.op("dve", lambda e, j=j, n0=n0: e.tensor_scalar(out=dst[:, n0:n0 + 512], in0=P.t[j][:], scalar1=float(scale), scalar2=None, op0=ALU.mult),
             reads=[P.b[j]], writes=[bdst])


def proj_tm(c, P, W, bW, xT, bxT, col0, ncol, dstfn, bdst, scale, ntile, pc):
    g = max(1, 512 // ncol)
    g = min(g, 4)
    for t0 in range(0, ntile, g):
        j = 1 + (pc[0] % 2); pc[0] += 1
        gg = min(g, ntile - t0)
        for ti in range(gg):
            t = t0 + ti
            for k in range(8):
                c.op("pe", lambda e, j=j, k=k, t=t, ti=ti: e.matmul(P.t[j][:, ti * ncol:(ti + 1) * ncol], xT[:, k, t * 128:(t + 1) * 128], W[:, k, col0:col0 + ncol],
                                                                     start=(k == 0), stop=(k == 7)), reads=[bW, bxT], writes=[P.b[j]])
        c.op("dve", lambda e, j=j, t0=t0, gg=gg: e.tensor_scalar(out=dstfn(t0, gg), in0=P.t[j][:, 0:gg * ncol].rearrange("p (g n) -> p g n", g=gg),
                                                                scalar1=float(scale), scalar2=None, op0=ALU.mult), reads=[P.b[j]], writes=[bdst])


def emit_mix_a(c, K, P, x_dram, w_in, w_out, gate_bias, norm_g, lng, lnb, w_r, na_tab, x1_dram, aff_dram, nseq, stage=99, dbg=None):
    Wout = c.sb("a_wout", [128, 8, D], BF16); bWout = Buf("wout")
    stg = [c.sb("a_stg%d" % i, [128, 8, 128], F32) for i in range(2)]; bstg = [Buf("stg0"), Buf("stg1")]
    st = [t[:].rearrange("p k n -> p (k n)") for t in stg]; bst = bstg
    scnt = [0]
    load_cast_weight(c, Wout, bWout, w_out, D, st, bst, scnt)
    WS = {n: c.sb("a_ws_" + n, [128, 8, 128], BF16) for n in ("q", "k", "v", "o")}
    bWS = {n: Buf("ws_" + n) for n in WS}
    WG = c.sb("a_wg16", [128, 8, 16], BF16); bWG = Buf("wg16")

    def load_w(name, col0, ncol=128):
        dst, bd = (WS[name], bWS[name]) if name != "g" else (WG, bWG)
        i = scnt[0] % 2; scnt[0] += 1
        c.dma("sp", lambda e, i=i: e.dma_start(out=stg[i][:, :, 0:ncol], in_=w_in[:, col0:col0 + ncol].rearrange("(k p) n -> p k n", p=128)), writes=[bstg[i]])
        c.op("pool", lambda e, i=i: e.tensor_copy(out=dst[:, :, 0:ncol], in_=stg[i][:, :, 0:ncol]), reads=[bstg[i]], writes=[bd])
    Wr = c.sb("a_wr", [128, 8, 16], F32); bWr = Buf("wr")
    c.dma("sp", lambda e: e.dma_start(out=Wr[:], in_=w_r.rearrange("(k p) e -> p k e", p=128)), writes=[bWr])
    gb = c.sb("a_gb", [128, 16], F32)
    ng = c.sb("a_ng", [128, 512], F32)
    gam = c.sb("a_gam", [128, D], F32)
    bet = c.sb("a_bet", [128, D], F32)
    bgb = Buf("gb")
    c.dma("sp", lambda e: e.dma_start(out=gb[:], in_=gate_bias.partition_broadcast(128)), writes=[bgb])
    c.dma("sp", lambda e: e.dma_start(out=ng[:], in_=norm_g.partition_broadcast(128)), writes=[bgb])
    c.dma("sp", lambda e: e.dma_start(out=gam[:], in_=lng.partition_broadcast(128)), writes=[bgb])
    c.dma("sp", lambda e: e.dma_start(out=bet[:], in_=lnb.partition_broadcast(128)), writes=[bgb])
    ones_f = K["ones_f"]
    tri = c.sb("a_tri", [128, 128], F32)
    trir = c.sb("a_trir", [128, 128], F32)
    ones_b = c.sb("a_onesb", [128, 128], BF16)
    c.op("pool", lambda e: e.affine_select(out=tri[:], in_=ones_f[:], pattern=[[1, 128]], compare_op=ALU.is_ge, fill=0.0, base=0, channel_multiplier=-1),
         reads=[K["b"]], writes=[K["b"]])
    c.op("pool", lambda e: e.affine_select(out=trir[:], in_=ones_f[:], pattern=[[-1, 128]], compare_op=ALU.is_ge, fill=0.0, base=0, channel_multiplier=1),
         reads=[K["b"]], writes=[K["b"]])
    c.op("pool", lambda e: e.tensor_copy(out=ones_b[:], in_=ones_f[:]), reads=[K["b"]], writes=[K["b"]])
    lntmp = {"st": c.sb("ln_st", [128, 2, 6], F32), "mv": c.sb("ln_mv", [128, 2], F32), "rs": c.sb("ln_rs", [128, 1], F32), "b": Buf("lnt")}

    xT = c.sb("a_xT", [128, 8, SEQ], BF16); bxT = Buf("xT")
    yT = c.sb("a_yT", [128, 8, SEQ], BF16); byT = [Buf("yT%d" % i) for i in range(8)]
    xin = [c.sb("a_xin%d" % i, [128, D], F32) for i in range(2)]; bxin = [Buf("xin0"), Buf("xin1")]
    QT = c.sb("a_QT", [128, SEQ], BF16); bQT = Buf("QT")
    KT = c.sb("a_KT", [128, SEQ], BF16); bKT = Buf("KT")
    Vb = c.sb("a_V", [128, NTILE * 132], BF16); bV = Buf("V")
    Ktm = c.sb("a_Ktm", [128, NTILE, 128], BF16); bKtm = Buf("Ktm")
    sgob = c.sb("a_sgob", [128, NTILE, 128], F32); bsgob = Buf("sgob")
    hacc = c.sb("a_hacc", [128, NTILE, 128], F32); bhacc = Buf("hacc")
    EB = c.sb("a_EB", [128, 3, 16, 64], F32); bEB = Buf("EB")
    pexp = [c.sb("a_pexp%d" % i, [128, 5 * 64], F32) for i in range(3)]; bpexp = [Buf("pexp%d" % i) for i in range(3)]
    PT = [c.sb("a_PT%d" % i, [128, 5, 64], BF16) for i in range(3)]; bPT = [Buf("PT%d" % i) for i in range(3)]
    rec = c.sb("a_rec", [128, 512], F32); brec = Buf("rec")
    G = c.sb("a_G", [128, NTILE, 16], F32); bG = Buf("G")
    nlf = c.sb("a_nlf", [128, NTILE, 8], F32)
    uu = c.sb("a_u", [128, NTILE, 8], F32)
    vv = c.sb("a_v", [128, NTILE, 8], F32)
    eL = c.sb("a_eL", [128, NTILE, 8], F32)
    bgate = Buf("gate")
    CN = [c.sb("a_CN%d" % i, [128, 129], F32) for i in range(2)]; bCN = [Buf("CN0"), Buf("CN1")]
    CNb = [c.sb("a_CNb%d" % i, [128, 129], BF16) for i in range(2)]; bCNb = [Buf("CNb0"), Buf("CNb1")]
    ctmp = [c.sb("a_ctmp%d" % i, [128, 129], F32) for i in range(2)]; bctmp = [Buf("ctmp0"), Buf("ctmp1")]
    St = [c.sb("a_St%d" % i, [128, 128], BF16) for i in range(2)]; bSt = [Buf("St0"), Buf("St1")]
    Kt = [c.sb("a_Kt%d" % i, [128, 128], BF16) for i in range(2)]; bKt = [Buf("Kt0"), Buf("Kt1")]
    sm = [c.sb("a_sm%d" % i, [128, 8], F32) for i in range(2)]; bsm = [Buf("sm0"), Buf("sm1")]
    lnw = c.sb("a_lnw", [128, NTILE, 128], F32); blnw = Buf("lnw")
    lns = c.sb("a_lns", [128, NTILE, 4], F32); blns = Buf("lns")
    x1T = c.sb("a_x1T", [128, 8, 128], F32); bx1T = Buf("x1T")
    rt = c.sb("a_rt", [128, 16], F32); brt = Buf("rt")
    rtm = c.sb("a_rtm", [128, 4], F32)
    affo = [c.sb("a_aff%d" % i, [128, 16], F32) for i in range(2)]; baffo = [Buf("aff0"), Buf("aff1")]
    xc = [0]; pc = [0]; sc = [0]; hc = [0]; zc = [0]

    for s in range(nseq):
        r0 = s * SEQ
        emit_xT(c, K, P, x_dram, r0, NTILE, xT, bxT, xin, bxin, xc)
        for hp in range(4):
            load_w("q", hp * 128); load_w("k", 512 + hp * 128); load_w("v", 1024 + hp * 128)
            proj_fm(c, P, WS["q"], bWS["q"], xT, bxT, 0, QT, bQT, 0.125, SEQ, pc)
            proj_fm(c, P, WS["k"], bWS["k"], xT, bxT, 0, KT, bKT, 1.0, SEQ, pc)
            Vv = Vb[:, 0:NTILE * 128].rearrange("p (t n) -> p t n", t=NTILE)
            proj_tm(c, P, WS["v"], bWS["v"], xT, bxT, 0, 128, lambda t0, gg: Vv[:, t0:t0 + gg, :], bV, 1.0, NTILE, pc)
            for hh in range(2):
                pb = hh * 64
                c.dma("sp", lambda e, hp=hp, hh=hh: e.dma_start(out=EB[:].rearrange("p v i q -> p v (i q)"),
                                                         in_=na_tab[2 * hp + hh].rearrange("v p i q -> p v (i q)")), writes=[bEB])
                c.op("act", lambda e: e.activation(out=EB[:].rearrange("p v i q -> p (v i q)"), in_=EB[:].rearrange("p v i q -> p (v i q)"), func=AF.Exp),
                     reads=[bEB], writes=[bEB])
                SB = (3, 4, 7)

                def na_front(r, hh=hh, pb=pb):
                    rs = min(max(r - 4, 0), 24)
                    a0, a1 = rs // 2, (rs + 7) // 2
                    nt = a1 - a0 + 1
                    si = r % 3
                    psS = P.t[SB[si]]; bS = P.b[SB[si]]
                    for j in range(nt):
                        a = a0 + j
                        c.op("pe", lambda e, j=j, a=a, psS=psS, pb=pb, r=r: e.matmul(psS[:, j * 64:(j + 1) * 64], KT[pb:pb + 64, a * 128:(a + 1) * 128],
                                                                                    QT[pb:pb + 64, r * 64:(r + 1) * 64], start=True, stop=True),
                             reads=[bKT, bQT], writes=[bS])
                    c.op("act", lambda e, si=si, psS=psS, nt=nt: e.activation(out=pexp[si][:, 0:nt * 64], in_=psS[:, 0:nt * 64], func=AF.Exp),
                         reads=[bS], writes=[bpexp[si]])
                    for j in range(nt):
                        a = a0 + j
                        var = 1 if 2 * a < rs else (2 if 2 * a + 1 >= rs + 8 else 0)
                        ii = 2 * a - r + 7 + 1
                        c.op("dve", lambda e, si=si, j=j, var=var, ii=ii: e.tensor_tensor(out=PT[si][:, j, :], in0=pexp[si][:, j * 64:(j + 1) * 64],
                                                                                   in1=EB[:, var, ii, :], op=ALU.mult),
                             reads=[bpexp[si], bEB], writes=[bPT[si]])

                def na_back(r, hh=hh, pb=pb, hp=hp):
                    rs = min(max(r - 4, 0), 24)
                    a0, a1 = rs // 2, (rs + 7) // 2
                    nt = a1 - a0 + 1
                    si = r % 3
                    rr = r % 8
                    for j in range(nt):
                        a = a0 + j
                        c.op("pe", lambda e, si=si, j=j, a=a, rr=rr, nt=nt: e.matmul(P.t[5][:, rr * 64:(rr + 1) * 64], Vv[:, a, :], PT[si][:, j, :],
                                                                                  start=(j == 0), stop=(j == nt - 1)), reads=[bV, bPT[si]], writes=[P.b[5]])
                    for j in range(nt):
                        c.op("pe", lambda e, si=si, j=j, rr=rr, nt=nt: e.matmul(P.t[6][:, rr * 64:(rr + 1) * 64], ones_b[:], PT[si][:, j, :],
                                                                             start=(j == 0), stop=(j == nt - 1)), reads=[K["b"], bPT[si]], writes=[P.b[6]])
                    if rr == 7:
                        q0 = (r - 7) * 64
                        c.op("dve", lambda e: e.reciprocal(out=rec[:], in_=P.t[6][:]), reads=[P.b[6]], writes=[brec])
                        c.op("dve", lambda e, pb=pb, hp=hp, q0=q0: e.tensor_tensor(out=yT[pb:pb + 64, hp, q0:q0 + 512], in0=P.t[5][pb:pb + 64, :],
                                                                                in1=rec[pb:pb + 64, :], op=ALU.mult),
                             reads=[P.b[5], brec], writes=[byT[hp]])
                LOOK = 2
                for r in range(32 + LOOK):
                    if r < 32:
                        na_front(r)
                    if r >= LOOK:
                        na_back(r - LOOK)
        if stage == 1:
            continue
        load_w("g", 3584, 16)
        Gps = P.t[1][:, 0:NTILE * 16].rearrange("p (t n) -> p t n", t=NTILE)
        for t in range(NTILE):
            for k in range(8):
                c.op("pe", lambda e, t=t, k=k: e.matmul(Gps[:, t, :], xT[:, k, t * 128:(t + 1) * 128], WG[:, k, :], start=(k == 0), stop=(k == 7)),
                     reads=[bxT, bWG], writes=[P.b[1]])
        c.op("dve", lambda e: e.tensor_tensor(out=G[:], in0=Gps, in1=gb[:].unsqueeze(1).to_broadcast([128, NTILE, 16]), op=ALU.add),
             reads=[P.b[1], bgb], writes=[bG])
        c.op("act", lambda e: e.activation(out=nlf[:], in_=G[:, :, 8:16], func=AF.Exp, scale=-1.0), reads=[bG], writes=[bgate])
        c.op("act", lambda e: e.activation(out=nlf[:], in_=nlf[:], func=AF.Ln, bias=1.0, scale=1.0), reads=[bgate], writes=[bgate])
        cum = P.t[2][:, 0:NTILE * 8].rearrange("p (t n) -> p t n", t=NTILE)
        tot = P.t[2][:, 256:256 + NTILE * 8].rearrange("p (t n) -> p t n", t=NTILE)
        for t in range(NTILE):
            c.op("pe", lambda e, t=t: e.matmul(cum[:, t, 0:4], tri[:], nlf[:, t, 0:4], start=True, stop=True), reads=[bgate, K["b"]], writes=[P.b[2]])
            c.op("pe", lambda e, t=t: e.matmul(cum[:, t, 4:8], trir[:], nlf[:, t, 4:8], start=True, stop=True), reads=[bgate, K["b"]], writes=[P.b[2]])
            c.op("pe", lambda e, t=t: e.matmul(tot[:, t, :], ones_f[:], nlf[:, t, :], start=True, stop=True), reads=[bgate, K["b"]], writes=[P.b[2]])
        c.op("act", lambda e: e.activation(out=uu[:], in_=cum, func=AF.Exp, scale=-1.0), reads=[P.b[2]], writes=[bgate])
        c.op("act", lambda e: e.activation(out=eL[:], in_=tot, func=AF.Exp, scale=-1.0), reads=[P.b[2]], writes=[bgate])
        c.op("dve", lambda e: e.tensor_tensor(out=vv[:], in0=cum, in1=G[:, :, 0:8], op=ALU.add), reads=[P.b[2], bG], writes=[bgate])
        c.op("act", lambda e: e.activation(out=vv[:], in_=vv[:], func=AF.Exp), reads=[bgate], writes=[bgate])
        for h in range(4):
            load_w("q", 1536 + h * 128); load_w("k", 2048 + h * 128); load_w("v", 2560 + h * 128); load_w("o", 3072 + h * 128)
            proj_fm(c, P, WS["q"], bWS["q"], xT, bxT, 0, QT, bQT, 1.0, SEQ, pc)
            proj_fm(c, P, WS["k"], bWS["k"], xT, bxT, 0, KT, bKT, 128 ** -0.5, SEQ, pc)
            proj_tm(c, P, WS["k"], bWS["k"], xT, bxT, 0, 128, lambda t0, gg: Ktm[:, t0:t0 + gg, :], bKtm, 128 ** -0.5, NTILE, pc)
            Va = Vb[:, 0:NTILE * 129].rearrange("p (t n) -> p t n", t=NTILE)
            proj_tm(c, P, WS["v"], bWS["v"], xT, bxT, 0, 128, lambda t0, gg: Va[:, t0:t0 + gg, 0:128], bV, 1.0, NTILE, pc)
            c.op("pool", lambda e: e.memset(Va[:, :, 128:129], 1.0), reads=[], writes=[bV])
            proj_tm(c, P, WS["o"], bWS["o"], xT, bxT, 0, 128, lambda t0, gg: sgob[:, t0:t0 + gg, :], bsgob, 1.0, NTILE, pc)
            c.op("act", lambda e: e.activation(out=sgob[:].rearrange("p t n -> p (t n)"), in_=sgob[:].rearrange("p t n -> p (t n)"), func=AF.Sigmoid),
                 reads=[bsgob], writes=[bsgob])
            c.op("pool", lambda e: e.memset(hacc[:], 0.0), writes=[bhacc])
            for dr in range(2):
                c.op("pool", lambda e, dr=dr: e.memset(CN[dr][:], 0.0), writes=[bCN[dr]])
                c.op("pool", lambda e, dr=dr: e.memset(CNb[dr][:], 0.0), writes=[bCNb[dr]])
            for step in range(NTILE):
              for dr in range(2):
                    gi = dr * 4 + h
                    mask = tri if dr == 0 else trir
                    ci = step if dr == 0 else NTILE - 1 - step
                    si = sc[0] % 2; sc[0] += 1
                    psS = P.t[3 + si]; bS = P.b[3 + si]
                    psH = P.t[5] if dr == 0 else P.t[7]; bH = P.b[5] if dr == 0 else P.b[7]
                    smd = sm[dr]; bsmd = bsm[dr]
                    cs = slice(ci * 128, (ci + 1) * 128)
                    c.op("pe", lambda e, psS=psS, cs=cs: e.matmul(psS[:, 0:128], KT[:, cs], QT[:, cs], start=True, stop=True), reads=[bKT, bQT], writes=[bS])
                    c.op("dve", lambda e, psS=psS, si=si, ci=ci, gi=gi, mask=mask: e.scalar_tensor_tensor(
                        out=St[si][:], in0=psS[:, 0:128], scalar=vv[:, ci, gi:gi + 1], in1=mask[:], op0=ALU.mult, op1=ALU.mult),
                        reads=[bS, bgate, K["b"]], writes=[bSt[si]])
                    c.op("pe", lambda e, psH=psH, si=si, ci=ci: e.matmul(psH[:, 0:129], St[si][:], Va[:, ci, :], start=True, stop=False),
                         reads=[bSt[si], bV], writes=[bH])
                    c.op("pe", lambda e, psH=psH, cs=cs, dr=dr: e.matmul(psH[:, 0:129], QT[:, cs], CNb[dr][:], start=False, stop=True),
                         reads=[bQT, bCNb[dr]], writes=[bH])
                    c.op("dve", lambda e, psH=psH, ci=ci, gi=gi, smd=smd: e.tensor_tensor(out=smd[:, 0:1], in0=psH[:, 128:129], in1=uu[:, ci, gi:gi + 1], op=ALU.mult),
                         reads=[bH, bgate], writes=[bsmd])
                    c.op("dve", lambda e, smd=smd: e.tensor_scalar(out=smd[:, 4:5], in0=smd[:, 0:1], scalar1=-1.0, scalar2=None, op0=ALU.mult), reads=[bsmd], writes=[bsmd])
                    c.op("dve", lambda e, smd=smd: e.tensor_tensor(out=smd[:, 5:6], in0=smd[:, 0:1], in1=smd[:, 4:5], op=ALU.max), reads=[bsmd], writes=[bsmd])
                    c.op("dve", lambda e, smd=smd: e.tensor_scalar(out=smd[:, 1:2], in0=smd[:, 5:6], scalar1=1.0, scalar2=None, op0=ALU.max), reads=[bsmd], writes=[bsmd])
                    c.op("dve", lambda e, smd=smd: e.reciprocal(out=smd[:, 2:3], in_=smd[:, 1:2]), reads=[bsmd], writes=[bsmd])
                    c.op("dve", lambda e, ci=ci, gi=gi, smd=smd: e.tensor_tensor(out=smd[:, 3:4], in0=smd[:, 2:3], in1=uu[:, ci, gi:gi + 1], op=ALU.mult),
                         reads=[bsmd, bgate], writes=[bsmd])
                    c.op("dve", lambda e, psH=psH, ci=ci, smd=smd: e.scalar_tensor_tensor(out=hacc[:, ci, :], in0=psH[:, 0:128], scalar=smd[:, 3:4], in1=hacc[:, ci, :],
                                                                               op0=ALU.mult, op1=ALU.add), reads=[bH, bsmd, bhacc], writes=[bhacc])
                    c.op("pool", lambda e, si=si, ci=ci, gi=gi: e.tensor_scalar(out=Kt[si][:], in0=Ktm[:, ci, :], scalar1=vv[:, ci, gi:gi + 1], scalar2=None, op0=ALU.mult),
                         reads=[bKtm, bgate], writes=[bKt[si]])
                    c.op("pe", lambda e, si=si, ci=ci: e.matmul(P.t[6][:, 0:129], Kt[si][:], Va[:, ci, :], start=True, stop=True),
                         reads=[bKt[si], bV], writes=[P.b[6]])
                    c.op("dve", lambda e, dr=dr: e.tensor_tensor(out=ctmp[dr][:], in0=P.t[6][:, 0:129], in1=CN[dr][:], op=ALU.add), reads=[P.b[6], bCN[dr]], writes=[bctmp[dr]])
                    c.op("dve", lambda e, ci=ci, gi=gi, dr=dr: e.tensor_scalar(out=CN[dr][:], in0=ctmp[dr][:], scalar1=eL[:, ci, gi:gi + 1], scalar2=None, op0=ALU.mult),
                         reads=[bctmp[dr], bgate], writes=[bCN[dr]])
                    c.op("pool", lambda e, ci=ci, gi=gi, dr=dr: e.tensor_scalar(out=CNb[dr][:], in0=ctmp[dr][:], scalar1=eL[:, ci, gi:gi + 1], scalar2=None, op0=ALU.mult),
                         reads=[bctmp[dr], bgate], writes=[bCNb[dr]])
            c.op("dve", lambda e: e.tensor_reduce(out=lns[:, :, 0], in_=hacc[:], axis=AX.X, op=ALU.add), reads=[bhacc], writes=[blns])
            c.op("dve", lambda e: e.tensor_scalar(out=lns[:, :, 0], in0=lns[:, :, 0], scalar1=1.0 / 128, scalar2=None, op0=ALU.mult), reads=[blns], writes=[blns])
            c.op("dve", lambda e: e.tensor_tensor(out=lnw[:], in0=hacc[:], in1=lns[:, :, 0:1].to_broadcast([128, NTILE, 128]), op=ALU.subtract),
                 reads=[bhacc, blns], writes=[blnw])
            c.op("pool", lambda e: e.tensor_tensor(out=hacc[:], in0=lnw[:], in1=lnw[:], op=ALU.mult), reads=[blnw, bhacc], writes=[bhacc])
            c.op("dve", lambda e: e.tensor_reduce(out=lns[:, :, 1], in_=hacc[:], axis=AX.X, op=ALU.add), reads=[bhacc], writes=[blns])
            c.op("dve", lambda e: e.tensor_scalar(out=lns[:, :, 1], in0=lns[:, :, 1], scalar1=1.0 / 128, scalar2=EPS, op0=ALU.mult, op1=ALU.add), reads=[blns], writes=[blns])
            c.op("act", lambda e: e.activation(out=lns[:, :, 2], in_=lns[:, :, 1], func=AF.Sqrt), reads=[blns], writes=[blns])
            c.op("dve", lambda e: e.reciprocal(out=lns[:, :, 3], in_=lns[:, :, 2]), reads=[blns], writes=[blns])
            c.op("dve", lambda e: e.tensor_tensor(out=lnw[:], in0=lnw[:], in1=lns[:, :, 3:4].to_broadcast([128, NTILE, 128]), op=ALU.mult), reads=[blnw, blns], writes=[blnw])
            c.op("pool", lambda e, h=h: e.tensor_tensor(out=lnw[:], in0=lnw[:], in1=ng[:, h * 128:(h + 1) * 128].unsqueeze(1).to_broadcast([128, NTILE, 128]), op=ALU.mult),
                 reads=[blnw, bgb], writes=[blnw])
            c.op("dve", lambda e: e.tensor_tensor(out=lnw[:], in0=lnw[:], in1=sgob[:], op=ALU.mult), reads=[blnw, bsgob], writes=[blnw])
            for tq in range(4):
                for k in range(4):
                    t = tq * 4 + k
                    c.op("pe", lambda e, t=t, k=k: e.transpose(P.t[0][:, k * 128:(k + 1) * 128], lnw[:, t, :], K["ident_f"][:]), reads=[blnw, K["b"]], writes=[P.b[0]])
                c.op("dve", lambda e, tq=tq, h=h: e.tensor_copy(out=yT[:, 4 + h, tq * 512:(tq + 1) * 512], in_=P.t[0][:]), reads=[P.b[0]], writes=[byT[4 + h]])
        if stage == 2:
            continue
        for t in range(NTILE):
            i = xc[0] % 2; xc[0] += 1
            zi = i
            rr0 = r0 + t * 128
            c.dma("sp", lambda e, i=i, rr0=rr0: e.dma_start(out=xin[i][:], in_=x_dram[rr0:rr0 + 128, :]), writes=[bxin[i]])
            for hf in range(2):
                for k in range(8):
                    c.op("pe", lambda e, hf=hf, k=k, t=t: e.matmul(P.t[1 + hf][:], yT[:, k, t * 128:(t + 1) * 128], Wout[:, k, hf * 512:(hf + 1) * 512],
                                                                start=(k == 0), stop=(k == 7)), reads=[byT[k], bWout], writes=[P.b[1 + hf]])
                c.op("dve", lambda e, hf=hf, i=i, zi=zi: e.scalar_tensor_tensor(out=xin[zi][:, hf * 512:(hf + 1) * 512], in0=xin[i][:, hf * 512:(hf + 1) * 512], scalar=float(ALPHA),
                                                                       in1=P.t[1 + hf][:], op0=ALU.mult, op1=ALU.add), reads=[bxin[i], P.b[1 + hf]], writes=[bxin[zi]])
            emit_ln(c, xin[zi], bxin[zi], xin[zi], bxin[zi], gam, bet, bgb, lntmp)
            c.dma("sp", lambda e, zi=zi, rr0=rr0: e.dma_start(out=x1_dram[rr0:rr0 + 128, :], in_=xin[zi][:]), reads=[bxin[zi]])
            emit_router(c, K, P, xin[zi], bxin[zi], Wr, bWr, x1T, bx1T, rt, brt, rtm, affo[zi], baffo[zi], aff_dram, rr0)


def emit_router(c, K, P, z, bz, Wr, bWr, x1T, bx1T, rt, brt, rtm, affo, baffo, aff_dram, rr0):
    idf = K["ident_f"]
    for kh in range(2):
        for k in range(4):
            kk = kh * 4 + k
            c.op("pe", lambda e, k=k, kk=kk: e.transpose(P.t[0][:, k * 128:(k + 1) * 128], z[:, kk * 128:(kk + 1) * 128], idf[:]), reads=[bz, K["b"]], writes=[P.b[0]])
        c.op("dve", lambda e, kh=kh: e.tensor_copy(out=x1T[:, kh * 4:(kh + 1) * 4, :], in_=P.t[0][:].rearrange("p (k t) -> p k t", k=4)), reads=[P.b[0]], writes=[bx1T])
    for k in range(8):
        c.op("pe", lambda e, k=k: e.matmul(P.t[3][:, 0:16], x1T[:, k, :], Wr[:, k, :], start=(k == 0), stop=(k == 7)), reads=[bx1T, bWr], writes=[P.b[3]])
    c.op("dve", lambda e: e.tensor_reduce(out=rtm[:, 0:1], in_=P.t[3][:, 0:16], axis=AX.X, op=ALU.max), reads=[P.b[3]], writes=[brt])
    c.op("dve", lambda e: e.tensor_scalar(out=rtm[:, 1:2], in0=rtm[:, 0:1], scalar1=-1.0, scalar2=None, op0=ALU.mult), reads=[brt], writes=[brt])
    c.op("act", lambda e: e.activation(out=rt[:], in_=P.t[3][:, 0:16], func=AF.Exp, bias=rtm[:, 1:2], scale=1.0, accum_out=rtm[:, 2:3]), reads=[P.b[3], brt], writes=[brt])
    c.op("dve", lambda e: e.reciprocal(out=rtm[:, 3:4], in_=rtm[:, 2:3]), reads=[brt], writes=[brt])
    c.op("dve", lambda e: e.tensor_scalar(out=affo[:], in0=rt[:], scalar1=rtm[:, 3:4], scalar2=None, op0=ALU.mult), reads=[brt], writes=[baffo])
    c.dma("sp", lambda e: e.dma_start(out=aff_dram[rr0:rr0 + 128, :], in_=affo[:]), reads=[baffo])


def rope_tables_host():
    d = 64
    inv = (10000.0 ** (-np.arange(0, d, 2, dtype=np.float32) / d)).astype(np.float32)
    ang = np.arange(SEQ, dtype=np.float32)[:, None] * inv[None, :]
    cos, sin = np.cos(ang).astype(np.float32), np.sin(ang).astype(np.float32)
    COS = np.concatenate([cos, cos], 1).T
    SINS = np.concatenate([-sin, sin], 1).T
    COS = np.ascontiguousarray(np.concatenate([COS, COS], 0), dtype=np.float32)
    SINS = np.ascontiguousarray(np.concatenate([SINS, SINS], 0), dtype=np.float32)
    p = np.arange(128)[:, None]
    x = np.arange(3968)[None, :]
    dl = p - x + 1920
    m = (np.abs(dl) <= 64).astype(np.float32) + ((dl % 4 == 0) & (np.abs(dl) <= 256)) + ((dl % 16 == 0) & (np.abs(dl) <= 1024))
    return COS, SINS, np.ascontiguousarray(m.astype(np.float32))


def emit_mix_b(c, K, P, x_dram, w_in, w_out, lng, lnb, w_r, cos_d, sins_d, tab_d, x1_dram, aff_dram, nseq):
    Wout = c.sb("b_wout", [128, 8, D], BF16); bWout = Buf("wout")
    stg = [c.sb("b_stg%d" % i, [128, 8, 128], F32) for i in range(2)]; bstg = [Buf("stg0"), Buf("stg1")]
    st = [t[:].rearrange("p k n -> p (k n)") for t in stg]
    scnt = [0]
    load_cast_weight(c, Wout, bWout, w_out, D, st, bstg, scnt)
    WS = {n: c.sb("b_ws_" + n, [128, 8, 128], BF16) for n in ("q", "qs", "k", "ks", "v")}
    bWS = {n: Buf("ws_" + n) for n in WS}

    def load_w(name, col0, swap=False):
        dst, bd = WS[name], bWS[name]
        i = scnt[0] % 2; scnt[0] += 1
        src = w_in[:, col0:col0 + 128]
        if not swap:
            c.dma("sp", lambda e, i=i: e.dma_start(out=stg[i][:], in_=src.rearrange("(k p) n -> p k n", p=128)), writes=[bstg[i]])
        else:
            s5 = src.rearrange("(k p) (h two d) -> p k h two d", p=128, h=2, two=2)
            d5 = stg[i][:].rearrange("p k (h two d) -> p k h two d", h=2, two=2)
            for a in range(2):
                for hd in range(2):
                    c.dma("sp", lambda e, i=i, a=a, hd=hd: e.dma_start(out=d5[:, :, hd, 1 - a, :], in_=s5[:, :, hd, a, :]), writes=[bstg[i]])
        c.op("pool", lambda e, i=i: e.tensor_copy(out=dst[:], in_=stg[i][:]), reads=[bstg[i]], writes=[bd])

    Wr = c.sb("b_wr", [128, 8, 16], F32); bWr = Buf("wr")
    c.dma("sp", lambda e: e.dma_start(out=Wr[:], in_=w_r.rearrange("(k p) e -> p k e", p=128)), writes=[bWr])
    gam = c.sb("b_gam", [128, D], F32)
    bet = c.sb("b_bet", [128, D], F32)
    bgb = Buf("gb")
    c.dma("sp", lambda e: e.dma_start(out=gam[:], in_=lng.partition_broadcast(128)), writes=[bgb])
    c.dma("sp", lambda e: e.dma_start(out=bet[:], in_=lnb.partition_broadcast(128)), writes=[bgb])
    COS = c.sb("b_cos", [128, SEQ], F32); SINS = c.sb("b_sins", [128, SEQ], F32)
    TABf = c.sb("b_tabf", [128, 3968], F32); TAB = c.sb("b_tab", [128, 3968], BF16)
    btab = Buf("tab")
    c.dma("sp", lambda e: e.dma_start(out=COS[:], in_=cos_d), writes=[btab])
    c.dma("sp", lambda e: e.dma_start(out=SINS[:], in_=sins_d), writes=[btab])
    c.dma("sp", lambda e: e.dma_start(out=TABf[:], in_=tab_d), writes=[btab])
    c.op("pool", lambda e: e.tensor_copy(out=TAB[:], in_=TABf[:]), reads=[btab], writes=[btab])
    ones_f = K["ones_f"]
    ones_b = c.sb("b_onesb", [128, 128], BF16)
    c.op("pool", lambda e: e.tensor_copy(out=ones_b[:], in_=ones_f[:]), reads=[K["b"]], writes=[K["b"]])
    lntmp = {"st": c.sb("ln_st", [128, 2, 6], F32), "mv": c.sb("ln_mv", [128, 2], F32), "rs": c.sb("ln_rs", [128, 1], F32), "b": Buf("lnt")}

    xT = c.sb("b_xT", [128, 8, SEQ], BF16); bxT = Buf("xT")
    yT = c.sb("b_yT", [128, 8, SEQ], BF16); byT = [Buf("yT%d" % i) for i in range(8)]
    xin = [c.sb("b_xin%d" % i, [128, D], F32) for i in range(2)]; bxin = [Buf("xin0"), Buf("xin1")]
    QT = c.sb("b_QT", [128, SEQ], BF16); bQT = Buf("QT")
    KT = c.sb("b_KT", [128, SEQ], BF16); bKT = Buf("KT")
    Vv = c.sb("b_V", [128, NTILE, 128], BF16); bV = Buf("V")
    t1 = c.sb("b_t1", [128, 512], F32); bt1 = Buf("t1")
    t2 = c.sb("b_t2", [128, 512], F32); bt2 = Buf("t2")
    pexp = [c.sb("b_pexp%d" % i, [128, 512], BF16) for i in range(3)]; bpexp = [Buf("pexp%d" % i) for i in range(3)]
    PT = [c.sb("b_PT%d" % i, [128, 512], BF16) for i in range(3)]; bPT = [Buf("PT%d" % i) for i in range(3)]
    rec = c.sb("b_rec", [128, 512], F32); brec = Buf("rec")
    x1T = c.sb("b_x1T", [128, 8, 128], F32); bx1T = Buf("x1T")
    rt = c.sb("b_rt", [128, 16], F32); brt = Buf("rt")
    rtm = c.sb("b_rtm", [128, 4], F32)
    affo = [c.sb("b_aff%d" % i, [128, 16], F32) for i in range(2)]; baffo = [Buf("aff0"), Buf("aff1")]
    xc = [0]; pc = [0]; sc = [0]

    def proj_rope(wa, wb, dst, bdst):
        for n0 in range(0, SEQ, 512):
            for k in range(8):
                c.op("pe", lambda e, k=k, n0=n0: e.matmul(P.t[1][:], WS[wa][:, k, :], xT[:, k, n0:n0 + 512], start=(k == 0), stop=(k == 7)),
                     reads=[bWS[wa], bxT], writes=[P.b[1]])
            for k in range(8):
                c.op("pe", lambda e, k=k, n0=n0: e.matmul(P.t[2][:], WS[wb][:, k, :], xT[:, k, n0:n0 + 512], start=(k == 0), stop=(k == 7)),
                     reads=[bWS[wb], bxT], writes=[P.b[2]])
            c.op("dve", lambda e, n0=n0: e.tensor_tensor(out=t1[:], in0=P.t[1][:], in1=COS[:, n0:n0 + 512], op=ALU.mult), reads=[P.b[1], btab], writes=[bt1])
            c.op("dve", lambda e, n0=n0: e.tensor_tensor(out=t2[:], in0=P.t[2][:], in1=SINS[:, n0:n0 + 512], op=ALU.mult), reads=[P.b[2], btab], writes=[bt2])
            c.op("pool", lambda e, n0=n0: e.tensor_tensor(out=dst[:, n0:n0 + 512], in0=t1[:], in1=t2[:], op=ALU.add), reads=[bt1, bt2], writes=[bdst])

    for s in range(nseq):
        r0 = s * SEQ
        emit_xT(c, K, P, x_dram, r0, NTILE, xT, bxT, xin, bxin, xc)
        for hp in range(8):
            load_w("q", hp * 128); load_w("qs", hp * 128, swap=True)
            load_w("k", 1024 + hp * 128); load_w("ks", 1024 + hp * 128, swap=True)
            load_w("v", 2048 + hp * 128)
            proj_rope("q", "qs", QT, bQT)
            proj_rope("k", "ks", KT, bKT)
            proj_tm(c, P, WS["v"], bWS["v"], xT, bxT, 0, 128, lambda t0, gg: Vv[:, t0:t0 + gg, :], bV, 1.0, NTILE, pc)
            blocks = []
            for hh in range(2):
                for qb in range(4):
                    kts = [kt for kt in range(NTILE) if not (128 * kt - 512 * qb - 511 > 1024 or 128 * kt + 127 - 512 * qb < -1024)]
                    for j, kt in enumerate(kts):
                        blocks.append((hh, qb, j, kt, len(kts)))
            SB = (3, 4, 7)

            def front(bi, blocks=blocks):
                hh, qb, j, kt, n = blocks[bi]
                pb = hh * 64
                si = bi % 3
                psS = P.t[SB[si]]; bS = P.b[SB[si]]
                c.op("pe", lambda e, psS=psS, kt=kt, qb=qb, pb=pb: e.matmul(psS[:], KT[pb:pb + 64, kt * 128:(kt + 1) * 128], QT[pb:pb + 64, qb * 512:(qb + 1) * 512],
                                                                         start=True, stop=True), reads=[bKT, bQT], writes=[bS])
                c.op("act", lambda e, psS=psS, si=si: e.activation(out=pexp[si][:], in_=psS[:], func=AF.Exp, scale=0.125), reads=[bS], writes=[bpexp[si]])
                x0 = 512 * qb - 128 * kt + 1920
                c.op("dve", lambda e, si=si, x0=x0: e.tensor_tensor(out=PT[si][:], in0=pexp[si][:], in1=TAB[:, x0:x0 + 512], op=ALU.mult),
                     reads=[bpexp[si], btab], writes=[bPT[si]])

            def back(bi, hp=hp, blocks=blocks):
                hh, qb, j, kt, n = blocks[bi]
                pb = hh * 64
                si = bi % 3
                c.op("pe", lambda e, si=si, kt=kt, j=j, n=n: e.matmul(P.t[5][:], Vv[:, kt, :], PT[si][:], start=(j == 0), stop=(j == n - 1)),
                     reads=[bV, bPT[si]], writes=[P.b[5]])
                c.op("pe", lambda e, si=si, j=j, n=n: e.matmul(P.t[6][:], ones_b[:], PT[si][:], start=(j == 0), stop=(j == n - 1)),
                     reads=[K["b"], bPT[si]], writes=[P.b[6]])
                if j == n - 1:
                    c.op("dve", lambda e: e.reciprocal(out=rec[:], in_=P.t[6][:]), reads=[P.b[6]], writes=[brec])
                    c.op("dve", lambda e, pb=pb, hp=hp, qb=qb: e.tensor_tensor(out=yT[pb:pb + 64, hp, qb * 512:(qb + 1) * 512], in0=P.t[5][pb:pb + 64, :],
                                                                            in1=rec[pb:pb + 64, :], op=ALU.mult), reads=[P.b[5], brec], writes=[byT[hp]])
            LOOK = 2
            for bi in range(len(blocks) + LOOK):
                if bi < len(blocks):
                    front(bi)
                if bi >= LOOK:
                    back(bi - LOOK)
        for t in range(NTILE):
            i = xc[0] % 2; xc[0] += 1
            rr0 = r0 + t * 128
            c.dma("sp", lambda e, i=i, rr0=rr0: e.dma_start(out=xin[i][:], in_=x_dram[rr0:rr0 + 128, :]), writes=[bxin[i]])
            for hf in range(2):
                for k in range(8):
                    c.op("pe", lambda e, hf=hf, k=k, t=t: e.matmul(P.t[1 + hf][:], yT[:, k, t * 128:(t + 1) * 128], Wout[:, k, hf * 512:(hf + 1) * 512],
                                                                start=(k == 0), stop=(k == 7)), reads=[byT[k], bWout], writes=[P.b[1 + hf]])
                c.op("dve", lambda e, hf=hf, i=i: e.scalar_tensor_tensor(out=xin[i][:, hf * 512:(hf + 1) * 512], in0=xin[i][:, hf * 512:(hf + 1) * 512], scalar=float(ALPHA),
                                                                 in1=P.t[1 + hf][:], op0=ALU.mult, op1=ALU.add), reads=[bxin[i], P.b[1 + hf]], writes=[bxin[i]])
            emit_ln(c, xin[i], bxin[i], xin[i], bxin[i], gam, bet, bgb, lntmp)
            c.dma("sp", lambda e, i=i, rr0=rr0: e.dma_start(out=x1_dram[rr0:rr0 + 128, :], in_=xin[i][:]), reads=[bxin[i]])
            emit_router(c, K, P, xin[i], bxin[i], Wr, bWr, x1T, bx1T, rt, brt, rtm, affo[i], baffo[i], aff_dram, rr0)


NCORES = 8
TILES_P, TILES_S = 32, 64
E_ = 16
DFF_ = 1408


def build_ffn_program():
    NT = TILES_P + TILES_S
    n_p, n_s = NCORES * TILES_P, NCORES * TILES_S
    cap_p, cap_s = 2 * n_p * 128 // E_, 2 * n_s * 128 // E_
    nc = bass.Bass("TRN2", target_bir_lowering=False)
    di = lambda n, s: nc.dram_tensor(n, s, F32, kind="ExternalInput").ap()
    x = di("x", [NT * 128, D]); aff = di("aff", [NT * 128, E_]); affT = di("affT", [128, E_, n_p + n_s])
    wg = di("wg", [E_, D, DFF_]); wu = di("wu", [E_, D, DFF_]); wd = di("wd", [E_, DFF_, D])
    lng = di("lng", [D]); lnb = di("lnb", [D])
    y = nc.dram_tensor("y", [NT * 128, D], F32, kind="ExternalOutput").ap()
    c = Ctx(nc)
    K = emit_consts(c)
    emit_ffn2(c, K, x, aff, affT, wg, wu, wd, lng, lnb, y, TILES_P, TILES_S, n_p, n_s, cap_p, cap_s, E=E_, DFF=DFF_, TB=8, CAP=256)
    c.finish()
    return nc


def build_a_program(nseq=6):
    nc = bass.Bass("TRN2", target_bir_lowering=False)
    di = lambda n, s: nc.dram_tensor(n, s, F32, kind="ExternalInput").ap()
    x = di("x", [nseq * 2048, D]); w_in = di("w_in", [D, 3600]); w_out = di("w_out", [D, D]); gbias = di("gbias", [16]); ng = di("ng", [512])
    lng = di("lng", [D]); lnb = di("lnb", [D]); wr = di("wr", [D, 16]); tab = di("tab", [8, 3, 128, 16, 64])
    x1 = nc.dram_tensor("x1", [nseq * 2048, D], F32, kind="ExternalOutput").ap()
    aff = nc.dram_tensor("aff", [nseq * 2048, 16], F32, kind="ExternalOutput").ap()
    c = Ctx(nc); K = emit_consts(c); P = PSB(c)
    emit_mix_a(c, K, P, x, w_in, w_out, gbias, ng, lng, lnb, wr, tab, x1, aff, nseq)
    c.finish()
    return nc


def build_b_program(nseq=6):
    nc = bass.Bass("TRN2", target_bir_lowering=False)
    di = lambda n, s: nc.dram_tensor(n, s, F32, kind="ExternalInput").ap()
    x = di("x", [nseq * 2048, D]); w_in = di("w_in", [D, 3072]); w_out = di("w_out", [D, D])
    lng = di("lng", [D]); lnb = di("lnb", [D]); wr = di("wr", [D, 16]); cos = di("cos", [128, 2048]); sins = di("sins", [128, 2048]); tab = di("tab", [128, 3968])
    x1 = nc.dram_tensor("x1", [nseq * 2048, D], F32, kind="ExternalOutput").ap()
    aff = nc.dram_tensor("aff", [nseq * 2048, 16], F32, kind="ExternalOutput").ap()
    c = Ctx(nc); K = emit_consts(c); P = PSB(c)
    emit_mix_b(c, K, P, x, w_in, w_out, lng, lnb, wr, cos, sins, tab, x1, aff, nseq)
    c.finish()
    return nc


def _aff_layout(affs):
    def toT(a):
        return a.reshape(-1, 128, E_).transpose(1, 2, 0)
    ap = np.concatenate([a[:TILES_P * 128] for a in affs], 0)
    as_ = np.concatenate([a[TILES_P * 128:] for a in affs], 0)
    return np.ascontiguousarray(np.concatenate([toT(ap), toT(as_)], axis=2), dtype=np.float32)


def kernel(x_prompt, x_sample, even_w_in, ml_gate_bias, na_rpb, ml_norm_g, even_w_out, da_w_in, da_w_out,
           ln_mix_g, ln_mix_b, ec_router, ec_w_gate, ec_w_up, ec_w_down, ln_ffn_g, ln_ffn_b):
    f32 = lambda a: np.ascontiguousarray(np.asarray(a), dtype=np.float32)
    x_prompt, x_sample = f32(x_prompt), f32(x_sample)
    cores = list(range(NCORES))
    xs = [np.concatenate([x_prompt[2 * c:2 * c + 2].reshape(-1, D), x_sample[4 * c:4 * c + 4].reshape(-1, D)], 0) for c in cores]
    tabA = na_table_host(f32(na_rpb)[0])
    feedA = {"w_in": f32(even_w_in)[0], "w_out": f32(even_w_out)[0], "gbias": f32(ml_gate_bias)[0], "ng": f32(ml_norm_g)[0],
             "lng": f32(ln_mix_g)[0], "lnb": f32(ln_mix_b)[0], "wr": f32(ec_router)[0], "tab": tabA}
    ncA = build_a_program()
    res = run_bass_kernel_spmd(ncA, [dict(feedA, x=xs[c]) for c in cores], core_ids=cores)
    x1 = [res.results[c]["x1"] for c in cores]; aff = [res.results[c]["aff"] for c in cores]
    del ncA
    ncF = build_ffn_program()
    def run_ffn(xl, affl, layer):
        affT = _aff_layout(affl)
        feed = {"affT": affT, "wg": f32(ec_w_gate)[layer], "wu": f32(ec_w_up)[layer], "wd": f32(ec_w_down)[layer],
                "lng": f32(ln_ffn_g)[layer], "lnb": f32(ln_ffn_b)[layer]}
        r = run_bass_kernel_spmd(ncF, [dict(feed, x=xl[c], aff=affl[c]) for c in cores], core_ids=cores)
        return [r.results[c]["y"] for c in cores]
    x2 = run_ffn(x1, aff, 0)
    COS, SINS, TAB = rope_tables_host()
    feedB = {"w_in": f32(da_w_in)[0], "w_out": f32(da_w_out)[0], "lng": f32(ln_mix_g)[1], "lnb": f32(ln_mix_b)[1],
             "wr": f32(ec_router)[1], "cos": COS, "sins": SINS, "tab": TAB}
    ncB = build_b_program()
    res = run_bass_kernel_spmd(ncB, [dict(feedB, x=x2[c]) for c in cores], core_ids=cores)
    x3 = [res.results[c]["x1"] for c in cores]; aff2 = [res.results[c]["aff"] for c in cores]
    del ncB
    y = run_ffn(x3, aff2, 1)
    y_prompt = np.stack([y[c][:TILES_P * 128].reshape(2, 2048, D) for c in cores], 0).reshape(16, 2048, D)
    y_sample = np.stack([y[c][TILES_P * 128:].reshape(4, 2048, D) for c in cores], 0).reshape(32, 2048, D)
    return (np.ascontiguousarray(y_prompt, dtype=np.float32), np.ascontiguousarray(y_sample, dtype=np.float32))
```
